# Optimizing a Trainium2 kernel written in Bass

```python
import math
import numpy as np
import jax
import jax.numpy as jnp
from jax import lax

D_MODEL = 4096
BATCH = 2
SEQ = 4096
DEPTH = 1
DEC_BATCH = 16
DEC_SEQ = 32
PAST_LEN = 2048

CHUNK = 64
N_HEADS = 32
N_KV_HEADS = 8
HEAD_DIM = 128
ATTN_W = N_HEADS * HEAD_DIM
KV_W = N_KV_HEADS * HEAD_DIM
ROT_DIM = HEAD_DIM // 4
ROPE_THETA = 500000.0
IDX_HEADS = 32
IDX_DIM = 128
IDX_ROT_DIM = IDX_DIM // 4
TOPK_MAX = 256
Q_BLOCK = 128
LRU_W = D_MODEL
LRU_BLOCKS = 16
LRU_BW = LRU_W // LRU_BLOCKS
CONV_W = 4
LRU_C = 8.0
D_FF = -(-8 * D_MODEL // (3 * 256)) * 256
EPS = 1e-6
NEG = -1e30

IN_SIZES = (ATTN_W, KV_W, KV_W, IDX_HEADS * IDX_DIM, IDX_HEADS, IDX_DIM, LRU_W, LRU_W, D_MODEL, D_MODEL)
IN_SPLITS = tuple(int(s) for s in np.cumsum(IN_SIZES)[:-1])
IN_COLS = int(sum(IN_SIZES))

kernel_name = 'dsa_rglru_gated_hybrid_stream_step'


def rmsnorm(x, g):
    xf = x.astype(jnp.float32)
    y = xf * lax.rsqrt(jnp.mean(xf * xf, axis=-1, keepdims=True) + EPS) * g.astype(jnp.float32)
    return y.astype(x.dtype)


def rope_partial(x, pos, rot):
    half = rot // 2
    inv = ROPE_THETA ** (-jnp.arange(half, dtype=jnp.float32) * 2.0 / rot)
    ang = pos.astype(jnp.float32)[:, None] * inv[None, :]
    cos = jnp.cos(ang)[:, None, :]
    sin = jnp.sin(ang)[:, None, :]
    xf = x.astype(jnp.float32)
    x1 = xf[..., :half]
    x2 = xf[..., half:rot]
    out = jnp.concatenate([x1 * cos - x2 * sin, x2 * cos + x1 * sin, xf[..., rot:]], axis=-1)
    return out.astype(x.dtype)


def sparse_attend(q, iq, iw, q_pos, k_all, v_all, ik_all, k_pos, topk):
    f32 = jnp.float32
    B, Tq = q.shape[0], q.shape[1]
    s_idx = jnp.einsum('bthd,bsd->bths', iq.astype(f32), ik_all.astype(f32))
    s_idx = jnp.einsum('bths,bth->bts', jax.nn.relu(s_idx), iw.astype(f32))
    allowed = (k_pos[None, :] // CHUNK) <= (q_pos[:, None] // CHUNK)
    s_idx = jnp.where(allowed[None], s_idx, NEG)
    vals, sel = lax.top_k(s_idx, topk)
    valid = vals > 0.5 * NEG
    gather = jax.vmap(lambda kv, ix: kv[ix])
    k_sel = gather(k_all, sel)
    v_sel = gather(v_all, sel)
    qg = q.reshape(B, Tq, N_KV_HEADS, N_HEADS // N_KV_HEADS, HEAD_DIM).astype(f32)
    logits = jnp.einsum('btkgd,btskd->btkgs', qg, k_sel.astype(f32)) * (HEAD_DIM ** -0.5)
    logits = jnp.where(valid[:, :, None, None, :], logits, NEG)
    p = jax.nn.softmax(logits, axis=-1)
    o = jnp.einsum('btkgs,btskd->btkgd', p, v_sel.astype(f32))
    return o.reshape(B, Tq, ATTN_W).astype(q.dtype)


def rg_lru_branch(xl, conv0, h0, conv_w, conv_b, wa, ba, wx, bx, lam):
    f32 = jnp.float32
    B, T = xl.shape[0], xl.shape[1]
    xpad = jnp.concatenate([conv0.astype(xl.dtype), xl], axis=1)
    u = conv_b
    for j in range(CONV_W):
        u = u + xpad[:, j:j + T] * conv_w[j]
    new_conv = xpad[:, -(CONV_W - 1):]
    uf = u.astype(f32)
    ub = uf.reshape(B, T, LRU_BLOCKS, LRU_BW)
    r = jax.nn.sigmoid(jnp.einsum('btnc,ncd->btnd', ub, wa.astype(f32)).reshape(B, T, LRU_W) + ba.astype(f32))
    i = jax.nn.sigmoid(jnp.einsum('btnc,ncd->btnd', ub, wx.astype(f32)).reshape(B, T, LRU_W) + bx.astype(f32))
    log_a = -LRU_C * r * jax.nn.softplus(-lam.astype(f32))
    a = jnp.exp(log_a)
    b = jnp.sqrt(-jnp.expm1(2.0 * log_a)) * (i * uf)
    b = b.at[:, 0].add(a[:, 0] * h0.astype(f32))

    def combine(left, right):
        a1, b1 = left
        a2, b2 = right
        return a1 * a2, a2 * b1 + b2

    _, h = lax.associative_scan(combine, (a, b), axis=1)
    return h, h[:, -1], new_conv


def hybrid_layer(x, pos, past, conv0, h0, blocked, topk, p):
    B, T = x.shape[0], x.shape[1]
    xn = rmsnorm(x, p['norm_mix'])
    proj = xn @ p['w_in']
    q, k, v, iq, iw, ik, xl, yl, ga, gl = jnp.split(proj, list(IN_SPLITS), axis=-1)
    q = rope_partial(rmsnorm(q.reshape(B, T, N_HEADS, HEAD_DIM), p['norm_q']), pos, ROT_DIM)
    k = rope_partial(rmsnorm(k.reshape(B, T, N_KV_HEADS, HEAD_DIM), p['norm_k']), pos, ROT_DIM)
    v = v.reshape(B, T, N_KV_HEADS, HEAD_DIM)
    iq = rope_partial(iq.reshape(B, T, IDX_HEADS, IDX_DIM), pos, IDX_ROT_DIM)
    ik = rope_partial(rmsnorm(ik, p['norm_idx_k'])[:, :, None, :], pos, IDX_ROT_DIM)[:, :, 0, :]
    iw = iw * ((IDX_HEADS * IDX_DIM) ** -0.5)
    if past is None:
        k_all, v_all, ik_all, k_pos = k, v, ik, pos
    else:
        pk, pv, pik = past
        k_all = jnp.concatenate([pk.astype(k.dtype), k], axis=1)
        v_all = jnp.concatenate([pv.astype(v.dtype), v], axis=1)
        ik_all = jnp.concatenate([pik.astype(ik.dtype), ik], axis=1)
        k_pos = jnp.concatenate([jnp.arange(pk.shape[1], dtype=jnp.int32), pos])
    if blocked:
        nb = T // Q_BLOCK

        def to_blocks(a):
            return a.reshape((B, nb, Q_BLOCK) + a.shape[2:]).swapaxes(0, 1)

        xs = (to_blocks(q), to_blocks(iq), to_blocks(iw), pos.reshape(nb, Q_BLOCK))
        o = lax.map(lambda a: sparse_attend(a[0], a[1], a[2], a[3], k_all, v_all, ik_all, k_pos, topk), xs)
        o_attn = o.swapaxes(0, 1).reshape(B, T, ATTN_W)
    else:
        o_attn = sparse_attend(q, iq, iw, pos, k_all, v_all, ik_all, k_pos, topk)
    h_seq, h_last, conv_new = rg_lru_branch(xl, conv0, h0, p['conv_w'], p['conv_b'], p['lru_wa'],
                                            p['lru_ba'], p['lru_wx'], p['lru_bx'], p['lru_lambda'])
    o_lru = h_seq.astype(x.dtype) * jax.nn.gelu(yl)
    wb = p['w_branch']
    mixed = jax.nn.sigmoid(ga) * (o_attn @ wb[:ATTN_W]) + jax.nn.sigmoid(gl) * (o_lru @ wb[ATTN_W:])
    x = x + mixed @ p['w_out']
    xn2 = rmsnorm(x, p['norm_ffn'])
    x = x + (jax.nn.silu(xn2 @ p['w_gate']) * (xn2 @ p['w_up'])) @ p['w_down']
    return x, (k, v, ik, h_last, conv_new)


def setup_inputs(seed: int = 0) -> dict:
    key = jax.random.key(seed)
    ks = jax.random.split(key, 26)
    f32 = jnp.float32
    L = DEPTH

    def nrm(k, shape, scale):
        return scale * jax.random.normal(k, shape, f32)

    a0 = jax.random.uniform(ks[18], (L, LRU_W), f32, 0.9, 0.999)
    s = a0 ** (1.0 / LRU_C)
    lam = jnp.log(s) - jnp.log1p(-s)
    return {
        'x_prompt': nrm(ks[0], (BATCH, SEQ, D_MODEL), 1.0),
        'x_sample': nrm(ks[1], (DEC_BATCH, DEC_SEQ, D_MODEL), 1.0),
        'cache_k': nrm(ks[2], (L, DEC_BATCH, PAST_LEN, N_KV_HEADS, HEAD_DIM), 1.0),
        'cache_v': nrm(ks[3], (L, DEC_BATCH, PAST_LEN, N_KV_HEADS, HEAD_DIM), 1.0),
        'cache_idx_k': nrm(ks[4], (L, DEC_BATCH, PAST_LEN, IDX_DIM), 1.0),
        'state_lru': nrm(ks[5], (L, DEC_BATCH, LRU_W), 0.5),
        'state_conv': nrm(ks[6], (L, DEC_BATCH, CONV_W - 1, LRU_W), 1.0),
        'norm_mix': 1.0 + nrm(ks[7], (L, D_MODEL), 0.02),
        'w_in': nrm(ks[8], (L, D_MODEL, IN_COLS), D_MODEL ** -0.5),
        'norm_q': 1.0 + nrm(ks[9], (L, HEAD_DIM), 0.02),
        'norm_k': 1.0 + nrm(ks[10], (L, HEAD_DIM), 0.02),
        'norm_idx_k': 1.0 + nrm(ks[11], (L, IDX_DIM), 0.02),
        'conv_w': nrm(ks[12], (L, CONV_W, LRU_W), CONV_W ** -0.5),
        'conv_b': nrm(ks[13], (L, LRU_W), 0.01),
        'lru_wa': nrm(ks[14], (L, LRU_BLOCKS, LRU_BW, LRU_BW), LRU_BW ** -0.5),
        'lru_ba': nrm(ks[15], (L, LRU_W), 0.01),
        'lru_wx': nrm(ks[16], (L, LRU_BLOCKS, LRU_BW, LRU_BW), LRU_BW ** -0.5),
        'lru_bx': nrm(ks[17], (L, LRU_W), 0.01),
        'lru_lambda': lam,
        'w_branch': nrm(ks[19], (L, ATTN_W + LRU_W, D_MODEL), D_MODEL ** -0.5),
        'w_out': nrm(ks[20], (L, D_MODEL, D_MODEL), D_MODEL ** -0.5),
        'norm_ffn': 1.0 + nrm(ks[21], (L, D_MODEL), 0.02),
        'w_gate': nrm(ks[22], (L, D_MODEL, D_FF), D_MODEL ** -0.5),
        'w_up': nrm(ks[23], (L, D_MODEL, D_FF), D_MODEL ** -0.5),
        'w_down': nrm(ks[24], (L, D_FF, D_MODEL), D_FF ** -0.5),
    }


def reference(x_prompt, x_sample, cache_k, cache_v, cache_idx_k, state_lru, state_conv,
              norm_mix, w_in, norm_q, norm_k, norm_idx_k, conv_w, conv_b, lru_wa, lru_ba,
              lru_wx, lru_bx, lru_lambda, w_branch, w_out, norm_ffn, w_gate, w_up, w_down):
    Bp, Tp = x_prompt.shape[0], x_prompt.shape[1]
    Bs, Ts = x_sample.shape[0], x_sample.shape[1]
    past_len = cache_k.shape[2]
    topk_prompt = min(TOPK_MAX, Tp // 4)
    topk_sample = min(TOPK_MAX, (past_len + Ts) // 4)
    pos_p = jnp.arange(Tp, dtype=jnp.int32)
    pos_s = past_len + jnp.arange(Ts, dtype=jnp.int32)
    yp, ys = x_prompt, x_sample
    kp_l, vp_l, ikp_l, hp_l, cp_l = [], [], [], [], []
    ks_l, vs_l, iks_l, hs_l, cs_l = [], [], [], [], []
    for l in range(DEPTH):
        p = {'norm_mix': norm_mix[l], 'w_in': w_in[l], 'norm_q': norm_q[l], 'norm_k': norm_k[l],
             'norm_idx_k': norm_idx_k[l], 'conv_w': conv_w[l], 'conv_b': conv_b[l],
             'lru_wa': lru_wa[l], 'lru_ba': lru_ba[l], 'lru_wx': lru_wx[l], 'lru_bx': lru_bx[l],
             'lru_lambda': lru_lambda[l], 'w_branch': w_branch[l], 'w_out': w_out[l],
             'norm_ffn': norm_ffn[l], 'w_gate': w_gate[l], 'w_up': w_up[l], 'w_down': w_down[l]}
        conv0_p = jnp.zeros((Bp, CONV_W - 1, LRU_W), x_prompt.dtype)
        h0_p = jnp.zeros((Bp, LRU_W), jnp.float32)
        yp, (kp, vp, ikp, hp, cp) = hybrid_layer(yp, pos_p, None, conv0_p, h0_p, True, topk_prompt, p)
        ys, (kss, vss, ikss, hss, css) = hybrid_layer(
            ys, pos_s, (cache_k[l], cache_v[l], cache_idx_k[l]), state_conv[l], state_lru[l],
            False, topk_sample, p)
        kp_l.append(kp); vp_l.append(vp); ikp_l.append(ikp); hp_l.append(hp); cp_l.append(cp)
        ks_l.append(kss); vs_l.append(vss); iks_l.append(ikss); hs_l.append(hss); cs_l.append(css)
    k_prompt = jnp.stack(kp_l)
    v_prompt = jnp.stack(vp_l)
    idx_k_prompt = jnp.stack(ikp_l)
    lru_prompt = jnp.stack(hp_l)
    conv_prompt = jnp.stack(cp_l)
    k_sample = jnp.stack(ks_l)
    v_sample = jnp.stack(vs_l)
    idx_k_sample = jnp.stack(iks_l)
    lru_sample = jnp.stack(hs_l)
    conv_sample = jnp.stack(cs_l)
    return (yp, ys, k_prompt, v_prompt, idx_k_prompt, lru_prompt, conv_prompt,
            k_sample, v_sample, idx_k_sample, lru_sample, conv_sample)
```

```python
import math
from contextlib import ExitStack

import numpy as np
import concourse.bass as bass
import concourse.mybir as mybir
from concourse.bass_utils import run_bass_kernel_spmd

F32 = mybir.dt.float32
BF16 = mybir.dt.bfloat16
AF = mybir.ActivationFunctionType
ALU = mybir.AluOpType
AX = mybir.AxisListType

D = 4096
SEQ = 4096
NB = 32
DEC_SEQ = 32
PAST = 2048
SS = PAST + DEC_SEQ
NH = 32
NKV = 8
HD = 128
DFF = 11008
KC = 32
TO = 1088
EPS = 1e-6
C_Q, C_K, C_V, C_IQ, C_IW, C_IK, C_XL, C_YL, C_GA, C_GL = (
    0, 4096, 5120, 6144, 10240, 10272, 10400, 14496, 18592, 22688)
IN_COLS = 26784
NEGM = -1.0e30


class Op:
    __slots__ = ("eng", "fn", "deps", "dma", "signal", "pos", "sem", "semval", "K", "waits",
                 "barrier")

    def __init__(self, eng, fn, dma=False, barrier=False):
        self.eng = eng
        self.fn = fn
        self.deps = ()
        self.dma = dma
        self.signal = False
        self.pos = 0
        self.sem = None
        self.semval = 0
        self.K = None
        self.waits = ()
        self.barrier = barrier


class Sched:
    CE = ("pe", "act", "dve", "pool", "sp")
    NDSEM = 10

    def __init__(self):
        self.ops = []
        self.last_w = {}
        self.readers = {}
        self.last_op = {e: None for e in self.CE}

    def add(self, eng, fn, r=(), w=(), dma=False):
        idx = len(self.ops)
        op = Op(eng, fn, dma=dma)
        raw = set()
        oth = set()
        for t in r:
            lw = self.last_w.get(t)
            if lw is not None:
                raw.add(lw)
        for t in w:
            lw = self.last_w.get(t)
            if lw is not None:
                oth.add(lw)
            rs = self.readers.get(t)
            if rs:
                oth.update(rs)
        deps = set()
        for d in raw | oth:
            dop = self.ops[d]
            if (not dop.dma) and (not dma) and dop.eng == eng:
                if eng == "pe":
                    continue
                if d not in raw:
                    continue
            deps.add(d)
            if not dop.dma:
                dop.signal = True
        op.deps = tuple(sorted(deps))
        for t in r:
            self.readers.setdefault(t, []).append(idx)
        for t in w:
            self.last_w[t] = idx
            self.readers[t] = []
        self.ops.append(op)
        if not dma:
            self.last_op[eng] = idx
        return idx

    def barrier(self):
        lasts = dict(self.last_op)
        for e in self.CE:
            op = Op(e, None, barrier=True)
            deps = set()
            for e2, li in lasts.items():
                if li is not None and e2 != e:
                    deps.add(li)
                    self.ops[li].signal = True
            op.deps = tuple(sorted(deps))
            self.ops.append(op)
        self.last_w = {}
        self.readers = {}

    def analyze(self):
        CE = self.CE
        K = {e: {} for e in CE}
        pos = {e: 0 for e in CE}
        sig = {e: 0 for e in CE}
        sigcount = {e: {} for e in CE}
        dq = {e: {"next": 0, "cum": [0] * self.NDSEM, "last": [None] * self.NDSEM} for e in CE}
        for op in self.ops:
            E = op.eng
            KE = K[E]
            waits = {}

            def need(key, val, kafter):
                if KE.get(key, 0) >= val:
                    return
                if waits.get(key, 0) < val:
                    waits[key] = val
                if kafter:
                    for k2, v2 in kafter.items():
                        if KE.get(k2, 0) < v2:
                            KE[k2] = v2
                if KE.get(key, 0) < val:
                    KE[key] = val

            for d in op.deps:
                dop = self.ops[d]
                if dop.dma:
                    need(("S", dop.eng, dop.sem), dop.semval, dop.K)
                else:
                    need(dop.eng, dop.pos, dop.K)
            if op.barrier:
                for q in CE:
                    for s in range(self.NDSEM):
                        if dq[q]["cum"][s] > 0:
                            lo = dq[q]["last"][s]
                            need(("S", q, s), dq[q]["cum"][s], lo.K if lo else None)
            if op.dma:
                q = dq[E]
                s = q["next"]
                q["next"] = (s + 1) % self.NDSEM
                if q["last"][s] is not None:
                    need(("S", E, s), q["cum"][s], q["last"][s].K)
                q["cum"][s] += 16
                op.sem = s
                op.semval = q["cum"][s]
                q["last"][s] = op
                op.K = dict(KE)
            elif not op.barrier:
                pos[E] += 1
                op.pos = pos[E]
                if op.signal:
                    sig[E] += 1
                    sigcount[E][op.pos] = sig[E]
                    kk = dict(KE)
                    kk[E] = op.pos
                    op.K = kk
            op.waits = tuple(waits.items())
        self.sigcount = sigcount
        self.final_dma = {e: list(dq[e]["cum"]) for e in CE}

    def emit(self, nc, block, esem, dsem):
        self.analyze()
        per = {e: [] for e in self.CE}
        for op in self.ops:
            per[op.eng].append(op)
        sigcount = self.sigcount

        def run(e, eng):
            for op in per[e]:
                for key, val in op.waits:
                    if isinstance(key, tuple):
                        eng.wait_ge(dsem[key[1]][key[2]], val)
                    else:
                        eng.wait_ge(esem[key], sigcount[key][val])
                if op.fn is None:
                    continue
                ins = op.fn(eng)
                if op.dma:
                    ins.then_inc(dsem[e][op.sem], 16)
                elif op.signal:
                    ins.then_inc(esem[e], 1)
            if e == "sp":
                for q in self.CE:
                    for s, v in enumerate(self.final_dma[q]):
                        if v > 0:
                            eng.wait_ge(dsem[q][s], v)

        @block.tensor
        def _(eng):
            run("pe", eng)

        @block.scalar
        def _(eng):
            run("act", eng)

        @block.vector
        def _(eng):
            run("dve", eng)

        @block.gpsimd
        def _(eng):
            run("pool", eng)

        @block.sync
        def _(eng):
            run("sp", eng)


class Arena:
    def __init__(self, t, words):
        self.t = t
        self.words = words
        self.off = 0

    def alloc(self, free_shape, dtype, parts=128):
        n = 1
        for s in free_shape:
            n *= s
        nbytes = n * (2 if dtype == BF16 else 4)
        w = (nbytes + 31) // 32 * 8
        off = self.off
        assert off + w <= self.words, f"arena overflow {off + w} > {self.words}"
        self.off += w
        ap = self.t[0:parts, off:off + (nbytes + 3) // 4]
        if dtype == BF16:
            ap = ap.bitcast(BF16)
            if ap.shape[1] != n:
                ap = ap[:, 0:n]
        if len(free_shape) == 2:
            ap = ap.rearrange("p (a b) -> p a b", a=free_shape[0])
        elif len(free_shape) == 3:
            ap = ap.rearrange("p (a b c) -> p a b c", a=free_shape[0], b=free_shape[1])
        return ap

    def mark(self):
        return self.off

    def release(self, m):
        self.off = m


class Builder:
    def __init__(self, debug=False, phases=None, feed=None):
        self.debug = debug
        self.feed = feed or set()
        self.phases = phases
        self.nc = bass.Bass("TRN2", target_bir_lowering=False)
        self.S = Sched()
        self.uid = 0

    def fresh(self, p):
        self.uid += 1
        return (p, self.uid)

    def din(self, name, shape, dt=F32):
        return self.nc.dram_tensor(name, list(shape), dt, kind="ExternalInput").ap()

    def dout(self, name, shape, dt=F32):
        return self.nc.dram_tensor(name, list(shape), dt, kind="ExternalOutput").ap()

    def dscr(self, name, shape, dt=F32):
        kind = "ExternalOutput" if (self.debug and name in self.debug) else "Internal"
        if name in self.feed:
            kind = "ExternalInput"
        return self.nc.dram_tensor(name, list(shape), dt, kind=kind).ap()

    def dma(self, q, out, in_, r=(), w=()):
        self.S.add(q, lambda e: e.dma_start(out=out, in_=in_), r=r, w=w, dma=True)

    def act(self, out, in_, func, r=(), w=(), **kw):
        self.S.add("act", lambda e: e.activation(out=out, in_=in_, func=func, **kw), r=r, w=w)

    def tt(self, out, in0, in1, op, r=(), w=(), eng="dve"):
        self.S.add(eng, lambda e: e.tensor_tensor(out=out, in0=in0, in1=in1, op=op), r=r, w=w)

    def ts(self, out, in0, s1, s2, op0, op1=None, r=(), w=(), eng="dve", **kw):
        if op1 is None:
            self.S.add(eng, lambda e: e.tensor_scalar(out=out, in0=in0, scalar1=s1, scalar2=None,
                                                      op0=op0, **kw), r=r, w=w)
        else:
            self.S.add(eng, lambda e: e.tensor_scalar(out=out, in0=in0, scalar1=s1, scalar2=s2,
                                                      op0=op0, op1=op1, **kw), r=r, w=w)

    def stt(self, out, in0, scalar, in1, op0, op1, r=(), w=()):
        self.S.add("dve", lambda e: e.scalar_tensor_tensor(out=out, in0=in0, scalar=scalar, in1=in1,
                                                           op0=op0, op1=op1), r=r, w=w)

    def copy(self, eng, out, in_, r=(), w=()):
        if eng == "act":
            self.S.add("act", lambda e: e.activation(out=out, in_=in_, func=AF.Copy), r=r, w=w)
        else:
            self.S.add(eng, lambda e: e.tensor_copy(out=out, in_=in_), r=r, w=w)

    def mm_group(self, out, pairs, r=(), w=()):
        n = len(pairs)

        def fn(e):
            ins = None
            for i, (l, rr) in enumerate(pairs):
                ins = e.matmul(out, l, rr, start=(i == 0), stop=(i == n - 1))
            return ins
        self.S.add("pe", fn, r=r, w=w)

    def mm(self, out, lhsT, rhs, start, stop, r=(), w=()):
        self.S.add("pe", lambda e: e.matmul(out, lhsT, rhs, start=start, stop=stop), r=r, w=w)

    def transpose(self, out, in_, ident, r=(), w=()):
        self.S.add("pe", lambda e: e.transpose(out, in_, ident), r=r, w=w)


def build_program(debug=None, phases=None, feed=None):
    B = Builder(debug=debug, phases=phases, feed=feed)
    nc = B.nc
    S = B.S
    ALL = {"seq", "cache", "lru", "own", "attn", "mix", "ffn"}
    ph = set(phases) if phases is not None else ALL
    full = ph == ALL

    _in = {}

    def inp(name, shape):
        if name not in _in:
            _in[name] = B.din(name, shape)
        return _in[name]

    KT_scr = B.dscr("KT_scr", [NKV, 128, SEQ], BF16)
    V_scr = B.dscr("V_scr", [SEQ, NKV * HD], BF16)
    ikT_scr = B.dscr("ikT_scr", [128, SEQ], BF16)
    KT_s = B.dscr("KT_s", [2, NKV, 128, SS], BF16)
    V_s = B.dscr("V_s", [2, SS, NKV * HD], BF16)
    ikT_s = B.dscr("ikT_s", [2, 128, SS], BF16)
    xlT_scr = B.dscr("xlT_scr", [KC, 128, SEQ + 64], F32)
    hown_scr = B.dscr("hown_scr", [KC, 128, TO], F32)
    QT_scr = B.dscr("QT_scr", [NH, 128, TO], BF16)
    iqT_scr = B.dscr("iqT_scr", [NH, 128, TO], BF16)
    iw_scr = B.dscr("iw_scr", [TO, 32], F32)
    olruT_scr = B.dscr("olruT_scr", [KC, 128, TO], BF16)
    saT_scr = B.dscr("saT_scr", [KC, 128, TO], F32)
    slT_scr = B.dscr("slT_scr", [KC, 128, TO], F32)
    oattT_scr = B.dscr("oattT_scr", [NH, 128, TO], BF16)
    mixT_scr = B.dscr("mixT_scr", [KC, 128, TO], BF16)
    x1_scr = B.dscr("x1_scr", [TO, D], F32)
    hT_scr = B.dscr("hT_scr", [DFF // 128, 128, TO], BF16)

    with ExitStack() as es:
        ARENA_KB = 206
        arena_t = es.enter_context(nc.sbuf_tensor("arena", [128, ARENA_KB * 256], F32))
        A = Arena(arena_t, ARENA_KB * 256)
        psum = [es.enter_context(nc.psum_tensor(f"ps{i}", [128, 512], F32)) for i in range(8)]
        esem = {e: es.enter_context(nc.semaphore(f"e_{e}")) for e in Sched.CE}
        dsem = {e: [es.enter_context(nc.semaphore(f"d_{e}{i}")) for i in range(Sched.NDSEM)]
                for e in ("act", "pool", "sp")}
        dsem["pe"] = dsem["sp"]
        dsem["dve"] = dsem["sp"]

        ident_in = inp("ident", [128, 128])
        ident_b = A.alloc([128], BF16)
        ident_f = A.alloc([128], F32)
        B.dma("pool", ident_b, ident_in, w=["ident_b"])
        B.dma("sp", ident_f, ident_in, w=["ident_f"])
        eps_t = A.alloc([1], F32)
        S.add("dve", lambda e: e.memset(eps_t, EPS), w=["eps"])
        one_t = A.alloc([1], F32)
        S.add("dve", lambda e: e.memset(one_t, 1.0), w=["one"])
        g4 = {}

        def load_g4(nm):
            src = inp("norm_" + nm, [HD])
            t = A.alloc([4, 128], F32)
            for j in range(4):
                B.dma("sp", t[:, j, :], src.partition_broadcast(128), w=[("g4", nm, j)])
            g4[nm] = t

        if "seq" in ph:
            load_g4("k")
            load_g4("idx_k")
        if "own" in ph:
            load_g4("q")
        S.barrier()

        psn = [0]

        def next_ps():
            pb = psn[0] % 4
            psn[0] += 1
            return pb

        def norm_transpose(blocks, xnT, gT, xt, junk, ssb, extra=None):
            t0 = 0
            for bi, (src, rows) in enumerate(blocks):
                B.dma("sp", xt[0:rows], src, w=["xt"])
                if extra is not None:
                    extra(bi, rows)
                ss = ssb[bi % 2]
                sk = ("ssb", bi % 2)
                S.add("act", lambda e, rows=rows, ss=ss: e.activation(
                    out=junk[0:rows], in_=xt[0:rows], func=AF.Square, accum_out=ss[0:rows]),
                    r=["xt"], w=["junk", sk])
                B.act(ss[0:rows], ss[0:rows], AF.Sqrt, r=[sk, "eps"], w=[sk], scale=1.0 / D,
                      bias=eps_t[0:rows])
                S.add("dve", lambda e, ss=ss, rows=rows: e.reciprocal(out=ss[0:rows], in_=ss[0:rows]),
                      r=[sk], w=[sk])
                B.ts(xt[0:rows], xt[0:rows], ss[0:rows], None, ALU.mult, r=["xt", sk], w=["xt"])
                for j in range(8):
                    pb = 4 + (j % 2)
                    pv = psum[pb][:, :].rearrange("p (a b) -> p a b", a=4)
                    for q in range(4):
                        kc = j * 4 + q
                        B.transpose(pv[:, q, 0:rows], xt[0:rows, kc * 128:(kc + 1) * 128],
                                    ident_f[0:rows, 0:rows], r=["xt", "ident_f"], w=[("ps", pb)])
                    B.tt(xnT[:, j * 4:(j + 1) * 4, t0:t0 + rows], pv[:, :, 0:rows],
                         gT[:, j * 4:(j + 1) * 4].unsqueeze(2).to_broadcast([128, 4, rows]),
                         ALU.mult, r=[("ps", pb), "gT"], w=[("AT", bi)])
                t0 += rows

        wl = {}

        def load_w(wbufs, src, kcn, ncols):
            wid = id(wbufs)
            slot = wl.get(wid, 0) % len(wbufs)
            wl[wid] = wl.get(wid, 0) + 1
            slot = (wid, slot)
            v = src.rearrange("(kc p) n -> p kc n", p=128)
            step = 8 if ncols > 256 else 16
            for k0 in range(0, kcn, step):
                k1 = min(kcn, k0 + step)
                B.dma("pool", wbufs[slot[1]][:, k0:k1, 0:ncols], v[:, k0:k1, :], w=[("w", slot, k0 // 8)] +
                      ([("w", slot, k0 // 8 + 1)] if step == 16 else []))
            return slot

        def wtoks(slot, kcn):
            return [("w", slot, k) for k in range((kcn + 7) // 8)]

        def gemm_tok(AT, blocks, kcn, wbufs, wsrc, col_groups, evac, attoks=None):
            nxt = load_w(wbufs, wsrc(col_groups[0][0], col_groups[0][1]), kcn, col_groups[0][1])
            for gi, (c0, ncols, tag) in enumerate(col_groups):
                slot = nxt
                if gi + 1 < len(col_groups):
                    nxt = load_w(wbufs, wsrc(col_groups[gi + 1][0], col_groups[gi + 1][1]), kcn, col_groups[gi + 1][1])
                t0 = 0
                for bi, rows in enumerate(blocks):
                    pb = next_ps()
                    ps = psum[pb][0:rows, 0:ncols]
                    B.mm_group(ps, [(AT[:, kc, t0:t0 + rows], wbufs[slot[1]][:, kc, 0:ncols])
                                    for kc in range(kcn)],
                               r=(attoks if attoks is not None else [("AT", bi)]) + wtoks(slot, kcn),
                               w=[("ps", pb)])
                    evac(tag, bi, rows, t0, ps, ("ps", pb), ncols)
                    t0 += rows

        def gemm_feat(srcs, nblk, kcn, ncols_total, cgw, ttiles, evac, attoks=None):
            nxts = [load_w(wb, wsrc(0, cgw), kcn, cgw) for (AT, wb, wsrc) in srcs]
            for c0 in range(0, ncols_total, cgw):
                slots = nxts
                if c0 + cgw < ncols_total:
                    nxts = [load_w(wb, wsrc(c0 + cgw, cgw), kcn, cgw) for (AT, wb, wsrc) in srcs]
                for ch in range(cgw // 128):
                    chunk = (c0 // 128) + ch
                    for (tt0, n) in ttiles:
                        pss = []
                        for (AT, wb, wsrc), slot in zip(srcs, slots):
                            pb = next_ps()
                            ps = psum[pb][:, 0:n]
                            B.mm_group(ps, [(wb[slot[1]][:, kc, ch * 128:(ch + 1) * 128], AT[:, kc, tt0:tt0 + n])
                                            for kc in range(kcn)],
                                       r=(attoks if attoks is not None else [("AT", bi) for bi in range(nblk)])
                                       + wtoks(slot, kcn), w=[("ps", pb)])
                            pss.append((ps, ("ps", pb)))
                        evac(chunk, tt0, n, pss)

        stc = [0]

        def make_headproc(NST):
            st = dict(
                f=[A.alloc([512], F32) for _ in range(NST)],
                t=[A.alloc([512], F32) for _ in range(NST)],
                b=[A.alloc([512], BF16) for _ in range(NST)],
                T=[A.alloc([4, 128], BF16) for _ in range(NST)],
                s=[A.alloc([4], F32) for _ in range(NST)],
                r=[A.alloc([4, 16], F32) for _ in range(4 * NST)],
                n=NST)
            return st

        def headproc(st, ps, pstok, rows, ncols, gname, cs, cstok, dst_out, dstT_fn, rope=True):
            nh = ncols // 128
            i = stc[0] % st["n"]
            stc[0] += 1
            f, t, b_, T_, s_ = st["f"][i], st["t"][i], st["b"][i], st["T"][i], st["s"][i]
            r4 = st["r"][4 * i:4 * i + 4]
            tk = ("st", i)
            B.copy("act", f[0:rows, 0:ncols], ps, r=[pstok], w=[tk])
            fv = f[0:rows, 0:ncols].rearrange("p (h d) -> p h d", h=nh)
            tv = t[0:rows, 0:ncols].rearrange("p (h d) -> p h d", h=nh)
            if gname is not None:
                for h in range(nh):
                    S.add("act", lambda e, h=h: e.activation(out=t[0:rows, h * 128:(h + 1) * 128],
                                                            in_=f[0:rows, h * 128:(h + 1) * 128], func=AF.Square,
                                                            accum_out=s_[0:rows, h:h + 1]), r=[tk], w=[tk])
                B.act(s_[0:rows, 0:nh], s_[0:rows, 0:nh], AF.Sqrt, r=[tk, "eps"], w=[tk], scale=1.0 / 128,
                      bias=eps_t[0:rows])
                S.add("dve", lambda e: e.reciprocal(out=s_[0:rows, 0:nh], in_=s_[0:rows, 0:nh]), r=[tk], w=[tk])
                B.tt(fv, fv, s_[0:rows, 0:nh].unsqueeze(2).to_broadcast([rows, nh, 128]), ALU.mult,
                     r=[tk], w=[tk])
                B.tt(fv, fv, g4[gname][0:rows, 0:nh, :], ALU.mult,
                     r=[tk] + [("g4", gname, j) for j in range(4)], w=[tk])
            if rope:
                cb = cs[0:rows, 0:16].unsqueeze(1).to_broadcast([rows, nh, 16])
                sb = cs[0:rows, 16:32].unsqueeze(1).to_broadcast([rows, nh, 16])
                x1 = fv[:, :, 0:16]
                x2 = fv[:, :, 16:32]
                ra, rb, rc, rd = [q[0:rows, 0:nh, :] for q in r4]
                rt = [tk, cstok]
                tkp = ("stp", i)
                tkd = ("std", i)
                B.tt(ra, x1, cb, ALU.mult, r=rt, w=[tkp], eng="pool")
                B.tt(rb, x2, sb, ALU.mult, r=rt, w=[tkp], eng="pool")
                B.tt(rc, x2, cb, ALU.mult, r=rt, w=[tkd])
                B.tt(rd, x1, sb, ALU.mult, r=rt, w=[tkd])
                B.tt(x1, ra, rb, ALU.subtract, r=[tkp, tkd], w=[tk], eng="pool")
                B.tt(x2, rc, rd, ALU.add, r=[tkd, tkp], w=[tk])
            if dst_out is not None:
                B.dma("sp", dst_out, f[0:rows, 0:ncols], r=[tk], w=[B.fresh("o")])
            if dstT_fn is not None:
                B.copy("act", b_[0:rows, 0:ncols], f[0:rows, 0:ncols], r=[tk], w=[tk])
                pb = 6 + (stc[0] % 2)
                pT = psum[pb][:, :].bitcast(BF16)[:, 0:512].rearrange("p (a b) -> p a b", a=4)
                for h in range(nh):
                    B.transpose(pT[:, h, 0:rows], b_[0:rows, h * 128:(h + 1) * 128],
                                ident_b[0:rows, 0:rows], r=[tk, "ident_b"], w=[("ps", pb)])
                B.copy("dve", T_[:, 0:nh, 0:rows], pT[:, 0:nh, 0:rows], r=[("ps", pb)], w=[tk])
                dstT_fn(T_, nh, rows, tk)

        if "seq" in ph:
            xseq = inp("xseq", [SEQ, D])
            xown = inp("xown", [TO, D])
            w_in = inp("w_in", [D, IN_COLS])
            cs_seq = inp("cs_seq", [SEQ + 64, 32])
            o_k = B.dout("o_k", [SEQ + 64, NKV * HD])
            o_v = B.dout("o_v", [SEQ + 64, NKV * HD])
            o_ik = B.dout("o_ik", [SEQ + 64, HD])
            m0 = A.mark()
            gmixT = A.alloc([KC], F32)
            B.dma("sp", gmixT, inp("norm_mixT", [128, KC]), w=["gT"])
            xnT = A.alloc([KC, TO], BF16)
            wbufs = [A.alloc([KC, 512], BF16) for _ in range(2)]
            xt = A.alloc([D], F32)
            junk = A.alloc([D], BF16)
            ssb = [A.alloc([1], F32) for _ in range(2)]
            st = make_headproc(3)
            cst = [A.alloc([32], F32) for _ in range(9)]
            xl_st = [A.alloc([TO], F32) for _ in range(2)]

            for tt in range(4):
                blocks = [(xseq[tt * 1024 + j * 128: tt * 1024 + (j + 1) * 128, :], 128) for j in range(8)]
                orows = [tt * 1024 + j * 128 for j in range(8)]
                if tt == 3:
                    blocks.append((xown[1024:1088, :], 64))
                    orows.append(SEQ)
                ntok = sum(b[1] for b in blocks)

                def extra(bi, rows, orows=orows):
                    B.dma("sp", cst[bi][0:rows], cs_seq[orows[bi]:orows[bi] + rows, :], w=[("cst", bi)])
                norm_transpose(blocks, xnT, gmixT, xt, junk, ssb, extra)

                def evac(tag, bi, rows, t0, ps, pstok, ncols, orows=orows):
                    kind, half = tag
                    orow = orows[bi]
                    is_s = orow >= SEQ
                    if kind == "k":
                        def dstT(T_, nh, rows_, tk):
                            if not is_s:
                                B.dma("act", KT_scr[half * 4:half * 4 + 4, :, orow:orow + rows_]
                                      .rearrange("h d t -> d h t"), T_[:, 0:4, 0:rows_], r=[tk], w=[B.fresh("kt")])
                            else:
                                for sq in range(2):
                                    B.dma("act", KT_s[sq, half * 4:half * 4 + 4, :, PAST:SS]
                                          .rearrange("h d t -> d h t"), T_[:, 0:4, sq * 32:(sq + 1) * 32],
                                          r=[tk], w=[B.fresh("kt")])
                        headproc(st, ps, pstok, rows, ncols, "k", cst[bi], ("cst", bi),
                                 o_k[orow:orow + rows, half * 512:(half + 1) * 512], dstT)
                    elif kind == "ik":
                        def dstT(T_, nh, rows_, tk):
                            if not is_s:
                                B.dma("act", ikT_scr[:, orow:orow + rows_], T_[:, 0, 0:rows_], r=[tk],
                                      w=[B.fresh("ikt")])
                            else:
                                for sq in range(2):
                                    B.dma("act", ikT_s[sq, :, PAST:SS], T_[:, 0, sq * 32:(sq + 1) * 32],
                                          r=[tk], w=[B.fresh("ikt")])
                        headproc(st, ps, pstok, rows, ncols, "idx_k", cst[bi], ("cst", bi),
                                 o_ik[orow:orow + rows, :], dstT)
                    else:
                        i = stc[0] % st["n"]
                        stc[0] += 1
                        tk = ("st", i)
                        B.copy("act", st["f"][i][0:rows, 0:512], ps, r=[pstok], w=[tk])
                        B.copy("dve", st["b"][i][0:rows, 0:512], ps, r=[pstok], w=[tk])
                        B.dma("sp", o_v[orow:orow + rows, half * 512:(half + 1) * 512],
                              st["f"][i][0:rows, 0:512], r=[tk], w=[B.fresh("o")])
                        if not is_s:
                            B.dma("act", V_scr[orow:orow + rows, half * 512:(half + 1) * 512],
                                  st["b"][i][0:rows, 0:512], r=[tk], w=[B.fresh("v")])
                        else:
                            for sq in range(2):
                                B.dma("act", V_s[sq, PAST:SS, half * 512:(half + 1) * 512],
                                      st["b"][i][sq * 32:(sq + 1) * 32, 0:512], r=[tk], w=[B.fresh("v")])

                gemm_tok(xnT, [b[1] for b in blocks], KC, wbufs, lambda c0, n: w_in[:, c0:c0 + n],
                         [(C_K, 512, ("k", 0)), (C_K + 512, 512, ("k", 1)), (C_V, 512, ("v", 0)),
                          (C_V + 512, 512, ("v", 1)), (C_IK, 128, ("ik", 0))], evac)

                ttiles = [(0, 512), (512, 512)] + ([(1024, 64)] if tt == 3 else [])
                col0 = tt * 1024

                def evac_xl(chunk, tt0, n, pss, ntok=ntok, col0=col0, last=ttiles[-1][0]):
                    xs = xl_st[chunk % 2]
                    ps, pstok = pss[0]
                    B.copy("act" if (tt0 // 512) % 2 else "dve", xs[:, tt0:tt0 + n], ps, r=[pstok],
                           w=[("xls", chunk % 2)])
                    if tt0 == last:
                        B.dma("sp", xlT_scr[chunk, :, col0:col0 + ntok], xs[:, 0:ntok],
                              r=[("xls", chunk % 2)], w=[B.fresh("xl")])
                gemm_feat([(xnT, wbufs, lambda c0, n: w_in[:, C_XL + c0:C_XL + c0 + n])], len(blocks), KC,
                          4096, 512, ttiles, evac_xl)
            S.barrier()
            A.release(m0)

        if "cache" in ph:
            cache_k = inp("cache_k", [2, PAST, NKV * HD])
            cache_v = inp("cache_v", [2, PAST, NKV * HD])
            cache_ik = inp("cache_ik", [2, PAST, HD])
            m0 = A.mark()
            kb = A.alloc([16, 1024], BF16)
            vb = A.alloc([16, 1024], BF16)
            ib = A.alloc([16, 128], BF16)
            stg = [A.alloc([8, 128], BF16) for _ in range(2)]
            istg = A.alloc([16, 128], BF16)
            for sq in range(2):
                for q in range(4):
                    B.dma("pool", kb[:, q * 4:(q + 1) * 4, :],
                          cache_k[sq, q * 512:(q + 1) * 512, :].rearrange("(b p) c -> p b c", p=128), w=[("kb", q)])
                    B.dma("pool", vb[:, q * 4:(q + 1) * 4, :],
                          cache_v[sq, q * 512:(q + 1) * 512, :].rearrange("(b p) c -> p b c", p=128), w=[("vb", q)])
                B.dma("pool", ib, cache_ik[sq].rearrange("(b p) c -> p b c", p=128), w=["ib"])
                for q in range(4):
                    B.dma("act", V_s[sq, q * 512:(q + 1) * 512, :].rearrange("(b p) c -> p b c", p=128),
                          vb[:, q * 4:(q + 1) * 4, :], r=[("vb", q)], w=[B.fresh("vs")])
                for blk in range(16):
                    pb = 6 + (blk % 2)
                    pT = psum[pb][:, :].bitcast(BF16).rearrange("p (a b) -> p a b", a=8)
                    for h in range(8):
                        B.transpose(pT[:, h, :], kb[:, blk, h * 128:(h + 1) * 128], ident_b,
                                    r=[("kb", blk // 4), "ident_b"], w=[("ps", pb)])
                    sg = stg[blk % 2]
                    B.copy("dve" if blk % 2 else "act", sg, pT, r=[("ps", pb)], w=[("stg", blk % 2)])
                    B.dma("sp", KT_s[sq, :, :, blk * 128:(blk + 1) * 128].rearrange("h d t -> d h t"), sg,
                          r=[("stg", blk % 2)], w=[B.fresh("kts")])
                for half in range(2):
                    pb = 4 + half
                    pT = psum[pb][:, :].bitcast(BF16).rearrange("p (a b) -> p a b", a=8)
                    for j in range(8):
                        B.transpose(pT[:, j, :], ib[:, half * 8 + j, :], ident_b, r=["ib", "ident_b"],
                                    w=[("ps", pb)])
                    B.copy("dve", istg[:, half * 8:(half + 1) * 8, :], pT, r=[("ps", pb)], w=["istg"])
                B.dma("sp", ikT_s[sq, :, 0:PAST], istg, r=["istg"], w=[B.fresh("ikts")])
            S.barrier()
            A.release(m0)

        if "lru" in ph:
            convwT = inp("convwT", [128, 4, KC])
            convbT = inp("convbT", [128, KC])
            baT = inp("lru_baT", [128, KC])
            bxT = inp("lru_bxT", [128, KC])
            lamT = inp("lru_lamT", [128, KC])
            lru_wa = inp("lru_wa", [16, 256, 256])
            lru_wx = inp("lru_wx", [16, 256, 256])
            st_lruT = inp("state_lruT", [2, 128, KC])
            st_convT = inp("state_convT", [2, 128, KC, 3])
            selr = inp("selr", [128, 4])
            o_lru = B.dout("o_lru", [3, D])
            o_conv = B.dout("o_conv", [3, 3, D])
            m0 = A.mark()
            cw = A.alloc([4, KC], F32)
            cb = A.alloc([KC], F32)
            ba = A.alloc([KC], F32)
            bx = A.alloc([KC], F32)
            lam = A.alloc([KC], F32)
            cneg = A.alloc([KC], F32)
            cneg2 = A.alloc([KC], F32)
            tmpc = A.alloc([KC], F32)
            sel = A.alloc([4], F32)
            h0s = A.alloc([2, KC], F32)
            c0s = A.alloc([2, KC, 3], F32)
            hfin = A.alloc([3, KC], F32)
            cfin = A.alloc([3, 3, KC], F32)
            B.dma("sp", cw, convwT, w=["lc"])
            B.dma("sp", cb, convbT, w=["lc"])
            B.dma("sp", ba, baT, w=["lc"])
            B.dma("sp", bx, bxT, w=["lc"])
            B.dma("sp", lam, lamT, w=["lam"])
            B.dma("sp", sel, selr, w=["lc"])
            for sq in range(2):
                B.dma("sp", h0s[:, sq, :], st_lruT[sq], w=["lc"])
                B.dma("sp", c0s[:, sq, :, :], st_convT[sq], w=["lc"])
            B.ts(tmpc, lam, -1.0, None, ALU.mult, r=["lam"], w=["tmpc"])
            B.tt(tmpc, tmpc, lam, ALU.max, r=["lam", "tmpc"], w=["tmpc"])
            B.act(tmpc, tmpc, AF.Exp, r=["tmpc"], w=["tmpc"], scale=-1.0)
            B.act(tmpc, tmpc, AF.Ln, r=["tmpc", "one"], w=["tmpc"], bias=one_t[:, 0:1])
            B.ts(cneg, lam, -1.0, 0.0, ALU.mult, ALU.max, r=["lam"], w=["cneg"])
            B.tt(cneg, cneg, tmpc, ALU.add, r=["cneg", "tmpc"], w=["cneg"])
            B.ts(cneg2, cneg, -16.0, None, ALU.mult, r=["cneg"], w=["cneg2"])
            B.ts(cneg, cneg, -8.0, None, ALU.mult, r=["cneg"], w=["cneg"])

            XL = [[A.alloc([1027], F32) for _ in range(2)] for _ in range(2)]
            U2 = [[A.alloc([1024], F32) for _ in range(2)] for _ in range(2)]
            UB2 = [[A.alloc([1024], BF16) for _ in range(2)] for _ in range(2)]
            Rg2 = [[A.alloc([1024], F32) for _ in range(2)] for _ in range(2)]
            Ig2 = [[A.alloc([1024], F32) for _ in range(2)] for _ in range(2)]
            Aa2 = [[A.alloc([1024], F32) for _ in range(2)] for _ in range(2)]
            Hh2 = [[A.alloc([1024], F32) for _ in range(2)] for _ in range(2)]
            HO2 = [[A.alloc([256], F32) for _ in range(2)] for _ in range(2)]
            hprev = A.alloc([2], F32)
            wab = [A.alloc([2, 256], BF16) for _ in range(2)]
            wxb = [A.alloc([2, 256], BF16) for _ in range(2)]
            pieces = [(tt * 1024, 1024, "p", tt) for tt in range(4)] + [(SEQ, 32, "s", 0), (SEQ + 32, 32, "s", 1)]
            for nblk in range(16):
                wsl = nblk % 2
                B.dma("pool", wab[wsl], lru_wa[nblk].rearrange("(ch p) d -> p ch d", p=128), w=[("wa", wsl)])
                B.dma("pool", wxb[wsl], lru_wx[nblk].rearrange("(ch p) d -> p ch d", p=128), w=[("wx", wsl)])
                for pi, (col0, n, kind, idx) in enumerate(pieces):
                    xb = XL[pi % 2]
                    xprev = XL[(pi + 1) % 2]
                    pp = pi % 2
                    U, UB, Rg, Ig, Aa, Hh, HO = U2[pp], UB2[pp], Rg2[pp], Ig2[pp], Aa2[pp], Hh2[pp], HO2[pp]
                    for ch in range(2):
                        chunk = 2 * nblk + ch
                        xk = ("xl", pi % 2, ch)
                        B.dma("sp", xb[ch][:, 3:3 + n], xlT_scr[chunk, :, col0:col0 + n], w=[xk])
                        if kind == "p" and idx == 0:
                            S.add("dve", lambda e, t=xb[ch]: e.memset(t[:, 0:3], 0.0), w=[xk])
                        elif kind == "p":
                            B.copy("dve", xb[ch][:, 0:3], xprev[ch][:, 1024:1027],
                                   r=[("xl", (pi + 1) % 2, ch)], w=[xk])
                        else:
                            B.copy("dve", xb[ch][:, 0:3], c0s[:, idx, chunk, :], r=["lc"], w=[xk])
                        u = U[ch]
                        uk = ("u", pp, ch)
                        B.ts(u[:, 0:n], xb[ch][:, 3:3 + n], cw[:, 3, chunk:chunk + 1], cb[:, chunk:chunk + 1],
                             ALU.mult, ALU.add, r=[xk, "lc"], w=[uk])
                        for j in (2, 1, 0):
                            B.stt(u[:, 0:n], xb[ch][:, j:j + n], cw[:, j, chunk:chunk + 1], u[:, 0:n],
                                  ALU.mult, ALU.add, r=[xk, "lc", uk], w=[uk])
                        B.copy("act", UB[ch][:, 0:n], u[:, 0:n], r=[uk], w=[("ub", pp, ch)])
                    halves = [(0, min(n, 512))] + ([(512, 512)] if n > 512 else [])
                    for dh in range(2):
                        chunk = 2 * nblk + dh
                        for (gbuf, wbuf_, bias, gk, wk) in ((Rg, wab, ba, "rg", "wa"), (Ig, wxb, bx, "ig", "wx")):
                            for (h0, hn) in halves:
                                pb = next_ps()
                                ps = psum[pb][:, 0:hn]
                                B.mm_group(ps, [(wbuf_[wsl][:, ch, dh * 128:(dh + 1) * 128], UB[ch][:, h0:h0 + hn])
                                                for ch in range(2)],
                                           r=[("ub", pp, 0), ("ub", pp, 1), (wk, wsl)], w=[("ps", pb)])
                                B.act(gbuf[dh][:, h0:h0 + hn], ps, AF.Sigmoid, r=[("ps", pb), "lc"],
                                      w=[(gk, dh)], bias=bias[:, chunk:chunk + 1])
                    for dh in range(2):
                        chunk = 2 * nblk + dh
                        B.act(Aa[dh][:, 0:n], Rg[dh][:, 0:n], AF.Exp, r=[("rg", pp, dh), "cneg"], w=[("aa", pp, dh)],
                              scale=cneg[:, chunk:chunk + 1])
                        B.act(Rg[dh][:, 0:n], Rg[dh][:, 0:n], AF.Exp, r=[("rg", pp, dh), "cneg2"], w=[("rg", pp, dh)],
                              scale=cneg2[:, chunk:chunk + 1])
                    for dh in range(2):
                        B.ts(Rg[dh][:, 0:n], Rg[dh][:, 0:n], 1.0, -1.0, ALU.min, ALU.mult, r=[("rg", pp, dh)],
                             w=[("rg", pp, dh)])
                        B.act(Rg[dh][:, 0:n], Rg[dh][:, 0:n], AF.Sqrt, r=[("rg", pp, dh), "one"], w=[("rg", pp, dh)],
                              scale=1.0, bias=one_t[:, 0:1])
                    for dh in range(2):
                        chunk = 2 * nblk + dh
                        B.tt(Ig[dh][:, 0:n], Ig[dh][:, 0:n], U[dh][:, 0:n], ALU.mult, r=[("ig", pp, dh), ("u", pp, dh)],
                             w=[("ig", pp, dh)])
                        B.tt(Ig[dh][:, 0:n], Ig[dh][:, 0:n], Rg[dh][:, 0:n], ALU.mult, r=[("ig", pp, dh), ("rg", pp, dh)],
                             w=[("ig", pp, dh)])
                        if kind == "p" and idx == 0:
                            init = 0.0
                            ir = []
                        elif kind == "p":
                            init = hprev[:, dh:dh + 1]
                            ir = [("hp", dh)]
                        else:
                            init = h0s[:, idx, chunk:chunk + 1]
                            ir = ["lc"]
                        S.add("dve", lambda e, dh=dh, n=n, init=init, Hh=Hh, Aa=Aa, Ig=Ig: e.tensor_tensor_scan(
                            out=Hh[dh][:, 0:n], data0=Aa[dh][:, 0:n], data1=Ig[dh][:, 0:n], initial=init,
                            op0=ALU.mult, op1=ALU.add), r=[("aa", pp, dh), ("ig", pp, dh)] + ir, w=[("hh", pp, dh)])
                        if kind == "p" and idx < 3:
                            B.copy("dve", hprev[:, dh:dh + 1], Hh[dh][:, n - 1:n], r=[("hh", pp, dh)], w=[("hp", dh)])
                        if kind == "s" or idx == 3:
                            row = 0 if kind == "p" else 1 + idx
                            B.copy("dve", hfin[:, row, chunk:chunk + 1], Hh[dh][:, n - 1:n], r=[("hh", pp, dh)],
                                   w=["hfin"])
                            for j in range(3):
                                B.copy("dve", cfin[:, row, j, chunk:chunk + 1],
                                       XL[pi % 2][dh][:, n + j:n + j + 1], r=[("xl", pi % 2, dh)], w=["cfin"])
                        ho = HO[dh]
                        hk = ("ho", pp, dh)
                        if kind == "p":
                            for il in range(2):
                                B.ts(ho[:, il * 128:(il + 1) * 128], Hh[dh][:, (4 * il) * 128:(4 * il + 1) * 128],
                                     sel[:, 0:1], None, ALU.mult, r=[("hh", pp, dh), "lc"], w=[hk])
                                for j in range(1, 4):
                                    B.stt(ho[:, il * 128:(il + 1) * 128],
                                          Hh[dh][:, (4 * il + j) * 128:(4 * il + j + 1) * 128], sel[:, j:j + 1],
                                          ho[:, il * 128:(il + 1) * 128], ALU.mult, ALU.add,
                                          r=[("hh", pp, dh), "lc", hk], w=[hk])
                            B.dma("act", hown_scr[chunk, :, idx * 256:(idx + 1) * 256], ho, r=[hk], w=[B.fresh("ho")])
                        else:
                            B.dma("act", hown_scr[chunk, :, 1024 + idx * 32:1024 + (idx + 1) * 32], Hh[dh][:, 0:32],
                                  r=[("hh", pp, dh)], w=[B.fresh("ho")])
            fst = A.alloc([12, 128], F32)
            pbv = psum[4][:, :].rearrange("p (a b) -> p a b", a=4)
            pbv2 = psum[5][:, :].rearrange("p (a b) -> p a b", a=4)
            pbv3 = psum[6][:, :].rearrange("p (a b) -> p a b", a=4)
            for row in range(3):
                pv = (pbv, pbv2, pbv3)[row]
                pk = ("ps", 4 + row)
                B.transpose(pv[0:32, 0, :], hfin[:, row, :], ident_f, r=["hfin", "ident_f"], w=[pk])
                for j in range(3):
                    B.transpose(pv[0:32, 1 + j, :], cfin[:, row, j, :], ident_f, r=["cfin", "ident_f"], w=[pk])
                B.copy("dve", fst[0:32, row * 4:(row + 1) * 4, :], pv[0:32, :, :], r=[pk], w=["fst"])
                B.dma("sp", o_lru[row].rearrange("(kc p) -> kc p", p=128), fst[0:32, row * 4, :], r=["fst"],
                      w=[B.fresh("o")])
                for j in range(3):
                    B.dma("sp", o_conv[row, j].rearrange("(kc p) -> kc p", p=128), fst[0:32, row * 4 + 1 + j, :],
                          r=["fst"], w=[B.fresh("o")])
            S.barrier()
            A.release(m0)

        if "own" in ph:
            xown = inp("xown", [TO, D])
            w_in = inp("w_in", [D, IN_COLS])
            cs_own = inp("cs_own", [TO, 32])
            m0 = A.mark()
            gmixT = A.alloc([KC], F32)
            B.dma("sp", gmixT, inp("norm_mixT", [128, KC]), w=["gT"])
            xnT = A.alloc([KC, TO], BF16)
            wbufs = [A.alloc([KC, 512], BF16) for _ in range(2)]
            xt = A.alloc([D], F32)
            junk = A.alloc([D], BF16)
            ssb = [A.alloc([1], F32) for _ in range(2)]
            st = make_headproc(3)
            cst = [A.alloc([32], F32) for _ in range(9)]
            blocks = [(xown[j * 128:(j + 1) * 128, :], 128) for j in range(8)] + [(xown[1024:1088, :], 64)]
            brow = [b[1] for b in blocks]

            def extra(bi, rows):
                B.dma("sp", cst[bi][0:rows], cs_own[bi * 128:bi * 128 + rows, :], w=[("cst", bi)])
            norm_transpose(blocks, xnT, gmixT, xt, junk, ssb, extra)

            def evac(tag, bi, rows, t0, ps, pstok, ncols):
                kind, g = tag
                if kind == "q":
                    def dstT(T_, nh, rows_, tk):
                        B.dma("act", QT_scr[g * 4:g * 4 + 4, :, t0:t0 + rows_].rearrange("h d t -> d h t"),
                              T_[:, 0:4, 0:rows_], r=[tk], w=[B.fresh("qt")])
                    headproc(st, ps, pstok, rows, ncols, "q", cst[bi], ("cst", bi), None, dstT)
                elif kind == "iq":
                    def dstT(T_, nh, rows_, tk):
                        B.dma("act", iqT_scr[g * 4:g * 4 + 4, :, t0:t0 + rows_].rearrange("h d t -> d h t"),
                              T_[:, 0:4, 0:rows_], r=[tk], w=[B.fresh("iqt")])
                    headproc(st, ps, pstok, rows, ncols, None, cst[bi], ("cst", bi), None, dstT)
                else:
                    i = stc[0] % st["n"]
                    stc[0] += 1
                    tk = ("st", i)
                    B.copy("act", st["f"][i][0:rows, 0:32], ps, r=[pstok], w=[tk])
                    B.dma("sp", iw_scr[t0:t0 + rows, :], st["f"][i][0:rows, 0:32], r=[tk], w=[B.fresh("iw")])
            groups = [(C_Q + g * 512, 512, ("q", g)) for g in range(8)] + \
                     [(C_IQ + g * 512, 512, ("iq", g)) for g in range(8)] + [(C_IW, 32, ("iw", 0))]
            gemm_tok(xnT, brow, KC, wbufs, lambda c0, n: w_in[:, c0:c0 + n], groups, evac)

            ttiles = [(0, 512), (512, 512), (1024, 64)]
            NE = 2
            ey = [A.alloc([512], F32) for _ in range(NE)]
            et = [A.alloc([512], F32) for _ in range(NE)]
            eh = [A.alloc([512], F32) for _ in range(NE)]
            eb = [A.alloc([512], BF16) for _ in range(NE)]
            ec = [0]

            def evac_feat(which):
                def ev(chunk, tt0, n, pss):
                    ps, pstok = pss[0]
                    i = ec[0] % NE
                    ec[0] += 1
                    tk = ("e", i)
                    if which == "yl":
                        y, t, hh, ob = ey[i], et[i], eh[i], eb[i]
                        B.dma("sp", hh[:, 0:n], hown_scr[chunk, :, tt0:tt0 + n], w=[("eh", i)])
                        B.copy("act", y[:, 0:n], ps, r=[pstok], w=[tk])
                        B.tt(t[:, 0:n], y[:, 0:n], y[:, 0:n], ALU.mult, r=[tk], w=[tk])
                        B.ts(t[:, 0:n], t[:, 0:n], 0.044715, 1.0, ALU.mult, ALU.add, r=[tk], w=[tk])
                        B.tt(t[:, 0:n], t[:, 0:n], y[:, 0:n], ALU.mult, r=[tk], w=[tk])
                        B.act(t[:, 0:n], t[:, 0:n], AF.Sigmoid, r=[tk], w=[tk], scale=1.5957691216057308)
                        B.tt(t[:, 0:n], t[:, 0:n], y[:, 0:n], ALU.mult, r=[tk], w=[tk])
                        B.tt(ob[:, 0:n], t[:, 0:n], hh[:, 0:n], ALU.mult, r=[tk, ("eh", i)], w=[tk])
                        B.dma("act", olruT_scr[chunk, :, tt0:tt0 + n], ob[:, 0:n], r=[tk], w=[B.fresh("ol")])
                    else:
                        y = ey[i]
                        B.act(y[:, 0:n], ps, AF.Sigmoid, r=[pstok], w=[tk])
                        dst = saT_scr if which == "ga" else slT_scr
                        B.dma("act", dst[chunk, :, tt0:tt0 + n], y[:, 0:n], r=[tk], w=[B.fresh("sg")])
                return ev
            for which, cbase in (("yl", C_YL), ("ga", C_GA), ("gl", C_GL)):
                gemm_feat([(xnT, wbufs, lambda c0, n, cbase=cbase: w_in[:, cbase + c0:cbase + c0 + n])], 9, KC,
                          4096, 512, ttiles, evac_feat(which))
            S.barrier()
            A.release(m0)

        if "attn" in ph:
            dsel_in = inp("dsel", [128, 32 * 128])
            mbias_in = inp("maskbias", [128, 512])
            m0 = A.mark()
            dselt = A.alloc([32, 128], BF16)
            dv = dsel_in.rearrange("p (g t) -> p g t", g=32)
            for g0 in range(0, 32, 8):
                B.dma("pool", dselt[:, g0:g0 + 8, :], dv[:, g0:g0 + 8, :], w=[("dsel", g0)])
            S.add("dve", lambda e: e.memset(eps_t, EPS), r=[("dsel", g0) for g0 in range(0, 32, 8)], w=["dsel"])
            mbias = A.alloc([512], F32)
            B.dma("sp", mbias, mbias_in, w=["mbias"])
            ones_b = A.alloc([128], BF16)
            S.add("dve", lambda e: e.memset(ones_b, 1.0), w=["ones"])
            gq = A.alloc([128], F32)
            gk = A.alloc([128], F32)
            B.dma("sp", gq, inp("norm_q", [HD]).partition_broadcast(128), w=["gq"])
            B.dma("sp", gk, inp("norm_k", [HD]).partition_broadcast(128), w=["gk"])
            cq = A.alloc([1], F32)
            ck = A.alloc([1], F32)
            S.add("dve", lambda e: e.tensor_reduce(out=cq, in_=gq, axis=AX.X, op=ALU.max, apply_absolute_value=True),
                  r=["gq"], w=["cq"])
            S.add("dve", lambda e: e.tensor_reduce(out=ck, in_=gk, axis=AX.X, op=ALU.max, apply_absolute_value=True),
                  r=["gk"], w=["ck"])
            B.tt(cq, cq, ck, ALU.mult, r=["cq", "ck"], w=["cq"])
            B.ts(cq, cq, -math.sqrt(128.0), None, ALU.mult, r=["cq"], w=["cq"])

            iqTb = A.alloc([32, 128], BF16)
            iqTg = A.alloc([32, 128], BF16)
            ikT = A.alloc([SEQ], BF16)
            wsel = A.alloc([32, 128], BF16)
            iwb = A.alloc([32], F32)
            iwrep = A.alloc([32, 4], BF16)
            R1 = [A.alloc([512], BF16) for _ in range(3)]
            ImB = [A.alloc([SEQ], F32) for _ in range(2)]
            Wk = A.alloc([SEQ], F32)
            m8 = A.alloc([8], F32)
            thrB = [A.alloc([1], F32) for _ in range(2)]
            maskb = A.alloc([SEQ], BF16)
            maskTB = [A.alloc([32, 128], BF16) for _ in range(2)]
            KTg = [A.alloc([SEQ], BF16) for _ in range(2)]
            Vg = [A.alloc([32, 128], BF16) for _ in range(2)]
            QTg = [A.alloc([512], BF16) for _ in range(2)]
            Pt = [A.alloc([512], BF16) for _ in range(3)]
            Pm = [A.alloc([512], BF16) for _ in range(3)]
            rz = A.alloc([512], F32)
            ot = [A.alloc([512], BF16) for _ in range(2)]
            sc_att = 1.0 / math.sqrt(128.0)

            qblocks = [("p", i, i * 128, 128, 512 * (i + 1)) for i in range(7, -1, -1)] + \
                      [("s", sq, 1024 + sq * 32, 32, SS) for sq in range(2)]
            r1c = [0]
            pc = [0]
            gct = [0]

            def srcs(kind, idx):
                if kind == "p":
                    return ikT_scr, KT_scr, V_scr
                return ikT_s[idx], KT_s[idx], V_s[idx]

            def stageA(blk, par):
                kind, idx, t0, R, Sk = blk
                G = R // 4
                Im = ImB[par]
                imk = ("Im", par)
                ikT_src = srcs(kind, idx)[0]
                iqv = iqTb.rearrange("p a b -> p (a b)")[:, 0:32 * R].rearrange("p (h t) -> p h t", h=32)
                B.dma("sp", iqv, iqT_scr[:, :, t0:t0 + R].rearrange("h d t -> d h t"), w=["iqTb"])
                B.copy("pool", iqTg[:, 0:G, :].rearrange("p g (h t) -> p g h t", t=4),
                       iqv.rearrange("p h (g t) -> p g h t", t=4), r=["iqTb"], w=["iqTg"])
                B.dma("sp", ikT[:, 0:Sk], ikT_src[:, 0:Sk], w=["ikT"])
                B.dma("sp", iwb[0:R], iw_scr[t0:t0 + R, :], w=["iwb"])
                B.copy("pool", iwrep[0:R], iwb[0:R].unsqueeze(2).to_broadcast([R, 32, 4]), r=["iwb"], w=["iwrep"])
                pT = psum[3][:, :].bitcast(BF16)
                B.transpose(pT[:, 0:R], iwrep[0:R].rearrange("p h t -> p (h t)"), ident_b[0:R, 0:R],
                            r=["iwrep", "ident_b"], w=[("ps", 3)])
                B.tt(wsel[:, 0:G, 0:R], dselt[:, 0:G, 0:R], pT[:, 0:R].unsqueeze(1).to_broadcast([128, G, R]),
                     ALU.mult, r=[("ps", 3), "dsel"], w=["wsel"])
                chunks = [(c0, min(512, Sk - c0)) for c0 in range(0, Sk, 512)]
                items = [(c0, cn, g) for (c0, cn) in chunks for g in range(G)]
                pend = None
                for it in range(len(items) + 1):
                    cur_item = None
                    if it < len(items):
                        c0, cn, g = items[it]
                        pb = r1c[0] % 2
                        rr = R1[r1c[0] % 3]
                        rk = ("r1", r1c[0] % 3)
                        r1c[0] += 1
                        ps1 = psum[pb][:, 0:cn]
                        B.mm(ps1, iqTg[:, g, :], ikT[:, c0:c0 + cn], True, True, r=["iqTg", "ikT"], w=[("ps", pb)])
                        if g % 4 != 3:
                            B.act(rr[:, 0:cn], ps1, AF.Relu, r=[("ps", pb)], w=[rk])
                        else:
                            B.ts(rr[:, 0:cn], ps1, 0.0, None, ALU.max, r=[("ps", pb)], w=[rk])
                        cur_item = (c0, cn, g, rr, rk)
                    if pend is not None:
                        c0p, cnp, gp, rrp, rkp = pend
                        psI = psum[2][0:R, 0:cnp]
                        B.mm(psI, wsel[:, gp, 0:R], rrp[:, 0:cnp], gp == 0, gp == G - 1, r=["wsel", rkp], w=[("ps", 2)])
                        if gp == G - 1:
                            if kind == "p" and c0p + cnp == Sk:
                                B.tt(Im[0:R, c0p:c0p + cnp], psI, mbias[0:R, 0:cnp], ALU.add,
                                     r=[("ps", 2), "mbias"], w=[imk])
                            else:
                                B.copy("dve", Im[0:R, c0p:c0p + cnp], psI, r=[("ps", 2)], w=[imk])
                    pend = cur_item

            def stageB1(blk, par):
                kind, idx, t0, R, Sk = blk
                Im = ImB[par]
                imk = ("Im", par)
                cur = Im
                for rnd in range(32):
                    S.add("dve", lambda e, cur=cur, R=R, Sk=Sk: e.max(out=m8[0:R], in_=cur[0:R, 0:Sk]),
                          r=[imk, "Wk"], w=["m8"])
                    if rnd < 31:
                        S.add("dve", lambda e, cur=cur, R=R, Sk=Sk: e.match_replace(
                            out=Wk[0:R, 0:Sk], in_to_replace=m8[0:R], in_values=cur[0:R, 0:Sk], imm_value=-3.0e38),
                            r=[imk, "Wk", "m8"], w=["Wk"])
                        cur = Wk
                B.ts(thrB[par][0:R], m8[0:R, 7:8], -5.0e29, None, ALU.max, r=["m8"], w=[("thr", par)])

            def stageB2(blk, par):
                kind, idx, t0, R, Sk = blk
                Im = ImB[par]
                maskT = maskTB[par]
                mk = ("maskT", par)
                B.ts(maskb[0:R, 0:Sk], Im[0:R, 0:Sk], thrB[par][0:R], None, ALU.is_ge,
                     r=[("Im", par), ("thr", par)], w=["maskb"])
                nkb = (Sk + 127) // 128
                for k0 in range(0, nkb, 8):
                    pT8 = psum[3][:, :].bitcast(BF16).rearrange("p (a b) -> p a b", a=8)
                    k1 = min(nkb, k0 + 8)
                    for kb_ in range(k0, k1):
                        kn = min(128, Sk - kb_ * 128)
                        B.transpose(pT8[0:kn, kb_ - k0, 0:R], maskb[0:R, kb_ * 128:kb_ * 128 + kn], ident_b[0:R, 0:R],
                                    r=["maskb", "ident_b"], w=[("ps", 3)])
                    kfull = [kb_ for kb_ in range(k0, k1) if Sk - kb_ * 128 >= 128]
                    if kfull:
                        B.act(maskT[:, kfull[0]:kfull[-1] + 1, 0:R], pT8[:, 0:len(kfull), 0:R], AF.Copy,
                              r=[("ps", 3)], w=[mk], scale=30000.0, bias=-30000.0)
                    if len(kfull) < k1 - k0:
                        kb_ = k1 - 1
                        kn = Sk - kb_ * 128
                        B.act(maskT[0:kn, kb_, 0:R], pT8[0:kn, kb_ - k0, 0:R], AF.Copy, r=[("ps", 3)], w=[mk],
                              scale=30000.0, bias=-30000.0)

            def stageC(blk, par):
                kind, idx, t0, R, Sk = blk
                maskT = maskTB[par]
                mk = ("maskT", par)
                _, KT_src, V_src = srcs(kind, idx)
                nkb = (Sk + 127) // 128
                for g in range(NKV):
                    sl = gct[0] % 2
                    gct[0] += 1
                    B.dma("sp", KTg[sl][:, 0:Sk], KT_src[g, :, 0:Sk], w=[("KTg", sl)])
                    nfull = Sk // 128
                    B.dma("act", Vg[sl][:, 0:nfull, :],
                          V_src[0:nfull * 128, g * 128:(g + 1) * 128].rearrange("(b p) d -> p b d", p=128),
                          w=[("Vg", sl)])
                    if Sk % 128:
                        kn = Sk % 128
                        B.dma("act", Vg[sl][0:kn, nfull, :], V_src[nfull * 128:Sk, g * 128:(g + 1) * 128],
                              w=[("Vg", sl)])
                    B.dma("sp", QTg[sl][:, 0:4 * R].rearrange("p (h t) -> p h t", h=4),
                          QT_scr[4 * g:4 * g + 4, :, t0:t0 + R].rearrange("h d t -> d h t"), w=[("QTg", sl)])
                    N4 = 4 * R
                    psO = psum[6][:, 0:N4]
                    psZ = psum[7][:, 0:N4]
                    pendc = None
                    for it in range(nkb + 1):
                        curc = None
                        if it < nkb:
                            kb_ = it
                            kn = min(128, Sk - kb_ * 128)
                            pb = 4 + (pc[0] % 2)
                            pi_ = pc[0] % 3
                            pc[0] += 1
                            psL = psum[pb][0:kn, 0:N4]
                            B.mm(psL, KTg[sl][:, kb_ * 128:kb_ * 128 + kn], QTg[sl][:, 0:4 * R], True, False,
                                 r=[("KTg", sl), ("QTg", sl)], w=[("ps", pb)])
                            for h in range(4):
                                B.mm(psum[pb][0:kn, h * R:(h + 1) * R], ident_b[0:kn, 0:kn], maskT[0:kn, kb_, 0:R],
                                     False, h == 3, r=["ident_b", mk], w=[("ps", pb)])
                            B.act(Pt[pi_][0:kn, 0:N4], psL, AF.Exp, r=[("ps", pb), "cq"], w=[("pt", pi_)],
                                  scale=sc_att, bias=cq[0:kn, 0:1])
                            curc = (kb_, kn, pi_)
                        if pendc is not None:
                            kbp, knp, pip = pendc
                            B.mm(psO, Vg[sl][0:knp, kbp, :], Pt[pip][0:knp, 0:N4], kbp == 0, kbp == nkb - 1,
                                 r=[("Vg", sl), ("pt", pip)], w=[("ps", 6)])
                            B.mm(psZ, ones_b[0:knp, :], Pt[pip][0:knp, 0:N4], kbp == 0, kbp == nkb - 1,
                                 r=["ones", ("pt", pip)], w=[("ps", 7)])
                        pendc = curc
                    S.add("dve", lambda e, psZ=psZ, N4=N4: e.reciprocal(out=rz[:, 0:N4], in_=psZ), r=[("ps", 7)],
                          w=["rz"])
                    B.tt(ot[sl][:, 0:N4], psO, rz[:, 0:N4], ALU.mult, r=[("ps", 6), "rz"], w=[("ot", sl)])
                    B.dma("act", oattT_scr[4 * g:4 * g + 4, :, t0:t0 + R].rearrange("h d t -> d h t"),
                          ot[sl][:, 0:N4].rearrange("p (h t) -> p h t", h=4), r=[("ot", sl)], w=[B.fresh("oa")])

            NBK = len(qblocks)
            for step in range(NBK + 2):
                if step < NBK:
                    stageA(qblocks[step], step % 2)
                if 0 <= step - 1 < NBK:
                    stageB1(qblocks[step - 1], (step - 1) % 2)
                if 0 <= step - 2 < NBK:
                    stageC(qblocks[step - 2], (step - 2) % 2)
                if 0 <= step - 1 < NBK:
                    stageB2(qblocks[step - 1], (step - 1) % 2)
            S.barrier()
            A.release(m0)

        if "mix" in ph:
            w_branch = inp("w_branch", [2 * D, D])
            w_out = inp("w_out", [D, D])
            xown = inp("xown", [TO, D])
            m0 = A.mark()
            oaT = A.alloc([KC, TO], BF16)
            olT = A.alloc([KC, TO], BF16)
            wA = [A.alloc([KC, 128], BF16) for _ in range(2)]
            wL = [A.alloc([KC, 128], BF16) for _ in range(2)]
            for h0 in range(0, 32, 8):
                B.dma("sp", oaT[:, h0:h0 + 8, :], oattT_scr[h0:h0 + 8].rearrange("h d t -> d h t"),
                      w=[("ATx", h0)])
                B.dma("sp", olT[:, h0:h0 + 8, :], olruT_scr[h0:h0 + 8].rearrange("h d t -> d h t"),
                      w=[("ATy", h0)])
            ATR = [("ATx", h0) for h0 in (0, 8, 16, 24)] + [("ATy", h0) for h0 in (0, 8, 16, 24)]
            ttiles = [(0, 512), (512, 512), (1024, 64)]
            NE = 3
            sa_t = [A.alloc([512], F32) for _ in range(NE)]
            sl_t = [A.alloc([512], F32) for _ in range(NE)]
            ta_t = [A.alloc([512], F32) for _ in range(NE)]
            mb_t = [A.alloc([512], BF16) for _ in range(NE)]
            ec = [0]

            def evac_mix(chunk, tt0, n, pss):
                (psA, tokA), (psL, tokL) = pss
                i = ec[0] % NE
                ec[0] += 1
                tk = ("e", i)
                B.dma("sp", sa_t[i][:, 0:n], saT_scr[chunk, :, tt0:tt0 + n], w=[("esa", i)])
                B.dma("sp", sl_t[i][:, 0:n], slT_scr[chunk, :, tt0:tt0 + n], w=[("esl", i)])
                B.tt(ta_t[i][:, 0:n], psA, sa_t[i][:, 0:n], ALU.mult, r=[tokA, ("esa", i)], w=[tk])
                B.tt(sl_t[i][:, 0:n], psL, sl_t[i][:, 0:n], ALU.mult, r=[tokL, ("esl", i)], w=[("esl", i)])
                B.tt(mb_t[i][:, 0:n], ta_t[i][:, 0:n], sl_t[i][:, 0:n], ALU.add, r=[tk, ("esl", i)], w=[tk])
                B.dma("act", mixT_scr[chunk, :, tt0:tt0 + n], mb_t[i][:, 0:n], r=[tk], w=[B.fresh("mx")])
            gemm_feat([(oaT, wA, lambda c0, n: w_branch[0:D, c0:c0 + n]),
                       (olT, wL, lambda c0, n: w_branch[D:2 * D, c0:c0 + n])], 9, KC, 4096, 128, ttiles, evac_mix,
                      attoks=ATR)
            S.barrier()
            A.release(m0)

            m0 = A.mark()
            mxT = A.alloc([KC, TO], BF16)
            wbufs = [A.alloc([KC, 512], BF16) for _ in range(2)]
            for h0 in range(0, 32, 8):
                B.dma("sp", mxT[:, h0:h0 + 8, :], mixT_scr[h0:h0 + 8].rearrange("h d t -> d h t"), w=[("ATz", h0)])
            NE = 3
            xr = [A.alloc([512], F32) for _ in range(NE)]

            def evac_out(tag, bi, rows, t0, ps, pstok, ncols):
                i = ec[0] % NE
                ec[0] += 1
                c0 = tag
                B.dma("sp", xr[i][0:rows], xown[t0:t0 + rows, c0:c0 + 512], w=[("xr", i)])
                B.tt(xr[i][0:rows], ps, xr[i][0:rows], ALU.add, r=[pstok, ("xr", i)], w=[("xr", i)])
                B.dma("act", x1_scr[t0:t0 + rows, c0:c0 + 512], xr[i][0:rows], r=[("xr", i)], w=[B.fresh("x1")])
            gemm_tok(mxT, [128] * 8 + [64], KC, wbufs, lambda c0, n: w_out[:, c0:c0 + n],
                     [(g * 512, 512, g * 512) for g in range(8)], evac_out,
                     attoks=[("ATz", h0) for h0 in (0, 8, 16, 24)])
            S.barrier()
            A.release(m0)

        if "ffn" in ph:
            w_gate = inp("w_gate", [D, DFF])
            w_up = inp("w_up", [D, DFF])
            w_down = inp("w_down", [DFF, D])
            y_own = B.dout("y_own", [TO, D])
            m0 = A.mark()
            gffT = A.alloc([KC], F32)
            B.dma("sp", gffT, inp("norm_ffnT", [128, KC]), w=["gT"])
            xn2T = A.alloc([KC, TO], BF16)
            xt = A.alloc([D], F32)
            junk = A.alloc([D], BF16)
            ssb = [A.alloc([1], F32) for _ in range(2)]
            blocks = [(x1_scr[j * 128:(j + 1) * 128, :], 128) for j in range(8)] + [(x1_scr[1024:1088, :], 64)]
            norm_transpose(blocks, xn2T, gffT, xt, junk, ssb)
            wG = [A.alloc([KC, 256], BF16) for _ in range(2)]
            wU = [A.alloc([KC, 256], BF16) for _ in range(2)]
            ttiles = [(0, 512), (512, 512), (1024, 64)]
            NE = 3
            sg_t = [A.alloc([512], F32) for _ in range(NE)]
            hb_t = [A.alloc([512], BF16) for _ in range(NE)]
            ec = [0]

            def evac_ffn(chunk, tt0, n, pss):
                (psG, tokG), (psU, tokU) = pss
                i = ec[0] % NE
                ec[0] += 1
                tk = ("e", i)
                B.act(sg_t[i][:, 0:n], psG, AF.Silu, r=[tokG], w=[tk])
                B.tt(hb_t[i][:, 0:n], sg_t[i][:, 0:n], psU, ALU.mult, r=[tk, tokU], w=[tk])
                B.dma("act", hT_scr[chunk, :, tt0:tt0 + n], hb_t[i][:, 0:n], r=[tk], w=[B.fresh("ht")])
            gemm_feat([(xn2T, wG, lambda c0, n: w_gate[:, c0:c0 + n]),
                       (xn2T, wU, lambda c0, n: w_up[:, c0:c0 + n])], 9, KC, DFF, 256, ttiles, evac_ffn)
            S.barrier()
            A.release(m0)

            m0 = A.mark()
            hT = A.alloc([KC, TO], BF16)
            wbufs = [A.alloc([KC, 512], BF16) for _ in range(2)]
            NE = 3
            xr = [A.alloc([512], F32) for _ in range(NE)]
            pieces = [(0, 32), (32, 32), (64, 22)]
            for pi, (k0, kcn) in enumerate(pieces):
                step = 8
                toks = []
                for h0 in range(0, kcn, step):
                    h1 = min(kcn, h0 + step)
                    tk = ("ATz", h0)
                    toks.append(tk)
                    B.dma("sp", hT[:, h0:h1, :], hT_scr[k0 + h0:k0 + h1].rearrange("h d t -> d h t"), w=[tk])
                src = x1_scr
                dst = y_own

                def evac_dn(tag, bi, rows, t0, ps, pstok, ncols, pi=pi):
                    i = ec[0] % NE
                    ec[0] += 1
                    c0 = tag
                    prev = x1_scr if pi == 0 else y_own
                    B.dma("sp", xr[i][0:rows], prev[t0:t0 + rows, c0:c0 + 512], r=[("y", bi, c0)], w=[("xr", i)])
                    B.tt(xr[i][0:rows], ps, xr[i][0:rows], ALU.add, r=[pstok, ("xr", i)], w=[("xr", i)])
                    B.dma("act", y_own[t0:t0 + rows, c0:c0 + 512], xr[i][0:rows], r=[("xr", i)], w=[("y", bi, c0)])
                gemm_tok(hT, [128] * 8 + [64], kcn, wbufs,
                         lambda c0, n, k0=k0, kcn=kcn: w_down[k0 * 128:(k0 + kcn) * 128, c0:c0 + n],
                         [(g * 512, 512, g * 512) for g in range(8)], evac_dn, attoks=toks)
            S.barrier()
            A.release(m0)

        with nc.Block() as block:
            S.emit(nc, block, esem, dsem)
    return nc


def rope_tables(pos):
    half = 16
    inv = (np.float32(500000.0) ** (-np.arange(half, dtype=np.float32) * np.float32(2.0) / np.float32(32))
           ).astype(np.float32)
    ang = pos.astype(np.float32)[:, None] * inv[None, :]
    return np.concatenate([np.cos(ang), np.sin(ang)], axis=1).astype(np.float32)


def make_in_maps(inp, names=None):
    f = np.float32
    in_maps = []
    pos_seq = np.concatenate([np.arange(SEQ), PAST + np.arange(32), PAST + np.arange(32)])
    cs_seq = rope_tables(pos_seq)
    ident = np.eye(128, dtype=f)
    dsel = np.zeros((32, 4, 32, 128), f)
    for g in range(32):
        for tl in range(4):
            dsel[:, tl, g, 4 * g + tl] = 1.0 / 64.0
    dsel = dsel.reshape(128, 32 * 128)

    def T32(v):
        return np.ascontiguousarray(np.asarray(v).reshape(KC, 128).T)

    for c in range(8):
        p, r = c // 4, c % 4
        xp = inp["x_prompt"][p]
        own_blocks = [xp[(4 * i + r) * 128:(4 * i + r + 1) * 128] for i in range(8)]
        xown = np.concatenate(own_blocks + [inp["x_sample"][2 * c], inp["x_sample"][2 * c + 1]], axis=0)
        pos_own = np.concatenate([np.arange((4 * i + r) * 128, (4 * i + r + 1) * 128) for i in range(8)] +
                                 [PAST + np.arange(32), PAST + np.arange(32)])
        sel = np.zeros((128, 4), f)
        sel[:, r] = 1.0
        tl = np.arange(128)[:, None]
        slx = np.arange(512)[None, :]
        mb = np.where(slx < 128 * r + 64 + 64 * (tl >= 64), 0.0, NEGM).astype(f)
        sc = inp["state_conv"][0, 2 * c:2 * c + 2]
        m = {
            "xseq": np.ascontiguousarray(xp),
            "xown": np.ascontiguousarray(xown),
            "w_in": inp["w_in"][0],
            "norm_mixT": T32(inp["norm_mix"][0]),
            "norm_ffnT": T32(inp["norm_ffn"][0]),
            "norm_q": inp["norm_q"][0],
            "norm_k": inp["norm_k"][0],
            "norm_idx_k": inp["norm_idx_k"][0],
            "cs_seq": cs_seq,
            "cs_own": rope_tables(pos_own),
            "ident": ident,
            "dsel": dsel,
            "maskbias": mb,
            "cache_k": np.ascontiguousarray(inp["cache_k"][0, 2 * c:2 * c + 2].reshape(2, PAST, NKV * HD)),
            "cache_v": np.ascontiguousarray(inp["cache_v"][0, 2 * c:2 * c + 2].reshape(2, PAST, NKV * HD)),
            "cache_ik": np.ascontiguousarray(inp["cache_idx_k"][0, 2 * c:2 * c + 2]),
            "state_lruT": np.stack([T32(inp["state_lru"][0, 2 * c + q]) for q in range(2)]),
            "state_convT": np.ascontiguousarray(sc.reshape(2, 3, KC, 128).transpose(0, 3, 2, 1)),
            "convwT": np.ascontiguousarray(inp["conv_w"][0].reshape(4, KC, 128).transpose(2, 0, 1)),
            "convbT": T32(inp["conv_b"][0]),
            "lru_baT": T32(inp["lru_ba"][0]),
            "lru_bxT": T32(inp["lru_bx"][0]),
            "lru_lamT": T32(inp["lru_lambda"][0]),
            "lru_wa": inp["lru_wa"][0],
            "lru_wx": inp["lru_wx"][0],
            "selr": sel,
            "w_branch": inp["w_branch"][0],
            "w_out": inp["w_out"][0],
            "w_gate": inp["w_gate"][0],
            "w_up": inp["w_up"][0],
            "w_down": inp["w_down"][0],
        }
        if names is not None:
            m = {k: v for k, v in m.items() if k in names}
        in_maps.append(m)
    return in_maps


def input_names(nc):
    names = set()
    for alloc in nc.allocations:
        if isinstance(alloc, mybir.MemoryLocationSet) and alloc.kind == "ExternalInput":
            names.add(alloc.memorylocations[0].name)
    return names


_NC_CACHE = {}


def kernel(**inputs):
    inp = {k: np.asarray(v) for k, v in inputs.items()}
    if "nc" not in _NC_CACHE:
        _NC_CACHE["nc"] = build_program()
    nc = _NC_CACHE["nc"]
    in_maps = make_in_maps(inp, input_names(nc))
    res = run_bass_kernel_spmd(nc, in_maps, core_ids=list(range(8)))
    R = res.results
    f = np.float32
    y_prompt = np.zeros((2, SEQ, D), f)
    y_sample = np.zeros((16, DEC_SEQ, D), f)
    k_prompt = np.zeros((1, 2, SEQ, NKV, HD), f)
    v_prompt = np.zeros((1, 2, SEQ, NKV, HD), f)
    ik_prompt = np.zeros((1, 2, SEQ, HD), f)
    lru_prompt = np.zeros((1, 2, D), f)
    conv_prompt = np.zeros((1, 2, 3, D), f)
    k_sample = np.zeros((1, 16, DEC_SEQ, NKV, HD), f)
    v_sample = np.zeros((1, 16, DEC_SEQ, NKV, HD), f)
    ik_sample = np.zeros((1, 16, DEC_SEQ, HD), f)
    lru_sample = np.zeros((1, 16, D), f)
    conv_sample = np.zeros((1, 16, 3, D), f)
    for c in range(8):
        p, r = c // 4, c % 4
        o = R[c]
        y = o["y_own"]
        for i in range(8):
            b = 4 * i + r
            y_prompt[p, b * 128:(b + 1) * 128] = y[i * 128:(i + 1) * 128]
        for q in range(2):
            sq = 2 * c + q
            y_sample[sq] = y[1024 + q * 32:1024 + (q + 1) * 32]
            k_sample[0, sq] = o["o_k"][SEQ + q * 32:SEQ + (q + 1) * 32].reshape(32, NKV, HD)
            v_sample[0, sq] = o["o_v"][SEQ + q * 32:SEQ + (q + 1) * 32].reshape(32, NKV, HD)
            ik_sample[0, sq] = o["o_ik"][SEQ + q * 32:SEQ + (q + 1) * 32]
            lru_sample[0, sq] = o["o_lru"][1 + q]
            conv_sample[0, sq] = o["o_conv"][1 + q]
        if r == 0:
            k_prompt[0, p] = o["o_k"][:SEQ].reshape(SEQ, NKV, HD)
            v_prompt[0, p] = o["o_v"][:SEQ].reshape(SEQ, NKV, HD)
            ik_prompt[0, p] = o["o_ik"][:SEQ]
            lru_prompt[0, p] = o["o_lru"][0]
            conv_prompt[0, p] = o["o_conv"][0]
    return (y_prompt, y_sample, k_prompt, v_prompt, ik_prompt, lru_prompt, conv_prompt,
            k_sample, v_sample, ik_sample, lru_sample, conv_sample)
```

```python
import math
from contextlib import ExitStack

import numpy as np
import concourse.bass as bass
import concourse.mybir as mybir
from concourse.bass_utils import run_bass_kernel_spmd

F32 = mybir.dt.float32
BF16 = mybir.dt.bfloat16
AF = mybir.ActivationFunctionType
ALU = mybir.AluOpType
AX = mybir.AxisListType

D = 4096
SEQ = 4096
NB = 32
DEC_SEQ = 32
PAST = 2048
SS = PAST + DEC_SEQ
NH = 32
NKV = 8
HD = 128
DFF = 11008
KC = 32
TO = 1088
EPS = 1e-6
C_Q, C_K, C_V, C_IQ, C_IW, C_IK, C_XL, C_YL, C_GA, C_GL = (
    0, 4096, 5120, 6144, 10240, 10272, 10400, 14496, 18592, 22688)
IN_COLS = 26784
NEGM = -1.0e30


class Op:
    __slots__ = ("eng", "fn", "deps", "dma", "signal", "pos", "sem", "semval", "K", "waits",
                 "barrier")

    def __init__(self, eng, fn, dma=False, barrier=False):
        self.eng = eng
        self.fn = fn
        self.deps = ()
        self.dma = dma
        self.signal = False
        self.pos = 0
        self.sem = None
        self.semval = 0
        self.K = None
        self.waits = ()
        self.barrier = barrier


class Sched:
    CE = ("pe", "act", "dve", "pool", "sp")
    NDSEM = 10

    def __init__(self):
        self.ops = []
        self.last_w = {}
        self.readers = {}
        self.last_op = {e: None for e in self.CE}

    def add(self, eng, fn, r=(), w=(), dma=False):
        idx = len(self.ops)
        op = Op(eng, fn, dma=dma)
        raw = set()
        oth = set()
        for t in r:
            lw = self.last_w.get(t)
            if lw is not None:
                raw.add(lw)
        for t in w:
            lw = self.last_w.get(t)
            if lw is not None:
                oth.add(lw)
            rs = self.readers.get(t)
            if rs:
                oth.update(rs)
        deps = set()
        for d in raw | oth:
            dop = self.ops[d]
            if (not dop.dma) and (not dma) and dop.eng == eng:
                if eng == "pe":
                    continue
                if d not in raw:
                    continue
            deps.add(d)
            if not dop.dma:
                dop.signal = True
        op.deps = tuple(sorted(deps))
        for t in r:
            self.readers.setdefault(t, []).append(idx)
        for t in w:
            self.last_w[t] = idx
            self.readers[t] = []
        self.ops.append(op)
        if not dma:
            self.last_op[eng] = idx
        return idx

    def barrier(self):
        lasts = dict(self.last_op)
        for e in self.CE:
            op = Op(e, None, barrier=True)
            deps = set()
            for e2, li in lasts.items():
                if li is not None and e2 != e:
                    deps.add(li)
                    self.ops[li].signal = True
            op.deps = tuple(sorted(deps))
            self.ops.append(op)
        self.last_w = {}
        self.readers = {}

    def analyze(self):
        CE = self.CE
        K = {e: {} for e in CE}
        pos = {e: 0 for e in CE}
        sig = {e: 0 for e in CE}
        sigcount = {e: {} for e in CE}
        dq = {e: {"next": 0, "cum": [0] * self.NDSEM, "last": [None] * self.NDSEM} for e in CE}
        for op in self.ops:
            E = op.eng
            KE = K[E]
            waits = {}

            def need(key, val, kafter):
                if KE.get(key, 0) >= val:
                    return
                if waits.get(key, 0) < val:
                    waits[key] = val
                if kafter:
                    for k2, v2 in kafter.items():
                        if KE.get(k2, 0) < v2:
                            KE[k2] = v2
                if KE.get(key, 0) < val:
                    KE[key] = val

            for d in op.deps:
                dop = self.ops[d]
                if dop.dma:
                    need(("S", dop.eng, dop.sem), dop.semval, dop.K)
                else:
                    need(dop.eng, dop.pos, dop.K)
            if op.barrier:
                for q in CE:
                    for s in range(self.NDSEM):
                        if dq[q]["cum"][s] > 0:
                            lo = dq[q]["last"][s]
                            need(("S", q, s), dq[q]["cum"][s], lo.K if lo else None)
            if op.dma:
                q = dq[E]
                s = q["next"]
                q["next"] = (s + 1) % self.NDSEM
                if q["last"][s] is not None:
                    need(("S", E, s), q["cum"][s], q["last"][s].K)
                q["cum"][s] += 16
                op.sem = s
                op.semval = q["cum"][s]
                q["last"][s] = op
                op.K = dict(KE)
            elif not op.barrier:
                pos[E] += 1
                op.pos = pos[E]
                if op.signal:
                    sig[E] += 1
                    sigcount[E][op.pos] = sig[E]
                    kk = dict(KE)
                    kk[E] = op.pos
                    op.K = kk
            op.waits = tuple(waits.items())
        self.sigcount = sigcount
        self.final_dma = {e: list(dq[e]["cum"]) for e in CE}

    def emit(self, nc, block, esem, dsem):
        self.analyze()
        per = {e: [] for e in self.CE}
        for op in self.ops:
            per[op.eng].append(op)
        sigcount = self.sigcount

        def run(e, eng):
            for op in per[e]:
                for key, val in op.waits:
                    if isinstance(key, tuple):
                        eng.wait_ge(dsem[key[1]][key[2]], val)
                    else:
                        eng.wait_ge(esem[key], sigcount[key][val])
                if op.fn is None:
                    continue
                ins = op.fn(eng)
                if op.dma:
                    ins.then_inc(dsem[e][op.sem], 16)
                elif op.signal:
                    ins.then_inc(esem[e], 1)
            if e == "sp":
                for q in self.CE:
                    for s, v in enumerate(self.final_dma[q]):
                        if v > 0:
                            eng.wait_ge(dsem[q][s], v)

        @block.tensor
        def _(eng):
            run("pe", eng)

        @block.scalar
        def _(eng):
            run("act", eng)

        @block.vector
        def _(eng):
            run("dve", eng)

        @block.gpsimd
        def _(eng):
            run("pool", eng)

        @block.sync
        def _(eng):
            run("sp", eng)


class Arena:
    def __init__(self, t, words):
        self.t = t
        self.words = words
        self.off = 0

    def alloc(self, free_shape, dtype, parts=128):
        n = 1
        for s in free_shape:
            n *= s
        nbytes = n * (2 if dtype == BF16 else 4)
        w = (nbytes + 31) // 32 * 8
        off = self.off
        assert off + w <= self.words, f"arena overflow {off + w} > {self.words}"
        self.off += w
        ap = self.t[0:parts, off:off + (nbytes + 3) // 4]
        if dtype == BF16:
            ap = ap.bitcast(BF16)
            if ap.shape[1] != n:
                ap = ap[:, 0:n]
        if len(free_shape) == 2:
            ap = ap.rearrange("p (a b) -> p a b", a=free_shape[0])
        elif len(free_shape) == 3:
            ap = ap.rearrange("p (a b c) -> p a b c", a=free_shape[0], b=free_shape[1])
        return ap

    def mark(self):
        return self.off

    def release(self, m):
        self.off = m


class Builder:
    def __init__(self, debug=False, phases=None, feed=None):
        self.debug = debug
        self.feed = feed or set()
        self.phases = phases
        self.nc = bass.Bass("TRN2", target_bir_lowering=False)
        self.S = Sched()
        self.uid = 0

    def fresh(self, p):
        self.uid += 1
        return (p, self.uid)

    def din(self, name, shape, dt=F32):
        return self.nc.dram_tensor(name, list(shape), dt, kind="ExternalInput").ap()

    def dout(self, name, shape, dt=F32):
        return self.nc.dram_tensor(name, list(shape), dt, kind="ExternalOutput").ap()

    def dscr(self, name, shape, dt=F32):
        kind = "ExternalOutput" if (self.debug and name in self.debug) else "Internal"
        if name in self.feed:
            kind = "ExternalInput"
        return self.nc.dram_tensor(name, list(shape), dt, kind=kind).ap()

    def dma(self, q, out, in_, r=(), w=()):
        self.S.add(q, lambda e: e.dma_start(out=out, in_=in_), r=r, w=w, dma=True)

    def act(self, out, in_, func, r=(), w=(), **kw):
        self.S.add("act", lambda e: e.activation(out=out, in_=in_, func=func, **kw), r=r, w=w)

    def tt(self, out, in0, in1, op, r=(), w=(), eng="dve"):
        self.S.add(eng, lambda e: e.tensor_tensor(out=out, in0=in0, in1=in1, op=op), r=r, w=w)

    def ts(self, out, in0, s1, s2, op0, op1=None, r=(), w=(), eng="dve", **kw):
        if op1 is None:
            self.S.add(eng, lambda e: e.tensor_scalar(out=out, in0=in0, scalar1=s1, scalar2=None,
                                                      op0=op0, **kw), r=r, w=w)
        else:
            self.S.add(eng, lambda e: e.tensor_scalar(out=out, in0=in0, scalar1=s1, scalar2=s2,
                                                      op0=op0, op1=op1, **kw), r=r, w=w)

    def stt(self, out, in0, scalar, in1, op0, op1, r=(), w=()):
        self.S.add("dve", lambda e: e.scalar_tensor_tensor(out=out, in0=in0, scalar=scalar, in1=in1,
                                                           op0=op0, op1=op1), r=r, w=w)

    def copy(self, eng, out, in_, r=(), w=()):
        if eng == "act":
            self.S.add("act", lambda e: e.activation(out=out, in_=in_, func=AF.Copy), r=r, w=w)
        else:
            self.S.add(eng, lambda e: e.tensor_copy(out=out, in_=in_), r=r, w=w)

    def mm_group(self, out, pairs, r=(), w=()):
        n = len(pairs)

        def fn(e):
            ins = None
            for i, (l, rr) in enumerate(pairs):
                ins = e.matmul(out, l, rr, start=(i == 0), stop=(i == n - 1))
            return ins
        self.S.add("pe", fn, r=r, w=w)

    def mm(self, out, lhsT, rhs, start, stop, r=(), w=()):
        self.S.add("pe", lambda e: e.matmul(out, lhsT, rhs, start=start, stop=stop), r=r, w=w)

    def transpose(self, out, in_, ident, r=(), w=()):
        self.S.add("pe", lambda e: e.transpose(out, in_, ident), r=r, w=w)


def build_program(debug=None, phases=None, feed=None):
    B = Builder(debug=debug, phases=phases, feed=feed)
    nc = B.nc
    S = B.S
    ALL = {"seq", "cache", "lru", "own", "attn", "mix", "ffn"}
    ph = set(phases) if phases is not None else ALL
    full = ph == ALL

    _in = {}

    def inp(name, shape):
        if name not in _in:
            _in[name] = B.din(name, shape)
        return _in[name]

    KT_scr = B.dscr("KT_scr", [NKV, 128, SEQ], BF16)
    V_scr = B.dscr("V_scr", [SEQ, NKV * HD], BF16)
    ikT_scr = B.dscr("ikT_scr", [128, SEQ], BF16)
    KT_s = B.dscr("KT_s", [2, NKV, 128, SS], BF16)
    V_s = B.dscr("V_s", [2, SS, NKV * HD], BF16)
    ikT_s = B.dscr("ikT_s", [2, 128, SS], BF16)
    xlT_scr = B.dscr("xlT_scr", [KC, 128, SEQ + 64], F32)
    hown_scr = B.dscr("hown_scr", [KC, 128, TO], F32)
    QT_scr = B.dscr("QT_scr", [NH, 128, TO], BF16)
    iqT_scr = B.dscr("iqT_scr", [NH, 128, TO], BF16)
    iw_scr = B.dscr("iw_scr", [TO, 32], F32)
    olruT_scr = B.dscr("olruT_scr", [KC, 128, TO], BF16)
    saT_scr = B.dscr("saT_scr", [KC, 128, TO], F32)
    slT_scr = B.dscr("slT_scr", [KC, 128, TO], F32)
    oattT_scr = B.dscr("oattT_scr", [NH, 128, TO], BF16)
    mixT_scr = B.dscr("mixT_scr", [KC, 128, TO], BF16)
    x1_scr = B.dscr("x1_scr", [TO, D], F32)
    hT_scr = B.dscr("hT_scr", [DFF // 128, 128, TO], BF16)

    with ExitStack() as es:
        ARENA_KB = 206
        arena_t = es.enter_context(nc.sbuf_tensor("arena", [128, ARENA_KB * 256], F32))
        A = Arena(arena_t, ARENA_KB * 256)
        psum = [es.enter_context(nc.psum_tensor(f"ps{i}", [128, 512], F32)) for i in range(8)]
        esem = {e: es.enter_context(nc.semaphore(f"e_{e}")) for e in Sched.CE}
        dsem = {e: [es.enter_context(nc.semaphore(f"d_{e}{i}")) for i in range(Sched.NDSEM)]
                for e in ("act", "pool", "sp")}
        dsem["pe"] = dsem["sp"]
        dsem["dve"] = dsem["sp"]

        ident_in = inp("ident", [128, 128])
        ident_b = A.alloc([128], BF16)
        ident_f = A.alloc([128], F32)
        B.dma("pool", ident_b, ident_in, w=["ident_b"])
        B.dma("sp", ident_f, ident_in, w=["ident_f"])
        eps_t = A.alloc([1], F32)
        S.add("dve", lambda e: e.memset(eps_t, EPS), w=["eps"])
        one_t = A.alloc([1], F32)
        S.add("dve", lambda e: e.memset(one_t, 1.0), w=["one"])
        g4 = {}

        def load_g4(nm):
            src = inp("norm_" + nm, [HD])
            t = A.alloc([4, 128], F32)
            for j in range(4):
                B.dma("sp", t[:, j, :], src.partition_broadcast(128), w=[("g4", nm, j)])
            g4[nm] = t

        if "seq" in ph:
            load_g4("k")
            load_g4("idx_k")
        if "own" in ph:
            load_g4("q")
        S.barrier()

        psn = [0]

        def next_ps():
            pb = psn[0] % 4
            psn[0] += 1
            return pb

        def norm_transpose(blocks, xnT, gT, xt, junk, ssb, extra=None):
            t0 = 0
            for bi, (src, rows) in enumerate(blocks):
                B.dma("sp", xt[0:rows], src, w=["xt"])
                if extra is not None:
                    extra(bi, rows)
                ss = ssb[bi % 2]
                sk = ("ssb", bi % 2)
                S.add("act", lambda e, rows=rows, ss=ss: e.activation(
                    out=junk[0:rows], in_=xt[0:rows], func=AF.Square, accum_out=ss[0:rows]),
                    r=["xt"], w=["junk", sk])
                B.act(ss[0:rows], ss[0:rows], AF.Sqrt, r=[sk, "eps"], w=[sk], scale=1.0 / D,
                      bias=eps_t[0:rows])
                S.add("dve", lambda e, ss=ss, rows=rows: e.reciprocal(out=ss[0:rows], in_=ss[0:rows]),
                      r=[sk], w=[sk])
                B.ts(xt[0:rows], xt[0:rows], ss[0:rows], None, ALU.mult, r=["xt", sk], w=["xt"])
                for j in range(8):
                    pb = 4 + (j % 2)
                    pv = psum[pb][:, :].rearrange("p (a b) -> p a b", a=4)
                    for q in range(4):
                        kc = j * 4 + q
                        B.transpose(pv[:, q, 0:rows], xt[0:rows, kc * 128:(kc + 1) * 128],
                                    ident_f[0:rows, 0:rows], r=["xt", "ident_f"], w=[("ps", pb)])
                    B.tt(xnT[:, j * 4:(j + 1) * 4, t0:t0 + rows], pv[:, :, 0:rows],
                         gT[:, j * 4:(j + 1) * 4].unsqueeze(2).to_broadcast([128, 4, rows]),
                         ALU.mult, r=[("ps", pb), "gT"], w=[("AT", bi)])
                t0 += rows

        wl = {}

        def load_w(wbufs, src, kcn, ncols):
            wid = id(wbufs)
            slot = wl.get(wid, 0) % len(wbufs)
            wl[wid] = wl.get(wid, 0) + 1
            slot = (wid, slot)
            v = src.rearrange("(kc p) n -> p kc n", p=128)
            step = 8 if ncols > 256 else 16
            for k0 in range(0, kcn, step):
                k1 = min(kcn, k0 + step)
                B.dma("pool", wbufs[slot[1]][:, k0:k1, 0:ncols], v[:, k0:k1, :], w=[("w", slot, k0 // 8)] +
                      ([("w", slot, k0 // 8 + 1)] if step == 16 else []))
            return slot

        def wtoks(slot, kcn):
            return [("w", slot, k) for k in range((kcn + 7) // 8)]

        def gemm_tok(AT, blocks, kcn, wbufs, wsrc, col_groups, evac, attoks=None):
            nxt = load_w(wbufs, wsrc(col_groups[0][0], col_groups[0][1]), kcn, col_groups[0][1])
            for gi, (c0, ncols, tag) in enumerate(col_groups):
                slot = nxt
                if gi + 1 < len(col_groups):
                    nxt = load_w(wbufs, wsrc(col_groups[gi + 1][0], col_groups[gi + 1][1]), kcn, col_groups[gi + 1][1])
                t0 = 0
                for bi, rows in enumerate(blocks):
                    pb = next_ps()
                    ps = psum[pb][0:rows, 0:ncols]
                    B.mm_group(ps, [(AT[:, kc, t0:t0 + rows], wbufs[slot[1]][:, kc, 0:ncols])
                                    for kc in range(kcn)],
                               r=(attoks if attoks is not None else [("AT", bi)]) + wtoks(slot, kcn),
                               w=[("ps", pb)])
                    evac(tag, bi, rows, t0, ps, ("ps", pb), ncols)
                    t0 += rows

        def gemm_feat(srcs, nblk, kcn, ncols_total, cgw, ttiles, evac, attoks=None):
            nxts = [load_w(wb, wsrc(0, cgw), kcn, cgw) for (AT, wb, wsrc) in srcs]
            for c0 in range(0, ncols_total, cgw):
                slots = nxts
                if c0 + cgw < ncols_total:
                    nxts = [load_w(wb, wsrc(c0 + cgw, cgw), kcn, cgw) for (AT, wb, wsrc) in srcs]
                for ch in range(cgw // 128):
                    chunk = (c0 // 128) + ch
                    for (tt0, n) in ttiles:
                        pss = []
                        for (AT, wb, wsrc), slot in zip(srcs, slots):
                            pb = next_ps()
                            ps = psum[pb][:, 0:n]
                            B.mm_group(ps, [(wb[slot[1]][:, kc, ch * 128:(ch + 1) * 128], AT[:, kc, tt0:tt0 + n])
                                            for kc in range(kcn)],
                                       r=(attoks if attoks is not None else [("AT", bi) for bi in range(nblk)])
                                       + wtoks(slot, kcn), w=[("ps", pb)])
                            pss.append((ps, ("ps", pb)))
                        evac(chunk, tt0, n, pss)

        stc = [0]

        def make_headproc(NST):
            st = dict(
                f=[A.alloc([512], F32) for _ in range(NST)],
                t=[A.alloc([512], F32) for _ in range(NST)],
                b=[A.alloc([512], BF16) for _ in range(NST)],
                T=[A.alloc([4, 128], BF16) for _ in range(NST)],
                s=[A.alloc([4], F32) for _ in range(NST)],
                r=[A.alloc([4, 16], F32) for _ in range(4 * NST)],
                n=NST)
            return st

        def headproc(st, ps, pstok, rows, ncols, gname, cs, cstok, dst_out, dstT_fn, rope=True):
            nh = ncols // 128
            i = stc[0] % st["n"]
            stc[0] += 1
            f, t, b_, T_, s_ = st["f"][i], st["t"][i], st["b"][i], st["T"][i], st["s"][i]
            r4 = st["r"][4 * i:4 * i + 4]
            tk = ("st", i)
            B.copy("act", f[0:rows, 0:ncols], ps, r=[pstok], w=[tk])
            fv = f[0:rows, 0:ncols].rearrange("p (h d) -> p h d", h=nh)
            tv = t[0:rows, 0:ncols].rearrange("p (h d) -> p h d", h=nh)
            if gname is not None:
                for h in range(nh):
                    S.add("act", lambda e, h=h: e.activation(out=t[0:rows, h * 128:(h + 1) * 128],
                                                            in_=f[0:rows, h * 128:(h + 1) * 128], func=AF.Square,
                                                            accum_out=s_[0:rows, h:h + 1]), r=[tk], w=[tk])
                B.act(s_[0:rows, 0:nh], s_[0:rows, 0:nh], AF.Sqrt, r=[tk, "eps"], w=[tk], scale=1.0 / 128,
                      bias=eps_t[0:rows])
                S.add("dve", lambda e: e.reciprocal(out=s_[0:rows, 0:nh], in_=s_[0:rows, 0:nh]), r=[tk], w=[tk])
                B.tt(fv, fv, s_[0:rows, 0:nh].unsqueeze(2).to_broadcast([rows, nh, 128]), ALU.mult,
                     r=[tk], w=[tk])
                B.tt(fv, fv, g4[gname][0:rows, 0:nh, :], ALU.mult,
                     r=[tk] + [("g4", gname, j) for j in range(4)], w=[tk])
            if rope:
                cb = cs[0:rows, 0:16].unsqueeze(1).to_broadcast([rows, nh, 16])
                sb = cs[0:rows, 16:32].unsqueeze(1).to_broadcast([rows, nh, 16])
                x1 = fv[:, :, 0:16]
                x2 = fv[:, :, 16:32]
                ra, rb, rc, rd = [q[0:rows, 0:nh, :] for q in r4]
                rt = [tk, cstok]
                tkp = ("stp", i)
                tkd = ("std", i)
                B.tt(ra, x1, cb, ALU.mult, r=rt, w=[tkp], eng="pool")
                B.tt(rb, x2, sb, ALU.mult, r=rt, w=[tkp], eng="pool")
                B.tt(rc, x2, cb, ALU.mult, r=rt, w=[tkd])
                B.tt(rd, x1, sb, ALU.mult, r=rt, w=[tkd])
                B.tt(x1, ra, rb, ALU.subtract, r=[tkp, tkd], w=[tk], eng="pool")
                B.tt(x2, rc, rd, ALU.add, r=[tkd, tkp], w=[tk])
            if dst_out is not None:
                B.dma("sp", dst_out, f[0:rows, 0:ncols], r=[tk], w=[B.fresh("o")])
            if dstT_fn is not None:
                B.copy("act", b_[0:rows, 0:ncols], f[0:rows, 0:ncols], r=[tk], w=[tk])
                pb = 6 + (stc[0] % 2)
                pT = psum[pb][:, :].bitcast(BF16)[:, 0:512].rearrange("p (a b) -> p a b", a=4)
                for h in range(nh):
                    B.transpose(pT[:, h, 0:rows], b_[0:rows, h * 128:(h + 1) * 128],
                                ident_b[0:rows, 0:rows], r=[tk, "ident_b"], w=[("ps", pb)])
                B.copy("dve", T_[:, 0:nh, 0:rows], pT[:, 0:nh, 0:rows], r=[("ps", pb)], w=[tk])
                dstT_fn(T_, nh, rows, tk)

        if "seq" in ph:
            xseq = inp("xseq", [SEQ, D])
            xown = inp("xown", [TO, D])
            w_in = inp("w_in", [D, IN_COLS])
            cs_seq = inp("cs_seq", [SEQ + 64, 32])
            o_k = B.dout("o_k", [SEQ + 64, NKV * HD])
            o_v = B.dout("o_v", [SEQ + 64, NKV * HD])
            o_ik = B.dout("o_ik", [SEQ + 64, HD])
            m0 = A.mark()
            gmixT = A.alloc([KC], F32)
            B.dma("sp", gmixT, inp("norm_mixT", [128, KC]), w=["gT"])
            xnT = A.alloc([KC, TO], BF16)
            wbufs = [A.alloc([KC, 512], BF16) for _ in range(2)]
            xt = A.alloc([D], F32)
            junk = A.alloc([D], BF16)
            ssb = [A.alloc([1], F32) for _ in range(2)]
            st = make_headproc(3)
            cst = [A.alloc([32], F32) for _ in range(9)]
            xl_st = [A.alloc([TO], F32) for _ in range(2)]

            for tt in range(4):
                blocks = [(xseq[tt * 1024 + j * 128: tt * 1024 + (j + 1) * 128, :], 128) for j in range(8)]
                orows = [tt * 1024 + j * 128 for j in range(8)]
                if tt == 3:
                    blocks.append((xown[1024:1088, :], 64))
                    orows.append(SEQ)
                ntok = sum(b[1] for b in blocks)

                def extra(bi, rows, orows=orows):
                    B.dma("sp", cst[bi][0:rows], cs_seq[orows[bi]:orows[bi] + rows, :], w=[("cst", bi)])
                norm_transpose(blocks, xnT, gmixT, xt, junk, ssb, extra)

                def evac(tag, bi, rows, t0, ps, pstok, ncols, orows=orows):
                    kind, half = tag
                    orow = orows[bi]
                    is_s = orow >= SEQ
                    if kind == "k":
                        def dstT(T_, nh, rows_, tk):
                            if not is_s:
                                B.dma("act", KT_scr[half * 4:half * 4 + 4, :, orow:orow + rows_]
                                      .rearrange("h d t -> d h t"), T_[:, 0:4, 0:rows_], r=[tk], w=[B.fresh("kt")])
                            else:
                                for sq in range(2):
                                    B.dma("act", KT_s[sq, half * 4:half * 4 + 4, :, PAST:SS]
                                          .rearrange("h d t -> d h t"), T_[:, 0:4, sq * 32:(sq + 1) * 32],
                                          r=[tk], w=[B.fresh("kt")])
                        headproc(st, ps, pstok, rows, ncols, "k", cst[bi], ("cst", bi),
                                 o_k[orow:orow + rows, half * 512:(half + 1) * 512], dstT)
                    elif kind == "ik":
                        def dstT(T_, nh, rows_, tk):
                            if not is_s:
                                B.dma("act", ikT_scr[:, orow:orow + rows_], T_[:, 0, 0:rows_], r=[tk],
                                      w=[B.fresh("ikt")])
                            else:
                                for sq in range(2):
                                    B.dma("act", ikT_s[sq, :, PAST:SS], T_[:, 0, sq * 32:(sq + 1) * 32],
                                          r=[tk], w=[B.fresh("ikt")])
                        headproc(st, ps, pstok, rows, ncols, "idx_k", cst[bi], ("cst", bi),
                                 o_ik[orow:orow + rows, :], dstT)
                    else:
                        i = stc[0] % st["n"]
                        stc[0] += 1
                        tk = ("st", i)
                        B.copy("act", st["f"][i][0:rows, 0:512], ps, r=[pstok], w=[tk])
                        B.copy("dve", st["b"][i][0:rows, 0:512], ps, r=[pstok], w=[tk])
                        B.dma("sp", o_v[orow:orow + rows, half * 512:(half + 1) * 512],
                              st["f"][i][0:rows, 0:512], r=[tk], w=[B.fresh("o")])
                        if not is_s:
                            B.dma("act", V_scr[orow:orow + rows, half * 512:(half + 1) * 512],
                                  st["b"][i][0:rows, 0:512], r=[tk], w=[B.fresh("v")])
                        else:
                            for sq in range(2):
                                B.dma("act", V_s[sq, PAST:SS, half * 512:(half + 1) * 512],
                                      st["b"][i][sq * 32:(sq + 1) * 32, 0:512], r=[tk], w=[B.fresh("v")])

                gemm_tok(xnT, [b[1] for b in blocks], KC, wbufs, lambda c0, n: w_in[:, c0:c0 + n],
                         [(C_K, 512, ("k", 0)), (C_K + 512, 512, ("k", 1)), (C_V, 512, ("v", 0)),
                          (C_V + 512, 512, ("v", 1)), (C_IK, 128, ("ik", 0))], evac)

                ttiles = [(0, 512), (512, 512)] + ([(1024, 64)] if tt == 3 else [])
                col0 = tt * 1024

                def evac_xl(chunk, tt0, n, pss, ntok=ntok, col0=col0, last=ttiles[-1][0]):
                    xs = xl_st[chunk % 2]
                    ps, pstok = pss[0]
                    B.copy("act" if (tt0 // 512) % 2 else "dve", xs[:, tt0:tt0 + n], ps, r=[pstok],
                           w=[("xls", chunk % 2)])
                    if tt0 == last:
                        B.dma("sp", xlT_scr[chunk, :, col0:col0 + ntok], xs[:, 0:ntok],
                              r=[("xls", chunk % 2)], w=[B.fresh("xl")])
                gemm_feat([(xnT, wbufs, lambda c0, n: w_in[:, C_XL + c0:C_XL + c0 + n])], len(blocks), KC,
                          4096, 512, ttiles, evac_xl)
            S.barrier()
            A.release(m0)

        if "cache" in ph:
            cache_k = inp("cache_k", [2, PAST, NKV * HD])
            cache_v = inp("cache_v", [2, PAST, NKV * HD])
            cache_ik = inp("cache_ik", [2, PAST, HD])
            m0 = A.mark()
            kb = A.alloc([16, 1024], BF16)
            vb = A.alloc([16, 1024], BF16)
            ib = A.alloc([16, 128], BF16)
            stg = [A.alloc([8, 128], BF16) for _ in range(2)]
            istg = A.alloc([16, 128], BF16)
            for sq in range(2):
                for q in range(4):
                    B.dma("pool", kb[:, q * 4:(q + 1) * 4, :],
                          cache_k[sq, q * 512:(q + 1) * 512, :].rearrange("(b p) c -> p b c", p=128), w=[("kb", q)])
                    B.dma("pool", vb[:, q * 4:(q + 1) * 4, :],
                          cache_v[sq, q * 512:(q + 1) * 512, :].rearrange("(b p) c -> p b c", p=128), w=[("vb", q)])
                B.dma("pool", ib, cache_ik[sq].rearrange("(b p) c -> p b c", p=128), w=["ib"])
                for q in range(4):
                    B.dma("act", V_s[sq, q * 512:(q + 1) * 512, :].rearrange("(b p) c -> p b c", p=128),
                          vb[:, q * 4:(q + 1) * 4, :], r=[("vb", q)], w=[B.fresh("vs")])
                for blk in range(16):
                    pb = 6 + (blk % 2)
                    pT = psum[pb][:, :].bitcast(BF16).rearrange("p (a b) -> p a b", a=8)
                    for h in range(8):
                        B.transpose(pT[:, h, :], kb[:, blk, h * 128:(h + 1) * 128], ident_b,
                                    r=[("kb", blk // 4), "ident_b"], w=[("ps", pb)])
                    sg = stg[blk % 2]
                    B.copy("dve" if blk % 2 else "act", sg, pT, r=[("ps", pb)], w=[("stg", blk % 2)])
                    B.dma("sp", KT_s[sq, :, :, blk * 128:(blk + 1) * 128].rearrange("h d t -> d h t"), sg,
                          r=[("stg", blk % 2)], w=[B.fresh("kts")])
                for half in range(2):
                    pb = 4 + half
                    pT = psum[pb][:, :].bitcast(BF16).rearrange("p (a b) -> p a b", a=8)
                    for j in range(8):
                        B.transpose(pT[:, j, :], ib[:, half * 8 + j, :], ident_b, r=["ib", "ident_b"],
                                    w=[("ps", pb)])
                    B.copy("dve", istg[:, half * 8:(half + 1) * 8, :], pT, r=[("ps", pb)], w=["istg"])
                B.dma("sp", ikT_s[sq, :, 0:PAST], istg, r=["istg"], w=[B.fresh("ikts")])
            S.barrier()
            A.release(m0)

        if "lru" in ph:
            convwT = inp("convwT", [128, 4, KC])
            convbT = inp("convbT", [128, KC])
            baT = inp("lru_baT", [128, KC])
            bxT = inp("lru_bxT", [128, KC])
            lamT = inp("lru_lamT", [128, KC])
            lru_wa = inp("lru_wa", [16, 256, 256])
            lru_wx = inp("lru_wx", [16, 256, 256])
            st_lruT = inp("state_lruT", [2, 128, KC])
            st_convT = inp("state_convT", [2, 128, KC, 3])
            selr = inp("selr", [128, 4])
            o_lru = B.dout("o_lru", [3, D])
            o_conv = B.dout("o_conv", [3, 3, D])
            m0 = A.mark()
            cw = A.alloc([4, KC], F32)
            cb = A.alloc([KC], F32)
            ba = A.alloc([KC], F32)
            bx = A.alloc([KC], F32)
            lam = A.alloc([KC], F32)
            cneg = A.alloc([KC], F32)
            cneg2 = A.alloc([KC], F32)
            tmpc = A.alloc([KC], F32)
            sel = A.alloc([4], F32)
            h0s = A.alloc([2, KC], F32)
            c0s = A.alloc([2, KC, 3], F32)
            hfin = A.alloc([3, KC], F32)
            cfin = A.alloc([3, 3, KC], F32)
            B.dma("sp", cw, convwT, w=["lc"])
            B.dma("sp", cb, convbT, w=["lc"])
            B.dma("sp", ba, baT, w=["lc"])
            B.dma("sp", bx, bxT, w=["lc"])
            B.dma("sp", lam, lamT, w=["lam"])
            B.dma("sp", sel, selr, w=["lc"])
            for sq in range(2):
                B.dma("sp", h0s[:, sq, :], st_lruT[sq], w=["lc"])
                B.dma("sp", c0s[:, sq, :, :], st_convT[sq], w=["lc"])
            B.ts(tmpc, lam, -1.0, None, ALU.mult, r=["lam"], w=["tmpc"])
            B.tt(tmpc, tmpc, lam, ALU.max, r=["lam", "tmpc"], w=["tmpc"])
            B.act(tmpc, tmpc, AF.Exp, r=["tmpc"], w=["tmpc"], scale=-1.0)
            B.act(tmpc, tmpc, AF.Ln, r=["tmpc", "one"], w=["tmpc"], bias=one_t[:, 0:1])
            B.ts(cneg, lam, -1.0, 0.0, ALU.mult, ALU.max, r=["lam"], w=["cneg"])
            B.tt(cneg, cneg, tmpc, ALU.add, r=["cneg", "tmpc"], w=["cneg"])
            B.ts(cneg2, cneg, -16.0, None, ALU.mult, r=["cneg"], w=["cneg2"])
            B.ts(cneg, cneg, -8.0, None, ALU.mult, r=["cneg"], w=["cneg"])

            XL = [[A.alloc([1027], F32) for _ in range(2)] for _ in range(2)]
            U2 = [[A.alloc([1024], F32) for _ in range(2)] for _ in range(2)]
            UB2 = [[A.alloc([1024], BF16) for _ in range(2)] for _ in range(2)]
            Rg2 = [[A.alloc([1024], F32) for _ in range(2)] for _ in range(2)]
            Ig2 = [[A.alloc([1024], F32) for _ in range(2)] for _ in range(2)]
            Aa2 = [[A.alloc([1024], F32) for _ in range(2)] for _ in range(2)]
            Hh2 = [[A.alloc([1024], F32) for _ in range(2)] for _ in range(2)]
            HO2 = [[A.alloc([256], F32) for _ in range(2)] for _ in range(2)]
            hprev = A.alloc([2], F32)
            wab = [A.alloc([2, 256], BF16) for _ in range(2)]
            wxb = [A.alloc([2, 256], BF16) for _ in range(2)]
            pieces = [(tt * 1024, 1024, "p", tt) for tt in range(4)] + [(SEQ, 32, "s", 0), (SEQ + 32, 32, "s", 1)]
            for nblk in range(16):
                wsl = nblk % 2
                B.dma("pool", wab[wsl], lru_wa[nblk].rearrange("(ch p) d -> p ch d", p=128), w=[("wa", wsl)])
                B.dma("pool", wxb[wsl], lru_wx[nblk].rearrange("(ch p) d -> p ch d", p=128), w=[("wx", wsl)])
                for pi, (col0, n, kind, idx) in enumerate(pieces):
                    xb = XL[pi % 2]
                    xprev = XL[(pi + 1) % 2]
                    pp = pi % 2
                    U, UB, Rg, Ig, Aa, Hh, HO = U2[pp], UB2[pp], Rg2[pp], Ig2[pp], Aa2[pp], Hh2[pp], HO2[pp]
                    for ch in range(2):
                        chunk = 2 * nblk + ch
                        xk = ("xl", pi % 2, ch)
                        B.dma("sp", xb[ch][:, 3:3 + n], xlT_scr[chunk, :, col0:col0 + n], w=[xk])
                        if kind == "p" and idx == 0:
                            S.add("dve", lambda e, t=xb[ch]: e.memset(t[:, 0:3], 0.0), w=[xk])
                        elif kind == "p":
                            B.copy("dve", xb[ch][:, 0:3], xprev[ch][:, 1024:1027],
                                   r=[("xl", (pi + 1) % 2, ch)], w=[xk])
                        else:
                            B.copy("dve", xb[ch][:, 0:3], c0s[:, idx, chunk, :], r=["lc"], w=[xk])
                        u = U[ch]
                        uk = ("u", pp, ch)
                        B.ts(u[:, 0:n], xb[ch][:, 3:3 + n], cw[:, 3, chunk:chunk + 1], cb[:, chunk:chunk + 1],
                             ALU.mult, ALU.add, r=[xk, "lc"], w=[uk])
                        for j in (2, 1, 0):
                            B.stt(u[:, 0:n], xb[ch][:, j:j + n], cw[:, j, chunk:chunk + 1], u[:, 0:n],
                                  ALU.mult, ALU.add, r=[xk, "lc", uk], w=[uk])
                        B.copy("act", UB[ch][:, 0:n], u[:, 0:n], r=[uk], w=[("ub", pp, ch)])
                    halves = [(0, min(n, 512))] + ([(512, 512)] if n > 512 else [])
                    for dh in range(2):
                        chunk = 2 * nblk + dh
                        for (gbuf, wbuf_, bias, gk, wk) in ((Rg, wab, ba, "rg", "wa"), (Ig, wxb, bx, "ig", "wx")):
                            for (h0, hn) in halves:
                                pb = next_ps()
                                ps = psum[pb][:, 0:hn]
                                B.mm_group(ps, [(wbuf_[wsl][:, ch, dh * 128:(dh + 1) * 128], UB[ch][:, h0:h0 + hn])
                                                for ch in range(2)],
                                           r=[("ub", pp, 0), ("ub", pp, 1), (wk, wsl)], w=[("ps", pb)])
                                B.act(gbuf[dh][:, h0:h0 + hn], ps, AF.Sigmoid, r=[("ps", pb), "lc"],
                                      w=[(gk, dh)], bias=bias[:, chunk:chunk + 1])
                    for dh in range(2):
                        chunk = 2 * nblk + dh
                        B.act(Aa[dh][:, 0:n], Rg[dh][:, 0:n], AF.Exp, r=[("rg", pp, dh), "cneg"], w=[("aa", pp, dh)],
                              scale=cneg[:, chunk:chunk + 1])
                        B.act(Rg[dh][:, 0:n], Rg[dh][:, 0:n], AF.Exp, r=[("rg", pp, dh), "cneg2"], w=[("rg", pp, dh)],
                              scale=cneg2[:, chunk:chunk + 1])
                    for dh in range(2):
                        B.ts(Rg[dh][:, 0:n], Rg[dh][:, 0:n], 1.0, -1.0, ALU.min, ALU.mult, r=[("rg", pp, dh)],
                             w=[("rg", pp, dh)])
                        B.act(Rg[dh][:, 0:n], Rg[dh][:, 0:n], AF.Sqrt, r=[("rg", pp, dh), "one"], w=[("rg", pp, dh)],
                              scale=1.0, bias=one_t[:, 0:1])
                    for dh in range(2):
                        chunk = 2 * nblk + dh
                        B.tt(Ig[dh][:, 0:n], Ig[dh][:, 0:n], U[dh][:, 0:n], ALU.mult, r=[("ig", pp, dh), ("u", pp, dh)],
                             w=[("ig", pp, dh)])
                        B.tt(Ig[dh][:, 0:n], Ig[dh][:, 0:n], Rg[dh][:, 0:n], ALU.mult, r=[("ig", pp, dh), ("rg", pp, dh)],
                             w=[("ig", pp, dh)])
                        if kind == "p" and idx == 0:
                            init = 0.0
                            ir = []
                        elif kind == "p":
                            init = hprev[:, dh:dh + 1]
                            ir = [("hp", dh)]
                        else:
                            init = h0s[:, idx, chunk:chunk + 1]
                            ir = ["lc"]
                        S.add("dve", lambda e, dh=dh, n=n, init=init, Hh=Hh, Aa=Aa, Ig=Ig: e.tensor_tensor_scan(
                            out=Hh[dh][:, 0:n], data0=Aa[dh][:, 0:n], data1=Ig[dh][:, 0:n], initial=init,
                            op0=ALU.mult, op1=ALU.add), r=[("aa", pp, dh), ("ig", pp, dh)] + ir, w=[("hh", pp, dh)])
                        if kind == "p" and idx < 3:
                            B.copy("dve", hprev[:, dh:dh + 1], Hh[dh][:, n - 1:n], r=[("hh", pp, dh)], w=[("hp", dh)])
                        if kind == "s" or idx == 3:
                            row = 0 if kind == "p" else 1 + idx
                            B.copy("dve", hfin[:, row, chunk:chunk + 1], Hh[dh][:, n - 1:n], r=[("hh", pp, dh)],
                                   w=["hfin"])
                            for j in range(3):
                                B.copy("dve", cfin[:, row, j, chunk:chunk + 1],
                                       XL[pi % 2][dh][:, n + j:n + j + 1], r=[("xl", pi % 2, dh)], w=["cfin"])
                        ho = HO[dh]
                        hk = ("ho", pp, dh)
                        if kind == "p":
                            for il in range(2):
                                B.ts(ho[:, il * 128:(il + 1) * 128], Hh[dh][:, (4 * il) * 128:(4 * il + 1) * 128],
                                     sel[:, 0:1], None, ALU.mult, r=[("hh", pp, dh), "lc"], w=[hk])
                                for j in range(1, 4):
                                    B.stt(ho[:, il * 128:(il + 1) * 128],
                                          Hh[dh][:, (4 * il + j) * 128:(4 * il + j + 1) * 128], sel[:, j:j + 1],
                                          ho[:, il * 128:(il + 1) * 128], ALU.mult, ALU.add,
                                          r=[("hh", pp, dh), "lc", hk], w=[hk])
                            B.dma("act", hown_scr[chunk, :, idx * 256:(idx + 1) * 256], ho, r=[hk], w=[B.fresh("ho")])
                        else:
                            B.dma("act", hown_scr[chunk, :, 1024 + idx * 32:1024 + (idx + 1) * 32], Hh[dh][:, 0:32],
                                  r=[("hh", pp, dh)], w=[B.fresh("ho")])
            fst = A.alloc([12, 128], F32)
            pbv = psum[4][:, :].rearrange("p (a b) -> p a b", a=4)
            pbv2 = psum[5][:, :].rearrange("p (a b) -> p a b", a=4)
            pbv3 = psum[6][:, :].rearrange("p (a b) -> p a b", a=4)
            for row in range(3):
                pv = (pbv, pbv2, pbv3)[row]
                pk = ("ps", 4 + row)
                B.transpose(pv[0:32, 0, :], hfin[:, row, :], ident_f, r=["hfin", "ident_f"], w=[pk])
                for j in range(3):
                    B.transpose(pv[0:32, 1 + j, :], cfin[:, row, j, :], ident_f, r=["cfin", "ident_f"], w=[pk])
                B.copy("dve", fst[0:32, row * 4:(row + 1) * 4, :], pv[0:32, :, :], r=[pk], w=["fst"])
                B.dma("sp", o_lru[row].rearrange("(kc p) -> kc p", p=128), fst[0:32, row * 4, :], r=["fst"],
                      w=[B.fresh("o")])
                for j in range(3):
                    B.dma("sp", o_conv[row, j].rearrange("(kc p) -> kc p", p=128), fst[0:32, row * 4 + 1 + j, :],
                          r=["fst"], w=[B.fresh("o")])
            S.barrier()
            A.release(m0)

        if "own" in ph:
            xown = inp("xown", [TO, D])
            w_in = inp("w_in", [D, IN_COLS])
            cs_own = inp("cs_own", [TO, 32])
            m0 = A.mark()
            gmixT = A.alloc([KC], F32)
            B.dma("sp", gmixT, inp("norm_mixT", [128, KC]), w=["gT"])
            xnT = A.alloc([KC, TO], BF16)
            wbufs = [A.alloc([KC, 512], BF16) for _ in range(2)]
            xt = A.alloc([D], F32)
            junk = A.alloc([D], BF16)
            ssb = [A.alloc([1], F32) for _ in range(2)]
            st = make_headproc(3)
            cst = [A.alloc([32], F32) for _ in range(9)]
            blocks = [(xown[j * 128:(j + 1) * 128, :], 128) for j in range(8)] + [(xown[1024:1088, :], 64)]
            brow = [b[1] for b in blocks]

            def extra(bi, rows):
                B.dma("sp", cst[bi][0:rows], cs_own[bi * 128:bi * 128 + rows, :], w=[("cst", bi)])
            norm_transpose(blocks, xnT, gmixT, xt, junk, ssb, extra)

            def evac(tag, bi, rows, t0, ps, pstok, ncols):
                kind, g = tag
                if kind == "q":
                    def dstT(T_, nh, rows_, tk):
                        B.dma("act", QT_scr[g * 4:g * 4 + 4, :, t0:t0 + rows_].rearrange("h d t -> d h t"),
                              T_[:, 0:4, 0:rows_], r=[tk], w=[B.fresh("qt")])
                    headproc(st, ps, pstok, rows, ncols, "q", cst[bi], ("cst", bi), None, dstT)
                elif kind == "iq":
                    def dstT(T_, nh, rows_, tk):
                        B.dma("act", iqT_scr[g * 4:g * 4 + 4, :, t0:t0 + rows_].rearrange("h d t -> d h t"),
                              T_[:, 0:4, 0:rows_], r=[tk], w=[B.fresh("iqt")])
                    headproc(st, ps, pstok, rows, ncols, None, cst[bi], ("cst", bi), None, dstT)
                else:
                    i = stc[0] % st["n"]
                    stc[0] += 1
                    tk = ("st", i)
                    B.copy("act", st["f"][i][0:rows, 0:32], ps, r=[pstok], w=[tk])
                    B.dma("sp", iw_scr[t0:t0 + rows, :], st["f"][i][0:rows, 0:32], r=[tk], w=[B.fresh("iw")])
            groups = [(C_Q + g * 512, 512, ("q", g)) for g in range(8)] + \
                     [(C_IQ + g * 512, 512, ("iq", g)) for g in range(8)] + [(C_IW, 32, ("iw", 0))]
            gemm_tok(xnT, brow, KC, wbufs, lambda c0, n: w_in[:, c0:c0 + n], groups, evac)

            ttiles = [(0, 512), (512, 512), (1024, 64)]
            NE = 2
            ey = [A.alloc([512], F32) for _ in range(NE)]
            et = [A.alloc([512], F32) for _ in range(NE)]
            eh = [A.alloc([512], F32) for _ in range(NE)]
            eb = [A.alloc([512], BF16) for _ in range(NE)]
            ec = [0]

            def evac_feat(which):
                def ev(chunk, tt0, n, pss):
                    ps, pstok = pss[0]
                    i = ec[0] % NE
                    ec[0] += 1
                    tk = ("e", i)
                    if which == "yl":
                        y, t, hh, ob = ey[i], et[i], eh[i], eb[i]
                        B.dma("sp", hh[:, 0:n], hown_scr[chunk, :, tt0:tt0 + n], w=[("eh", i)])
                        B.copy("act", y[:, 0:n], ps, r=[pstok], w=[tk])
                        B.tt(t[:, 0:n], y[:, 0:n], y[:, 0:n], ALU.mult, r=[tk], w=[tk])
                        B.ts(t[:, 0:n], t[:, 0:n], 0.044715, 1.0, ALU.mult, ALU.add, r=[tk], w=[tk])
                        B.tt(t[:, 0:n], t[:, 0:n], y[:, 0:n], ALU.mult, r=[tk], w=[tk])
                        B.act(t[:, 0:n], t[:, 0:n], AF.Sigmoid, r=[tk], w=[tk], scale=1.5957691216057308)
                        B.tt(t[:, 0:n], t[:, 0:n], y[:, 0:n], ALU.mult, r=[tk], w=[tk])
                        B.tt(ob[:, 0:n], t[:, 0:n], hh[:, 0:n], ALU.mult, r=[tk, ("eh", i)], w=[tk])
                        B.dma("act", olruT_scr[chunk, :, tt0:tt0 + n], ob[:, 0:n], r=[tk], w=[B.fresh("ol")])
                    else:
                        y = ey[i]
                        B.act(y[:, 0:n], ps, AF.Sigmoid, r=[pstok], w=[tk])
                        dst = saT_scr if which == "ga" else slT_scr
                        B.dma("act", dst[chunk, :, tt0:tt0 + n], y[:, 0:n], r=[tk], w=[B.fresh("sg")])
                return ev
            for which, cbase in (("yl", C_YL), ("ga", C_GA), ("gl", C_GL)):
                gemm_feat([(xnT, wbufs, lambda c0, n, cbase=cbase: w_in[:, cbase + c0:cbase + c0 + n])], 9, KC,
                          4096, 512, ttiles, evac_feat(which))
            S.barrier()
            A.release(m0)

        if "attn" in ph:
            dsel_in = inp("dsel", [128, 32 * 128])
            mbias_in = inp("maskbias", [128, 512])
            m0 = A.mark()
            dselt = A.alloc([32, 128], BF16)
            dv = dsel_in.rearrange("p (g t) -> p g t", g=32)
            for g0 in range(0, 32, 8):
                B.dma("pool", dselt[:, g0:g0 + 8, :], dv[:, g0:g0 + 8, :], w=[("dsel", g0)])
            S.add("dve", lambda e: e.memset(eps_t, EPS), r=[("dsel", g0) for g0 in range(0, 32, 8)], w=["dsel"])
            mbias = A.alloc([512], F32)
            B.dma("sp", mbias, mbias_in, w=["mbias"])
            ones_b = A.alloc([128], BF16)
            S.add("dve", lambda e: e.memset(ones_b, 1.0), w=["ones"])
            gq = A.alloc([128], F32)
            gk = A.alloc([128], F32)
            B.dma("sp", gq, inp("norm_q", [HD]).partition_broadcast(128), w=["gq"])
            B.dma("sp", gk, inp("norm_k", [HD]).partition_broadcast(128), w=["gk"])
            cq = A.alloc([1], F32)
            ck = A.alloc([1], F32)
            S.add("dve", lambda e: e.tensor_reduce(out=cq, in_=gq, axis=AX.X, op=ALU.max, apply_absolute_value=True),
                  r=["gq"], w=["cq"])
            S.add("dve", lambda e: e.tensor_reduce(out=ck, in_=gk, axis=AX.X, op=ALU.max, apply_absolute_value=True),
                  r=["gk"], w=["ck"])
            B.tt(cq, cq, ck, ALU.mult, r=["cq", "ck"], w=["cq"])
            B.ts(cq, cq, -math.sqrt(128.0), None, ALU.mult, r=["cq"], w=["cq"])

            iqTb = A.alloc([32, 128], BF16)
            iqTg = A.alloc([32, 128], BF16)
            ikT = A.alloc([SEQ], BF16)
            wsel = A.alloc([32, 128], BF16)
            iwb = A.alloc([32], F32)
            iwrep = A.alloc([32, 4], BF16)
            R1 = [A.alloc([512], BF16) for _ in range(3)]
            ImB = [A.alloc([SEQ], F32) for _ in range(2)]
            Wk = A.alloc([SEQ], F32)
            m8 = A.alloc([8], F32)
            thrB = [A.alloc([1], F32) for _ in range(2)]
            maskb = A.alloc([SEQ], BF16)
            maskTB = [A.alloc([32, 128], BF16) for _ in range(2)]
            KTg = [A.alloc([SEQ], BF16) for _ in range(2)]
            Vg = [A.alloc([32, 128], BF16) for _ in range(2)]
            QTg = [A.alloc([512], BF16) for _ in range(2)]
            Pt = [A.alloc([512], BF16) for _ in range(3)]
            Pm = [A.alloc([512], BF16) for _ in range(3)]
            NEV = 4
            zs = [A.alloc([512], F32) for _ in range(NEV)]
            osb = [A.alloc([512], F32) for _ in range(NEV)]
            evc = [0]
            ot = [A.alloc([512], BF16) for _ in range(2)]
            sc_att = 1.0 / math.sqrt(128.0)

            qblocks = [("p", i, i * 128, 128, 512 * (i + 1)) for i in range(7, -1, -1)] + \
                      [("s", sq, 1024 + sq * 32, 32, SS) for sq in range(2)]
            r1c = [0]
            pc = [0]
            gct = [0]

            def srcs(kind, idx):
                if kind == "p":
                    return ikT_scr, KT_scr, V_scr
                return ikT_s[idx], KT_s[idx], V_s[idx]

            def stageA(blk, par):
                kind, idx, t0, R, Sk = blk
                G = R // 4
                Im = ImB[par]
                imk = ("Im", par)
                ikT_src = srcs(kind, idx)[0]
                iqv = iqTb.rearrange("p a b -> p (a b)")[:, 0:32 * R].rearrange("p (h t) -> p h t", h=32)
                B.dma("sp", iqv, iqT_scr[:, :, t0:t0 + R].rearrange("h d t -> d h t"), w=["iqTb"])
                B.copy("pool", iqTg[:, 0:G, :].rearrange("p g (h t) -> p g h t", t=4),
                       iqv.rearrange("p h (g t) -> p g h t", t=4), r=["iqTb"], w=["iqTg"])
                B.dma("sp", ikT[:, 0:Sk], ikT_src[:, 0:Sk], w=["ikT"])
                B.dma("sp", iwb[0:R], iw_scr[t0:t0 + R, :], w=["iwb"])
                B.copy("pool", iwrep[0:R], iwb[0:R].unsqueeze(2).to_broadcast([R, 32, 4]), r=["iwb"], w=["iwrep"])
                pT = psum[3][:, :].bitcast(BF16)
                B.transpose(pT[:, 0:R], iwrep[0:R].rearrange("p h t -> p (h t)"), ident_b[0:R, 0:R],
                            r=["iwrep", "ident_b"], w=[("ps", 3)])
                B.tt(wsel[:, 0:G, 0:R], dselt[:, 0:G, 0:R], pT[:, 0:R].unsqueeze(1).to_broadcast([128, G, R]),
                     ALU.mult, r=[("ps", 3), "dsel"], w=["wsel"])
                chunks = [(c0, min(512, Sk - c0)) for c0 in range(0, Sk, 512)]
                items = [(c0, cn, g) for (c0, cn) in chunks for g in range(G)]
                pend = None
                for it in range(len(items) + 1):
                    cur_item = None
                    if it < len(items):
                        c0, cn, g = items[it]
                        pb = r1c[0] % 2
                        rr = R1[r1c[0] % 3]
                        rk = ("r1", r1c[0] % 3)
                        r1c[0] += 1
                        ps1 = psum[pb][:, 0:cn]
                        B.mm(ps1, iqTg[:, g, :], ikT[:, c0:c0 + cn], True, True, r=["iqTg", "ikT"], w=[("ps", pb)])
                        B.act(rr[:, 0:cn], ps1, AF.Relu, r=[("ps", pb)], w=[rk])
                        cur_item = (c0, cn, g, rr, rk)
                    if pend is not None:
                        c0p, cnp, gp, rrp, rkp = pend
                        psI = psum[2][0:R, 0:cnp]
                        B.mm(psI, wsel[:, gp, 0:R], rrp[:, 0:cnp], gp == 0, gp == G - 1, r=["wsel", rkp], w=[("ps", 2)])
                        if gp == G - 1:
                            B.copy("act", Im[0:R, c0p:c0p + cnp], psI, r=[("ps", 2)], w=[imk])
                    pend = cur_item

            def stageB1(blk, par):
                kind, idx, t0, R, Sk = blk
                Im = ImB[par]
                imk = ("Im", par)
                if kind == "p":
                    B.tt(Im[0:R, Sk - 512:Sk], Im[0:R, Sk - 512:Sk], mbias[0:R, 0:512], ALU.add,
                         r=[imk, "mbias"], w=[imk])
                cur = Im
                for rnd in range(32):
                    S.add("dve", lambda e, cur=cur, R=R, Sk=Sk: e.max(out=m8[0:R], in_=cur[0:R, 0:Sk]),
                          r=[imk, "Wk"], w=["m8"])
                    if rnd < 31:
                        S.add("dve", lambda e, cur=cur, R=R, Sk=Sk: e.match_replace(
                            out=Wk[0:R, 0:Sk], in_to_replace=m8[0:R], in_values=cur[0:R, 0:Sk], imm_value=-3.0e38),
                            r=[imk, "Wk", "m8"], w=["Wk"])
                        cur = Wk
                B.ts(thrB[par][0:R], m8[0:R, 7:8], -5.0e29, None, ALU.max, r=["m8"], w=[("thr", par)])

            def stageB2(blk, par):
                kind, idx, t0, R, Sk = blk
                Im = ImB[par]
                maskT = maskTB[par]
                mk = ("maskT", par)
                B.ts(maskb[0:R, 0:Sk], Im[0:R, 0:Sk], thrB[par][0:R], None, ALU.is_ge,
                     r=[("Im", par), ("thr", par)], w=["maskb"])
                nkb = (Sk + 127) // 128
                for k0 in range(0, nkb, 8):
                    pT8 = psum[3][:, :].bitcast(BF16).rearrange("p (a b) -> p a b", a=8)
                    k1 = min(nkb, k0 + 8)
                    for kb_ in range(k0, k1):
                        kn = min(128, Sk - kb_ * 128)
                        B.transpose(pT8[0:kn, kb_ - k0, 0:R], maskb[0:R, kb_ * 128:kb_ * 128 + kn], ident_b[0:R, 0:R],
                                    r=["maskb", "ident_b"], w=[("ps", 3)])
                    kfull = [kb_ for kb_ in range(k0, k1) if Sk - kb_ * 128 >= 128]
                    if kfull:
                        B.act(maskT[:, kfull[0]:kfull[-1] + 1, 0:R], pT8[:, 0:len(kfull), 0:R], AF.Copy,
                              r=[("ps", 3)], w=[mk], scale=30000.0, bias=-30000.0)
                    if len(kfull) < k1 - k0:
                        kb_ = k1 - 1
                        kn = Sk - kb_ * 128
                        B.act(maskT[0:kn, kb_, 0:R], pT8[0:kn, kb_ - k0, 0:R], AF.Copy, r=[("ps", 3)], w=[mk],
                              scale=30000.0, bias=-30000.0)

            def stageC(blk, par):
                kind, idx, t0, R, Sk = blk
                maskT = maskTB[par]
                mk = ("maskT", par)
                _, KT_src, V_src = srcs(kind, idx)
                nkb = (Sk + 127) // 128
                for g in range(NKV):
                    sl = gct[0] % 2
                    gct[0] += 1
                    B.dma("sp", KTg[sl][:, 0:Sk], KT_src[g, :, 0:Sk], w=[("KTg", sl)])
                    nfull = Sk // 128
                    B.dma("act", Vg[sl][:, 0:nfull, :],
                          V_src[0:nfull * 128, g * 128:(g + 1) * 128].rearrange("(b p) d -> p b d", p=128),
                          w=[("Vg", sl)])
                    if Sk % 128:
                        kn = Sk % 128
                        B.dma("act", Vg[sl][0:kn, nfull, :], V_src[nfull * 128:Sk, g * 128:(g + 1) * 128],
                              w=[("Vg", sl)])
                    B.dma("sp", QTg[sl][:, 0:4 * R].rearrange("p (h t) -> p h t", h=4),
                          QT_scr[4 * g:4 * g + 4, :, t0:t0 + R].rearrange("h d t -> d h t"), w=[("QTg", sl)])
                    N4 = 4 * R
                    psO = psum[6][:, 0:N4]
                    psZ = psum[7][:, 0:N4]
                    pendc = None
                    for it in range(nkb + 1):
                        curc = None
                        if it < nkb:
                            kb_ = it
                            kn = min(128, Sk - kb_ * 128)
                            pb = 4 + (pc[0] % 2)
                            pi_ = pc[0] % 3
                            pc[0] += 1
                            psL = psum[pb][0:kn, 0:N4]
                            B.mm(psL, KTg[sl][:, kb_ * 128:kb_ * 128 + kn], QTg[sl][:, 0:4 * R], True, False,
                                 r=[("KTg", sl), ("QTg", sl)], w=[("ps", pb)])
                            for h in range(4):
                                B.mm(psum[pb][0:kn, h * R:(h + 1) * R], ident_b[0:kn, 0:kn], maskT[0:kn, kb_, 0:R],
                                     False, h == 3, r=["ident_b", mk], w=[("ps", pb)])
                            B.act(Pt[pi_][0:kn, 0:N4], psL, AF.Exp, r=[("ps", pb), "cq"], w=[("pt", pi_)],
                                  scale=sc_att, bias=cq[0:kn, 0:1])
                            curc = (kb_, kn, pi_)
                        if pendc is not None:
                            kbp, knp, pip = pendc
                            B.mm(psO, Vg[sl][0:knp, kbp, :], Pt[pip][0:knp, 0:N4], kbp == 0, kbp == nkb - 1,
                                 r=[("Vg", sl), ("pt", pip)], w=[("ps", 6)])
                            B.mm(psZ, ones_b[0:knp, :], Pt[pip][0:knp, 0:N4], kbp == 0, kbp == nkb - 1,
                                 r=["ones", ("pt", pip)], w=[("ps", 7)])
                        pendc = curc
                    ei = evc[0] % NEV
                    evc[0] += 1
                    B.copy("act", zs[ei][:, 0:N4], psZ, r=[("ps", 7)], w=[("zs", ei)])
                    B.copy("act", osb[ei][:, 0:N4], psO, r=[("ps", 6)], w=[("os", ei)])
                    S.add("dve", lambda e, ei=ei, N4=N4: e.reciprocal(out=zs[ei][:, 0:N4], in_=zs[ei][:, 0:N4]),
                          r=[("zs", ei)], w=[("zs", ei)])
                    B.tt(ot[sl][:, 0:N4], osb[ei][:, 0:N4], zs[ei][:, 0:N4], ALU.mult, r=[("os", ei), ("zs", ei)],
                         w=[("ot", sl)])
                    B.dma("act", oattT_scr[4 * g:4 * g + 4, :, t0:t0 + R].rearrange("h d t -> d h t"),
                          ot[sl][:, 0:N4].rearrange("p (h t) -> p h t", h=4), r=[("ot", sl)], w=[B.fresh("oa")])

            NBK = len(qblocks)
            for step in range(NBK + 2):
                if step < NBK:
                    stageA(qblocks[step], step % 2)
                if 0 <= step - 1 < NBK:
                    stageB1(qblocks[step - 1], (step - 1) % 2)
                if 0 <= step - 2 < NBK:
                    stageC(qblocks[step - 2], (step - 2) % 2)
                if 0 <= step - 1 < NBK:
                    stageB2(qblocks[step - 1], (step - 1) % 2)
            S.barrier()
            A.release(m0)

        if "mix" in ph:
            w_branch = inp("w_branch", [2 * D, D])
            w_out = inp("w_out", [D, D])
            xown = inp("xown", [TO, D])
            m0 = A.mark()
            oaT = A.alloc([KC, TO], BF16)
            olT = A.alloc([KC, TO], BF16)
            wA = [A.alloc([KC, 128], BF16) for _ in range(2)]
            wL = [A.alloc([KC, 128], BF16) for _ in range(2)]
            for h0 in range(0, 32, 8):
                B.dma("sp", oaT[:, h0:h0 + 8, :], oattT_scr[h0:h0 + 8].rearrange("h d t -> d h t"),
                      w=[("ATx", h0)])
                B.dma("sp", olT[:, h0:h0 + 8, :], olruT_scr[h0:h0 + 8].rearrange("h d t -> d h t"),
                      w=[("ATy", h0)])
            ATR = [("ATx", h0) for h0 in (0, 8, 16, 24)] + [("ATy", h0) for h0 in (0, 8, 16, 24)]
            ttiles = [(0, 512), (512, 512), (1024, 64)]
            NE = 3
            sa_t = [A.alloc([512], F32) for _ in range(NE)]
            sl_t = [A.alloc([512], F32) for _ in range(NE)]
            ta_t = [A.alloc([512], F32) for _ in range(NE)]
            mb_t = [A.alloc([512], BF16) for _ in range(NE)]
            ec = [0]

            def evac_mix(chunk, tt0, n, pss):
                (psA, tokA), (psL, tokL) = pss
                i = ec[0] % NE
                ec[0] += 1
                tk = ("e", i)
                B.dma("sp", sa_t[i][:, 0:n], saT_scr[chunk, :, tt0:tt0 + n], w=[("esa", i)])
                B.dma("sp", sl_t[i][:, 0:n], slT_scr[chunk, :, tt0:tt0 + n], w=[("esl", i)])
                B.tt(ta_t[i][:, 0:n], psA, sa_t[i][:, 0:n], ALU.mult, r=[tokA, ("esa", i)], w=[tk])
                B.tt(sl_t[i][:, 0:n], psL, sl_t[i][:, 0:n], ALU.mult, r=[tokL, ("esl", i)], w=[("esl", i)])
                B.tt(mb_t[i][:, 0:n], ta_t[i][:, 0:n], sl_t[i][:, 0:n], ALU.add, r=[tk, ("esl", i)], w=[tk])
                B.dma("act", mixT_scr[chunk, :, tt0:tt0 + n], mb_t[i][:, 0:n], r=[tk], w=[B.fresh("mx")])
            gemm_feat([(oaT, wA, lambda c0, n: w_branch[0:D, c0:c0 + n]),
                       (olT, wL, lambda c0, n: w_branch[D:2 * D, c0:c0 + n])], 9, KC, 4096, 128, ttiles, evac_mix,
                      attoks=ATR)
            S.barrier()
            A.release(m0)

            m0 = A.mark()
            mxT = A.alloc([KC, TO], BF16)
            wbufs = [A.alloc([KC, 512], BF16) for _ in range(2)]
            for h0 in range(0, 32, 8):
                B.dma("sp", mxT[:, h0:h0 + 8, :], mixT_scr[h0:h0 + 8].rearrange("h d t -> d h t"), w=[("ATz", h0)])
            NE = 3
            xr = [A.alloc([512], F32) for _ in range(NE)]

            def evac_out(tag, bi, rows, t0, ps, pstok, ncols):
                i = ec[0] % NE
                ec[0] += 1
                c0 = tag
                B.dma("sp", xr[i][0:rows], xown[t0:t0 + rows, c0:c0 + 512], w=[("xr", i)])
                B.tt(xr[i][0:rows], ps, xr[i][0:rows], ALU.add, r=[pstok, ("xr", i)], w=[("xr", i)])
                B.dma("act", x1_scr[t0:t0 + rows, c0:c0 + 512], xr[i][0:rows], r=[("xr", i)], w=[B.fresh("x1")])
            gemm_tok(mxT, [128] * 8 + [64], KC, wbufs, lambda c0, n: w_out[:, c0:c0 + n],
                     [(g * 512, 512, g * 512) for g in range(8)], evac_out,
                     attoks=[("ATz", h0) for h0 in (0, 8, 16, 24)])
            S.barrier()
            A.release(m0)

        if "ffn" in ph:
            w_gate = inp("w_gate", [D, DFF])
            w_up = inp("w_up", [D, DFF])
            w_down = inp("w_down", [DFF, D])
            y_own = B.dout("y_own", [TO, D])
            m0 = A.mark()
            gffT = A.alloc([KC], F32)
            B.dma("sp", gffT, inp("norm_ffnT", [128, KC]), w=["gT"])
            xn2T = A.alloc([KC, TO], BF16)
            xt = A.alloc([D], F32)
            junk = A.alloc([D], BF16)
            ssb = [A.alloc([1], F32) for _ in range(2)]
            blocks = [(x1_scr[j * 128:(j + 1) * 128, :], 128) for j in range(8)] + [(x1_scr[1024:1088, :], 64)]
            norm_transpose(blocks, xn2T, gffT, xt, junk, ssb)
            wG = [A.alloc([KC, 256], BF16) for _ in range(2)]
            wU = [A.alloc([KC, 256], BF16) for _ in range(2)]
            ttiles = [(0, 512), (512, 512), (1024, 64)]
            NE = 3
            sg_t = [A.alloc([512], F32) for _ in range(NE)]
            hb_t = [A.alloc([512], BF16) for _ in range(NE)]
            ec = [0]

            def evac_ffn(chunk, tt0, n, pss):
                (psG, tokG), (psU, tokU) = pss
                i = ec[0] % NE
                ec[0] += 1
                tk = ("e", i)
                B.act(sg_t[i][:, 0:n], psG, AF.Silu, r=[tokG], w=[tk])
                B.tt(hb_t[i][:, 0:n], sg_t[i][:, 0:n], psU, ALU.mult, r=[tk, tokU], w=[tk])
                B.dma("act", hT_scr[chunk, :, tt0:tt0 + n], hb_t[i][:, 0:n], r=[tk], w=[B.fresh("ht")])
            gemm_feat([(xn2T, wG, lambda c0, n: w_gate[:, c0:c0 + n]),
                       (xn2T, wU, lambda c0, n: w_up[:, c0:c0 + n])], 9, KC, DFF, 256, ttiles, evac_ffn)
            S.barrier()
            A.release(m0)

            m0 = A.mark()
            hT = A.alloc([KC, TO], BF16)
            wbufs = [A.alloc([KC, 512], BF16) for _ in range(2)]
            NE = 3
            xr = [A.alloc([512], F32) for _ in range(NE)]
            pieces = [(0, 32), (32, 32), (64, 22)]
            for pi, (k0, kcn) in enumerate(pieces):
                step = 8
                toks = []
                for h0 in range(0, kcn, step):
                    h1 = min(kcn, h0 + step)
                    tk = ("ATz", h0)
                    toks.append(tk)
                    B.dma("sp", hT[:, h0:h1, :], hT_scr[k0 + h0:k0 + h1].rearrange("h d t -> d h t"), w=[tk])
                src = x1_scr
                dst = y_own

                def evac_dn(tag, bi, rows, t0, ps, pstok, ncols, pi=pi):
                    i = ec[0] % NE
                    ec[0] += 1
                    c0 = tag
                    prev = x1_scr if pi == 0 else y_own
                    B.dma("sp", xr[i][0:rows], prev[t0:t0 + rows, c0:c0 + 512], r=[("y", bi, c0)], w=[("xr", i)])
                    B.tt(xr[i][0:rows], ps, xr[i][0:rows], ALU.add, r=[pstok, ("xr", i)], w=[("xr", i)])
                    B.dma("act", y_own[t0:t0 + rows, c0:c0 + 512], xr[i][0:rows], r=[("xr", i)], w=[("y", bi, c0)])
                gemm_tok(hT, [128] * 8 + [64], kcn, wbufs,
                         lambda c0, n, k0=k0, kcn=kcn: w_down[k0 * 128:(k0 + kcn) * 128, c0:c0 + n],
                         [(g * 512, 512, g * 512) for g in range(8)], evac_dn, attoks=toks)
            S.barrier()
            A.release(m0)

        with nc.Block() as block:
            S.emit(nc, block, esem, dsem)
    return nc


def rope_tables(pos):
    half = 16
    inv = (np.float32(500000.0) ** (-np.arange(half, dtype=np.float32) * np.float32(2.0) / np.float32(32))
           ).astype(np.float32)
    ang = pos.astype(np.float32)[:, None] * inv[None, :]
    return np.concatenate([np.cos(ang), np.sin(ang)], axis=1).astype(np.float32)


def make_in_maps(inp, names=None):
    f = np.float32
    in_maps = []
    pos_seq = np.concatenate([np.arange(SEQ), PAST + np.arange(32), PAST + np.arange(32)])
    cs_seq = rope_tables(pos_seq)
    ident = np.eye(128, dtype=f)
    dsel = np.zeros((32, 4, 32, 128), f)
    for g in range(32):
        for tl in range(4):
            dsel[:, tl, g, 4 * g + tl] = 1.0 / 64.0
    dsel = dsel.reshape(128, 32 * 128)

    def T32(v):
        return np.ascontiguousarray(np.asarray(v).reshape(KC, 128).T)

    for c in range(8):
        p, r = c // 4, c % 4
        xp = inp["x_prompt"][p]
        own_blocks = [xp[(4 * i + r) * 128:(4 * i + r + 1) * 128] for i in range(8)]
        xown = np.concatenate(own_blocks + [inp["x_sample"][2 * c], inp["x_sample"][2 * c + 1]], axis=0)
        pos_own = np.concatenate([np.arange((4 * i + r) * 128, (4 * i + r + 1) * 128) for i in range(8)] +
                                 [PAST + np.arange(32), PAST + np.arange(32)])
        sel = np.zeros((128, 4), f)
        sel[:, r] = 1.0
        tl = np.arange(128)[:, None]
        slx = np.arange(512)[None, :]
        mb = np.where(slx < 128 * r + 64 + 64 * (tl >= 64), 0.0, NEGM).astype(f)
        sc = inp["state_conv"][0, 2 * c:2 * c + 2]
        m = {
            "xseq": np.ascontiguousarray(xp),
            "xown": np.ascontiguousarray(xown),
            "w_in": inp["w_in"][0],
            "norm_mixT": T32(inp["norm_mix"][0]),
            "norm_ffnT": T32(inp["norm_ffn"][0]),
            "norm_q": inp["norm_q"][0],
            "norm_k": inp["norm_k"][0],
            "norm_idx_k": inp["norm_idx_k"][0],
            "cs_seq": cs_seq,
            "cs_own": rope_tables(pos_own),
            "ident": ident,
            "dsel": dsel,
            "maskbias": mb,
            "cache_k": np.ascontiguousarray(inp["cache_k"][0, 2 * c:2 * c + 2].reshape(2, PAST, NKV * HD)),
            "cache_v": np.ascontiguousarray(inp["cache_v"][0, 2 * c:2 * c + 2].reshape(2, PAST, NKV * HD)),
            "cache_ik": np.ascontiguousarray(inp["cache_idx_k"][0, 2 * c:2 * c + 2]),
            "state_lruT": np.stack([T32(inp["state_lru"][0, 2 * c + q]) for q in range(2)]),
            "state_convT": np.ascontiguousarray(sc.reshape(2, 3, KC, 128).transpose(0, 3, 2, 1)),
            "convwT": np.ascontiguousarray(inp["conv_w"][0].reshape(4, KC, 128).transpose(2, 0, 1)),
            "convbT": T32(inp["conv_b"][0]),
            "lru_baT": T32(inp["lru_ba"][0]),
            "lru_bxT": T32(inp["lru_bx"][0]),
            "lru_lamT": T32(inp["lru_lambda"][0]),
            "lru_wa": inp["lru_wa"][0],
            "lru_wx": inp["lru_wx"][0],
            "selr": sel,
            "w_branch": inp["w_branch"][0],
            "w_out": inp["w_out"][0],
            "w_gate": inp["w_gate"][0],
            "w_up": inp["w_up"][0],
            "w_down": inp["w_down"][0],
        }
        if names is not None:
            m = {k: v for k, v in m.items() if k in names}
        in_maps.append(m)
    return in_maps


def input_names(nc):
    names = set()
    for alloc in nc.allocations:
        if isinstance(alloc, mybir.MemoryLocationSet) and alloc.kind == "ExternalInput":
            names.add(alloc.memorylocations[0].name)
    return names


_NC_CACHE = {}


def kernel(**inputs):
    inp = {k: np.asarray(v) for k, v in inputs.items()}
    if "nc" not in _NC_CACHE:
        _NC_CACHE["nc"] = build_program()
    nc = _NC_CACHE["nc"]
    in_maps = make_in_maps(inp, input_names(nc))
    res = run_bass_kernel_spmd(nc, in_maps, core_ids=list(range(8)))
    R = res.results
    f = np.float32
    y_prompt = np.zeros((2, SEQ, D), f)
    y_sample = np.zeros((16, DEC_SEQ, D), f)
    k_prompt = np.zeros((1, 2, SEQ, NKV, HD), f)
    v_prompt = np.zeros((1, 2, SEQ, NKV, HD), f)
    ik_prompt = np.zeros((1, 2, SEQ, HD), f)
    lru_prompt = np.zeros((1, 2, D), f)
    conv_prompt = np.zeros((1, 2, 3, D), f)
    k_sample = np.zeros((1, 16, DEC_SEQ, NKV, HD), f)
    v_sample = np.zeros((1, 16, DEC_SEQ, NKV, HD), f)
    ik_sample = np.zeros((1, 16, DEC_SEQ, HD), f)
    lru_sample = np.zeros((1, 16, D), f)
    conv_sample = np.zeros((1, 16, 3, D), f)
    for c in range(8):
        p, r = c // 4, c % 4
        o = R[c]
        y = o["y_own"]
        for i in range(8):
            b = 4 * i + r
            y_prompt[p, b * 128:(b + 1) * 128] = y[i * 128:(i + 1) * 128]
        for q in range(2):
            sq = 2 * c + q
            y_sample[sq] = y[1024 + q * 32:1024 + (q + 1) * 32]
            k_sample[0, sq] = o["o_k"][SEQ + q * 32:SEQ + (q + 1) * 32].reshape(32, NKV, HD)
            v_sample[0, sq] = o["o_v"][SEQ + q * 32:SEQ + (q + 1) * 32].reshape(32, NKV, HD)
            ik_sample[0, sq] = o["o_ik"][SEQ + q * 32:SEQ + (q + 1) * 32]
            lru_sample[0, sq] = o["o_lru"][1 + q]
            conv_sample[0, sq] = o["o_conv"][1 + q]
        if r == 0:
            k_prompt[0, p] = o["o_k"][:SEQ].reshape(SEQ, NKV, HD)
            v_prompt[0, p] = o["o_v"][:SEQ].reshape(SEQ, NKV, HD)
            ik_prompt[0, p] = o["o_ik"][:SEQ]
            lru_prompt[0, p] = o["o_lru"][0]
            conv_prompt[0, p] = o["o_conv"][0]
    return (y_prompt, y_sample, k_prompt, v_prompt, ik_prompt, lru_prompt, conv_prompt,
            k_sample, v_sample, ik_sample, lru_sample, conv_sample)
```

```python
import math
from contextlib import ExitStack

import numpy as np
import concourse.bass as bass
import concourse.mybir as mybir
from concourse.bass_utils import run_bass_kernel_spmd

F32 = mybir.dt.float32
BF16 = mybir.dt.bfloat16
AF = mybir.ActivationFunctionType
ALU = mybir.AluOpType
AX = mybir.AxisListType

D = 4096
SEQ = 4096
NB = 32
DEC_SEQ = 32
PAST = 2048
SS = PAST + DEC_SEQ
NH = 32
NKV = 8
HD = 128
DFF = 11008
KC = 32
TO = 1088
EPS = 1e-6
C_Q, C_K, C_V, C_IQ, C_IW, C_IK, C_XL, C_YL, C_GA, C_GL = (
    0, 4096, 5120, 6144, 10240, 10272, 10400, 14496, 18592, 22688)
IN_COLS = 26784
NEGM = -1.0e30


class Op:
    __slots__ = ("eng", "fn", "deps", "dma", "signal", "pos", "sem", "semval", "K", "waits",
                 "barrier")

    def __init__(self, eng, fn, dma=False, barrier=False):
        self.eng = eng
        self.fn = fn
        self.deps = ()
        self.dma = dma
        self.signal = False
        self.pos = 0
        self.sem = None
        self.semval = 0
        self.K = None
        self.waits = ()
        self.barrier = barrier


class Sched:
    CE = ("pe", "act", "dve", "pool", "sp")
    NDSEM = 10

    def __init__(self):
        self.ops = []
        self.last_w = {}
        self.readers = {}
        self.last_op = {e: None for e in self.CE}

    def add(self, eng, fn, r=(), w=(), dma=False):
        idx = len(self.ops)
        op = Op(eng, fn, dma=dma)
        raw = set()
        oth = set()
        for t in r:
            lw = self.last_w.get(t)
            if lw is not None:
                raw.add(lw)
        for t in w:
            lw = self.last_w.get(t)
            if lw is not None:
                oth.add(lw)
            rs = self.readers.get(t)
            if rs:
                oth.update(rs)
        deps = set()
        for d in raw | oth:
            dop = self.ops[d]
            if (not dop.dma) and (not dma) and dop.eng == eng:
                if eng == "pe":
                    continue
                if d not in raw:
                    continue
            deps.add(d)
            if not dop.dma:
                dop.signal = True
        op.deps = tuple(sorted(deps))
        for t in r:
            self.readers.setdefault(t, []).append(idx)
        for t in w:
            self.last_w[t] = idx
            self.readers[t] = []
        self.ops.append(op)
        if not dma:
            self.last_op[eng] = idx
        return idx

    def barrier(self):
        lasts = dict(self.last_op)
        for e in self.CE:
            op = Op(e, None, barrier=True)
            deps = set()
            for e2, li in lasts.items():
                if li is not None and e2 != e:
                    deps.add(li)
                    self.ops[li].signal = True
            op.deps = tuple(sorted(deps))
            self.ops.append(op)
        self.last_w = {}
        self.readers = {}

    def analyze(self):
        CE = self.CE
        K = {e: {} for e in CE}
        pos = {e: 0 for e in CE}
        sig = {e: 0 for e in CE}
        sigcount = {e: {} for e in CE}
        dq = {e: {"next": 0, "cum": [0] * self.NDSEM, "last": [None] * self.NDSEM} for e in CE}
        for op in self.ops:
            E = op.eng
            KE = K[E]
            waits = {}

            def need(key, val, kafter):
                if KE.get(key, 0) >= val:
                    return
                if waits.get(key, 0) < val:
                    waits[key] = val
                if kafter:
                    for k2, v2 in kafter.items():
                        if KE.get(k2, 0) < v2:
                            KE[k2] = v2
                if KE.get(key, 0) < val:
                    KE[key] = val

            for d in op.deps:
                dop = self.ops[d]
                if dop.dma:
                    need(("S", dop.eng, dop.sem), dop.semval, dop.K)
                else:
                    need(dop.eng, dop.pos, dop.K)
            if op.barrier:
                for q in CE:
                    for s in range(self.NDSEM):
                        if dq[q]["cum"][s] > 0:
                            lo = dq[q]["last"][s]
                            need(("S", q, s), dq[q]["cum"][s], lo.K if lo else None)
            if op.dma:
                q = dq[E]
                s = q["next"]
                q["next"] = (s + 1) % self.NDSEM
                if q["last"][s] is not None:
                    need(("S", E, s), q["cum"][s], q["last"][s].K)
                q["cum"][s] += 16
                op.sem = s
                op.semval = q["cum"][s]
                q["last"][s] = op
                op.K = dict(KE)
            elif not op.barrier:
                pos[E] += 1
                op.pos = pos[E]
                if op.signal:
                    sig[E] += 1
                    sigcount[E][op.pos] = sig[E]
                    kk = dict(KE)
                    kk[E] = op.pos
                    op.K = kk
            op.waits = tuple(waits.items())
        self.sigcount = sigcount
        self.final_dma = {e: list(dq[e]["cum"]) for e in CE}

    def emit(self, nc, block, esem, dsem):
        self.analyze()
        per = {e: [] for e in self.CE}
        for op in self.ops:
            per[op.eng].append(op)
        sigcount = self.sigcount

        def run(e, eng):
            for op in per[e]:
                for key, val in op.waits:
                    if isinstance(key, tuple):
                        eng.wait_ge(dsem[key[1]][key[2]], val)
                    else:
                        eng.wait_ge(esem[key], sigcount[key][val])
                if op.fn is None:
                    continue
                ins = op.fn(eng)
                if op.dma:
                    ins.then_inc(dsem[e][op.sem], 16)
                elif op.signal:
                    ins.then_inc(esem[e], 1)
            if e == "sp":
                for q in self.CE:
                    for s, v in enumerate(self.final_dma[q]):
                        if v > 0:
                            eng.wait_ge(dsem[q][s], v)

        @block.tensor
        def _(eng):
            run("pe", eng)

        @block.scalar
        def _(eng):
            run("act", eng)

        @block.vector
        def _(eng):
            run("dve", eng)

        @block.gpsimd
        def _(eng):
            run("pool", eng)

        @block.sync
        def _(eng):
            run("sp", eng)


class Arena:
    def __init__(self, t, words):
        self.t = t
        self.words = words
        self.off = 0

    def alloc(self, free_shape, dtype, parts=128):
        n = 1
        for s in free_shape:
            n *= s
        nbytes = n * (2 if dtype == BF16 else 4)
        w = (nbytes + 31) // 32 * 8
        off = self.off
        assert off + w <= self.words, f"arena overflow {off + w} > {self.words}"
        self.off += w
        ap = self.t[0:parts, off:off + (nbytes + 3) // 4]
        if dtype == BF16:
            ap = ap.bitcast(BF16)
            if ap.shape[1] != n:
                ap = ap[:, 0:n]
        if len(free_shape) == 2:
            ap = ap.rearrange("p (a b) -> p a b", a=free_shape[0])
        elif len(free_shape) == 3:
            ap = ap.rearrange("p (a b c) -> p a b c", a=free_shape[0], b=free_shape[1])
        return ap

    def mark(self):
        return self.off

    def release(self, m):
        self.off = m


class Builder:
    def __init__(self, debug=False, phases=None, feed=None):
        self.debug = debug
        self.feed = feed or set()
        self.phases = phases
        self.nc = bass.Bass("TRN2", target_bir_lowering=False)
        self.S = Sched()
        self.uid = 0

    def fresh(self, p):
        self.uid += 1
        return (p, self.uid)

    def din(self, name, shape, dt=F32):
        return self.nc.dram_tensor(name, list(shape), dt, kind="ExternalInput").ap()

    def dout(self, name, shape, dt=F32):
        return self.nc.dram_tensor(name, list(shape), dt, kind="ExternalOutput").ap()

    def dscr(self, name, shape, dt=F32):
        kind = "ExternalOutput" if (self.debug and name in self.debug) else "Internal"
        if name in self.feed:
            kind = "ExternalInput"
        return self.nc.dram_tensor(name, list(shape), dt, kind=kind).ap()

    def dma(self, q, out, in_, r=(), w=()):
        self.S.add(q, lambda e: e.dma_start(out=out, in_=in_), r=r, w=w, dma=True)

    def act(self, out, in_, func, r=(), w=(), **kw):
        self.S.add("act", lambda e: e.activation(out=out, in_=in_, func=func, **kw), r=r, w=w)

    def tt(self, out, in0, in1, op, r=(), w=(), eng="dve"):
        self.S.add(eng, lambda e: e.tensor_tensor(out=out, in0=in0, in1=in1, op=op), r=r, w=w)

    def ts(self, out, in0, s1, s2, op0, op1=None, r=(), w=(), eng="dve", **kw):
        if op1 is None:
            self.S.add(eng, lambda e: e.tensor_scalar(out=out, in0=in0, scalar1=s1, scalar2=None,
                                                      op0=op0, **kw), r=r, w=w)
        else:
            self.S.add(eng, lambda e: e.tensor_scalar(out=out, in0=in0, scalar1=s1, scalar2=s2,
                                                      op0=op0, op1=op1, **kw), r=r, w=w)

    def stt(self, out, in0, scalar, in1, op0, op1, r=(), w=()):
        self.S.add("dve", lambda e: e.scalar_tensor_tensor(out=out, in0=in0, scalar=scalar, in1=in1,
                                                           op0=op0, op1=op1), r=r, w=w)

    def copy(self, eng, out, in_, r=(), w=()):
        if eng == "act":
            self.S.add("act", lambda e: e.activation(out=out, in_=in_, func=AF.Copy), r=r, w=w)
        else:
            self.S.add(eng, lambda e: e.tensor_copy(out=out, in_=in_), r=r, w=w)

    def mm_group(self, out, pairs, r=(), w=()):
        n = len(pairs)

        def fn(e):
            ins = None
            for i, (l, rr) in enumerate(pairs):
                ins = e.matmul(out, l, rr, start=(i == 0), stop=(i == n - 1))
            return ins
        self.S.add("pe", fn, r=r, w=w)

    def mm(self, out, lhsT, rhs, start, stop, r=(), w=()):
        self.S.add("pe", lambda e: e.matmul(out, lhsT, rhs, start=start, stop=stop), r=r, w=w)

    def transpose(self, out, in_, ident, r=(), w=()):
        self.S.add("pe", lambda e: e.transpose(out, in_, ident), r=r, w=w)


def build_program(debug=None, phases=None, feed=None):
    B = Builder(debug=debug, phases=phases, feed=feed)
    nc = B.nc
    S = B.S
    ALL = {"seq", "cache", "lru", "own", "attn", "mix", "ffn"}
    ph = set(phases) if phases is not None else ALL
    full = ph == ALL

    _in = {}

    def inp(name, shape):
        if name not in _in:
            _in[name] = B.din(name, shape)
        return _in[name]

    KT_scr = B.dscr("KT_scr", [NKV, 128, SEQ], BF16)
    V_scr = B.dscr("V_scr", [SEQ, NKV * HD], BF16)
    ikT_scr = B.dscr("ikT_scr", [128, SEQ], BF16)
    KT_s = B.dscr("KT_s", [2, NKV, 128, SS], BF16)
    V_s = B.dscr("V_s", [2, SS, NKV * HD], BF16)
    ikT_s = B.dscr("ikT_s", [2, 128, SS], BF16)
    xlT_scr = B.dscr("xlT_scr", [KC, 128, SEQ + 64], F32)
    hown_scr = B.dscr("hown_scr", [KC, 128, TO], F32)
    QT_scr = B.dscr("QT_scr", [NH, 128, TO], BF16)
    iqT_scr = B.dscr("iqT_scr", [NH, 128, TO], BF16)
    iw_scr = B.dscr("iw_scr", [TO, 32], F32)
    olruT_scr = B.dscr("olruT_scr", [KC, 128, TO], BF16)
    saT_scr = B.dscr("saT_scr", [KC, 128, TO], F32)
    slT_scr = B.dscr("slT_scr", [KC, 128, TO], F32)
    oattT_scr = B.dscr("oattT_scr", [NH, 128, TO], BF16)
    mixT_scr = B.dscr("mixT_scr", [KC, 128, TO], BF16)
    x1_scr = B.dscr("x1_scr", [TO, D], F32)
    hT_scr = B.dscr("hT_scr", [DFF // 128, 128, TO], BF16)

    with ExitStack() as es:
        ARENA_KB = 206
        arena_t = es.enter_context(nc.sbuf_tensor("arena", [128, ARENA_KB * 256], F32))
        A = Arena(arena_t, ARENA_KB * 256)
        psum = [es.enter_context(nc.psum_tensor(f"ps{i}", [128, 512], F32)) for i in range(8)]
        esem = {e: es.enter_context(nc.semaphore(f"e_{e}")) for e in Sched.CE}
        dsem = {e: [es.enter_context(nc.semaphore(f"d_{e}{i}")) for i in range(Sched.NDSEM)]
                for e in ("act", "pool", "sp")}
        dsem["pe"] = dsem["sp"]
        dsem["dve"] = dsem["sp"]

        ident_in = inp("ident", [128, 128])
        ident_b = A.alloc([128], BF16)
        ident_f = A.alloc([128], F32)
        B.dma("pool", ident_b, ident_in, w=["ident_b"])
        B.dma("sp", ident_f, ident_in, w=["ident_f"])
        eps_t = A.alloc([1], F32)
        S.add("dve", lambda e: e.memset(eps_t, EPS), w=["eps"])
        one_t = A.alloc([1], F32)
        S.add("dve", lambda e: e.memset(one_t, 1.0), w=["one"])
        g4 = {}

        def load_g4(nm):
            src = inp("norm_" + nm, [HD])
            t = A.alloc([4, 128], F32)
            for j in range(4):
                B.dma("sp", t[:, j, :], src.partition_broadcast(128), w=[("g4", nm, j)])
            g4[nm] = t

        if "seq" in ph:
            load_g4("k")
            load_g4("idx_k")
        if "own" in ph:
            load_g4("q")
        S.barrier()

        psn = [0]

        def next_ps():
            pb = psn[0] % 4
            psn[0] += 1
            return pb

        def norm_transpose(blocks, xnT, gT, xt, junk, ssb, extra=None):
            t0 = 0
            for bi, (src, rows) in enumerate(blocks):
                B.dma("sp", xt[0:rows], src, w=["xt"])
                if extra is not None:
                    extra(bi, rows)
                ss = ssb[bi % 2]
                sk = ("ssb", bi % 2)
                S.add("act", lambda e, rows=rows, ss=ss: e.activation(
                    out=junk[0:rows], in_=xt[0:rows], func=AF.Square, accum_out=ss[0:rows]),
                    r=["xt"], w=["junk", sk])
                B.act(ss[0:rows], ss[0:rows], AF.Sqrt, r=[sk, "eps"], w=[sk], scale=1.0 / D,
                      bias=eps_t[0:rows])
                S.add("dve", lambda e, ss=ss, rows=rows: e.reciprocal(out=ss[0:rows], in_=ss[0:rows]),
                      r=[sk], w=[sk])
                B.ts(xt[0:rows], xt[0:rows], ss[0:rows], None, ALU.mult, r=["xt", sk], w=["xt"])
                for j in range(8):
                    pb = 4 + (j % 2)
                    pv = psum[pb][:, :].rearrange("p (a b) -> p a b", a=4)
                    for q in range(4):
                        kc = j * 4 + q
                        B.transpose(pv[:, q, 0:rows], xt[0:rows, kc * 128:(kc + 1) * 128],
                                    ident_f[0:rows, 0:rows], r=["xt", "ident_f"], w=[("ps", pb)])
                    B.tt(xnT[:, j * 4:(j + 1) * 4, t0:t0 + rows], pv[:, :, 0:rows],
                         gT[:, j * 4:(j + 1) * 4].unsqueeze(2).to_broadcast([128, 4, rows]),
                         ALU.mult, r=[("ps", pb), "gT"], w=[("AT", bi)])
                t0 += rows

        wl = {}

        def load_w(wbufs, src, kcn, ncols):
            wid = id(wbufs)
            slot = wl.get(wid, 0) % len(wbufs)
            wl[wid] = wl.get(wid, 0) + 1
            slot = (wid, slot)
            v = src.rearrange("(kc p) n -> p kc n", p=128)
            step = 8 if ncols > 256 else 16
            for k0 in range(0, kcn, step):
                k1 = min(kcn, k0 + step)
                B.dma("pool", wbufs[slot[1]][:, k0:k1, 0:ncols], v[:, k0:k1, :], w=[("w", slot, k0 // 8)] +
                      ([("w", slot, k0 // 8 + 1)] if step == 16 else []))
            return slot

        def wtoks(slot, kcn):
            return [("w", slot, k) for k in range((kcn + 7) // 8)]

        def gemm_tok(AT, blocks, kcn, wbufs, wsrc, col_groups, evac, attoks=None):
            nxt = load_w(wbufs, wsrc(col_groups[0][0], col_groups[0][1]), kcn, col_groups[0][1])
            for gi, (c0, ncols, tag) in enumerate(col_groups):
                slot = nxt
                if gi + 1 < len(col_groups):
                    nxt = load_w(wbufs, wsrc(col_groups[gi + 1][0], col_groups[gi + 1][1]), kcn, col_groups[gi + 1][1])
                t0 = 0
                for bi, rows in enumerate(blocks):
                    pb = next_ps()
                    ps = psum[pb][0:rows, 0:ncols]
                    B.mm_group(ps, [(AT[:, kc, t0:t0 + rows], wbufs[slot[1]][:, kc, 0:ncols])
                                    for kc in range(kcn)],
                               r=(attoks if attoks is not None else [("AT", bi)]) + wtoks(slot, kcn),
                               w=[("ps", pb)])
                    evac(tag, bi, rows, t0, ps, ("ps", pb), ncols)
                    t0 += rows

        def gemm_feat(srcs, nblk, kcn, ncols_total, cgw, ttiles, evac, attoks=None):
            nxts = [load_w(wb, wsrc(0, cgw), kcn, cgw) for (AT, wb, wsrc) in srcs]
            for c0 in range(0, ncols_total, cgw):
                slots = nxts
                if c0 + cgw < ncols_total:
                    nxts = [load_w(wb, wsrc(c0 + cgw, cgw), kcn, cgw) for (AT, wb, wsrc) in srcs]
                for ch in range(cgw // 128):
                    chunk = (c0 // 128) + ch
                    for (tt0, n) in ttiles:
                        pss = []
                        for (AT, wb, wsrc), slot in zip(srcs, slots):
                            pb = next_ps()
                            ps = psum[pb][:, 0:n]
                            B.mm_group(ps, [(wb[slot[1]][:, kc, ch * 128:(ch + 1) * 128], AT[:, kc, tt0:tt0 + n])
                                            for kc in range(kcn)],
                                       r=(attoks if attoks is not None else [("AT", bi) for bi in range(nblk)])
                                       + wtoks(slot, kcn), w=[("ps", pb)])
                            pss.append((ps, ("ps", pb)))
                        evac(chunk, tt0, n, pss)

        stc = [0]

        def make_headproc(NST):
            st = dict(
                f=[A.alloc([512], F32) for _ in range(NST)],
                t=[A.alloc([512], F32) for _ in range(NST)],
                b=[A.alloc([512], BF16) for _ in range(NST)],
                T=[A.alloc([4, 128], BF16) for _ in range(NST)],
                s=[A.alloc([4], F32) for _ in range(NST)],
                r=[A.alloc([4, 16], F32) for _ in range(4 * NST)],
                n=NST)
            return st

        def merge_st(a, b):
            return dict(f=a["f"] + b["f"], t=a["t"] + b["t"], b=a["b"] + b["b"], T=a["T"] + b["T"],
                        s=a["s"] + b["s"], r=a["r"] + b["r"], n=a["n"] + b["n"])

        def union_xt_junk(st):
            mU = A.mark()
            xt_ = A.alloc([D], F32)
            junk_ = A.alloc([D], BF16)
            mE = A.mark()
            A.release(mU)
            st2 = make_headproc(3)
            assert A.mark() <= mE
            A.off = mE
            return xt_, junk_, merge_st(st, st2)

        def headproc(st, ps, pstok, rows, ncols, gname, cs, cstok, dst_out, dstT_fn, rope=True):
            nh = ncols // 128
            i = stc[0] % st["n"]
            stc[0] += 1
            f, t, b_, T_, s_ = st["f"][i], st["t"][i], st["b"][i], st["T"][i], st["s"][i]
            r4 = st["r"][4 * i:4 * i + 4]
            tk = ("st", i)
            B.copy("act", f[0:rows, 0:ncols], ps, r=[pstok], w=[tk])
            fv = f[0:rows, 0:ncols].rearrange("p (h d) -> p h d", h=nh)
            tv = t[0:rows, 0:ncols].rearrange("p (h d) -> p h d", h=nh)
            if gname is not None:
                for h in range(nh):
                    S.add("act", lambda e, h=h: e.activation(out=t[0:rows, h * 128:(h + 1) * 128],
                                                            in_=f[0:rows, h * 128:(h + 1) * 128], func=AF.Square,
                                                            accum_out=s_[0:rows, h:h + 1]), r=[tk], w=[tk])
                B.act(s_[0:rows, 0:nh], s_[0:rows, 0:nh], AF.Sqrt, r=[tk, "eps"], w=[tk], scale=1.0 / 128,
                      bias=eps_t[0:rows])
                S.add("dve", lambda e: e.reciprocal(out=s_[0:rows, 0:nh], in_=s_[0:rows, 0:nh]), r=[tk], w=[tk])
                B.tt(fv, fv, s_[0:rows, 0:nh].unsqueeze(2).to_broadcast([rows, nh, 128]), ALU.mult,
                     r=[tk], w=[tk])
                B.tt(fv, fv, g4[gname][0:rows, 0:nh, :], ALU.mult,
                     r=[tk] + [("g4", gname, j) for j in range(4)], w=[tk])
            if rope:
                cb = cs[0:rows, 0:16].unsqueeze(1).to_broadcast([rows, nh, 16])
                sb = cs[0:rows, 16:32].unsqueeze(1).to_broadcast([rows, nh, 16])
                x1 = fv[:, :, 0:16]
                x2 = fv[:, :, 16:32]
                ra, rb, rc, rd = [q[0:rows, 0:nh, :] for q in r4]
                rt = [tk, cstok]
                tkp = ("stp", i)
                tkd = ("std", i)
                B.tt(ra, x1, cb, ALU.mult, r=rt, w=[tkp], eng="pool")
                B.tt(rb, x2, sb, ALU.mult, r=rt, w=[tkp], eng="pool")
                B.tt(rc, x2, cb, ALU.mult, r=rt, w=[tkd])
                B.tt(rd, x1, sb, ALU.mult, r=rt, w=[tkd])
                B.tt(x1, ra, rb, ALU.subtract, r=[tkp, tkd], w=[tk], eng="pool")
                B.tt(x2, rc, rd, ALU.add, r=[tkd, tkp], w=[tk])
            if dst_out is not None:
                B.dma("sp", dst_out, f[0:rows, 0:ncols], r=[tk], w=[B.fresh("o")])
            if dstT_fn is not None:
                B.copy("act", b_[0:rows, 0:ncols], f[0:rows, 0:ncols], r=[tk], w=[tk])
                pb = 6 + (stc[0] % 2)
                pT = psum[pb][:, :].bitcast(BF16)[:, 0:512].rearrange("p (a b) -> p a b", a=4)
                for h in range(nh):
                    B.transpose(pT[:, h, 0:rows], b_[0:rows, h * 128:(h + 1) * 128],
                                ident_b[0:rows, 0:rows], r=[tk, "ident_b"], w=[("ps", pb)])
                B.copy("dve", T_[:, 0:nh, 0:rows], pT[:, 0:nh, 0:rows], r=[("ps", pb)], w=[tk])
                dstT_fn(T_, nh, rows, tk)

        if "seq" in ph:
            xseq = inp("xseq", [SEQ, D])
            xown = inp("xown", [TO, D])
            w_in = inp("w_in", [D, IN_COLS])
            cs_seq = inp("cs_seq", [SEQ + 64, 32])
            o_k = B.dout("o_k", [SEQ + 64, NKV * HD])
            o_v = B.dout("o_v", [SEQ + 64, NKV * HD])
            o_ik = B.dout("o_ik", [SEQ + 64, HD])
            m0 = A.mark()
            gmixT = A.alloc([KC], F32)
            B.dma("sp", gmixT, inp("norm_mixT", [128, KC]), w=["gT"])
            xnT = A.alloc([KC, TO], BF16)
            wbufs = [A.alloc([KC, 512], BF16) for _ in range(2)]
            ssb = [A.alloc([1], F32) for _ in range(2)]
            st = make_headproc(3)
            cst = [A.alloc([32], F32) for _ in range(9)]
            xl_st = [A.alloc([TO], F32) for _ in range(2)]
            xt, junk, st = union_xt_junk(st)

            for tt in range(4):
                blocks = [(xseq[tt * 1024 + j * 128: tt * 1024 + (j + 1) * 128, :], 128) for j in range(8)]
                orows = [tt * 1024 + j * 128 for j in range(8)]
                if tt == 3:
                    blocks.append((xown[1024:1088, :], 64))
                    orows.append(SEQ)
                ntok = sum(b[1] for b in blocks)

                def extra(bi, rows, orows=orows):
                    B.dma("sp", cst[bi][0:rows], cs_seq[orows[bi]:orows[bi] + rows, :], w=[("cst", bi)])
                norm_transpose(blocks, xnT, gmixT, xt, junk, ssb, extra)
                S.barrier()

                def evac(tag, bi, rows, t0, ps, pstok, ncols, orows=orows):
                    kind, half = tag
                    orow = orows[bi]
                    is_s = orow >= SEQ
                    if kind == "k":
                        def dstT(T_, nh, rows_, tk):
                            if not is_s:
                                B.dma("act", KT_scr[half * 4:half * 4 + 4, :, orow:orow + rows_]
                                      .rearrange("h d t -> d h t"), T_[:, 0:4, 0:rows_], r=[tk], w=[B.fresh("kt")])
                            else:
                                for sq in range(2):
                                    B.dma("act", KT_s[sq, half * 4:half * 4 + 4, :, PAST:SS]
                                          .rearrange("h d t -> d h t"), T_[:, 0:4, sq * 32:(sq + 1) * 32],
                                          r=[tk], w=[B.fresh("kt")])
                        headproc(st, ps, pstok, rows, ncols, "k", cst[bi], ("cst", bi),
                                 o_k[orow:orow + rows, half * 512:(half + 1) * 512], dstT)
                    elif kind == "ik":
                        def dstT(T_, nh, rows_, tk):
                            if not is_s:
                                B.dma("act", ikT_scr[:, orow:orow + rows_], T_[:, 0, 0:rows_], r=[tk],
                                      w=[B.fresh("ikt")])
                            else:
                                for sq in range(2):
                                    B.dma("act", ikT_s[sq, :, PAST:SS], T_[:, 0, sq * 32:(sq + 1) * 32],
                                          r=[tk], w=[B.fresh("ikt")])
                        headproc(st, ps, pstok, rows, ncols, "idx_k", cst[bi], ("cst", bi),
                                 o_ik[orow:orow + rows, :], dstT)
                    else:
                        i = stc[0] % st["n"]
                        stc[0] += 1
                        tk = ("st", i)
                        B.copy("act", st["f"][i][0:rows, 0:512], ps, r=[pstok], w=[tk])
                        B.copy("dve", st["b"][i][0:rows, 0:512], ps, r=[pstok], w=[tk])
                        B.dma("sp", o_v[orow:orow + rows, half * 512:(half + 1) * 512],
                              st["f"][i][0:rows, 0:512], r=[tk], w=[B.fresh("o")])
                        if not is_s:
                            B.dma("act", V_scr[orow:orow + rows, half * 512:(half + 1) * 512],
                                  st["b"][i][0:rows, 0:512], r=[tk], w=[B.fresh("v")])
                        else:
                            for sq in range(2):
                                B.dma("act", V_s[sq, PAST:SS, half * 512:(half + 1) * 512],
                                      st["b"][i][sq * 32:(sq + 1) * 32, 0:512], r=[tk], w=[B.fresh("v")])

                gemm_tok(xnT, [b[1] for b in blocks], KC, wbufs, lambda c0, n: w_in[:, c0:c0 + n],
                         [(C_K, 512, ("k", 0)), (C_K + 512, 512, ("k", 1)), (C_V, 512, ("v", 0)),
                          (C_V + 512, 512, ("v", 1)), (C_IK, 128, ("ik", 0))], evac)

                ttiles = [(0, 512), (512, 512)] + ([(1024, 64)] if tt == 3 else [])
                col0 = tt * 1024

                def evac_xl(chunk, tt0, n, pss, ntok=ntok, col0=col0, last=ttiles[-1][0]):
                    xs = xl_st[chunk % 2]
                    ps, pstok = pss[0]
                    B.copy("act" if (tt0 // 512) % 2 else "dve", xs[:, tt0:tt0 + n], ps, r=[pstok],
                           w=[("xls", chunk % 2)])
                    if tt0 == last:
                        B.dma("sp", xlT_scr[chunk, :, col0:col0 + ntok], xs[:, 0:ntok],
                              r=[("xls", chunk % 2)], w=[B.fresh("xl")])
                gemm_feat([(xnT, wbufs, lambda c0, n: w_in[:, C_XL + c0:C_XL + c0 + n])], len(blocks), KC,
                          4096, 512, ttiles, evac_xl)
                S.barrier()
            A.release(m0)

        if "cache" in ph:
            cache_k = inp("cache_k", [2, PAST, NKV * HD])
            cache_v = inp("cache_v", [2, PAST, NKV * HD])
            cache_ik = inp("cache_ik", [2, PAST, HD])
            m0 = A.mark()
            kb = A.alloc([16, 1024], BF16)
            vb = A.alloc([16, 1024], BF16)
            ib = A.alloc([16, 128], BF16)
            stg = [A.alloc([8, 128], BF16) for _ in range(2)]
            istg = A.alloc([16, 128], BF16)
            for sq in range(2):
                for q in range(4):
                    B.dma("pool", kb[:, q * 4:(q + 1) * 4, :],
                          cache_k[sq, q * 512:(q + 1) * 512, :].rearrange("(b p) c -> p b c", p=128), w=[("kb", q)])
                    B.dma("pool", vb[:, q * 4:(q + 1) * 4, :],
                          cache_v[sq, q * 512:(q + 1) * 512, :].rearrange("(b p) c -> p b c", p=128), w=[("vb", q)])
                B.dma("pool", ib, cache_ik[sq].rearrange("(b p) c -> p b c", p=128), w=["ib"])
                for q in range(4):
                    B.dma("act", V_s[sq, q * 512:(q + 1) * 512, :].rearrange("(b p) c -> p b c", p=128),
                          vb[:, q * 4:(q + 1) * 4, :], r=[("vb", q)], w=[B.fresh("vs")])
                for blk in range(16):
                    pb = 6 + (blk % 2)
                    pT = psum[pb][:, :].bitcast(BF16).rearrange("p (a b) -> p a b", a=8)
                    for h in range(8):
                        B.transpose(pT[:, h, :], kb[:, blk, h * 128:(h + 1) * 128], ident_b,
                                    r=[("kb", blk // 4), "ident_b"], w=[("ps", pb)])
                    sg = stg[blk % 2]
                    B.copy("dve" if blk % 2 else "act", sg, pT, r=[("ps", pb)], w=[("stg", blk % 2)])
                    B.dma("sp", KT_s[sq, :, :, blk * 128:(blk + 1) * 128].rearrange("h d t -> d h t"), sg,
                          r=[("stg", blk % 2)], w=[B.fresh("kts")])
                for half in range(2):
                    pb = 4 + half
                    pT = psum[pb][:, :].bitcast(BF16).rearrange("p (a b) -> p a b", a=8)
                    for j in range(8):
                        B.transpose(pT[:, j, :], ib[:, half * 8 + j, :], ident_b, r=["ib", "ident_b"],
                                    w=[("ps", pb)])
                    B.copy("dve", istg[:, half * 8:(half + 1) * 8, :], pT, r=[("ps", pb)], w=["istg"])
                B.dma("sp", ikT_s[sq, :, 0:PAST], istg, r=["istg"], w=[B.fresh("ikts")])
            S.barrier()
            A.release(m0)

        if "lru" in ph:
            convwT = inp("convwT", [128, 4, KC])
            convbT = inp("convbT", [128, KC])
            baT = inp("lru_baT", [128, KC])
            bxT = inp("lru_bxT", [128, KC])
            lamT = inp("lru_lamT", [128, KC])
            lru_wa = inp("lru_wa", [16, 256, 256])
            lru_wx = inp("lru_wx", [16, 256, 256])
            st_lruT = inp("state_lruT", [2, 128, KC])
            st_convT = inp("state_convT", [2, 128, KC, 3])
            selr = inp("selr", [128, 4])
            o_lru = B.dout("o_lru", [3, D])
            o_conv = B.dout("o_conv", [3, 3, D])
            m0 = A.mark()
            cw = A.alloc([4, KC], F32)
            cb = A.alloc([KC], F32)
            ba = A.alloc([KC], F32)
            bx = A.alloc([KC], F32)
            lam = A.alloc([KC], F32)
            cneg = A.alloc([KC], F32)
            cneg2 = A.alloc([KC], F32)
            tmpc = A.alloc([KC], F32)
            sel = A.alloc([4], F32)
            h0s = A.alloc([2, KC], F32)
            c0s = A.alloc([2, KC, 3], F32)
            hfin = A.alloc([3, KC], F32)
            cfin = A.alloc([3, 3, KC], F32)
            B.dma("sp", cw, convwT, w=["lc"])
            B.dma("sp", cb, convbT, w=["lc"])
            B.dma("sp", ba, baT, w=["lc"])
            B.dma("sp", bx, bxT, w=["lc"])
            B.dma("sp", lam, lamT, w=["lam"])
            B.dma("sp", sel, selr, w=["lc"])
            for sq in range(2):
                B.dma("sp", h0s[:, sq, :], st_lruT[sq], w=["lc"])
                B.dma("sp", c0s[:, sq, :, :], st_convT[sq], w=["lc"])
            B.ts(tmpc, lam, -1.0, None, ALU.mult, r=["lam"], w=["tmpc"])
            B.tt(tmpc, tmpc, lam, ALU.max, r=["lam", "tmpc"], w=["tmpc"])
            B.act(tmpc, tmpc, AF.Exp, r=["tmpc"], w=["tmpc"], scale=-1.0)
            B.act(tmpc, tmpc, AF.Ln, r=["tmpc", "one"], w=["tmpc"], bias=one_t[:, 0:1])
            B.ts(cneg, lam, -1.0, 0.0, ALU.mult, ALU.max, r=["lam"], w=["cneg"])
            B.tt(cneg, cneg, tmpc, ALU.add, r=["cneg", "tmpc"], w=["cneg"])
            B.ts(cneg2, cneg, -16.0, None, ALU.mult, r=["cneg"], w=["cneg2"])
            B.ts(cneg, cneg, -8.0, None, ALU.mult, r=["cneg"], w=["cneg"])

            XL = [[A.alloc([1027], F32) for _ in range(2)] for _ in range(2)]
            U2 = [[A.alloc([1024], F32) for _ in range(2)] for _ in range(2)]
            UB2 = [[A.alloc([1024], BF16) for _ in range(2)] for _ in range(2)]
            Rg2 = [[A.alloc([1024], F32) for _ in range(2)] for _ in range(2)]
            Ig2 = [[A.alloc([1024], F32) for _ in range(2)] for _ in range(2)]
            Aa2 = [[A.alloc([1024], F32) for _ in range(2)] for _ in range(2)]
            Hh2 = [[A.alloc([1024], F32) for _ in range(2)] for _ in range(2)]
            HO2 = [[A.alloc([256], F32) for _ in range(2)] for _ in range(2)]
            hprev = A.alloc([2], F32)
            wab = [A.alloc([2, 256], BF16) for _ in range(2)]
            wxb = [A.alloc([2, 256], BF16) for _ in range(2)]
            pieces = [(tt * 1024, 1024, "p", tt) for tt in range(4)] + [(SEQ, 32, "s", 0), (SEQ + 32, 32, "s", 1)]
            for nblk in range(16):
                wsl = nblk % 2
                B.dma("pool", wab[wsl], lru_wa[nblk].rearrange("(ch p) d -> p ch d", p=128), w=[("wa", wsl)])
                B.dma("pool", wxb[wsl], lru_wx[nblk].rearrange("(ch p) d -> p ch d", p=128), w=[("wx", wsl)])
                for pi, (col0, n, kind, idx) in enumerate(pieces):
                    xb = XL[pi % 2]
                    xprev = XL[(pi + 1) % 2]
                    pp = pi % 2
                    U, UB, Rg, Ig, Aa, Hh, HO = U2[pp], UB2[pp], Rg2[pp], Ig2[pp], Aa2[pp], Hh2[pp], HO2[pp]
                    for ch in range(2):
                        chunk = 2 * nblk + ch
                        xk = ("xl", pi % 2, ch)
                        B.dma("sp", xb[ch][:, 3:3 + n], xlT_scr[chunk, :, col0:col0 + n], w=[xk])
                        if kind == "p" and idx == 0:
                            S.add("dve", lambda e, t=xb[ch]: e.memset(t[:, 0:3], 0.0), w=[xk])
                        elif kind == "p":
                            B.copy("dve", xb[ch][:, 0:3], xprev[ch][:, 1024:1027],
                                   r=[("xl", (pi + 1) % 2, ch)], w=[xk])
                        else:
                            B.copy("dve", xb[ch][:, 0:3], c0s[:, idx, chunk, :], r=["lc"], w=[xk])
                        u = U[ch]
                        uk = ("u", pp, ch)
                        B.act(u[:, 0:n], xb[ch][:, 3:3 + n], AF.Identity, r=[xk, "lc"], w=[uk],
                              scale=cw[:, 3, chunk:chunk + 1], bias=cb[:, chunk:chunk + 1])
                        for j in (2, 1, 0):
                            B.stt(u[:, 0:n], xb[ch][:, j:j + n], cw[:, j, chunk:chunk + 1], u[:, 0:n],
                                  ALU.mult, ALU.add, r=[xk, "lc", uk], w=[uk])
                        B.copy("act", UB[ch][:, 0:n], u[:, 0:n], r=[uk], w=[("ub", pp, ch)])
                    halves = [(0, min(n, 512))] + ([(512, 512)] if n > 512 else [])
                    for dh in range(2):
                        chunk = 2 * nblk + dh
                        for (gbuf, wbuf_, bias, gk, wk) in ((Rg, wab, ba, "rg", "wa"), (Ig, wxb, bx, "ig", "wx")):
                            for (h0, hn) in halves:
                                pb = next_ps()
                                ps = psum[pb][:, 0:hn]
                                B.mm_group(ps, [(wbuf_[wsl][:, ch, dh * 128:(dh + 1) * 128], UB[ch][:, h0:h0 + hn])
                                                for ch in range(2)],
                                           r=[("ub", pp, 0), ("ub", pp, 1), (wk, wsl)], w=[("ps", pb)])
                                B.act(gbuf[dh][:, h0:h0 + hn], ps, AF.Sigmoid, r=[("ps", pb), "lc"],
                                      w=[(gk, pp, dh)], bias=bias[:, chunk:chunk + 1])
                    for dh in range(2):
                        chunk = 2 * nblk + dh
                        B.act(Aa[dh][:, 0:n], Rg[dh][:, 0:n], AF.Exp, r=[("rg", pp, dh), "cneg"], w=[("aa", pp, dh)],
                              scale=cneg[:, chunk:chunk + 1])
                        B.act(Rg[dh][:, 0:n], Rg[dh][:, 0:n], AF.Exp, r=[("rg", pp, dh), "cneg2"], w=[("rg", pp, dh)],
                              scale=cneg2[:, chunk:chunk + 1])
                    for dh in range(2):
                        B.act(Rg[dh][:, 0:n], Rg[dh][:, 0:n], AF.Relu, r=[("rg", pp, dh), "one"], w=[("rg", pp, dh)],
                              scale=-1.0, bias=one_t[:, 0:1])
                        B.act(Rg[dh][:, 0:n], Rg[dh][:, 0:n], AF.Sqrt, r=[("rg", pp, dh)], w=[("rg", pp, dh)])
                    for dh in range(2):
                        chunk = 2 * nblk + dh
                        B.tt(Ig[dh][:, 0:n], Ig[dh][:, 0:n], U[dh][:, 0:n], ALU.mult, r=[("ig", pp, dh), ("u", pp, dh)],
                             w=[("ig", pp, dh)])
                        B.tt(Ig[dh][:, 0:n], Ig[dh][:, 0:n], Rg[dh][:, 0:n], ALU.mult, r=[("ig", pp, dh), ("rg", pp, dh)],
                             w=[("ig", pp, dh)])
                        if kind == "p" and idx == 0:
                            init = 0.0
                            ir = []
                        elif kind == "p":
                            init = hprev[:, dh:dh + 1]
                            ir = [("hp", dh)]
                        else:
                            init = h0s[:, idx, chunk:chunk + 1]
                            ir = ["lc"]
                        S.add("dve", lambda e, dh=dh, n=n, init=init, Hh=Hh, Aa=Aa, Ig=Ig: e.tensor_tensor_scan(
                            out=Hh[dh][:, 0:n], data0=Aa[dh][:, 0:n], data1=Ig[dh][:, 0:n], initial=init,
                            op0=ALU.mult, op1=ALU.add), r=[("aa", pp, dh), ("ig", pp, dh)] + ir, w=[("hh", pp, dh)])
                        if kind == "p" and idx < 3:
                            B.copy("dve", hprev[:, dh:dh + 1], Hh[dh][:, n - 1:n], r=[("hh", pp, dh)], w=[("hp", dh)])
                        if kind == "s" or idx == 3:
                            row = 0 if kind == "p" else 1 + idx
                            B.copy("dve", hfin[:, row, chunk:chunk + 1], Hh[dh][:, n - 1:n], r=[("hh", pp, dh)],
                                   w=["hfin"])
                            for j in range(3):
                                B.copy("dve", cfin[:, row, j, chunk:chunk + 1],
                                       XL[pi % 2][dh][:, n + j:n + j + 1], r=[("xl", pi % 2, dh)], w=["cfin"])
                        ho = HO[dh]
                        hk = ("ho", pp, dh)
                        if kind == "p":
                            for il in range(2):
                                B.ts(ho[:, il * 128:(il + 1) * 128], Hh[dh][:, (4 * il) * 128:(4 * il + 1) * 128],
                                     sel[:, 0:1], None, ALU.mult, r=[("hh", pp, dh), "lc"], w=[hk])
                                for j in range(1, 4):
                                    B.stt(ho[:, il * 128:(il + 1) * 128],
                                          Hh[dh][:, (4 * il + j) * 128:(4 * il + j + 1) * 128], sel[:, j:j + 1],
                                          ho[:, il * 128:(il + 1) * 128], ALU.mult, ALU.add,
                                          r=[("hh", pp, dh), "lc", hk], w=[hk])
                            B.dma("act", hown_scr[chunk, :, idx * 256:(idx + 1) * 256], ho, r=[hk], w=[B.fresh("ho")])
                        else:
                            B.dma("act", hown_scr[chunk, :, 1024 + idx * 32:1024 + (idx + 1) * 32], Hh[dh][:, 0:32],
                                  r=[("hh", pp, dh)], w=[B.fresh("ho")])
            fst = A.alloc([12, 128], F32)
            pbv = psum[4][:, :].rearrange("p (a b) -> p a b", a=4)
            pbv2 = psum[5][:, :].rearrange("p (a b) -> p a b", a=4)
            pbv3 = psum[6][:, :].rearrange("p (a b) -> p a b", a=4)
            for row in range(3):
                pv = (pbv, pbv2, pbv3)[row]
                pk = ("ps", 4 + row)
                B.transpose(pv[0:32, 0, :], hfin[:, row, :], ident_f, r=["hfin", "ident_f"], w=[pk])
                for j in range(3):
                    B.transpose(pv[0:32, 1 + j, :], cfin[:, row, j, :], ident_f, r=["cfin", "ident_f"], w=[pk])
                B.copy("dve", fst[0:32, row * 4:(row + 1) * 4, :], pv[0:32, :, :], r=[pk], w=["fst"])
                B.dma("sp", o_lru[row].rearrange("(kc p) -> kc p", p=128), fst[0:32, row * 4, :], r=["fst"],
                      w=[B.fresh("o")])
                for j in range(3):
                    B.dma("sp", o_conv[row, j].rearrange("(kc p) -> kc p", p=128), fst[0:32, row * 4 + 1 + j, :],
                          r=["fst"], w=[B.fresh("o")])
            S.barrier()
            A.release(m0)

        if "own" in ph:
            xown = inp("xown", [TO, D])
            w_in = inp("w_in", [D, IN_COLS])
            cs_own = inp("cs_own", [TO, 32])
            m0 = A.mark()
            gmixT = A.alloc([KC], F32)
            B.dma("sp", gmixT, inp("norm_mixT", [128, KC]), w=["gT"])
            xnT = A.alloc([KC, TO], BF16)
            wbufs = [A.alloc([KC, 512], BF16) for _ in range(2)]
            ssb = [A.alloc([1], F32) for _ in range(2)]
            st = make_headproc(3)
            cst = [A.alloc([32], F32) for _ in range(9)]
            xt, junk, st = union_xt_junk(st)
            blocks = [(xown[j * 128:(j + 1) * 128, :], 128) for j in range(8)] + [(xown[1024:1088, :], 64)]
            brow = [b[1] for b in blocks]

            def extra(bi, rows):
                B.dma("sp", cst[bi][0:rows], cs_own[bi * 128:bi * 128 + rows, :], w=[("cst", bi)])
            norm_transpose(blocks, xnT, gmixT, xt, junk, ssb, extra)
            S.barrier()

            def evac(tag, bi, rows, t0, ps, pstok, ncols):
                kind, g = tag
                if kind == "q":
                    def dstT(T_, nh, rows_, tk):
                        B.dma("act", QT_scr[g * 4:g * 4 + 4, :, t0:t0 + rows_].rearrange("h d t -> d h t"),
                              T_[:, 0:4, 0:rows_], r=[tk], w=[B.fresh("qt")])
                    headproc(st, ps, pstok, rows, ncols, "q", cst[bi], ("cst", bi), None, dstT)
                elif kind == "iq":
                    def dstT(T_, nh, rows_, tk):
                        B.dma("act", iqT_scr[g * 4:g * 4 + 4, :, t0:t0 + rows_].rearrange("h d t -> d h t"),
                              T_[:, 0:4, 0:rows_], r=[tk], w=[B.fresh("iqt")])
                    headproc(st, ps, pstok, rows, ncols, None, cst[bi], ("cst", bi), None, dstT)
                else:
                    i = stc[0] % st["n"]
                    stc[0] += 1
                    tk = ("st", i)
                    B.copy("act", st["f"][i][0:rows, 0:32], ps, r=[pstok], w=[tk])
                    B.dma("sp", iw_scr[t0:t0 + rows, :], st["f"][i][0:rows, 0:32], r=[tk], w=[B.fresh("iw")])
            groups = [(C_Q + g * 512, 512, ("q", g)) for g in range(8)] + \
                     [(C_IQ + g * 512, 512, ("iq", g)) for g in range(8)] + [(C_IW, 32, ("iw", 0))]
            gemm_tok(xnT, brow, KC, wbufs, lambda c0, n: w_in[:, c0:c0 + n], groups, evac)

            ttiles = [(0, 512), (512, 512), (1024, 64)]
            NE = 2
            ey = [A.alloc([512], F32) for _ in range(NE)]
            et = [A.alloc([512], F32) for _ in range(NE)]
            eh = [A.alloc([512], F32) for _ in range(NE)]
            eb = [A.alloc([512], BF16) for _ in range(NE)]
            ec = [0]

            def evac_feat(which):
                def ev(chunk, tt0, n, pss):
                    ps, pstok = pss[0]
                    i = ec[0] % NE
                    ec[0] += 1
                    tk = ("e", i)
                    if which == "yl":
                        y, t, hh, ob = ey[i], et[i], eh[i], eb[i]
                        B.dma("sp", hh[:, 0:n], hown_scr[chunk, :, tt0:tt0 + n], w=[("eh", i)])
                        B.copy("act", y[:, 0:n], ps, r=[pstok], w=[tk])
                        B.tt(t[:, 0:n], y[:, 0:n], y[:, 0:n], ALU.mult, r=[tk], w=[tk])
                        B.ts(t[:, 0:n], t[:, 0:n], 0.044715, 1.0, ALU.mult, ALU.add, r=[tk], w=[tk])
                        B.tt(t[:, 0:n], t[:, 0:n], y[:, 0:n], ALU.mult, r=[tk], w=[tk])
                        B.act(t[:, 0:n], t[:, 0:n], AF.Sigmoid, r=[tk], w=[tk], scale=1.5957691216057308)
                        B.tt(t[:, 0:n], t[:, 0:n], y[:, 0:n], ALU.mult, r=[tk], w=[tk])
                        B.tt(ob[:, 0:n], t[:, 0:n], hh[:, 0:n], ALU.mult, r=[tk, ("eh", i)], w=[tk])
                        B.dma("act", olruT_scr[chunk, :, tt0:tt0 + n], ob[:, 0:n], r=[tk], w=[B.fresh("ol")])
                    else:
                        y = ey[i]
                        B.act(y[:, 0:n], ps, AF.Sigmoid, r=[pstok], w=[tk])
                        dst = saT_scr if which == "ga" else slT_scr
                        B.dma("act", dst[chunk, :, tt0:tt0 + n], y[:, 0:n], r=[tk], w=[B.fresh("sg")])
                return ev
            for which, cbase in (("yl", C_YL), ("ga", C_GA), ("gl", C_GL)):
                gemm_feat([(xnT, wbufs, lambda c0, n, cbase=cbase: w_in[:, cbase + c0:cbase + c0 + n])], 9, KC,
                          4096, 512, ttiles, evac_feat(which))
            S.barrier()
            A.release(m0)

        if "attn" in ph:
            dsel_in = inp("dsel", [128, 32 * 128])
            mbias_in = inp("maskbias", [128, 512])
            m0 = A.mark()
            dselt = A.alloc([32, 128], BF16)
            dv = dsel_in.rearrange("p (g t) -> p g t", g=32)
            for g0 in range(0, 32, 8):
                B.dma("pool", dselt[:, g0:g0 + 8, :], dv[:, g0:g0 + 8, :], w=[("dsel", g0)])
            S.add("dve", lambda e: e.memset(eps_t, EPS), r=[("dsel", g0) for g0 in range(0, 32, 8)], w=["dsel"])
            mbias = A.alloc([512], F32)
            B.dma("sp", mbias, mbias_in, w=["mbias"])
            ones_b = A.alloc([128], BF16)
            S.add("dve", lambda e: e.memset(ones_b, 1.0), w=["ones"])
            gq = A.alloc([128], F32)
            gk = A.alloc([128], F32)
            B.dma("sp", gq, inp("norm_q", [HD]).partition_broadcast(128), w=["gq"])
            B.dma("sp", gk, inp("norm_k", [HD]).partition_broadcast(128), w=["gk"])
            cq = A.alloc([1], F32)
            ck = A.alloc([1], F32)
            S.add("dve", lambda e: e.tensor_reduce(out=cq, in_=gq, axis=AX.X, op=ALU.max, apply_absolute_value=True),
                  r=["gq"], w=["cq"])
            S.add("dve", lambda e: e.tensor_reduce(out=ck, in_=gk, axis=AX.X, op=ALU.max, apply_absolute_value=True),
                  r=["gk"], w=["ck"])
            B.tt(cq, cq, ck, ALU.mult, r=["cq", "ck"], w=["cq"])
            B.ts(cq, cq, -math.sqrt(128.0), None, ALU.mult, r=["cq"], w=["cq"])

            iqTb = A.alloc([32, 128], BF16)
            iqTg = A.alloc([32, 128], BF16)
            ikT = A.alloc([SEQ], BF16)
            wsel = A.alloc([32, 128], BF16)
            iwb = A.alloc([32], F32)
            iwrep = A.alloc([32, 4], BF16)
            R1 = [A.alloc([512], BF16) for _ in range(3)]
            ImB = [A.alloc([SEQ], F32) for _ in range(2)]
            Wk = A.alloc([SEQ], F32)
            m8 = A.alloc([8], F32)
            thrB = [A.alloc([1], F32) for _ in range(2)]
            maskb = A.alloc([SEQ], BF16)
            maskTB = [A.alloc([32, 128], BF16) for _ in range(2)]
            KTg = [A.alloc([SEQ], BF16) for _ in range(2)]
            Vg = [A.alloc([32, 128], BF16) for _ in range(2)]
            QTg = [A.alloc([512], BF16) for _ in range(2)]
            Pt = [A.alloc([512], BF16) for _ in range(3)]
            Pm = [A.alloc([512], BF16) for _ in range(3)]
            NEV = 4
            zs = [A.alloc([512], F32) for _ in range(NEV)]
            osb = [A.alloc([512], F32) for _ in range(NEV)]
            evc = [0]
            ot = [A.alloc([512], BF16) for _ in range(2)]
            sc_att = 1.0 / math.sqrt(128.0)

            qblocks = [("p", i, i * 128, 128, 512 * (i + 1)) for i in range(7, -1, -1)] + \
                      [("s", sq, 1024 + sq * 32, 32, SS) for sq in range(2)]
            r1c = [0]
            pc = [0]
            gct = [0]

            def srcs(kind, idx):
                if kind == "p":
                    return ikT_scr, KT_scr, V_scr
                return ikT_s[idx], KT_s[idx], V_s[idx]

            def stageA(blk, par):
                kind, idx, t0, R, Sk = blk
                G = R // 4
                Im = ImB[par]
                imk = ("Im", par)
                ikT_src = srcs(kind, idx)[0]
                iqv = iqTb.rearrange("p a b -> p (a b)")[:, 0:32 * R].rearrange("p (h t) -> p h t", h=32)
                B.dma("sp", iqv, iqT_scr[:, :, t0:t0 + R].rearrange("h d t -> d h t"), w=["iqTb"])
                B.copy("pool", iqTg[:, 0:G, :].rearrange("p g (h t) -> p g h t", t=4),
                       iqv.rearrange("p h (g t) -> p g h t", t=4), r=["iqTb"], w=["iqTg"])
                B.dma("sp", ikT[:, 0:Sk], ikT_src[:, 0:Sk], w=["ikT"])
                B.dma("sp", iwb[0:R], iw_scr[t0:t0 + R, :], w=["iwb"])
                B.copy("pool", iwrep[0:R], iwb[0:R].unsqueeze(2).to_broadcast([R, 32, 4]), r=["iwb"], w=["iwrep"])
                pT = psum[3][:, :].bitcast(BF16)
                B.transpose(pT[:, 0:R], iwrep[0:R].rearrange("p h t -> p (h t)"), ident_b[0:R, 0:R],
                            r=["iwrep", "ident_b"], w=[("ps", 3)])
                B.tt(wsel[:, 0:G, 0:R], dselt[:, 0:G, 0:R], pT[:, 0:R].unsqueeze(1).to_broadcast([128, G, R]),
                     ALU.mult, r=[("ps", 3), "dsel"], w=["wsel"])
                chunks = [(c0, min(512, Sk - c0)) for c0 in range(0, Sk, 512)]
                items = [(c0, cn, g) for (c0, cn) in chunks for g in range(G)]
                pend = None
                for it in range(len(items) + 1):
                    cur_item = None
                    if it < len(items):
                        c0, cn, g = items[it]
                        pb = r1c[0] % 2
                        rr = R1[r1c[0] % 3]
                        rk = ("r1", r1c[0] % 3)
                        r1c[0] += 1
                        ps1 = psum[pb][:, 0:cn]
                        B.mm(ps1, iqTg[:, g, :], ikT[:, c0:c0 + cn], True, True, r=["iqTg", "ikT"], w=[("ps", pb)])
                        B.act(rr[:, 0:cn], ps1, AF.Relu, r=[("ps", pb)], w=[rk])
                        cur_item = (c0, cn, g, rr, rk)
                    if pend is not None:
                        c0p, cnp, gp, rrp, rkp = pend
                        psI = psum[2][0:R, 0:cnp]
                        B.mm(psI, wsel[:, gp, 0:R], rrp[:, 0:cnp], gp == 0, gp == G - 1, r=["wsel", rkp], w=[("ps", 2)])
                        if gp == G - 1:
                            B.copy("act", Im[0:R, c0p:c0p + cnp], psI, r=[("ps", 2)], w=[imk])
                    pend = cur_item

            def stageB1(blk, par):
                kind, idx, t0, R, Sk = blk
                Im = ImB[par]
                imk = ("Im", par)
                if kind == "p":
                    B.tt(Im[0:R, Sk - 512:Sk], Im[0:R, Sk - 512:Sk], mbias[0:R, 0:512], ALU.add,
                         r=[imk, "mbias"], w=[imk])
                cur = Im
                for rnd in range(32):
                    S.add("dve", lambda e, cur=cur, R=R, Sk=Sk: e.max(out=m8[0:R], in_=cur[0:R, 0:Sk]),
                          r=[imk, "Wk"], w=["m8"])
                    if rnd < 31:
                        S.add("dve", lambda e, cur=cur, R=R, Sk=Sk: e.match_replace(
                            out=Wk[0:R, 0:Sk], in_to_replace=m8[0:R], in_values=cur[0:R, 0:Sk], imm_value=-3.0e38),
                            r=[imk, "Wk", "m8"], w=["Wk"])
                        cur = Wk
                B.ts(thrB[par][0:R], m8[0:R, 7:8], -5.0e29, None, ALU.max, r=["m8"], w=[("thr", par)])

            def stageB2(blk, par):
                kind, idx, t0, R, Sk = blk
                Im = ImB[par]
                maskT = maskTB[par]
                mk = ("maskT", par)
                B.ts(maskb[0:R, 0:Sk], Im[0:R, 0:Sk], thrB[par][0:R], None, ALU.is_ge,
                     r=[("Im", par), ("thr", par)], w=["maskb"])
                nkb = (Sk + 127) // 128
                for k0 in range(0, nkb, 8):
                    pT8 = psum[3][:, :].bitcast(BF16).rearrange("p (a b) -> p a b", a=8)
                    k1 = min(nkb, k0 + 8)
                    for kb_ in range(k0, k1):
                        kn = min(128, Sk - kb_ * 128)
                        B.transpose(pT8[0:kn, kb_ - k0, 0:R], maskb[0:R, kb_ * 128:kb_ * 128 + kn], ident_b[0:R, 0:R],
                                    r=["maskb", "ident_b"], w=[("ps", 3)])
                    kfull = [kb_ for kb_ in range(k0, k1) if Sk - kb_ * 128 >= 128]
                    if kfull:
                        B.act(maskT[:, kfull[0]:kfull[-1] + 1, 0:R], pT8[:, 0:len(kfull), 0:R], AF.Copy,
                              r=[("ps", 3)], w=[mk], scale=30000.0, bias=-30000.0)
                    if len(kfull) < k1 - k0:
                        kb_ = k1 - 1
                        kn = Sk - kb_ * 128
                        B.act(maskT[0:kn, kb_, 0:R], pT8[0:kn, kb_ - k0, 0:R], AF.Copy, r=[("ps", 3)], w=[mk],
                              scale=30000.0, bias=-30000.0)

            def stageC(blk, par):
                kind, idx, t0, R, Sk = blk
                maskT = maskTB[par]
                mk = ("maskT", par)
                _, KT_src, V_src = srcs(kind, idx)
                nkb = (Sk + 127) // 128
                for g in range(NKV):
                    sl = gct[0] % 2
                    gct[0] += 1
                    B.dma("sp", KTg[sl][:, 0:Sk], KT_src[g, :, 0:Sk], w=[("KTg", sl)])
                    nfull = Sk // 128
                    B.dma("act", Vg[sl][:, 0:nfull, :],
                          V_src[0:nfull * 128, g * 128:(g + 1) * 128].rearrange("(b p) d -> p b d", p=128),
                          w=[("Vg", sl)])
                    if Sk % 128:
                        kn = Sk % 128
                        B.dma("act", Vg[sl][0:kn, nfull, :], V_src[nfull * 128:Sk, g * 128:(g + 1) * 128],
                              w=[("Vg", sl)])
                    B.dma("sp", QTg[sl][:, 0:4 * R].rearrange("p (h t) -> p h t", h=4),
                          QT_scr[4 * g:4 * g + 4, :, t0:t0 + R].rearrange("h d t -> d h t"), w=[("QTg", sl)])
                    N4 = 4 * R
                    psO = psum[6][:, 0:N4]
                    psZ = psum[7][:, 0:N4]
                    pendc = None
                    for it in range(nkb + 1):
                        curc = None
                        if it < nkb:
                            kb_ = it
                            kn = min(128, Sk - kb_ * 128)
                            pb = 4 + (pc[0] % 2)
                            pi_ = pc[0] % 3
                            pc[0] += 1
                            psL = psum[pb][0:kn, 0:N4]
                            B.mm(psL, KTg[sl][:, kb_ * 128:kb_ * 128 + kn], QTg[sl][:, 0:4 * R], True, False,
                                 r=[("KTg", sl), ("QTg", sl)], w=[("ps", pb)])
                            for h in range(4):
                                B.mm(psum[pb][0:kn, h * R:(h + 1) * R], ident_b[0:kn, 0:kn], maskT[0:kn, kb_, 0:R],
                                     False, h == 3, r=["ident_b", mk], w=[("ps", pb)])
                            B.act(Pt[pi_][0:kn, 0:N4], psL, AF.Exp, r=[("ps", pb), "cq"], w=[("pt", pi_)],
                                  scale=sc_att, bias=cq[0:kn, 0:1])
                            curc = (kb_, kn, pi_)
                        if pendc is not None:
                            kbp, knp, pip = pendc
                            B.mm(psO, Vg[sl][0:knp, kbp, :], Pt[pip][0:knp, 0:N4], kbp == 0, kbp == nkb - 1,
                                 r=[("Vg", sl), ("pt", pip)], w=[("ps", 6)])
                            B.mm(psZ, ones_b[0:knp, :], Pt[pip][0:knp, 0:N4], kbp == 0, kbp == nkb - 1,
                                 r=["ones", ("pt", pip)], w=[("ps", 7)])
                        pendc = curc
                    ei = evc[0] % NEV
                    evc[0] += 1
                    B.copy("act", zs[ei][:, 0:N4], psZ, r=[("ps", 7)], w=[("zs", ei)])
                    B.copy("act", osb[ei][:, 0:N4], psO, r=[("ps", 6)], w=[("os", ei)])
                    S.add("dve", lambda e, ei=ei, N4=N4: e.reciprocal(out=zs[ei][:, 0:N4], in_=zs[ei][:, 0:N4]),
                          r=[("zs", ei)], w=[("zs", ei)])
                    B.tt(ot[sl][:, 0:N4], osb[ei][:, 0:N4], zs[ei][:, 0:N4], ALU.mult, r=[("os", ei), ("zs", ei)],
                         w=[("ot", sl)])
                    B.dma("act", oattT_scr[4 * g:4 * g + 4, :, t0:t0 + R].rearrange("h d t -> d h t"),
                          ot[sl][:, 0:N4].rearrange("p (h t) -> p h t", h=4), r=[("ot", sl)], w=[B.fresh("oa")])

            NBK = len(qblocks)
            for step in range(NBK + 2):
                if step < NBK:
                    stageA(qblocks[step], step % 2)
                if 0 <= step - 1 < NBK:
                    stageB1(qblocks[step - 1], (step - 1) % 2)
                if 0 <= step - 2 < NBK:
                    stageC(qblocks[step - 2], (step - 2) % 2)
                if 0 <= step - 1 < NBK:
                    stageB2(qblocks[step - 1], (step - 1) % 2)
            S.barrier()
            A.release(m0)

        if "mix" in ph:
            w_branch = inp("w_branch", [2 * D, D])
            w_out = inp("w_out", [D, D])
            xown = inp("xown", [TO, D])
            m0 = A.mark()
            oaT = A.alloc([KC, TO], BF16)
            olT = A.alloc([KC, TO], BF16)
            wA = [A.alloc([KC, 128], BF16) for _ in range(2)]
            wL = [A.alloc([KC, 128], BF16) for _ in range(2)]
            for h0 in range(0, 32, 8):
                B.dma("sp", oaT[:, h0:h0 + 8, :], oattT_scr[h0:h0 + 8].rearrange("h d t -> d h t"),
                      w=[("ATx", h0)])
                B.dma("sp", olT[:, h0:h0 + 8, :], olruT_scr[h0:h0 + 8].rearrange("h d t -> d h t"),
                      w=[("ATy", h0)])
            ATR = [("ATx", h0) for h0 in (0, 8, 16, 24)] + [("ATy", h0) for h0 in (0, 8, 16, 24)]
            ttiles = [(0, 512), (512, 512), (1024, 64)]
            NE = 3
            sa_t = [A.alloc([512], F32) for _ in range(NE)]
            sl_t = [A.alloc([512], F32) for _ in range(NE)]
            ta_t = [A.alloc([512], F32) for _ in range(NE)]
            mb_t = [A.alloc([512], BF16) for _ in range(NE)]
            ec = [0]

            def evac_mix(chunk, tt0, n, pss):
                (psA, tokA), (psL, tokL) = pss
                i = ec[0] % NE
                ec[0] += 1
                tk = ("e", i)
                B.dma("sp", sa_t[i][:, 0:n], saT_scr[chunk, :, tt0:tt0 + n], w=[("esa", i)])
                B.dma("sp", sl_t[i][:, 0:n], slT_scr[chunk, :, tt0:tt0 + n], w=[("esl", i)])
                B.tt(ta_t[i][:, 0:n], psA, sa_t[i][:, 0:n], ALU.mult, r=[tokA, ("esa", i)], w=[tk])
                B.tt(sl_t[i][:, 0:n], psL, sl_t[i][:, 0:n], ALU.mult, r=[tokL, ("esl", i)], w=[("esl", i)])
                B.tt(mb_t[i][:, 0:n], ta_t[i][:, 0:n], sl_t[i][:, 0:n], ALU.add, r=[tk, ("esl", i)], w=[tk])
                B.dma("act", mixT_scr[chunk, :, tt0:tt0 + n], mb_t[i][:, 0:n], r=[tk], w=[B.fresh("mx")])
            gemm_feat([(oaT, wA, lambda c0, n: w_branch[0:D, c0:c0 + n]),
                       (olT, wL, lambda c0, n: w_branch[D:2 * D, c0:c0 + n])], 9, KC, 4096, 128, ttiles, evac_mix,
                      attoks=ATR)
            S.barrier()
            A.release(m0)

            m0 = A.mark()
            mxT = A.alloc([KC, TO], BF16)
            wbufs = [A.alloc([KC, 512], BF16) for _ in range(2)]
            for h0 in range(0, 32, 8):
                B.dma("sp", mxT[:, h0:h0 + 8, :], mixT_scr[h0:h0 + 8].rearrange("h d t -> d h t"), w=[("ATz", h0)])
            NE = 3
            xr = [A.alloc([512], F32) for _ in range(NE)]

            def evac_out(tag, bi, rows, t0, ps, pstok, ncols):
                i = ec[0] % NE
                ec[0] += 1
                c0 = tag
                B.dma("sp", xr[i][0:rows], xown[t0:t0 + rows, c0:c0 + 512], w=[("xr", i)])
                B.tt(xr[i][0:rows], ps, xr[i][0:rows], ALU.add, r=[pstok, ("xr", i)], w=[("xr", i)])
                B.dma("act", x1_scr[t0:t0 + rows, c0:c0 + 512], xr[i][0:rows], r=[("xr", i)], w=[B.fresh("x1")])
            gemm_tok(mxT, [128] * 8 + [64], KC, wbufs, lambda c0, n: w_out[:, c0:c0 + n],
                     [(g * 512, 512, g * 512) for g in range(8)], evac_out,
                     attoks=[("ATz", h0) for h0 in (0, 8, 16, 24)])
            S.barrier()
            A.release(m0)

        if "ffn" in ph:
            w_gate = inp("w_gate", [D, DFF])
            w_up = inp("w_up", [D, DFF])
            w_down = inp("w_down", [DFF, D])
            y_own = B.dout("y_own", [TO, D])
            m0 = A.mark()
            gffT = A.alloc([KC], F32)
            B.dma("sp", gffT, inp("norm_ffnT", [128, KC]), w=["gT"])
            xn2T = A.alloc([KC, TO], BF16)
            xt = A.alloc([D], F32)
            junk = A.alloc([D], BF16)
            ssb = [A.alloc([1], F32) for _ in range(2)]
            blocks = [(x1_scr[j * 128:(j + 1) * 128, :], 128) for j in range(8)] + [(x1_scr[1024:1088, :], 64)]
            norm_transpose(blocks, xn2T, gffT, xt, junk, ssb)
            wG = [A.alloc([KC, 256], BF16) for _ in range(2)]
            wU = [A.alloc([KC, 256], BF16) for _ in range(2)]
            ttiles = [(0, 512), (512, 512), (1024, 64)]
            NE = 3
            sg_t = [A.alloc([512], F32) for _ in range(NE)]
            hb_t = [A.alloc([512], BF16) for _ in range(NE)]
            ec = [0]

            def evac_ffn(chunk, tt0, n, pss):
                (psG, tokG), (psU, tokU) = pss
                i = ec[0] % NE
                ec[0] += 1
                tk = ("e", i)
                B.act(sg_t[i][:, 0:n], psG, AF.Silu, r=[tokG], w=[tk])
                B.tt(hb_t[i][:, 0:n], sg_t[i][:, 0:n], psU, ALU.mult, r=[tk, tokU], w=[tk])
                B.dma("act", hT_scr[chunk, :, tt0:tt0 + n], hb_t[i][:, 0:n], r=[tk], w=[B.fresh("ht")])
            gemm_feat([(xn2T, wG, lambda c0, n: w_gate[:, c0:c0 + n]),
                       (xn2T, wU, lambda c0, n: w_up[:, c0:c0 + n])], 9, KC, DFF, 256, ttiles, evac_ffn)
            S.barrier()
            A.release(m0)

            m0 = A.mark()
            hT = A.alloc([KC, TO], BF16)
            wbufs = [A.alloc([KC, 512], BF16) for _ in range(2)]
            NE = 3
            xr = [A.alloc([512], F32) for _ in range(NE)]
            pieces = [(0, 32), (32, 32), (64, 22)]
            for pi, (k0, kcn) in enumerate(pieces):
                step = 8
                toks = []
                for h0 in range(0, kcn, step):
                    h1 = min(kcn, h0 + step)
                    tk = ("ATz", h0)
                    toks.append(tk)
                    B.dma("sp", hT[:, h0:h1, :], hT_scr[k0 + h0:k0 + h1].rearrange("h d t -> d h t"), w=[tk])
                src = x1_scr
                dst = y_own

                def evac_dn(tag, bi, rows, t0, ps, pstok, ncols, pi=pi):
                    i = ec[0] % NE
                    ec[0] += 1
                    c0 = tag
                    prev = x1_scr if pi == 0 else y_own
                    B.dma("sp", xr[i][0:rows], prev[t0:t0 + rows, c0:c0 + 512], r=[("y", bi, c0)], w=[("xr", i)])
                    B.tt(xr[i][0:rows], ps, xr[i][0:rows], ALU.add, r=[pstok, ("xr", i)], w=[("xr", i)])
                    B.dma("act", y_own[t0:t0 + rows, c0:c0 + 512], xr[i][0:rows], r=[("xr", i)], w=[("y", bi, c0)])
                gemm_tok(hT, [128] * 8 + [64], kcn, wbufs,
                         lambda c0, n, k0=k0, kcn=kcn: w_down[k0 * 128:(k0 + kcn) * 128, c0:c0 + n],
                         [(g * 512, 512, g * 512) for g in range(8)], evac_dn, attoks=toks)
            S.barrier()
            A.release(m0)

        with nc.Block() as block:
            S.emit(nc, block, esem, dsem)
    return nc


def rope_tables(pos):
    half = 16
    inv = (np.float32(500000.0) ** (-np.arange(half, dtype=np.float32) * np.float32(2.0) / np.float32(32))
           ).astype(np.float32)
    ang = pos.astype(np.float32)[:, None] * inv[None, :]
    return np.concatenate([np.cos(ang), np.sin(ang)], axis=1).astype(np.float32)


def make_in_maps(inp, names=None):
    f = np.float32
    in_maps = []
    pos_seq = np.concatenate([np.arange(SEQ), PAST + np.arange(32), PAST + np.arange(32)])
    cs_seq = rope_tables(pos_seq)
    ident = np.eye(128, dtype=f)
    dsel = np.zeros((32, 4, 32, 128), f)
    for g in range(32):
        for tl in range(4):
            dsel[:, tl, g, 4 * g + tl] = 1.0 / 64.0
    dsel = dsel.reshape(128, 32 * 128)

    def T32(v):
        return np.ascontiguousarray(np.asarray(v).reshape(KC, 128).T)

    for c in range(8):
        p, r = c // 4, c % 4
        xp = inp["x_prompt"][p]
        own_blocks = [xp[(4 * i + r) * 128:(4 * i + r + 1) * 128] for i in range(8)]
        xown = np.concatenate(own_blocks + [inp["x_sample"][2 * c], inp["x_sample"][2 * c + 1]], axis=0)
        pos_own = np.concatenate([np.arange((4 * i + r) * 128, (4 * i + r + 1) * 128) for i in range(8)] +
                                 [PAST + np.arange(32), PAST + np.arange(32)])
        sel = np.zeros((128, 4), f)
        sel[:, r] = 1.0
        tl = np.arange(128)[:, None]
        slx = np.arange(512)[None, :]
        mb = np.where(slx < 128 * r + 64 + 64 * (tl >= 64), 0.0, NEGM).astype(f)
        sc = inp["state_conv"][0, 2 * c:2 * c + 2]
        m = {
            "xseq": np.ascontiguousarray(xp),
            "xown": np.ascontiguousarray(xown),
            "w_in": inp["w_in"][0],
            "norm_mixT": T32(inp["norm_mix"][0]),
            "norm_ffnT": T32(inp["norm_ffn"][0]),
            "norm_q": inp["norm_q"][0],
            "norm_k": inp["norm_k"][0],
            "norm_idx_k": inp["norm_idx_k"][0],
            "cs_seq": cs_seq,
            "cs_own": rope_tables(pos_own),
            "ident": ident,
            "dsel": dsel,
            "maskbias": mb,
            "cache_k": np.ascontiguousarray(inp["cache_k"][0, 2 * c:2 * c + 2].reshape(2, PAST, NKV * HD)),
            "cache_v": np.ascontiguousarray(inp["cache_v"][0, 2 * c:2 * c + 2].reshape(2, PAST, NKV * HD)),
            "cache_ik": np.ascontiguousarray(inp["cache_idx_k"][0, 2 * c:2 * c + 2]),
            "state_lruT": np.stack([T32(inp["state_lru"][0, 2 * c + q]) for q in range(2)]),
            "state_convT": np.ascontiguousarray(sc.reshape(2, 3, KC, 128).transpose(0, 3, 2, 1)),
            "convwT": np.ascontiguousarray(inp["conv_w"][0].reshape(4, KC, 128).transpose(2, 0, 1)),
            "convbT": T32(inp["conv_b"][0]),
            "lru_baT": T32(inp["lru_ba"][0]),
            "lru_bxT": T32(inp["lru_bx"][0]),
            "lru_lamT": T32(inp["lru_lambda"][0]),
            "lru_wa": inp["lru_wa"][0],
            "lru_wx": inp["lru_wx"][0],
            "selr": sel,
            "w_branch": inp["w_branch"][0],
            "w_out": inp["w_out"][0],
            "w_gate": inp["w_gate"][0],
            "w_up": inp["w_up"][0],
            "w_down": inp["w_down"][0],
        }
        if names is not None:
            m = {k: v for k, v in m.items() if k in names}
        in_maps.append(m)
    return in_maps


def input_names(nc):
    names = set()
    for alloc in nc.allocations:
        if isinstance(alloc, mybir.MemoryLocationSet) and alloc.kind == "ExternalInput":
            names.add(alloc.memorylocations[0].name)
    return names


_NC_CACHE = {}


def kernel(**inputs):
    inp = {k: np.asarray(v) for k, v in inputs.items()}
    if "nc" not in _NC_CACHE:
        _NC_CACHE["nc"] = build_program()
    nc = _NC_CACHE["nc"]
    in_maps = make_in_maps(inp, input_names(nc))
    res = run_bass_kernel_spmd(nc, in_maps, core_ids=list(range(8)))
    R = res.results
    f = np.float32
    y_prompt = np.zeros((2, SEQ, D), f)
    y_sample = np.zeros((16, DEC_SEQ, D), f)
    k_prompt = np.zeros((1, 2, SEQ, NKV, HD), f)
    v_prompt = np.zeros((1, 2, SEQ, NKV, HD), f)
    ik_prompt = np.zeros((1, 2, SEQ, HD), f)
    lru_prompt = np.zeros((1, 2, D), f)
    conv_prompt = np.zeros((1, 2, 3, D), f)
    k_sample = np.zeros((1, 16, DEC_SEQ, NKV, HD), f)
    v_sample = np.zeros((1, 16, DEC_SEQ, NKV, HD), f)
    ik_sample = np.zeros((1, 16, DEC_SEQ, HD), f)
    lru_sample = np.zeros((1, 16, D), f)
    conv_sample = np.zeros((1, 16, 3, D), f)
    for c in range(8):
        p, r = c // 4, c % 4
        o = R[c]
        y = o["y_own"]
        for i in range(8):
            b = 4 * i + r
            y_prompt[p, b * 128:(b + 1) * 128] = y[i * 128:(i + 1) * 128]
        for q in range(2):
            sq = 2 * c + q
            y_sample[sq] = y[1024 + q * 32:1024 + (q + 1) * 32]
            k_sample[0, sq] = o["o_k"][SEQ + q * 32:SEQ + (q + 1) * 32].reshape(32, NKV, HD)
            v_sample[0, sq] = o["o_v"][SEQ + q * 32:SEQ + (q + 1) * 32].reshape(32, NKV, HD)
            ik_sample[0, sq] = o["o_ik"][SEQ + q * 32:SEQ + (q + 1) * 32]
            lru_sample[0, sq] = o["o_lru"][1 + q]
            conv_sample[0, sq] = o["o_conv"][1 + q]
        if r == 0:
            k_prompt[0, p] = o["o_k"][:SEQ].reshape(SEQ, NKV, HD)
            v_prompt[0, p] = o["o_v"][:SEQ].reshape(SEQ, NKV, HD)
            ik_prompt[0, p] = o["o_ik"][:SEQ]
            lru_prompt[0, p] = o["o_lru"][0]
            conv_prompt[0, p] = o["o_conv"][0]
    return (y_prompt, y_sample, k_prompt, v_prompt, ik_prompt, lru_prompt, conv_prompt,
            k_sample, v_sample, ik_sample, lru_sample, conv_sample)
```

```python
import math
from contextlib import ExitStack

import numpy as np
import concourse.bass as bass
import concourse.mybir as mybir
from concourse.bass_utils import run_bass_kernel_spmd

F32 = mybir.dt.float32
BF16 = mybir.dt.bfloat16
AF = mybir.ActivationFunctionType
ALU = mybir.AluOpType
AX = mybir.AxisListType

D = 4096
SEQ = 4096
NB = 32
DEC_SEQ = 32
PAST = 2048
SS = PAST + DEC_SEQ
NH = 32
NKV = 8
HD = 128
DFF = 11008
KC = 32
TO = 1088
EPS = 1e-6
C_Q, C_K, C_V, C_IQ, C_IW, C_IK, C_XL, C_YL, C_GA, C_GL = (
    0, 4096, 5120, 6144, 10240, 10272, 10400, 14496, 18592, 22688)
IN_COLS = 26784
NEGM = -1.0e30


class Op:
    __slots__ = ("eng", "fn", "deps", "dma", "signal", "pos", "sem", "semval", "K", "waits",
                 "barrier")

    def __init__(self, eng, fn, dma=False, barrier=False):
        self.eng = eng
        self.fn = fn
        self.deps = ()
        self.dma = dma
        self.signal = False
        self.pos = 0
        self.sem = None
        self.semval = 0
        self.K = None
        self.waits = ()
        self.barrier = barrier


class Sched:
    CE = ("pe", "act", "dve", "pool", "sp")
    NDSEM = 10

    def __init__(self):
        self.ops = []
        self.last_w = {}
        self.readers = {}
        self.last_op = {e: None for e in self.CE}

    def add(self, eng, fn, r=(), w=(), dma=False):
        idx = len(self.ops)
        op = Op(eng, fn, dma=dma)
        raw = set()
        oth = set()
        for t in r:
            lw = self.last_w.get(t)
            if lw is not None:
                raw.add(lw)
        for t in w:
            lw = self.last_w.get(t)
            if lw is not None:
                oth.add(lw)
            rs = self.readers.get(t)
            if rs:
                oth.update(rs)
        deps = set()
        for d in raw | oth:
            dop = self.ops[d]
            if (not dop.dma) and (not dma) and dop.eng == eng:
                if eng == "pe":
                    continue
                if d not in raw:
                    continue
            deps.add(d)
            if not dop.dma:
                dop.signal = True
        op.deps = tuple(sorted(deps))
        for t in r:
            self.readers.setdefault(t, []).append(idx)
        for t in w:
            self.last_w[t] = idx
            self.readers[t] = []
        self.ops.append(op)
        if not dma:
            self.last_op[eng] = idx
        return idx

    def barrier(self):
        lasts = dict(self.last_op)
        for e in self.CE:
            op = Op(e, None, barrier=True)
            deps = set()
            for e2, li in lasts.items():
                if li is not None and e2 != e:
                    deps.add(li)
                    self.ops[li].signal = True
            op.deps = tuple(sorted(deps))
            self.ops.append(op)
        self.last_w = {}
        self.readers = {}

    def analyze(self):
        CE = self.CE
        K = {e: {} for e in CE}
        pos = {e: 0 for e in CE}
        sig = {e: 0 for e in CE}
        sigcount = {e: {} for e in CE}
        dq = {e: {"next": 0, "cum": [0] * self.NDSEM, "last": [None] * self.NDSEM} for e in CE}
        for op in self.ops:
            E = op.eng
            KE = K[E]
            waits = {}

            def need(key, val, kafter):
                if KE.get(key, 0) >= val:
                    return
                if waits.get(key, 0) < val:
                    waits[key] = val
                if kafter:
                    for k2, v2 in kafter.items():
                        if KE.get(k2, 0) < v2:
                            KE[k2] = v2
                if KE.get(key, 0) < val:
                    KE[key] = val

            for d in op.deps:
                dop = self.ops[d]
                if dop.dma:
                    need(("S", dop.eng, dop.sem), dop.semval, dop.K)
                else:
                    need(dop.eng, dop.pos, dop.K)
            if op.barrier:
                for q in CE:
                    for s in range(self.NDSEM):
                        if dq[q]["cum"][s] > 0:
                            lo = dq[q]["last"][s]
                            need(("S", q, s), dq[q]["cum"][s], lo.K if lo else None)
            if op.dma:
                q = dq[E]
                s = q["next"]
                q["next"] = (s + 1) % self.NDSEM
                if q["last"][s] is not None:
                    need(("S", E, s), q["cum"][s], q["last"][s].K)
                q["cum"][s] += 16
                op.sem = s
                op.semval = q["cum"][s]
                q["last"][s] = op
                op.K = dict(KE)
            elif not op.barrier:
                pos[E] += 1
                op.pos = pos[E]
                if op.signal:
                    sig[E] += 1
                    sigcount[E][op.pos] = sig[E]
                    kk = dict(KE)
                    kk[E] = op.pos
                    op.K = kk
            op.waits = tuple(waits.items())
        self.sigcount = sigcount
        self.final_dma = {e: list(dq[e]["cum"]) for e in CE}

    def emit(self, nc, block, esem, dsem):
        self.analyze()
        per = {e: [] for e in self.CE}
        for op in self.ops:
            per[op.eng].append(op)
        sigcount = self.sigcount

        def run(e, eng):
            for op in per[e]:
                for key, val in op.waits:
                    if isinstance(key, tuple):
                        eng.wait_ge(dsem[key[1]][key[2]], val)
                    else:
                        eng.wait_ge(esem[key], sigcount[key][val])
                if op.fn is None:
                    continue
                ins = op.fn(eng)
                if op.dma:
                    ins.then_inc(dsem[e][op.sem], 16)
                elif op.signal:
                    ins.then_inc(esem[e], 1)
            if e == "sp":
                for q in self.CE:
                    for s, v in enumerate(self.final_dma[q]):
                        if v > 0:
                            eng.wait_ge(dsem[q][s], v)

        @block.tensor
        def _(eng):
            run("pe", eng)

        @block.scalar
        def _(eng):
            run("act", eng)

        @block.vector
        def _(eng):
            run("dve", eng)

        @block.gpsimd
        def _(eng):
            run("pool", eng)

        @block.sync
        def _(eng):
            run("sp", eng)


class Arena:
    def __init__(self, t, words):
        self.t = t
        self.words = words
        self.off = 0

    def alloc(self, free_shape, dtype, parts=128):
        n = 1
        for s in free_shape:
            n *= s
        nbytes = n * (2 if dtype == BF16 else 4)
        w = (nbytes + 31) // 32 * 8
        off = self.off
        assert off + w <= self.words, f"arena overflow {off + w} > {self.words}"
        self.off += w
        ap = self.t[0:parts, off:off + (nbytes + 3) // 4]
        if dtype == BF16:
            ap = ap.bitcast(BF16)
            if ap.shape[1] != n:
                ap = ap[:, 0:n]
        if len(free_shape) == 2:
            ap = ap.rearrange("p (a b) -> p a b", a=free_shape[0])
        elif len(free_shape) == 3:
            ap = ap.rearrange("p (a b c) -> p a b c", a=free_shape[0], b=free_shape[1])
        return ap

    def mark(self):
        return self.off

    def release(self, m):
        self.off = m


class Builder:
    def __init__(self, debug=False, phases=None, feed=None):
        self.debug = debug
        self.feed = feed or set()
        self.phases = phases
        self.nc = bass.Bass("TRN2", target_bir_lowering=False)
        self.S = Sched()
        self.uid = 0

    def fresh(self, p):
        self.uid += 1
        return (p, self.uid)

    def din(self, name, shape, dt=F32):
        return self.nc.dram_tensor(name, list(shape), dt, kind="ExternalInput").ap()

    def dout(self, name, shape, dt=F32):
        return self.nc.dram_tensor(name, list(shape), dt, kind="ExternalOutput").ap()

    def dscr(self, name, shape, dt=F32):
        kind = "ExternalOutput" if (self.debug and name in self.debug) else "Internal"
        if name in self.feed:
            kind = "ExternalInput"
        return self.nc.dram_tensor(name, list(shape), dt, kind=kind).ap()

    def dma(self, q, out, in_, r=(), w=()):
        self.S.add(q, lambda e: e.dma_start(out=out, in_=in_), r=r, w=w, dma=True)

    def act(self, out, in_, func, r=(), w=(), **kw):
        self.S.add("act", lambda e: e.activation(out=out, in_=in_, func=func, **kw), r=r, w=w)

    def tt(self, out, in0, in1, op, r=(), w=(), eng="dve"):
        self.S.add(eng, lambda e: e.tensor_tensor(out=out, in0=in0, in1=in1, op=op), r=r, w=w)

    def ts(self, out, in0, s1, s2, op0, op1=None, r=(), w=(), eng="dve", **kw):
        if op1 is None:
            self.S.add(eng, lambda e: e.tensor_scalar(out=out, in0=in0, scalar1=s1, scalar2=None,
                                                      op0=op0, **kw), r=r, w=w)
        else:
            self.S.add(eng, lambda e: e.tensor_scalar(out=out, in0=in0, scalar1=s1, scalar2=s2,
                                                      op0=op0, op1=op1, **kw), r=r, w=w)

    def stt(self, out, in0, scalar, in1, op0, op1, r=(), w=()):
        self.S.add("dve", lambda e: e.scalar_tensor_tensor(out=out, in0=in0, scalar=scalar, in1=in1,
                                                           op0=op0, op1=op1), r=r, w=w)

    def copy(self, eng, out, in_, r=(), w=()):
        if eng == "act":
            self.S.add("act", lambda e: e.activation(out=out, in_=in_, func=AF.Copy), r=r, w=w)
        else:
            self.S.add(eng, lambda e: e.tensor_copy(out=out, in_=in_), r=r, w=w)

    def mm_group(self, out, pairs, r=(), w=()):
        n = len(pairs)

        def fn(e):
            ins = None
            for i, (l, rr) in enumerate(pairs):
                ins = e.matmul(out, l, rr, start=(i == 0), stop=(i == n - 1))
            return ins
        self.S.add("pe", fn, r=r, w=w)

    def mm(self, out, lhsT, rhs, start, stop, r=(), w=()):
        self.S.add("pe", lambda e: e.matmul(out, lhsT, rhs, start=start, stop=stop), r=r, w=w)

    def transpose(self, out, in_, ident, r=(), w=()):
        self.S.add("pe", lambda e: e.transpose(out, in_, ident), r=r, w=w)


def build_program(debug=None, phases=None, feed=None):
    B = Builder(debug=debug, phases=phases, feed=feed)
    nc = B.nc
    S = B.S
    ALL = {"seq", "cache", "lru", "own", "attn", "mix", "ffn"}
    ph = set(phases) if phases is not None else ALL
    full = ph == ALL

    _in = {}

    def inp(name, shape):
        if name not in _in:
            _in[name] = B.din(name, shape)
        return _in[name]

    KT_scr = B.dscr("KT_scr", [NKV, 128, SEQ], BF16)
    V_scr = B.dscr("V_scr", [SEQ, NKV * HD], BF16)
    ikT_scr = B.dscr("ikT_scr", [128, SEQ], BF16)
    KT_s = B.dscr("KT_s", [2, NKV, 128, SS], BF16)
    V_s = B.dscr("V_s", [2, SS, NKV * HD], BF16)
    ikT_s = B.dscr("ikT_s", [2, 128, SS], BF16)
    xlT_scr = B.dscr("xlT_scr", [KC, 128, SEQ + 64], F32)
    hown_scr = B.dscr("hown_scr", [KC, 128, TO], F32)
    QT_scr = B.dscr("QT_scr", [NH, 128, TO], BF16)
    iqT_scr = B.dscr("iqT_scr", [NH, 128, TO], BF16)
    iw_scr = B.dscr("iw_scr", [TO, 32], F32)
    olruT_scr = B.dscr("olruT_scr", [KC, 128, TO], BF16)
    saT_scr = B.dscr("saT_scr", [KC, 128, TO], F32)
    slT_scr = B.dscr("slT_scr", [KC, 128, TO], F32)
    oattT_scr = B.dscr("oattT_scr", [NH, 128, TO], BF16)
    mixT_scr = B.dscr("mixT_scr", [KC, 128, TO], BF16)
    x1_scr = B.dscr("x1_scr", [TO, D], F32)
    hT_scr = B.dscr("hT_scr", [DFF // 128, 128, TO], BF16)

    with ExitStack() as es:
        ARENA_KB = 206
        arena_t = es.enter_context(nc.sbuf_tensor("arena", [128, ARENA_KB * 256], F32))
        A = Arena(arena_t, ARENA_KB * 256)
        psum = [es.enter_context(nc.psum_tensor(f"ps{i}", [128, 512], F32)) for i in range(8)]
        esem = {e: es.enter_context(nc.semaphore(f"e_{e}")) for e in Sched.CE}
        dsem = {e: [es.enter_context(nc.semaphore(f"d_{e}{i}")) for i in range(Sched.NDSEM)]
                for e in ("act", "pool", "sp")}
        dsem["pe"] = dsem["sp"]
        dsem["dve"] = dsem["sp"]

        ident_in = inp("ident", [128, 128])
        ident_b = A.alloc([128], BF16)
        ident_f = A.alloc([128], F32)
        B.dma("pool", ident_b, ident_in, w=["ident_b"])
        B.dma("sp", ident_f, ident_in, w=["ident_f"])
        eps_t = A.alloc([1], F32)
        S.add("dve", lambda e: e.memset(eps_t, EPS), w=["eps"])
        one_t = A.alloc([1], F32)
        S.add("dve", lambda e: e.memset(one_t, 1.0), w=["one"])
        g4 = {}

        def load_g4(nm):
            src = inp("norm_" + nm, [HD])
            t = A.alloc([4, 128], F32)
            for j in range(4):
                B.dma("sp", t[:, j, :], src.partition_broadcast(128), w=[("g4", nm, j)])
            g4[nm] = t

        if "seq" in ph:
            load_g4("k")
            load_g4("idx_k")
        if "own" in ph:
            load_g4("q")
        S.barrier()

        psn = [0]

        def next_ps():
            pb = psn[0] % 4
            psn[0] += 1
            return pb

        def norm_transpose(blocks, xnT, gT, xt, junk, ssb, extra=None):
            t0 = 0
            for bi, (src, rows) in enumerate(blocks):
                B.dma("sp", xt[0:rows], src, w=["xt"])
                if extra is not None:
                    extra(bi, rows)
                ss = ssb[bi % 2]
                sk = ("ssb", bi % 2)
                S.add("act", lambda e, rows=rows, ss=ss: e.activation(
                    out=junk[0:rows], in_=xt[0:rows], func=AF.Square, accum_out=ss[0:rows]),
                    r=["xt"], w=["junk", sk])
                B.act(ss[0:rows], ss[0:rows], AF.Sqrt, r=[sk, "eps"], w=[sk], scale=1.0 / D,
                      bias=eps_t[0:rows])
                S.add("dve", lambda e, ss=ss, rows=rows: e.reciprocal(out=ss[0:rows], in_=ss[0:rows]),
                      r=[sk], w=[sk])
                B.ts(xt[0:rows], xt[0:rows], ss[0:rows], None, ALU.mult, r=["xt", sk], w=["xt"])
                for j in range(8):
                    pb = 4 + (j % 2)
                    pv = psum[pb][:, :].rearrange("p (a b) -> p a b", a=4)
                    for q in range(4):
                        kc = j * 4 + q
                        B.transpose(pv[:, q, 0:rows], xt[0:rows, kc * 128:(kc + 1) * 128],
                                    ident_f[0:rows, 0:rows], r=["xt", "ident_f"], w=[("ps", pb)])
                    B.tt(xnT[:, j * 4:(j + 1) * 4, t0:t0 + rows], pv[:, :, 0:rows],
                         gT[:, j * 4:(j + 1) * 4].unsqueeze(2).to_broadcast([128, 4, rows]),
                         ALU.mult, r=[("ps", pb), "gT"], w=[("AT", bi)])
                t0 += rows

        wl = {}

        def load_w(wbufs, src, kcn, ncols):
            wid = id(wbufs)
            slot = wl.get(wid, 0) % len(wbufs)
            wl[wid] = wl.get(wid, 0) + 1
            slot = (wid, slot)
            v = src.rearrange("(kc p) n -> p kc n", p=128)
            step = 8 if ncols > 256 else 16
            for k0 in range(0, kcn, step):
                k1 = min(kcn, k0 + step)
                B.dma("pool", wbufs[slot[1]][:, k0:k1, 0:ncols], v[:, k0:k1, :], w=[("w", slot, k0 // 8)] +
                      ([("w", slot, k0 // 8 + 1)] if step == 16 else []))
            return slot

        def wtoks(slot, kcn):
            return [("w", slot, k) for k in range((kcn + 7) // 8)]

        def gemm_tok(AT, blocks, kcn, wbufs, wsrc, col_groups, evac, attoks=None):
            nxt = load_w(wbufs, wsrc(col_groups[0][0], col_groups[0][1]), kcn, col_groups[0][1])
            for gi, (c0, ncols, tag) in enumerate(col_groups):
                slot = nxt
                if gi + 1 < len(col_groups):
                    nxt = load_w(wbufs, wsrc(col_groups[gi + 1][0], col_groups[gi + 1][1]), kcn, col_groups[gi + 1][1])
                t0 = 0
                for bi, rows in enumerate(blocks):
                    pb = next_ps()
                    ps = psum[pb][0:rows, 0:ncols]
                    B.mm_group(ps, [(AT[:, kc, t0:t0 + rows], wbufs[slot[1]][:, kc, 0:ncols])
                                    for kc in range(kcn)],
                               r=(attoks if attoks is not None else [("AT", bi)]) + wtoks(slot, kcn),
                               w=[("ps", pb)])
                    evac(tag, bi, rows, t0, ps, ("ps", pb), ncols)
                    t0 += rows

        def gemm_feat(srcs, nblk, kcn, ncols_total, cgw, ttiles, evac, attoks=None):
            nxts = [load_w(wb, wsrc(0, cgw), kcn, cgw) for (AT, wb, wsrc) in srcs]
            for c0 in range(0, ncols_total, cgw):
                slots = nxts
                if c0 + cgw < ncols_total:
                    nxts = [load_w(wb, wsrc(c0 + cgw, cgw), kcn, cgw) for (AT, wb, wsrc) in srcs]
                for ch in range(cgw // 128):
                    chunk = (c0 // 128) + ch
                    for (tt0, n) in ttiles:
                        pss = []
                        for (AT, wb, wsrc), slot in zip(srcs, slots):
                            pb = next_ps()
                            ps = psum[pb][:, 0:n]
                            B.mm_group(ps, [(wb[slot[1]][:, kc, ch * 128:(ch + 1) * 128], AT[:, kc, tt0:tt0 + n])
                                            for kc in range(kcn)],
                                       r=(attoks if attoks is not None else [("AT", bi) for bi in range(nblk)])
                                       + wtoks(slot, kcn), w=[("ps", pb)])
                            pss.append((ps, ("ps", pb)))
                        evac(chunk, tt0, n, pss)

        stc = [0]

        def make_headproc(NST):
            st = dict(
                f=[A.alloc([512], F32) for _ in range(NST)],
                t=[A.alloc([512], F32) for _ in range(NST)],
                b=[A.alloc([512], BF16) for _ in range(NST)],
                T=[A.alloc([4, 128], BF16) for _ in range(NST)],
                s=[A.alloc([4], F32) for _ in range(NST)],
                r=[A.alloc([4, 16], F32) for _ in range(4 * NST)],
                n=NST)
            return st

        def headproc(st, ps, pstok, rows, ncols, gname, cs, cstok, dst_out, dstT_fn, rope=True):
            nh = ncols // 128
            i = stc[0] % st["n"]
            stc[0] += 1
            f, t, b_, T_, s_ = st["f"][i], st["t"][i], st["b"][i], st["T"][i], st["s"][i]
            r4 = st["r"][4 * i:4 * i + 4]
            tk = ("st", i)
            B.copy("act", f[0:rows, 0:ncols], ps, r=[pstok], w=[tk])
            fv = f[0:rows, 0:ncols].rearrange("p (h d) -> p h d", h=nh)
            tv = t[0:rows, 0:ncols].rearrange("p (h d) -> p h d", h=nh)
            if gname is not None:
                for h in range(nh):
                    S.add("act", lambda e, h=h: e.activation(out=t[0:rows, h * 128:(h + 1) * 128],
                                                            in_=f[0:rows, h * 128:(h + 1) * 128], func=AF.Square,
                                                            accum_out=s_[0:rows, h:h + 1]), r=[tk], w=[tk])
                B.act(s_[0:rows, 0:nh], s_[0:rows, 0:nh], AF.Sqrt, r=[tk, "eps"], w=[tk], scale=1.0 / 128,
                      bias=eps_t[0:rows])
                S.add("dve", lambda e: e.reciprocal(out=s_[0:rows, 0:nh], in_=s_[0:rows, 0:nh]), r=[tk], w=[tk])
                B.tt(fv, fv, s_[0:rows, 0:nh].unsqueeze(2).to_broadcast([rows, nh, 128]), ALU.mult,
                     r=[tk], w=[tk])
                B.tt(fv, fv, g4[gname][0:rows, 0:nh, :], ALU.mult,
                     r=[tk] + [("g4", gname, j) for j in range(4)], w=[tk])
            if rope:
                cb = cs[0:rows, 0:16].unsqueeze(1).to_broadcast([rows, nh, 16])
                sb = cs[0:rows, 16:32].unsqueeze(1).to_broadcast([rows, nh, 16])
                x1 = fv[:, :, 0:16]
                x2 = fv[:, :, 16:32]
                ra, rb, rc, rd = [q[0:rows, 0:nh, :] for q in r4]
                rt = [tk, cstok]
                tkp = ("stp", i)
                tkd = ("std", i)
                B.tt(ra, x1, cb, ALU.mult, r=rt, w=[tkp], eng="pool")
                B.tt(rb, x2, sb, ALU.mult, r=rt, w=[tkp], eng="pool")
                B.tt(rc, x2, cb, ALU.mult, r=rt, w=[tkd])
                B.tt(rd, x1, sb, ALU.mult, r=rt, w=[tkd])
                B.tt(x1, ra, rb, ALU.subtract, r=[tkp, tkd], w=[tk], eng="pool")
                B.tt(x2, rc, rd, ALU.add, r=[tkd, tkp], w=[tk])
            if dst_out is not None:
                B.dma("sp", dst_out, f[0:rows, 0:ncols], r=[tk], w=[B.fresh("o")])
            if dstT_fn is not None:
                B.copy("act", b_[0:rows, 0:ncols], f[0:rows, 0:ncols], r=[tk], w=[tk])
                pb = 6 + (stc[0] % 2)
                pT = psum[pb][:, :].bitcast(BF16)[:, 0:512].rearrange("p (a b) -> p a b", a=4)
                for h in range(nh):
                    B.transpose(pT[:, h, 0:rows], b_[0:rows, h * 128:(h + 1) * 128],
                                ident_b[0:rows, 0:rows], r=[tk, "ident_b"], w=[("ps", pb)])
                B.copy("dve", T_[:, 0:nh, 0:rows], pT[:, 0:nh, 0:rows], r=[("ps", pb)], w=[tk])
                dstT_fn(T_, nh, rows, tk)

        if "seq" in ph:
            xseq = inp("xseq", [SEQ, D])
            xown = inp("xown", [TO, D])
            w_in = inp("w_in", [D, IN_COLS])
            cs_seq = inp("cs_seq", [SEQ + 64, 32])
            o_k = B.dout("o_k", [SEQ + 64, NKV * HD])
            o_v = B.dout("o_v", [SEQ + 64, NKV * HD])
            o_ik = B.dout("o_ik", [SEQ + 64, HD])
            m0 = A.mark()
            gmixT = A.alloc([KC], F32)
            B.dma("sp", gmixT, inp("norm_mixT", [128, KC]), w=["gT"])
            xnT = A.alloc([KC, TO], BF16)
            wbufs = [A.alloc([KC, 512], BF16) for _ in range(2)]
            xt = A.alloc([D], F32)
            junk = A.alloc([D], BF16)
            ssb = [A.alloc([1], F32) for _ in range(2)]
            st = make_headproc(3)
            cst = [A.alloc([32], F32) for _ in range(9)]
            xl_st = [A.alloc([TO], F32) for _ in range(2)]

            for tt in range(4):
                blocks = [(xseq[tt * 1024 + j * 128: tt * 1024 + (j + 1) * 128, :], 128) for j in range(8)]
                orows = [tt * 1024 + j * 128 for j in range(8)]
                if tt == 3:
                    blocks.append((xown[1024:1088, :], 64))
                    orows.append(SEQ)
                ntok = sum(b[1] for b in blocks)

                def extra(bi, rows, orows=orows):
                    B.dma("sp", cst[bi][0:rows], cs_seq[orows[bi]:orows[bi] + rows, :], w=[("cst", bi)])
                norm_transpose(blocks, xnT, gmixT, xt, junk, ssb, extra)

                def evac(tag, bi, rows, t0, ps, pstok, ncols, orows=orows):
                    kind, half = tag
                    orow = orows[bi]
                    is_s = orow >= SEQ
                    if kind == "k":
                        def dstT(T_, nh, rows_, tk):
                            if not is_s:
                                B.dma("act", KT_scr[half * 4:half * 4 + 4, :, orow:orow + rows_]
                                      .rearrange("h d t -> d h t"), T_[:, 0:4, 0:rows_], r=[tk], w=[B.fresh("kt")])
                            else:
                                for sq in range(2):
                                    B.dma("act", KT_s[sq, half * 4:half * 4 + 4, :, PAST:SS]
                                          .rearrange("h d t -> d h t"), T_[:, 0:4, sq * 32:(sq + 1) * 32],
                                          r=[tk], w=[B.fresh("kt")])
                        headproc(st, ps, pstok, rows, ncols, "k", cst[bi], ("cst", bi),
                                 o_k[orow:orow + rows, half * 512:(half + 1) * 512], dstT)
                    elif kind == "ik":
                        def dstT(T_, nh, rows_, tk):
                            if not is_s:
                                B.dma("act", ikT_scr[:, orow:orow + rows_], T_[:, 0, 0:rows_], r=[tk],
                                      w=[B.fresh("ikt")])
                            else:
                                for sq in range(2):
                                    B.dma("act", ikT_s[sq, :, PAST:SS], T_[:, 0, sq * 32:(sq + 1) * 32],
                                          r=[tk], w=[B.fresh("ikt")])
                        headproc(st, ps, pstok, rows, ncols, "idx_k", cst[bi], ("cst", bi),
                                 o_ik[orow:orow + rows, :], dstT)
                    else:
                        i = stc[0] % st["n"]
                        stc[0] += 1
                        tk = ("st", i)
                        B.copy("act", st["f"][i][0:rows, 0:512], ps, r=[pstok], w=[tk])
                        B.copy("dve", st["b"][i][0:rows, 0:512], ps, r=[pstok], w=[tk])
                        B.dma("sp", o_v[orow:orow + rows, half * 512:(half + 1) * 512],
                              st["f"][i][0:rows, 0:512], r=[tk], w=[B.fresh("o")])
                        if not is_s:
                            B.dma("act", V_scr[orow:orow + rows, half * 512:(half + 1) * 512],
                                  st["b"][i][0:rows, 0:512], r=[tk], w=[B.fresh("v")])
                        else:
                            for sq in range(2):
                                B.dma("act", V_s[sq, PAST:SS, half * 512:(half + 1) * 512],
                                      st["b"][i][sq * 32:(sq + 1) * 32, 0:512], r=[tk], w=[B.fresh("v")])

                gemm_tok(xnT, [b[1] for b in blocks], KC, wbufs, lambda c0, n: w_in[:, c0:c0 + n],
                         [(C_K, 512, ("k", 0)), (C_K + 512, 512, ("k", 1)), (C_V, 512, ("v", 0)),
                          (C_V + 512, 512, ("v", 1)), (C_IK, 128, ("ik", 0))], evac)

                ttiles = [(0, 512), (512, 512)] + ([(1024, 64)] if tt == 3 else [])
                col0 = tt * 1024

                def evac_xl(chunk, tt0, n, pss, ntok=ntok, col0=col0, last=ttiles[-1][0]):
                    xs = xl_st[chunk % 2]
                    ps, pstok = pss[0]
                    B.copy("act" if (tt0 // 512) % 2 else "dve", xs[:, tt0:tt0 + n], ps, r=[pstok],
                           w=[("xls", chunk % 2)])
                    if tt0 == last:
                        B.dma("sp", xlT_scr[chunk, :, col0:col0 + ntok], xs[:, 0:ntok],
                              r=[("xls", chunk % 2)], w=[B.fresh("xl")])
                gemm_feat([(xnT, wbufs, lambda c0, n: w_in[:, C_XL + c0:C_XL + c0 + n])], len(blocks), KC,
                          4096, 512, ttiles, evac_xl)
            S.barrier()
            A.release(m0)

        if "cache" in ph:
            cache_k = inp("cache_k", [2, PAST, NKV * HD])
            cache_v = inp("cache_v", [2, PAST, NKV * HD])
            cache_ik = inp("cache_ik", [2, PAST, HD])
            m0 = A.mark()
            kb = A.alloc([16, 1024], BF16)
            vb = A.alloc([16, 1024], BF16)
            ib = A.alloc([16, 128], BF16)
            stg = [A.alloc([8, 128], BF16) for _ in range(2)]
            istg = A.alloc([16, 128], BF16)
            for sq in range(2):
                for q in range(4):
                    B.dma("pool", kb[:, q * 4:(q + 1) * 4, :],
                          cache_k[sq, q * 512:(q + 1) * 512, :].rearrange("(b p) c -> p b c", p=128), w=[("kb", q)])
                    B.dma("pool", vb[:, q * 4:(q + 1) * 4, :],
                          cache_v[sq, q * 512:(q + 1) * 512, :].rearrange("(b p) c -> p b c", p=128), w=[("vb", q)])
                B.dma("pool", ib, cache_ik[sq].rearrange("(b p) c -> p b c", p=128), w=["ib"])
                for q in range(4):
                    B.dma("act", V_s[sq, q * 512:(q + 1) * 512, :].rearrange("(b p) c -> p b c", p=128),
                          vb[:, q * 4:(q + 1) * 4, :], r=[("vb", q)], w=[B.fresh("vs")])
                for blk in range(16):
                    pb = 6 + (blk % 2)
                    pT = psum[pb][:, :].bitcast(BF16).rearrange("p (a b) -> p a b", a=8)
                    for h in range(8):
                        B.transpose(pT[:, h, :], kb[:, blk, h * 128:(h + 1) * 128], ident_b,
                                    r=[("kb", blk // 4), "ident_b"], w=[("ps", pb)])
                    sg = stg[blk % 2]
                    B.copy("dve" if blk % 2 else "act", sg, pT, r=[("ps", pb)], w=[("stg", blk % 2)])
                    B.dma("sp", KT_s[sq, :, :, blk * 128:(blk + 1) * 128].rearrange("h d t -> d h t"), sg,
                          r=[("stg", blk % 2)], w=[B.fresh("kts")])
                for half in range(2):
                    pb = 4 + half
                    pT = psum[pb][:, :].bitcast(BF16).rearrange("p (a b) -> p a b", a=8)
                    for j in range(8):
                        B.transpose(pT[:, j, :], ib[:, half * 8 + j, :], ident_b, r=["ib", "ident_b"],
                                    w=[("ps", pb)])
                    B.copy("dve", istg[:, half * 8:(half + 1) * 8, :], pT, r=[("ps", pb)], w=["istg"])
                B.dma("sp", ikT_s[sq, :, 0:PAST], istg, r=["istg"], w=[B.fresh("ikts")])
            S.barrier()
            A.release(m0)

        if "lru" in ph:
            convwT = inp("convwT", [128, 4, KC])
            convbT = inp("convbT", [128, KC])
            baT = inp("lru_baT", [128, KC])
            bxT = inp("lru_bxT", [128, KC])
            lamT = inp("lru_lamT", [128, KC])
            lru_wa = inp("lru_wa", [16, 256, 256])
            lru_wx = inp("lru_wx", [16, 256, 256])
            st_lruT = inp("state_lruT", [2, 128, KC])
            st_convT = inp("state_convT", [2, 128, KC, 3])
            selr = inp("selr", [128, 4])
            o_lru = B.dout("o_lru", [3, D])
            o_conv = B.dout("o_conv", [3, 3, D])
            m0 = A.mark()
            cw = A.alloc([4, KC], F32)
            cb = A.alloc([KC], F32)
            ba = A.alloc([KC], F32)
            bx = A.alloc([KC], F32)
            lam = A.alloc([KC], F32)
            cneg = A.alloc([KC], F32)
            cneg2 = A.alloc([KC], F32)
            tmpc = A.alloc([KC], F32)
            sel = A.alloc([4], F32)
            h0s = A.alloc([2, KC], F32)
            c0s = A.alloc([2, KC, 3], F32)
            hfin = A.alloc([3, KC], F32)
            cfin = A.alloc([3, 3, KC], F32)
            B.dma("sp", cw, convwT, w=["lc"])
            B.dma("sp", cb, convbT, w=["lc"])
            B.dma("sp", ba, baT, w=["lc"])
            B.dma("sp", bx, bxT, w=["lc"])
            B.dma("sp", lam, lamT, w=["lam"])
            B.dma("sp", sel, selr, w=["lc"])
            for sq in range(2):
                B.dma("sp", h0s[:, sq, :], st_lruT[sq], w=["lc"])
                B.dma("sp", c0s[:, sq, :, :], st_convT[sq], w=["lc"])
            B.ts(tmpc, lam, -1.0, None, ALU.mult, r=["lam"], w=["tmpc"])
            B.tt(tmpc, tmpc, lam, ALU.max, r=["lam", "tmpc"], w=["tmpc"])
            B.act(tmpc, tmpc, AF.Exp, r=["tmpc"], w=["tmpc"], scale=-1.0)
            B.act(tmpc, tmpc, AF.Ln, r=["tmpc", "one"], w=["tmpc"], bias=one_t[:, 0:1])
            B.ts(cneg, lam, -1.0, 0.0, ALU.mult, ALU.max, r=["lam"], w=["cneg"])
            B.tt(cneg, cneg, tmpc, ALU.add, r=["cneg", "tmpc"], w=["cneg"])
            B.ts(cneg2, cneg, -16.0, None, ALU.mult, r=["cneg"], w=["cneg2"])
            B.ts(cneg, cneg, -8.0, None, ALU.mult, r=["cneg"], w=["cneg"])

            XL = [[A.alloc([1027], F32) for _ in range(2)] for _ in range(2)]
            U2 = [[A.alloc([1024], F32) for _ in range(2)] for _ in range(2)]
            UB2 = [[A.alloc([1024], BF16) for _ in range(2)] for _ in range(2)]
            Rg2 = [[A.alloc([1024], F32) for _ in range(2)] for _ in range(2)]
            Ig2 = [[A.alloc([1024], F32) for _ in range(2)] for _ in range(2)]
            Aa2 = [[A.alloc([1024], F32) for _ in range(2)] for _ in range(2)]
            Hh2 = [[A.alloc([1024], F32) for _ in range(2)] for _ in range(2)]
            HO2 = [[A.alloc([256], F32) for _ in range(2)] for _ in range(2)]
            hprev = A.alloc([2], F32)
            wab = [A.alloc([2, 256], BF16) for _ in range(2)]
            wxb = [A.alloc([2, 256], BF16) for _ in range(2)]
            pieces = [(tt * 1024, 1024, "p", tt) for tt in range(4)] + [(SEQ, 32, "s", 0), (SEQ + 32, 32, "s", 1)]
            for nblk in range(16):
                wsl = nblk % 2
                B.dma("pool", wab[wsl], lru_wa[nblk].rearrange("(ch p) d -> p ch d", p=128), w=[("wa", wsl)])
                B.dma("pool", wxb[wsl], lru_wx[nblk].rearrange("(ch p) d -> p ch d", p=128), w=[("wx", wsl)])
                for pi, (col0, n, kind, idx) in enumerate(pieces):
                    xb = XL[pi % 2]
                    xprev = XL[(pi + 1) % 2]
                    pp = pi % 2
                    U, UB, Rg, Ig, Aa, Hh, HO = U2[pp], UB2[pp], Rg2[pp], Ig2[pp], Aa2[pp], Hh2[pp], HO2[pp]
                    for ch in range(2):
                        chunk = 2 * nblk + ch
                        xk = ("xl", pi % 2, ch)
                        B.dma("sp", xb[ch][:, 3:3 + n], xlT_scr[chunk, :, col0:col0 + n], w=[xk])
                        if kind == "p" and idx == 0:
                            S.add("dve", lambda e, t=xb[ch]: e.memset(t[:, 0:3], 0.0), w=[xk])
                        elif kind == "p":
                            B.copy("dve", xb[ch][:, 0:3], xprev[ch][:, 1024:1027],
                                   r=[("xl", (pi + 1) % 2, ch)], w=[xk])
                        else:
                            B.copy("dve", xb[ch][:, 0:3], c0s[:, idx, chunk, :], r=["lc"], w=[xk])
                        u = U[ch]
                        uk = ("u", pp, ch)
                        B.ts(u[:, 0:n], xb[ch][:, 3:3 + n], cw[:, 3, chunk:chunk + 1], cb[:, chunk:chunk + 1],
                             ALU.mult, ALU.add, r=[xk, "lc"], w=[uk])
                        for j in (2, 1, 0):
                            B.stt(u[:, 0:n], xb[ch][:, j:j + n], cw[:, j, chunk:chunk + 1], u[:, 0:n],
                                  ALU.mult, ALU.add, r=[xk, "lc", uk], w=[uk])
                        B.copy("act", UB[ch][:, 0:n], u[:, 0:n], r=[uk], w=[("ub", pp, ch)])
                    halves = [(0, min(n, 512))] + ([(512, 512)] if n > 512 else [])
                    for dh in range(2):
                        chunk = 2 * nblk + dh
                        for (gbuf, wbuf_, bias, gk, wk) in ((Rg, wab, ba, "rg", "wa"), (Ig, wxb, bx, "ig", "wx")):
                            for (h0, hn) in halves:
                                pb = next_ps()
                                ps = psum[pb][:, 0:hn]
                                B.mm_group(ps, [(wbuf_[wsl][:, ch, dh * 128:(dh + 1) * 128], UB[ch][:, h0:h0 + hn])
                                                for ch in range(2)],
                                           r=[("ub", pp, 0), ("ub", pp, 1), (wk, wsl)], w=[("ps", pb)])
                                B.act(gbuf[dh][:, h0:h0 + hn], ps, AF.Sigmoid, r=[("ps", pb), "lc"],
                                      w=[(gk, pp, dh)], bias=bias[:, chunk:chunk + 1])
                    for dh in range(2):
                        chunk = 2 * nblk + dh
                        B.act(Aa[dh][:, 0:n], Rg[dh][:, 0:n], AF.Exp, r=[("rg", pp, dh), "cneg"], w=[("aa", pp, dh)],
                              scale=cneg[:, chunk:chunk + 1])
                        B.act(Rg[dh][:, 0:n], Rg[dh][:, 0:n], AF.Exp, r=[("rg", pp, dh), "cneg2"], w=[("rg", pp, dh)],
                              scale=cneg2[:, chunk:chunk + 1])
                    for dh in range(2):
                        B.ts(Rg[dh][:, 0:n], Rg[dh][:, 0:n], 1.0, -1.0, ALU.min, ALU.mult, r=[("rg", pp, dh)],
                             w=[("rg", pp, dh)])
                        B.act(Rg[dh][:, 0:n], Rg[dh][:, 0:n], AF.Sqrt, r=[("rg", pp, dh), "one"], w=[("rg", pp, dh)],
                              scale=1.0, bias=one_t[:, 0:1])
                    for dh in range(2):
                        chunk = 2 * nblk + dh
                        B.tt(Ig[dh][:, 0:n], Ig[dh][:, 0:n], U[dh][:, 0:n], ALU.mult, r=[("ig", pp, dh), ("u", pp, dh)],
                             w=[("ig", pp, dh)])
                        B.tt(Ig[dh][:, 0:n], Ig[dh][:, 0:n], Rg[dh][:, 0:n], ALU.mult, r=[("ig", pp, dh), ("rg", pp, dh)],
                             w=[("ig", pp, dh)])
                        if kind == "p" and idx == 0:
                            init = 0.0
                            ir = []
                        elif kind == "p":
                            init = hprev[:, dh:dh + 1]
                            ir = [("hp", dh)]
                        else:
                            init = h0s[:, idx, chunk:chunk + 1]
                            ir = ["lc"]
                        S.add("dve", lambda e, dh=dh, n=n, init=init, Hh=Hh, Aa=Aa, Ig=Ig: e.tensor_tensor_scan(
                            out=Hh[dh][:, 0:n], data0=Aa[dh][:, 0:n], data1=Ig[dh][:, 0:n], initial=init,
                            op0=ALU.mult, op1=ALU.add), r=[("aa", pp, dh), ("ig", pp, dh)] + ir, w=[("hh", pp, dh)])
                        if kind == "p" and idx < 3:
                            B.copy("dve", hprev[:, dh:dh + 1], Hh[dh][:, n - 1:n], r=[("hh", pp, dh)], w=[("hp", dh)])
                        if kind == "s" or idx == 3:
                            row = 0 if kind == "p" else 1 + idx
                            B.copy("dve", hfin[:, row, chunk:chunk + 1], Hh[dh][:, n - 1:n], r=[("hh", pp, dh)],
                                   w=["hfin"])
                            for j in range(3):
                                B.copy("dve", cfin[:, row, j, chunk:chunk + 1],
                                       XL[pi % 2][dh][:, n + j:n + j + 1], r=[("xl", pi % 2, dh)], w=["cfin"])
                        ho = HO[dh]
                        hk = ("ho", pp, dh)
                        if kind == "p":
                            for il in range(2):
                                B.ts(ho[:, il * 128:(il + 1) * 128], Hh[dh][:, (4 * il) * 128:(4 * il + 1) * 128],
                                     sel[:, 0:1], None, ALU.mult, r=[("hh", pp, dh), "lc"], w=[hk])
                                for j in range(1, 4):
                                    B.stt(ho[:, il * 128:(il + 1) * 128],
                                          Hh[dh][:, (4 * il + j) * 128:(4 * il + j + 1) * 128], sel[:, j:j + 1],
                                          ho[:, il * 128:(il + 1) * 128], ALU.mult, ALU.add,
                                          r=[("hh", pp, dh), "lc", hk], w=[hk])
                            B.dma("act", hown_scr[chunk, :, idx * 256:(idx + 1) * 256], ho, r=[hk], w=[B.fresh("ho")])
                        else:
                            B.dma("act", hown_scr[chunk, :, 1024 + idx * 32:1024 + (idx + 1) * 32], Hh[dh][:, 0:32],
                                  r=[("hh", pp, dh)], w=[B.fresh("ho")])
            fst = A.alloc([12, 128], F32)
            pbv = psum[4][:, :].rearrange("p (a b) -> p a b", a=4)
            pbv2 = psum[5][:, :].rearrange("p (a b) -> p a b", a=4)
            pbv3 = psum[6][:, :].rearrange("p (a b) -> p a b", a=4)
            for row in range(3):
                pv = (pbv, pbv2, pbv3)[row]
                pk = ("ps", 4 + row)
                B.transpose(pv[0:32, 0, :], hfin[:, row, :], ident_f, r=["hfin", "ident_f"], w=[pk])
                for j in range(3):
                    B.transpose(pv[0:32, 1 + j, :], cfin[:, row, j, :], ident_f, r=["cfin", "ident_f"], w=[pk])
                B.copy("dve", fst[0:32, row * 4:(row + 1) * 4, :], pv[0:32, :, :], r=[pk], w=["fst"])
                B.dma("sp", o_lru[row].rearrange("(kc p) -> kc p", p=128), fst[0:32, row * 4, :], r=["fst"],
                      w=[B.fresh("o")])
                for j in range(3):
                    B.dma("sp", o_conv[row, j].rearrange("(kc p) -> kc p", p=128), fst[0:32, row * 4 + 1 + j, :],
                          r=["fst"], w=[B.fresh("o")])
            S.barrier()
            A.release(m0)

        if "own" in ph:
            xown = inp("xown", [TO, D])
            w_in = inp("w_in", [D, IN_COLS])
            cs_own = inp("cs_own", [TO, 32])
            m0 = A.mark()
            gmixT = A.alloc([KC], F32)
            B.dma("sp", gmixT, inp("norm_mixT", [128, KC]), w=["gT"])
            xnT = A.alloc([KC, TO], BF16)
            wbufs = [A.alloc([KC, 512], BF16) for _ in range(2)]
            xt = A.alloc([D], F32)
            junk = A.alloc([D], BF16)
            ssb = [A.alloc([1], F32) for _ in range(2)]
            st = make_headproc(3)
            cst = [A.alloc([32], F32) for _ in range(9)]
            blocks = [(xown[j * 128:(j + 1) * 128, :], 128) for j in range(8)] + [(xown[1024:1088, :], 64)]
            brow = [b[1] for b in blocks]

            def extra(bi, rows):
                B.dma("sp", cst[bi][0:rows], cs_own[bi * 128:bi * 128 + rows, :], w=[("cst", bi)])
            norm_transpose(blocks, xnT, gmixT, xt, junk, ssb, extra)

            def evac(tag, bi, rows, t0, ps, pstok, ncols):
                kind, g = tag
                if kind == "q":
                    def dstT(T_, nh, rows_, tk):
                        B.dma("act", QT_scr[g * 4:g * 4 + 4, :, t0:t0 + rows_].rearrange("h d t -> d h t"),
                              T_[:, 0:4, 0:rows_], r=[tk], w=[B.fresh("qt")])
                    headproc(st, ps, pstok, rows, ncols, "q", cst[bi], ("cst", bi), None, dstT)
                elif kind == "iq":
                    def dstT(T_, nh, rows_, tk):
                        B.dma("act", iqT_scr[g * 4:g * 4 + 4, :, t0:t0 + rows_].rearrange("h d t -> d h t"),
                              T_[:, 0:4, 0:rows_], r=[tk], w=[B.fresh("iqt")])
                    headproc(st, ps, pstok, rows, ncols, None, cst[bi], ("cst", bi), None, dstT)
                else:
                    i = stc[0] % st["n"]
                    stc[0] += 1
                    tk = ("st", i)
                    B.copy("act", st["f"][i][0:rows, 0:32], ps, r=[pstok], w=[tk])
                    B.dma("sp", iw_scr[t0:t0 + rows, :], st["f"][i][0:rows, 0:32], r=[tk], w=[B.fresh("iw")])
            groups = [(C_Q + g * 512, 512, ("q", g)) for g in range(8)] + \
                     [(C_IQ + g * 512, 512, ("iq", g)) for g in range(8)] + [(C_IW, 32, ("iw", 0))]
            gemm_tok(xnT, brow, KC, wbufs, lambda c0, n: w_in[:, c0:c0 + n], groups, evac)

            ttiles = [(0, 512), (512, 512), (1024, 64)]
            NE = 2
            ey = [A.alloc([512], F32) for _ in range(NE)]
            et = [A.alloc([512], F32) for _ in range(NE)]
            eh = [A.alloc([512], F32) for _ in range(NE)]
            eb = [A.alloc([512], BF16) for _ in range(NE)]
            ec = [0]

            def evac_feat(which):
                def ev(chunk, tt0, n, pss):
                    ps, pstok = pss[0]
                    i = ec[0] % NE
                    ec[0] += 1
                    tk = ("e", i)
                    if which == "yl":
                        y, t, hh, ob = ey[i], et[i], eh[i], eb[i]
                        B.dma("sp", hh[:, 0:n], hown_scr[chunk, :, tt0:tt0 + n], w=[("eh", i)])
                        B.copy("act", y[:, 0:n], ps, r=[pstok], w=[tk])
                        B.tt(t[:, 0:n], y[:, 0:n], y[:, 0:n], ALU.mult, r=[tk], w=[tk])
                        B.ts(t[:, 0:n], t[:, 0:n], 0.044715, 1.0, ALU.mult, ALU.add, r=[tk], w=[tk])
                        B.tt(t[:, 0:n], t[:, 0:n], y[:, 0:n], ALU.mult, r=[tk], w=[tk])
                        B.act(t[:, 0:n], t[:, 0:n], AF.Sigmoid, r=[tk], w=[tk], scale=1.5957691216057308)
                        B.tt(t[:, 0:n], t[:, 0:n], y[:, 0:n], ALU.mult, r=[tk], w=[tk])
                        B.tt(ob[:, 0:n], t[:, 0:n], hh[:, 0:n], ALU.mult, r=[tk, ("eh", i)], w=[tk])
                        B.dma("act", olruT_scr[chunk, :, tt0:tt0 + n], ob[:, 0:n], r=[tk], w=[B.fresh("ol")])
                    else:
                        y = ey[i]
                        B.act(y[:, 0:n], ps, AF.Sigmoid, r=[pstok], w=[tk])
                        dst = saT_scr if which == "ga" else slT_scr
                        B.dma("act", dst[chunk, :, tt0:tt0 + n], y[:, 0:n], r=[tk], w=[B.fresh("sg")])
                return ev
            for which, cbase in (("yl", C_YL), ("ga", C_GA), ("gl", C_GL)):
                gemm_feat([(xnT, wbufs, lambda c0, n, cbase=cbase: w_in[:, cbase + c0:cbase + c0 + n])], 9, KC,
                          4096, 512, ttiles, evac_feat(which))
            S.barrier()
            A.release(m0)

        if "attn" in ph:
            dsel_in = inp("dsel", [128, 32 * 128])
            mbias_in = inp("maskbias", [128, 512])
            m0 = A.mark()
            dselt = A.alloc([32, 128], BF16)
            dv = dsel_in.rearrange("p (g t) -> p g t", g=32)
            for g0 in range(0, 32, 8):
                B.dma("pool", dselt[:, g0:g0 + 8, :], dv[:, g0:g0 + 8, :], w=[("dsel", g0)])
            S.add("dve", lambda e: e.memset(eps_t, EPS), r=[("dsel", g0) for g0 in range(0, 32, 8)], w=["dsel"])
            mbias = A.alloc([512], F32)
            B.dma("sp", mbias, mbias_in, w=["mbias"])
            ones_b = A.alloc([128], BF16)
            S.add("dve", lambda e: e.memset(ones_b, 1.0), w=["ones"])
            gq = A.alloc([128], F32)
            gk = A.alloc([128], F32)
            B.dma("sp", gq, inp("norm_q", [HD]).partition_broadcast(128), w=["gq"])
            B.dma("sp", gk, inp("norm_k", [HD]).partition_broadcast(128), w=["gk"])
            cq = A.alloc([1], F32)
            ck = A.alloc([1], F32)
            S.add("dve", lambda e: e.tensor_reduce(out=cq, in_=gq, axis=AX.X, op=ALU.max, apply_absolute_value=True),
                  r=["gq"], w=["cq"])
            S.add("dve", lambda e: e.tensor_reduce(out=ck, in_=gk, axis=AX.X, op=ALU.max, apply_absolute_value=True),
                  r=["gk"], w=["ck"])
            B.tt(cq, cq, ck, ALU.mult, r=["cq", "ck"], w=["cq"])
            B.ts(cq, cq, -math.sqrt(128.0), None, ALU.mult, r=["cq"], w=["cq"])

            iqTb = A.alloc([32, 128], BF16)
            iqTg = A.alloc([32, 128], BF16)
            ikT = A.alloc([SEQ], BF16)
            wsel = A.alloc([32, 128], BF16)
            iwb = A.alloc([32], F32)
            iwrep = A.alloc([32, 4], BF16)
            R1 = [A.alloc([512], BF16) for _ in range(3)]
            ImB = [A.alloc([SEQ], F32) for _ in range(2)]
            Wk = A.alloc([SEQ], F32)
            m8 = A.alloc([8], F32)
            thrB = [A.alloc([1], F32) for _ in range(2)]
            maskb = A.alloc([SEQ], BF16)
            maskTB = [A.alloc([32, 128], BF16) for _ in range(2)]
            KTg = [A.alloc([SEQ], BF16) for _ in range(2)]
            Vg = [A.alloc([32, 128], BF16) for _ in range(2)]
            QTg = [A.alloc([512], BF16) for _ in range(2)]
            Pt = [A.alloc([512], BF16) for _ in range(3)]
            Pm = [A.alloc([512], BF16) for _ in range(3)]
            NEV = 4
            zs = [A.alloc([512], F32) for _ in range(NEV)]
            osb = [A.alloc([512], F32) for _ in range(NEV)]
            evc = [0]
            ot = [A.alloc([512], BF16) for _ in range(2)]
            sc_att = 1.0 / math.sqrt(128.0)

            qblocks = [("p", i, i * 128, 128, 512 * (i + 1)) for i in range(7, -1, -1)] + \
                      [("s", sq, 1024 + sq * 32, 32, SS) for sq in range(2)]
            r1c = [0]
            pc = [0]
            gct = [0]

            def srcs(kind, idx):
                if kind == "p":
                    return ikT_scr, KT_scr, V_scr
                return ikT_s[idx], KT_s[idx], V_s[idx]

            def stageA(blk, par):
                kind, idx, t0, R, Sk = blk
                G = R // 4
                Im = ImB[par]
                imk = ("Im", par)
                ikT_src = srcs(kind, idx)[0]
                iqv = iqTb.rearrange("p a b -> p (a b)")[:, 0:32 * R].rearrange("p (h t) -> p h t", h=32)
                B.dma("sp", iqv, iqT_scr[:, :, t0:t0 + R].rearrange("h d t -> d h t"), w=["iqTb"])
                B.copy("pool", iqTg[:, 0:G, :].rearrange("p g (h t) -> p g h t", t=4),
                       iqv.rearrange("p h (g t) -> p g h t", t=4), r=["iqTb"], w=["iqTg"])
                B.dma("sp", ikT[:, 0:Sk], ikT_src[:, 0:Sk], w=["ikT"])
                B.dma("sp", iwb[0:R], iw_scr[t0:t0 + R, :], w=["iwb"])
                B.copy("pool", iwrep[0:R], iwb[0:R].unsqueeze(2).to_broadcast([R, 32, 4]), r=["iwb"], w=["iwrep"])
                pT = psum[3][:, :].bitcast(BF16)
                B.transpose(pT[:, 0:R], iwrep[0:R].rearrange("p h t -> p (h t)"), ident_b[0:R, 0:R],
                            r=["iwrep", "ident_b"], w=[("ps", 3)])
                B.tt(wsel[:, 0:G, 0:R], dselt[:, 0:G, 0:R], pT[:, 0:R].unsqueeze(1).to_broadcast([128, G, R]),
                     ALU.mult, r=[("ps", 3), "dsel"], w=["wsel"])
                chunks = [(c0, min(512, Sk - c0)) for c0 in range(0, Sk, 512)]
                items = [(c0, cn, g) for (c0, cn) in chunks for g in range(G)]
                pend = None
                for it in range(len(items) + 1):
                    cur_item = None
                    if it < len(items):
                        c0, cn, g = items[it]
                        pb = r1c[0] % 2
                        rr = R1[r1c[0] % 3]
                        rk = ("r1", r1c[0] % 3)
                        r1c[0] += 1
                        ps1 = psum[pb][:, 0:cn]
                        B.mm(ps1, iqTg[:, g, :], ikT[:, c0:c0 + cn], True, True, r=["iqTg", "ikT"], w=[("ps", pb)])
                        B.act(rr[:, 0:cn], ps1, AF.Relu, r=[("ps", pb)], w=[rk])
                        cur_item = (c0, cn, g, rr, rk)
                    if pend is not None:
                        c0p, cnp, gp, rrp, rkp = pend
                        psI = psum[2][0:R, 0:cnp]
                        B.mm(psI, wsel[:, gp, 0:R], rrp[:, 0:cnp], gp == 0, gp == G - 1, r=["wsel", rkp], w=[("ps", 2)])
                        if gp == G - 1:
                            B.copy("act", Im[0:R, c0p:c0p + cnp], psI, r=[("ps", 2)], w=[imk])
                    pend = cur_item

            def stageB1(blk, par):
                kind, idx, t0, R, Sk = blk
                Im = ImB[par]
                imk = ("Im", par)
                if kind == "p":
                    B.tt(Im[0:R, Sk - 512:Sk], Im[0:R, Sk - 512:Sk], mbias[0:R, 0:512], ALU.add,
                         r=[imk, "mbias"], w=[imk])
                cur = Im
                for rnd in range(32):
                    S.add("dve", lambda e, cur=cur, R=R, Sk=Sk: e.max(out=m8[0:R], in_=cur[0:R, 0:Sk]),
                          r=[imk, "Wk"], w=["m8"])
                    if rnd < 31:
                        S.add("dve", lambda e, cur=cur, R=R, Sk=Sk: e.match_replace(
                            out=Wk[0:R, 0:Sk], in_to_replace=m8[0:R], in_values=cur[0:R, 0:Sk], imm_value=-3.0e38),
                            r=[imk, "Wk", "m8"], w=["Wk"])
                        cur = Wk
                B.ts(thrB[par][0:R], m8[0:R, 7:8], -5.0e29, None, ALU.max, r=["m8"], w=[("thr", par)])

            def stageB2(blk, par):
                kind, idx, t0, R, Sk = blk
                Im = ImB[par]
                maskT = maskTB[par]
                mk = ("maskT", par)
                B.ts(maskb[0:R, 0:Sk], Im[0:R, 0:Sk], thrB[par][0:R], None, ALU.is_ge,
                     r=[("Im", par), ("thr", par)], w=["maskb"])
                nkb = (Sk + 127) // 128
                for k0 in range(0, nkb, 8):
                    pT8 = psum[3][:, :].bitcast(BF16).rearrange("p (a b) -> p a b", a=8)
                    k1 = min(nkb, k0 + 8)
                    for kb_ in range(k0, k1):
                        kn = min(128, Sk - kb_ * 128)
                        B.transpose(pT8[0:kn, kb_ - k0, 0:R], maskb[0:R, kb_ * 128:kb_ * 128 + kn], ident_b[0:R, 0:R],
                                    r=["maskb", "ident_b"], w=[("ps", 3)])
                    kfull = [kb_ for kb_ in range(k0, k1) if Sk - kb_ * 128 >= 128]
                    if kfull:
                        B.act(maskT[:, kfull[0]:kfull[-1] + 1, 0:R], pT8[:, 0:len(kfull), 0:R], AF.Copy,
                              r=[("ps", 3)], w=[mk], scale=30000.0, bias=-30000.0)
                    if len(kfull) < k1 - k0:
                        kb_ = k1 - 1
                        kn = Sk - kb_ * 128
                        B.act(maskT[0:kn, kb_, 0:R], pT8[0:kn, kb_ - k0, 0:R], AF.Copy, r=[("ps", 3)], w=[mk],
                              scale=30000.0, bias=-30000.0)

            def stageC(blk, par):
                kind, idx, t0, R, Sk = blk
                maskT = maskTB[par]
                mk = ("maskT", par)
                _, KT_src, V_src = srcs(kind, idx)
                nkb = (Sk + 127) // 128
                for g in range(NKV):
                    sl = gct[0] % 2
                    gct[0] += 1
                    B.dma("sp", KTg[sl][:, 0:Sk], KT_src[g, :, 0:Sk], w=[("KTg", sl)])
                    nfull = Sk // 128
                    B.dma("act", Vg[sl][:, 0:nfull, :],
                          V_src[0:nfull * 128, g * 128:(g + 1) * 128].rearrange("(b p) d -> p b d", p=128),
                          w=[("Vg", sl)])
                    if Sk % 128:
                        kn = Sk % 128
                        B.dma("act", Vg[sl][0:kn, nfull, :], V_src[nfull * 128:Sk, g * 128:(g + 1) * 128],
                              w=[("Vg", sl)])
                    B.dma("sp", QTg[sl][:, 0:4 * R].rearrange("p (h t) -> p h t", h=4),
                          QT_scr[4 * g:4 * g + 4, :, t0:t0 + R].rearrange("h d t -> d h t"), w=[("QTg", sl)])
                    N4 = 4 * R
                    psO = psum[6][:, 0:N4]
                    psZ = psum[7][:, 0:N4]
                    pendc = None
                    for it in range(nkb + 1):
                        curc = None
                        if it < nkb:
                            kb_ = it
                            kn = min(128, Sk - kb_ * 128)
                            pb = 4 + (pc[0] % 2)
                            pi_ = pc[0] % 3
                            pc[0] += 1
                            psL = psum[pb][0:kn, 0:N4]
                            B.mm(psL, KTg[sl][:, kb_ * 128:kb_ * 128 + kn], QTg[sl][:, 0:4 * R], True, False,
                                 r=[("KTg", sl), ("QTg", sl)], w=[("ps", pb)])
                            for h in range(4):
                                B.mm(psum[pb][0:kn, h * R:(h + 1) * R], ident_b[0:kn, 0:kn], maskT[0:kn, kb_, 0:R],
                                     False, h == 3, r=["ident_b", mk], w=[("ps", pb)])
                            B.act(Pt[pi_][0:kn, 0:N4], psL, AF.Exp, r=[("ps", pb), "cq"], w=[("pt", pi_)],
                                  scale=sc_att, bias=cq[0:kn, 0:1])
                            curc = (kb_, kn, pi_)
                        if pendc is not None:
                            kbp, knp, pip = pendc
                            B.mm(psO, Vg[sl][0:knp, kbp, :], Pt[pip][0:knp, 0:N4], kbp == 0, kbp == nkb - 1,
                                 r=[("Vg", sl), ("pt", pip)], w=[("ps", 6)])
                            B.mm(psZ, ones_b[0:knp, :], Pt[pip][0:knp, 0:N4], kbp == 0, kbp == nkb - 1,
                                 r=["ones", ("pt", pip)], w=[("ps", 7)])
                        pendc = curc
                    ei = evc[0] % NEV
                    evc[0] += 1
                    B.copy("act", zs[ei][:, 0:N4], psZ, r=[("ps", 7)], w=[("zs", ei)])
                    B.copy("act", osb[ei][:, 0:N4], psO, r=[("ps", 6)], w=[("os", ei)])
                    S.add("dve", lambda e, ei=ei, N4=N4: e.reciprocal(out=zs[ei][:, 0:N4], in_=zs[ei][:, 0:N4]),
                          r=[("zs", ei)], w=[("zs", ei)])
                    B.tt(ot[sl][:, 0:N4], osb[ei][:, 0:N4], zs[ei][:, 0:N4], ALU.mult, r=[("os", ei), ("zs", ei)],
                         w=[("ot", sl)])
                    B.dma("act", oattT_scr[4 * g:4 * g + 4, :, t0:t0 + R].rearrange("h d t -> d h t"),
                          ot[sl][:, 0:N4].rearrange("p (h t) -> p h t", h=4), r=[("ot", sl)], w=[B.fresh("oa")])

            NBK = len(qblocks)
            for step in range(NBK + 2):
                if step < NBK:
                    stageA(qblocks[step], step % 2)
                if 0 <= step - 1 < NBK:
                    stageB1(qblocks[step - 1], (step - 1) % 2)
                if 0 <= step - 2 < NBK:
                    stageC(qblocks[step - 2], (step - 2) % 2)
                if 0 <= step - 1 < NBK:
                    stageB2(qblocks[step - 1], (step - 1) % 2)
            S.barrier()
            A.release(m0)

        if "mix" in ph:
            w_branch = inp("w_branch", [2 * D, D])
            w_out = inp("w_out", [D, D])
            xown = inp("xown", [TO, D])
            m0 = A.mark()
            oaT = A.alloc([KC, TO], BF16)
            olT = A.alloc([KC, TO], BF16)
            wA = [A.alloc([KC, 128], BF16) for _ in range(2)]
            wL = [A.alloc([KC, 128], BF16) for _ in range(2)]
            for h0 in range(0, 32, 8):
                B.dma("sp", oaT[:, h0:h0 + 8, :], oattT_scr[h0:h0 + 8].rearrange("h d t -> d h t"),
                      w=[("ATx", h0)])
                B.dma("sp", olT[:, h0:h0 + 8, :], olruT_scr[h0:h0 + 8].rearrange("h d t -> d h t"),
                      w=[("ATy", h0)])
            ATR = [("ATx", h0) for h0 in (0, 8, 16, 24)] + [("ATy", h0) for h0 in (0, 8, 16, 24)]
            ttiles = [(0, 512), (512, 512), (1024, 64)]
            NE = 3
            sa_t = [A.alloc([512], F32) for _ in range(NE)]
            sl_t = [A.alloc([512], F32) for _ in range(NE)]
            ta_t = [A.alloc([512], F32) for _ in range(NE)]
            mb_t = [A.alloc([512], BF16) for _ in range(NE)]
            ec = [0]

            def evac_mix(chunk, tt0, n, pss):
                (psA, tokA), (psL, tokL) = pss
                i = ec[0] % NE
                ec[0] += 1
                tk = ("e", i)
                B.dma("sp", sa_t[i][:, 0:n], saT_scr[chunk, :, tt0:tt0 + n], w=[("esa", i)])
                B.dma("sp", sl_t[i][:, 0:n], slT_scr[chunk, :, tt0:tt0 + n], w=[("esl", i)])
                B.tt(ta_t[i][:, 0:n], psA, sa_t[i][:, 0:n], ALU.mult, r=[tokA, ("esa", i)], w=[tk])
                B.tt(sl_t[i][:, 0:n], psL, sl_t[i][:, 0:n], ALU.mult, r=[tokL, ("esl", i)], w=[("esl", i)])
                B.tt(mb_t[i][:, 0:n], ta_t[i][:, 0:n], sl_t[i][:, 0:n], ALU.add, r=[tk, ("esl", i)], w=[tk])
                B.dma("act", mixT_scr[chunk, :, tt0:tt0 + n], mb_t[i][:, 0:n], r=[tk], w=[B.fresh("mx")])
            gemm_feat([(oaT, wA, lambda c0, n: w_branch[0:D, c0:c0 + n]),
                       (olT, wL, lambda c0, n: w_branch[D:2 * D, c0:c0 + n])], 9, KC, 4096, 128, ttiles, evac_mix,
                      attoks=ATR)
            S.barrier()
            A.release(m0)

            m0 = A.mark()
            mxT = A.alloc([KC, TO], BF16)
            wbufs = [A.alloc([KC, 512], BF16) for _ in range(2)]
            for h0 in range(0, 32, 8):
                B.dma("sp", mxT[:, h0:h0 + 8, :], mixT_scr[h0:h0 + 8].rearrange("h d t -> d h t"), w=[("ATz", h0)])
            NE = 3
            xr = [A.alloc([512], F32) for _ in range(NE)]

            def evac_out(tag, bi, rows, t0, ps, pstok, ncols):
                i = ec[0] % NE
                ec[0] += 1
                c0 = tag
                B.dma("sp", xr[i][0:rows], xown[t0:t0 + rows, c0:c0 + 512], w=[("xr", i)])
                B.tt(xr[i][0:rows], ps, xr[i][0:rows], ALU.add, r=[pstok, ("xr", i)], w=[("xr", i)])
                B.dma("act", x1_scr[t0:t0 + rows, c0:c0 + 512], xr[i][0:rows], r=[("xr", i)], w=[B.fresh("x1")])
            gemm_tok(mxT, [128] * 8 + [64], KC, wbufs, lambda c0, n: w_out[:, c0:c0 + n],
                     [(g * 512, 512, g * 512) for g in range(8)], evac_out,
                     attoks=[("ATz", h0) for h0 in (0, 8, 16, 24)])
            S.barrier()
            A.release(m0)

        if "ffn" in ph:
            w_gate = inp("w_gate", [D, DFF])
            w_up = inp("w_up", [D, DFF])
            w_down = inp("w_down", [DFF, D])
            y_own = B.dout("y_own", [TO, D])
            m0 = A.mark()
            gffT = A.alloc([KC], F32)
            B.dma("sp", gffT, inp("norm_ffnT", [128, KC]), w=["gT"])
            xn2T = A.alloc([KC, TO], BF16)
            xt = A.alloc([D], F32)
            junk = A.alloc([D], BF16)
            ssb = [A.alloc([1], F32) for _ in range(2)]
            blocks = [(x1_scr[j * 128:(j + 1) * 128, :], 128) for j in range(8)] + [(x1_scr[1024:1088, :], 64)]
            norm_transpose(blocks, xn2T, gffT, xt, junk, ssb)
            wG = [A.alloc([KC, 256], BF16) for _ in range(2)]
            wU = [A.alloc([KC, 256], BF16) for _ in range(2)]
            ttiles = [(0, 512), (512, 512), (1024, 64)]
            NE = 3
            sg_t = [A.alloc([512], F32) for _ in range(NE)]
            hb_t = [A.alloc([512], BF16) for _ in range(NE)]
            ec = [0]

            def evac_ffn(chunk, tt0, n, pss):
                (psG, tokG), (psU, tokU) = pss
                i = ec[0] % NE
                ec[0] += 1
                tk = ("e", i)
                B.act(sg_t[i][:, 0:n], psG, AF.Silu, r=[tokG], w=[tk])
                B.tt(hb_t[i][:, 0:n], sg_t[i][:, 0:n], psU, ALU.mult, r=[tk, tokU], w=[tk])
                B.dma("act", hT_scr[chunk, :, tt0:tt0 + n], hb_t[i][:, 0:n], r=[tk], w=[B.fresh("ht")])
            gemm_feat([(xn2T, wG, lambda c0, n: w_gate[:, c0:c0 + n]),
                       (xn2T, wU, lambda c0, n: w_up[:, c0:c0 + n])], 9, KC, DFF, 256, ttiles, evac_ffn)
            S.barrier()
            A.release(m0)

            m0 = A.mark()
            hT = A.alloc([KC, TO], BF16)
            wbufs = [A.alloc([KC, 512], BF16) for _ in range(2)]
            NE = 3
            xr = [A.alloc([512], F32) for _ in range(NE)]
            pieces = [(0, 32), (32, 32), (64, 22)]
            for pi, (k0, kcn) in enumerate(pieces):
                step = 8
                toks = []
                for h0 in range(0, kcn, step):
                    h1 = min(kcn, h0 + step)
                    tk = ("ATz", h0)
                    toks.append(tk)
                    B.dma("sp", hT[:, h0:h1, :], hT_scr[k0 + h0:k0 + h1].rearrange("h d t -> d h t"), w=[tk])
                src = x1_scr
                dst = y_own

                def evac_dn(tag, bi, rows, t0, ps, pstok, ncols, pi=pi):
                    i = ec[0] % NE
                    ec[0] += 1
                    c0 = tag
                    prev = x1_scr if pi == 0 else y_own
                    B.dma("sp", xr[i][0:rows], prev[t0:t0 + rows, c0:c0 + 512], r=[("y", bi, c0)], w=[("xr", i)])
                    B.tt(xr[i][0:rows], ps, xr[i][0:rows], ALU.add, r=[pstok, ("xr", i)], w=[("xr", i)])
                    B.dma("act", y_own[t0:t0 + rows, c0:c0 + 512], xr[i][0:rows], r=[("xr", i)], w=[("y", bi, c0)])
                gemm_tok(hT, [128] * 8 + [64], kcn, wbufs,
                         lambda c0, n, k0=k0, kcn=kcn: w_down[k0 * 128:(k0 + kcn) * 128, c0:c0 + n],
                         [(g * 512, 512, g * 512) for g in range(8)], evac_dn, attoks=toks)
            S.barrier()
            A.release(m0)

        with nc.Block() as block:
            S.emit(nc, block, esem, dsem)
    return nc


def rope_tables(pos):
    half = 16
    inv = (np.float32(500000.0) ** (-np.arange(half, dtype=np.float32) * np.float32(2.0) / np.float32(32))
           ).astype(np.float32)
    ang = pos.astype(np.float32)[:, None] * inv[None, :]
    return np.concatenate([np.cos(ang), np.sin(ang)], axis=1).astype(np.float32)


def make_in_maps(inp, names=None):
    f = np.float32
    in_maps = []
    pos_seq = np.concatenate([np.arange(SEQ), PAST + np.arange(32), PAST + np.arange(32)])
    cs_seq = rope_tables(pos_seq)
    ident = np.eye(128, dtype=f)
    dsel = np.zeros((32, 4, 32, 128), f)
    for g in range(32):
        for tl in range(4):
            dsel[:, tl, g, 4 * g + tl] = 1.0 / 64.0
    dsel = dsel.reshape(128, 32 * 128)

    def T32(v):
        return np.ascontiguousarray(np.asarray(v).reshape(KC, 128).T)

    for c in range(8):
        p, r = c // 4, c % 4
        xp = inp["x_prompt"][p]
        own_blocks = [xp[(4 * i + r) * 128:(4 * i + r + 1) * 128] for i in range(8)]
        xown = np.concatenate(own_blocks + [inp["x_sample"][2 * c], inp["x_sample"][2 * c + 1]], axis=0)
        pos_own = np.concatenate([np.arange((4 * i + r) * 128, (4 * i + r + 1) * 128) for i in range(8)] +
                                 [PAST + np.arange(32), PAST + np.arange(32)])
        sel = np.zeros((128, 4), f)
        sel[:, r] = 1.0
        tl = np.arange(128)[:, None]
        slx = np.arange(512)[None, :]
        mb = np.where(slx < 128 * r + 64 + 64 * (tl >= 64), 0.0, NEGM).astype(f)
        sc = inp["state_conv"][0, 2 * c:2 * c + 2]
        m = {
            "xseq": np.ascontiguousarray(xp),
            "xown": np.ascontiguousarray(xown),
            "w_in": inp["w_in"][0],
            "norm_mixT": T32(inp["norm_mix"][0]),
            "norm_ffnT": T32(inp["norm_ffn"][0]),
            "norm_q": inp["norm_q"][0],
            "norm_k": inp["norm_k"][0],
            "norm_idx_k": inp["norm_idx_k"][0],
            "cs_seq": cs_seq,
            "cs_own": rope_tables(pos_own),
            "ident": ident,
            "dsel": dsel,
            "maskbias": mb,
            "cache_k": np.ascontiguousarray(inp["cache_k"][0, 2 * c:2 * c + 2].reshape(2, PAST, NKV * HD)),
            "cache_v": np.ascontiguousarray(inp["cache_v"][0, 2 * c:2 * c + 2].reshape(2, PAST, NKV * HD)),
            "cache_ik": np.ascontiguousarray(inp["cache_idx_k"][0, 2 * c:2 * c + 2]),
            "state_lruT": np.stack([T32(inp["state_lru"][0, 2 * c + q]) for q in range(2)]),
            "state_convT": np.ascontiguousarray(sc.reshape(2, 3, KC, 128).transpose(0, 3, 2, 1)),
            "convwT": np.ascontiguousarray(inp["conv_w"][0].reshape(4, KC, 128).transpose(2, 0, 1)),
            "convbT": T32(inp["conv_b"][0]),
            "lru_baT": T32(inp["lru_ba"][0]),
            "lru_bxT": T32(inp["lru_bx"][0]),
            "lru_lamT": T32(inp["lru_lambda"][0]),
            "lru_wa": inp["lru_wa"][0],
            "lru_wx": inp["lru_wx"][0],
            "selr": sel,
            "w_branch": inp["w_branch"][0],
            "w_out": inp["w_out"][0],
            "w_gate": inp["w_gate"][0],
            "w_up": inp["w_up"][0],
            "w_down": inp["w_down"][0],
        }
        if names is not None:
            m = {k: v for k, v in m.items() if k in names}
        in_maps.append(m)
    return in_maps


def input_names(nc):
    names = set()
    for alloc in nc.allocations:
        if isinstance(alloc, mybir.MemoryLocationSet) and alloc.kind == "ExternalInput":
            names.add(alloc.memorylocations[0].name)
    return names


_NC_CACHE = {}


def kernel(**inputs):
    inp = {k: np.asarray(v) for k, v in inputs.items()}
    if "nc" not in _NC_CACHE:
        _NC_CACHE["nc"] = build_program()
    nc = _NC_CACHE["nc"]
    in_maps = make_in_maps(inp, input_names(nc))
    res = run_bass_kernel_spmd(nc, in_maps, core_ids=list(range(8)))
    R = res.results
    f = np.float32
    y_prompt = np.zeros((2, SEQ, D), f)
    y_sample = np.zeros((16, DEC_SEQ, D), f)
    k_prompt = np.zeros((1, 2, SEQ, NKV, HD), f)
    v_prompt = np.zeros((1, 2, SEQ, NKV, HD), f)
    ik_prompt = np.zeros((1, 2, SEQ, HD), f)
    lru_prompt = np.zeros((1, 2, D), f)
    conv_prompt = np.zeros((1, 2, 3, D), f)
    k_sample = np.zeros((1, 16, DEC_SEQ, NKV, HD), f)
    v_sample = np.zeros((1, 16, DEC_SEQ, NKV, HD), f)
    ik_sample = np.zeros((1, 16, DEC_SEQ, HD), f)
    lru_sample = np.zeros((1, 16, D), f)
    conv_sample = np.zeros((1, 16, 3, D), f)
    for c in range(8):
        p, r = c // 4, c % 4
        o = R[c]
        y = o["y_own"]
        for i in range(8):
            b = 4 * i + r
            y_prompt[p, b * 128:(b + 1) * 128] = y[i * 128:(i + 1) * 128]
        for q in range(2):
            sq = 2 * c + q
            y_sample[sq] = y[1024 + q * 32:1024 + (q + 1) * 32]
            k_sample[0, sq] = o["o_k"][SEQ + q * 32:SEQ + (q + 1) * 32].reshape(32, NKV, HD)
            v_sample[0, sq] = o["o_v"][SEQ + q * 32:SEQ + (q + 1) * 32].reshape(32, NKV, HD)
            ik_sample[0, sq] = o["o_ik"][SEQ + q * 32:SEQ + (q + 1) * 32]
            lru_sample[0, sq] = o["o_lru"][1 + q]
            conv_sample[0, sq] = o["o_conv"][1 + q]
        if r == 0:
            k_prompt[0, p] = o["o_k"][:SEQ].reshape(SEQ, NKV, HD)
            v_prompt[0, p] = o["o_v"][:SEQ].reshape(SEQ, NKV, HD)
            ik_prompt[0, p] = o["o_ik"][:SEQ]
            lru_prompt[0, p] = o["o_lru"][0]
            conv_prompt[0, p] = o["o_conv"][0]
    return (y_prompt, y_sample, k_prompt, v_prompt, ik_prompt, lru_prompt, conv_prompt,
            k_sample, v_sample, ik_sample, lru_sample, conv_sample)
```

```python
import math
from contextlib import ExitStack

import numpy as np
import concourse.bass as bass
import concourse.mybir as mybir
from concourse.bass_utils import run_bass_kernel_spmd

F32 = mybir.dt.float32
BF16 = mybir.dt.bfloat16
AF = mybir.ActivationFunctionType
ALU = mybir.AluOpType
AX = mybir.AxisListType

D = 4096
SEQ = 4096
NB = 32
DEC_SEQ = 32
PAST = 2048
SS = PAST + DEC_SEQ
NH = 32
NKV = 8
HD = 128
DFF = 11008
KC = 32
TO = 1088
EPS = 1e-6
C_Q, C_K, C_V, C_IQ, C_IW, C_IK, C_XL, C_YL, C_GA, C_GL = (
    0, 4096, 5120, 6144, 10240, 10272, 10400, 14496, 18592, 22688)
IN_COLS = 26784
NEGM = -1.0e30


class Op:
    __slots__ = ("eng", "fn", "deps", "dma", "signal", "pos", "sem", "semval", "K", "waits",
                 "barrier")

    def __init__(self, eng, fn, dma=False, barrier=False):
        self.eng = eng
        self.fn = fn
        self.deps = ()
        self.dma = dma
        self.signal = False
        self.pos = 0
        self.sem = None
        self.semval = 0
        self.K = None
        self.waits = ()
        self.barrier = barrier


class Sched:
    CE = ("pe", "act", "dve", "pool", "sp")
    NDSEM = 10

    def __init__(self):
        self.ops = []
        self.last_w = {}
        self.readers = {}
        self.last_op = {e: None for e in self.CE}

    def add(self, eng, fn, r=(), w=(), dma=False):
        idx = len(self.ops)
        op = Op(eng, fn, dma=dma)
        raw = set()
        oth = set()
        for t in r:
            lw = self.last_w.get(t)
            if lw is not None:
                raw.add(lw)
        for t in w:
            lw = self.last_w.get(t)
            if lw is not None:
                oth.add(lw)
            rs = self.readers.get(t)
            if rs:
                oth.update(rs)
        deps = set()
        for d in raw | oth:
            dop = self.ops[d]
            if (not dop.dma) and (not dma) and dop.eng == eng:
                if eng == "pe":
                    continue
                if d not in raw:
                    continue
            deps.add(d)
            if not dop.dma:
                dop.signal = True
        op.deps = tuple(sorted(deps))
        for t in r:
            self.readers.setdefault(t, []).append(idx)
        for t in w:
            self.last_w[t] = idx
            self.readers[t] = []
        self.ops.append(op)
        if not dma:
            self.last_op[eng] = idx
        return idx

    def barrier(self):
        lasts = dict(self.last_op)
        for e in self.CE:
            op = Op(e, None, barrier=True)
            deps = set()
            for e2, li in lasts.items():
                if li is not None and e2 != e:
                    deps.add(li)
                    self.ops[li].signal = True
            op.deps = tuple(sorted(deps))
            self.ops.append(op)
        self.last_w = {}
        self.readers = {}

    def analyze(self):
        CE = self.CE
        K = {e: {} for e in CE}
        pos = {e: 0 for e in CE}
        sig = {e: 0 for e in CE}
        sigcount = {e: {} for e in CE}
        dq = {e: {"next": 0, "cum": [0] * self.NDSEM, "last": [None] * self.NDSEM} for e in CE}
        for op in self.ops:
            E = op.eng
            KE = K[E]
            waits = {}

            def need(key, val, kafter):
                if KE.get(key, 0) >= val:
                    return
                if waits.get(key, 0) < val:
                    waits[key] = val
                if kafter:
                    for k2, v2 in kafter.items():
                        if KE.get(k2, 0) < v2:
                            KE[k2] = v2
                if KE.get(key, 0) < val:
                    KE[key] = val

            for d in op.deps:
                dop = self.ops[d]
                if dop.dma:
                    need(("S", dop.eng, dop.sem), dop.semval, dop.K)
                else:
                    need(dop.eng, dop.pos, dop.K)
            if op.barrier:
                for q in CE:
                    for s in range(self.NDSEM):
                        if dq[q]["cum"][s] > 0:
                            lo = dq[q]["last"][s]
                            need(("S", q, s), dq[q]["cum"][s], lo.K if lo else None)
            if op.dma:
                q = dq[E]
                s = q["next"]
                q["next"] = (s + 1) % self.NDSEM
                if q["last"][s] is not None:
                    need(("S", E, s), q["cum"][s], q["last"][s].K)
                q["cum"][s] += 16
                op.sem = s
                op.semval = q["cum"][s]
                q["last"][s] = op
                op.K = dict(KE)
            elif not op.barrier:
                pos[E] += 1
                op.pos = pos[E]
                if op.signal:
                    sig[E] += 1
                    sigcount[E][op.pos] = sig[E]
                    kk = dict(KE)
                    kk[E] = op.pos
                    op.K = kk
            op.waits = tuple(waits.items())
        self.sigcount = sigcount
        self.final_dma = {e: list(dq[e]["cum"]) for e in CE}

    def emit(self, nc, block, esem, dsem):
        self.analyze()
        per = {e: [] for e in self.CE}
        for op in self.ops:
            per[op.eng].append(op)
        sigcount = self.sigcount

        def run(e, eng):
            for op in per[e]:
                for key, val in op.waits:
                    if isinstance(key, tuple):
                        eng.wait_ge(dsem[key[1]][key[2]], val)
                    else:
                        eng.wait_ge(esem[key], sigcount[key][val])
                if op.fn is None:
                    continue
                ins = op.fn(eng)
                if op.dma:
                    ins.then_inc(dsem[e][op.sem], 16)
                elif op.signal:
                    ins.then_inc(esem[e], 1)
            if e == "sp":
                for q in self.CE:
                    for s, v in enumerate(self.final_dma[q]):
                        if v > 0:
                            eng.wait_ge(dsem[q][s], v)

        @block.tensor
        def _(eng):
            run("pe", eng)

        @block.scalar
        def _(eng):
            run("act", eng)

        @block.vector
        def _(eng):
            run("dve", eng)

        @block.gpsimd
        def _(eng):
            run("pool", eng)

        @block.sync
        def _(eng):
            run("sp", eng)


class Arena:
    def __init__(self, t, words):
        self.t = t
        self.words = words
        self.off = 0

    def alloc(self, free_shape, dtype, parts=128):
        n = 1
        for s in free_shape:
            n *= s
        nbytes = n * (2 if dtype == BF16 else 4)
        w = (nbytes + 31) // 32 * 8
        off = self.off
        assert off + w <= self.words, f"arena overflow {off + w} > {self.words}"
        self.off += w
        ap = self.t[0:parts, off:off + (nbytes + 3) // 4]
        if dtype == BF16:
            ap = ap.bitcast(BF16)
            if ap.shape[1] != n:
                ap = ap[:, 0:n]
        if len(free_shape) == 2:
            ap = ap.rearrange("p (a b) -> p a b", a=free_shape[0])
        elif len(free_shape) == 3:
            ap = ap.rearrange("p (a b c) -> p a b c", a=free_shape[0], b=free_shape[1])
        return ap

    def mark(self):
        return self.off

    def release(self, m):
        self.off = m


class Builder:
    def __init__(self, debug=False, phases=None, feed=None):
        self.debug = debug
        self.feed = feed or set()
        self.phases = phases
        self.nc = bass.Bass("TRN2", target_bir_lowering=False)
        self.S = Sched()
        self.uid = 0

    def fresh(self, p):
        self.uid += 1
        return (p, self.uid)

    def din(self, name, shape, dt=F32):
        return self.nc.dram_tensor(name, list(shape), dt, kind="ExternalInput").ap()

    def dout(self, name, shape, dt=F32):
        return self.nc.dram_tensor(name, list(shape), dt, kind="ExternalOutput").ap()

    def dscr(self, name, shape, dt=F32):
        kind = "ExternalOutput" if (self.debug and name in self.debug) else "Internal"
        if name in self.feed:
            kind = "ExternalInput"
        return self.nc.dram_tensor(name, list(shape), dt, kind=kind).ap()

    def dma(self, q, out, in_, r=(), w=()):
        self.S.add(q, lambda e: e.dma_start(out=out, in_=in_), r=r, w=w, dma=True)

    def act(self, out, in_, func, r=(), w=(), **kw):
        self.S.add("act", lambda e: e.activation(out=out, in_=in_, func=func, **kw), r=r, w=w)

    def tt(self, out, in0, in1, op, r=(), w=(), eng="dve"):
        self.S.add(eng, lambda e: e.tensor_tensor(out=out, in0=in0, in1=in1, op=op), r=r, w=w)

    def ts(self, out, in0, s1, s2, op0, op1=None, r=(), w=(), eng="dve", **kw):
        if op1 is None:
            self.S.add(eng, lambda e: e.tensor_scalar(out=out, in0=in0, scalar1=s1, scalar2=None,
                                                      op0=op0, **kw), r=r, w=w)
        else:
            self.S.add(eng, lambda e: e.tensor_scalar(out=out, in0=in0, scalar1=s1, scalar2=s2,
                                                      op0=op0, op1=op1, **kw), r=r, w=w)

    def stt(self, out, in0, scalar, in1, op0, op1, r=(), w=()):
        self.S.add("dve", lambda e: e.scalar_tensor_tensor(out=out, in0=in0, scalar=scalar, in1=in1,
                                                           op0=op0, op1=op1), r=r, w=w)

    def copy(self, eng, out, in_, r=(), w=()):
        if eng == "act":
            self.S.add("act", lambda e: e.activation(out=out, in_=in_, func=AF.Copy), r=r, w=w)
        else:
            self.S.add(eng, lambda e: e.tensor_copy(out=out, in_=in_), r=r, w=w)

    def mm_group(self, out, pairs, r=(), w=()):
        n = len(pairs)

        def fn(e):
            ins = None
            for i, (l, rr) in enumerate(pairs):
                ins = e.matmul(out, l, rr, start=(i == 0), stop=(i == n - 1))
            return ins
        self.S.add("pe", fn, r=r, w=w)

    def mm(self, out, lhsT, rhs, start, stop, r=(), w=()):
        self.S.add("pe", lambda e: e.matmul(out, lhsT, rhs, start=start, stop=stop), r=r, w=w)

    def transpose(self, out, in_, ident, r=(), w=()):
        self.S.add("pe", lambda e: e.transpose(out, in_, ident), r=r, w=w)


def build_program(debug=None, phases=None, feed=None):
    B = Builder(debug=debug, phases=phases, feed=feed)
    nc = B.nc
    S = B.S
    ALL = {"seq", "cache", "lru", "own", "attn", "mix", "ffn"}
    ph = set(phases) if phases is not None else ALL
    full = ph == ALL

    _in = {}

    def inp(name, shape):
        if name not in _in:
            _in[name] = B.din(name, shape)
        return _in[name]

    KT_scr = B.dscr("KT_scr", [NKV, 128, SEQ], BF16)
    V_scr = B.dscr("V_scr", [SEQ, NKV * HD], BF16)
    ikT_scr = B.dscr("ikT_scr", [128, SEQ], BF16)
    KT_s = B.dscr("KT_s", [2, NKV, 128, SS], BF16)
    V_s = B.dscr("V_s", [2, SS, NKV * HD], BF16)
    ikT_s = B.dscr("ikT_s", [2, 128, SS], BF16)
    xlT_scr = B.dscr("xlT_scr", [KC, 128, SEQ + 64], F32)
    hown_scr = B.dscr("hown_scr", [KC, 128, TO], F32)
    QT_scr = B.dscr("QT_scr", [NH, 128, TO], BF16)
    iqT_scr = B.dscr("iqT_scr", [NH, 128, TO], BF16)
    iw_scr = B.dscr("iw_scr", [TO, 32], F32)
    olruT_scr = B.dscr("olruT_scr", [KC, 128, TO], BF16)
    saT_scr = B.dscr("saT_scr", [KC, 128, TO], F32)
    slT_scr = B.dscr("slT_scr", [KC, 128, TO], F32)
    oattT_scr = B.dscr("oattT_scr", [NH, 128, TO], BF16)
    mixT_scr = B.dscr("mixT_scr", [KC, 128, TO], BF16)
    x1_scr = B.dscr("x1_scr", [TO, D], F32)
    hT_scr = B.dscr("hT_scr", [DFF // 128, 128, TO], BF16)

    with ExitStack() as es:
        ARENA_KB = 206
        arena_t = es.enter_context(nc.sbuf_tensor("arena", [128, ARENA_KB * 256], F32))
        A = Arena(arena_t, ARENA_KB * 256)
        psum = [es.enter_context(nc.psum_tensor(f"ps{i}", [128, 512], F32)) for i in range(8)]
        esem = {e: es.enter_context(nc.semaphore(f"e_{e}")) for e in Sched.CE}
        dsem = {e: [es.enter_context(nc.semaphore(f"d_{e}{i}")) for i in range(Sched.NDSEM)]
                for e in ("act", "pool", "sp")}
        dsem["pe"] = dsem["sp"]
        dsem["dve"] = dsem["sp"]

        ident_in = inp("ident", [128, 128])
        ident_b = A.alloc([128], BF16)
        ident_f = A.alloc([128], F32)
        B.dma("pool", ident_b, ident_in, w=["ident_b"])
        B.dma("sp", ident_f, ident_in, w=["ident_f"])
        eps_t = A.alloc([1], F32)
        S.add("dve", lambda e: e.memset(eps_t, EPS), w=["eps"])
        one_t = A.alloc([1], F32)
        S.add("dve", lambda e: e.memset(one_t, 1.0), w=["one"])
        g4 = {}

        def load_g4(nm):
            src = inp("norm_" + nm, [HD])
            t = A.alloc([4, 128], F32)
            for j in range(4):
                B.dma("sp", t[:, j, :], src.partition_broadcast(128), w=[("g4", nm, j)])
            g4[nm] = t

        if "seq" in ph:
            load_g4("k")
            load_g4("idx_k")
        if "own" in ph:
            load_g4("q")
        S.barrier()

        psn = [0]
        nps = [4]

        def next_ps():
            pb = psn[0] % nps[0]
            psn[0] += 1
            return pb

        def norm_transpose(blocks, xnT, gT, xt, junk, ssb, extra=None):
            t0 = 0
            for bi, (src, rows) in enumerate(blocks):
                B.dma("sp", xt[0:rows], src, w=["xt"])
                if extra is not None:
                    extra(bi, rows)
                ss = ssb[bi % 2]
                sk = ("ssb", bi % 2)
                S.add("act", lambda e, rows=rows, ss=ss: e.activation(
                    out=junk[0:rows], in_=xt[0:rows], func=AF.Square, accum_out=ss[0:rows]),
                    r=["xt"], w=["junk", sk])
                B.act(ss[0:rows], ss[0:rows], AF.Sqrt, r=[sk, "eps"], w=[sk], scale=1.0 / D,
                      bias=eps_t[0:rows])
                S.add("dve", lambda e, ss=ss, rows=rows: e.reciprocal(out=ss[0:rows], in_=ss[0:rows]),
                      r=[sk], w=[sk])
                B.ts(xt[0:rows], xt[0:rows], ss[0:rows], None, ALU.mult, r=["xt", sk], w=["xt"])
                for j in range(8):
                    pb = 4 + (j % 2)
                    pv = psum[pb][:, :].rearrange("p (a b) -> p a b", a=4)
                    for q in range(4):
                        kc = j * 4 + q
                        B.transpose(pv[:, q, 0:rows], xt[0:rows, kc * 128:(kc + 1) * 128],
                                    ident_f[0:rows, 0:rows], r=["xt", "ident_f"], w=[("ps", pb)])
                    B.tt(xnT[:, j * 4:(j + 1) * 4, t0:t0 + rows], pv[:, :, 0:rows],
                         gT[:, j * 4:(j + 1) * 4].unsqueeze(2).to_broadcast([128, 4, rows]),
                         ALU.mult, r=[("ps", pb), "gT"], w=[("AT", bi)])
                t0 += rows

        wl = {}

        def load_w(wbufs, src, kcn, ncols):
            wid = id(wbufs)
            slot = wl.get(wid, 0) % len(wbufs)
            wl[wid] = wl.get(wid, 0) + 1
            slot = (wid, slot)
            v = src.rearrange("(kc p) n -> p kc n", p=128)
            step = 8 if ncols > 256 else 16
            for k0 in range(0, kcn, step):
                k1 = min(kcn, k0 + step)
                B.dma("pool", wbufs[slot[1]][:, k0:k1, 0:ncols], v[:, k0:k1, :], w=[("w", slot, k0 // 8)] +
                      ([("w", slot, k0 // 8 + 1)] if step == 16 else []))
            return slot

        def wtoks(slot, kcn):
            return [("w", slot, k) for k in range((kcn + 7) // 8)]

        def gemm_tok(AT, blocks, kcn, wbufs, wsrc, col_groups, evac, attoks=None):
            pend_ev = []
            nxt = load_w(wbufs, wsrc(col_groups[0][0], col_groups[0][1]), kcn, col_groups[0][1])
            for gi, (c0, ncols, tag) in enumerate(col_groups):
                slot = nxt
                if gi + 1 < len(col_groups):
                    nxt = load_w(wbufs, wsrc(col_groups[gi + 1][0], col_groups[gi + 1][1]), kcn, col_groups[gi + 1][1])
                t0 = 0
                for bi, rows in enumerate(blocks):
                    pb = next_ps()
                    ps = psum[pb][0:rows, 0:ncols]
                    B.mm_group(ps, [(AT[:, kc, t0:t0 + rows], wbufs[slot[1]][:, kc, 0:ncols])
                                    for kc in range(kcn)],
                               r=(attoks if attoks is not None else [("AT", bi)]) + wtoks(slot, kcn),
                               w=[("ps", pb)])
                    if pend_ev:
                        evac(*pend_ev.pop(0))
                    pend_ev.append((tag, bi, rows, t0, ps, ("ps", pb), ncols))
                    t0 += rows
            while pend_ev:
                evac(*pend_ev.pop(0))

        def gemm_feat(srcs, nblk, kcn, ncols_total, cgw, ttiles, evac, attoks=None):
            nxts = [load_w(wb, wsrc(0, cgw), kcn, cgw) for (AT, wb, wsrc) in srcs]
            for c0 in range(0, ncols_total, cgw):
                slots = nxts
                if c0 + cgw < ncols_total:
                    nxts = [load_w(wb, wsrc(c0 + cgw, cgw), kcn, cgw) for (AT, wb, wsrc) in srcs]
                for ch in range(cgw // 128):
                    chunk = (c0 // 128) + ch
                    for (tt0, n) in ttiles:
                        pss = []
                        for (AT, wb, wsrc), slot in zip(srcs, slots):
                            pb = next_ps()
                            ps = psum[pb][:, 0:n]
                            B.mm_group(ps, [(wb[slot[1]][:, kc, ch * 128:(ch + 1) * 128], AT[:, kc, tt0:tt0 + n])
                                            for kc in range(kcn)],
                                       r=(attoks if attoks is not None else [("AT", bi) for bi in range(nblk)])
                                       + wtoks(slot, kcn), w=[("ps", pb)])
                            pss.append((ps, ("ps", pb)))
                        evac(chunk, tt0, n, pss)

        stc = [0]

        def make_headproc(NST):
            st = dict(
                f=[A.alloc([512], F32) for _ in range(NST)],
                t=[A.alloc([512], F32) for _ in range(NST)],
                b=[A.alloc([512], BF16) for _ in range(NST)],
                T=[A.alloc([4, 128], BF16) for _ in range(NST)],
                s=[A.alloc([4], F32) for _ in range(NST)],
                r=[A.alloc([4, 16], F32) for _ in range(4 * NST)],
                n=NST)
            return st

        def headproc(st, ps, pstok, rows, ncols, gname, cs, cstok, dst_out, dstT_fn, rope=True):
            nh = ncols // 128
            i = stc[0] % st["n"]
            stc[0] += 1
            f, t, b_, T_, s_ = st["f"][i], st["t"][i], st["b"][i], st["T"][i], st["s"][i]
            r4 = st["r"][4 * i:4 * i + 4]
            tk = ("st", i)
            B.copy("act", f[0:rows, 0:ncols], ps, r=[pstok], w=[tk])
            fv = f[0:rows, 0:ncols].rearrange("p (h d) -> p h d", h=nh)
            tv = t[0:rows, 0:ncols].rearrange("p (h d) -> p h d", h=nh)
            if gname is not None:
                for h in range(nh):
                    S.add("act", lambda e, h=h: e.activation(out=t[0:rows, h * 128:(h + 1) * 128],
                                                            in_=f[0:rows, h * 128:(h + 1) * 128], func=AF.Square,
                                                            accum_out=s_[0:rows, h:h + 1]), r=[tk], w=[tk])
                B.act(s_[0:rows, 0:nh], s_[0:rows, 0:nh], AF.Sqrt, r=[tk, "eps"], w=[tk], scale=1.0 / 128,
                      bias=eps_t[0:rows])
                S.add("dve", lambda e: e.reciprocal(out=s_[0:rows, 0:nh], in_=s_[0:rows, 0:nh]), r=[tk], w=[tk])
                B.tt(fv, fv, s_[0:rows, 0:nh].unsqueeze(2).to_broadcast([rows, nh, 128]), ALU.mult,
                     r=[tk], w=[tk])
                B.tt(fv, fv, g4[gname][0:rows, 0:nh, :], ALU.mult,
                     r=[tk] + [("g4", gname, j) for j in range(4)], w=[tk])
            if rope:
                cb = cs[0:rows, 0:16].unsqueeze(1).to_broadcast([rows, nh, 16])
                sb = cs[0:rows, 16:32].unsqueeze(1).to_broadcast([rows, nh, 16])
                x1 = fv[:, :, 0:16]
                x2 = fv[:, :, 16:32]
                ra, rb, rc, rd = [q[0:rows, 0:nh, :] for q in r4]
                rt = [tk, cstok]
                tkp = ("stp", i)
                tkd = ("std", i)
                B.tt(ra, x1, cb, ALU.mult, r=rt, w=[tkp], eng="pool")
                B.tt(rb, x2, sb, ALU.mult, r=rt, w=[tkp], eng="pool")
                B.tt(rc, x2, cb, ALU.mult, r=rt, w=[tkd])
                B.tt(rd, x1, sb, ALU.mult, r=rt, w=[tkd])
                B.tt(x1, ra, rb, ALU.subtract, r=[tkp, tkd], w=[tk], eng="pool")
                B.tt(x2, rc, rd, ALU.add, r=[tkd, tkp], w=[tk])
            if dst_out is not None:
                B.dma("sp", dst_out, f[0:rows, 0:ncols], r=[tk], w=[B.fresh("o")])
            if dstT_fn is not None:
                B.copy("act", b_[0:rows, 0:ncols], f[0:rows, 0:ncols], r=[tk], w=[tk])
                pb = 6 + (stc[0] % 2)
                pT = psum[pb][:, :].bitcast(BF16)[:, 0:512].rearrange("p (a b) -> p a b", a=4)
                for h in range(nh):
                    B.transpose(pT[:, h, 0:rows], b_[0:rows, h * 128:(h + 1) * 128],
                                ident_b[0:rows, 0:rows], r=[tk, "ident_b"], w=[("ps", pb)])
                B.copy("dve", T_[:, 0:nh, 0:rows], pT[:, 0:nh, 0:rows], r=[("ps", pb)], w=[tk])
                dstT_fn(T_, nh, rows, tk)

        if "seq" in ph:
            xseq = inp("xseq", [SEQ, D])
            xown = inp("xown", [TO, D])
            w_in = inp("w_in", [D, IN_COLS])
            cs_seq = inp("cs_seq", [SEQ + 64, 32])
            o_k = B.dout("o_k", [SEQ + 64, NKV * HD])
            o_v = B.dout("o_v", [SEQ + 64, NKV * HD])
            o_ik = B.dout("o_ik", [SEQ + 64, HD])
            m0 = A.mark()
            gmixT = A.alloc([KC], F32)
            B.dma("sp", gmixT, inp("norm_mixT", [128, KC]), w=["gT"])
            xnT = A.alloc([KC, TO], BF16)
            wbufs = [A.alloc([KC, 512], BF16) for _ in range(2)]
            xt = A.alloc([D], F32)
            junk = A.alloc([D], BF16)
            ssb = [A.alloc([1], F32) for _ in range(2)]
            st = make_headproc(3)
            cst = [A.alloc([32], F32) for _ in range(9)]
            xl_st = [A.alloc([TO], F32) for _ in range(2)]

            for tt in range(4):
                blocks = [(xseq[tt * 1024 + j * 128: tt * 1024 + (j + 1) * 128, :], 128) for j in range(8)]
                orows = [tt * 1024 + j * 128 for j in range(8)]
                if tt == 3:
                    blocks.append((xown[1024:1088, :], 64))
                    orows.append(SEQ)
                ntok = sum(b[1] for b in blocks)

                def extra(bi, rows, orows=orows):
                    B.dma("sp", cst[bi][0:rows], cs_seq[orows[bi]:orows[bi] + rows, :], w=[("cst", bi)])
                norm_transpose(blocks, xnT, gmixT, xt, junk, ssb, extra)

                def evac(tag, bi, rows, t0, ps, pstok, ncols, orows=orows):
                    kind, half = tag
                    orow = orows[bi]
                    is_s = orow >= SEQ
                    if kind == "k":
                        def dstT(T_, nh, rows_, tk):
                            if not is_s:
                                B.dma("act", KT_scr[half * 4:half * 4 + 4, :, orow:orow + rows_]
                                      .rearrange("h d t -> d h t"), T_[:, 0:4, 0:rows_], r=[tk], w=[B.fresh("kt")])
                            else:
                                for sq in range(2):
                                    B.dma("act", KT_s[sq, half * 4:half * 4 + 4, :, PAST:SS]
                                          .rearrange("h d t -> d h t"), T_[:, 0:4, sq * 32:(sq + 1) * 32],
                                          r=[tk], w=[B.fresh("kt")])
                        headproc(st, ps, pstok, rows, ncols, "k", cst[bi], ("cst", bi),
                                 o_k[orow:orow + rows, half * 512:(half + 1) * 512], dstT)
                    elif kind == "ik":
                        def dstT(T_, nh, rows_, tk):
                            if not is_s:
                                B.dma("act", ikT_scr[:, orow:orow + rows_], T_[:, 0, 0:rows_], r=[tk],
                                      w=[B.fresh("ikt")])
                            else:
                                for sq in range(2):
                                    B.dma("act", ikT_s[sq, :, PAST:SS], T_[:, 0, sq * 32:(sq + 1) * 32],
                                          r=[tk], w=[B.fresh("ikt")])
                        headproc(st, ps, pstok, rows, ncols, "idx_k", cst[bi], ("cst", bi),
                                 o_ik[orow:orow + rows, :], dstT)
                    else:
                        i = stc[0] % st["n"]
                        stc[0] += 1
                        tk = ("st", i)
                        B.copy("act", st["f"][i][0:rows, 0:512], ps, r=[pstok], w=[tk])
                        B.copy("dve", st["b"][i][0:rows, 0:512], ps, r=[pstok], w=[tk])
                        B.dma("sp", o_v[orow:orow + rows, half * 512:(half + 1) * 512],
                              st["f"][i][0:rows, 0:512], r=[tk], w=[B.fresh("o")])
                        if not is_s:
                            B.dma("act", V_scr[orow:orow + rows, half * 512:(half + 1) * 512],
                                  st["b"][i][0:rows, 0:512], r=[tk], w=[B.fresh("v")])
                        else:
                            for sq in range(2):
                                B.dma("act", V_s[sq, PAST:SS, half * 512:(half + 1) * 512],
                                      st["b"][i][sq * 32:(sq + 1) * 32, 0:512], r=[tk], w=[B.fresh("v")])

                gemm_tok(xnT, [b[1] for b in blocks], KC, wbufs, lambda c0, n: w_in[:, c0:c0 + n],
                         [(C_K, 512, ("k", 0)), (C_K + 512, 512, ("k", 1)), (C_V, 512, ("v", 0)),
                          (C_V + 512, 512, ("v", 1)), (C_IK, 128, ("ik", 0))], evac)

                ttiles = [(0, 512), (512, 512)] + ([(1024, 64)] if tt == 3 else [])
                col0 = tt * 1024

                def evac_xl(chunk, tt0, n, pss, ntok=ntok, col0=col0, last=ttiles[-1][0]):
                    xs = xl_st[chunk % 2]
                    ps, pstok = pss[0]
                    B.copy("act" if (tt0 // 512) % 2 else "dve", xs[:, tt0:tt0 + n], ps, r=[pstok],
                           w=[("xls", chunk % 2)])
                    if tt0 == last:
                        B.dma("sp", xlT_scr[chunk, :, col0:col0 + ntok], xs[:, 0:ntok],
                              r=[("xls", chunk % 2)], w=[B.fresh("xl")])
                gemm_feat([(xnT, wbufs, lambda c0, n: w_in[:, C_XL + c0:C_XL + c0 + n])], len(blocks), KC,
                          4096, 512, ttiles, evac_xl)
            S.barrier()
            A.release(m0)

        if "cache" in ph:
            cache_k = inp("cache_k", [2, PAST, NKV * HD])
            cache_v = inp("cache_v", [2, PAST, NKV * HD])
            cache_ik = inp("cache_ik", [2, PAST, HD])
            m0 = A.mark()
            kb = A.alloc([16, 1024], BF16)
            vb = A.alloc([16, 1024], BF16)
            ib = A.alloc([16, 128], BF16)
            stg = [A.alloc([8, 128], BF16) for _ in range(2)]
            istg = A.alloc([16, 128], BF16)
            for sq in range(2):
                for q in range(4):
                    B.dma("pool", kb[:, q * 4:(q + 1) * 4, :],
                          cache_k[sq, q * 512:(q + 1) * 512, :].rearrange("(b p) c -> p b c", p=128), w=[("kb", q)])
                    B.dma("pool", vb[:, q * 4:(q + 1) * 4, :],
                          cache_v[sq, q * 512:(q + 1) * 512, :].rearrange("(b p) c -> p b c", p=128), w=[("vb", q)])
                B.dma("pool", ib, cache_ik[sq].rearrange("(b p) c -> p b c", p=128), w=["ib"])
                for q in range(4):
                    B.dma("act", V_s[sq, q * 512:(q + 1) * 512, :].rearrange("(b p) c -> p b c", p=128),
                          vb[:, q * 4:(q + 1) * 4, :], r=[("vb", q)], w=[B.fresh("vs")])
                for blk in range(16):
                    pb = 6 + (blk % 2)
                    pT = psum[pb][:, :].bitcast(BF16).rearrange("p (a b) -> p a b", a=8)
                    for h in range(8):
                        B.transpose(pT[:, h, :], kb[:, blk, h * 128:(h + 1) * 128], ident_b,
                                    r=[("kb", blk // 4), "ident_b"], w=[("ps", pb)])
                    sg = stg[blk % 2]
                    B.copy("dve" if blk % 2 else "act", sg, pT, r=[("ps", pb)], w=[("stg", blk % 2)])
                    B.dma("sp", KT_s[sq, :, :, blk * 128:(blk + 1) * 128].rearrange("h d t -> d h t"), sg,
                          r=[("stg", blk % 2)], w=[B.fresh("kts")])
                for half in range(2):
                    pb = 4 + half
                    pT = psum[pb][:, :].bitcast(BF16).rearrange("p (a b) -> p a b", a=8)
                    for j in range(8):
                        B.transpose(pT[:, j, :], ib[:, half * 8 + j, :], ident_b, r=["ib", "ident_b"],
                                    w=[("ps", pb)])
                    B.copy("dve", istg[:, half * 8:(half + 1) * 8, :], pT, r=[("ps", pb)], w=["istg"])
                B.dma("sp", ikT_s[sq, :, 0:PAST], istg, r=["istg"], w=[B.fresh("ikts")])
            S.barrier()
            A.release(m0)

        if "lru" in ph:
            convwT = inp("convwT", [128, 4, KC])
            convbT = inp("convbT", [128, KC])
            baT = inp("lru_baT", [128, KC])
            bxT = inp("lru_bxT", [128, KC])
            lamT = inp("lru_lamT", [128, KC])
            lru_wa = inp("lru_wa", [16, 256, 256])
            lru_wx = inp("lru_wx", [16, 256, 256])
            st_lruT = inp("state_lruT", [2, 128, KC])
            st_convT = inp("state_convT", [2, 128, KC, 3])
            selr = inp("selr", [128, 4])
            o_lru = B.dout("o_lru", [3, D])
            o_conv = B.dout("o_conv", [3, 3, D])
            m0 = A.mark()
            cw = A.alloc([4, KC], F32)
            cb = A.alloc([KC], F32)
            ba = A.alloc([KC], F32)
            bx = A.alloc([KC], F32)
            lam = A.alloc([KC], F32)
            cneg = A.alloc([KC], F32)
            cneg2 = A.alloc([KC], F32)
            tmpc = A.alloc([KC], F32)
            sel = A.alloc([4], F32)
            h0s = A.alloc([2, KC], F32)
            c0s = A.alloc([2, KC, 3], F32)
            hfin = A.alloc([3, KC], F32)
            cfin = A.alloc([3, 3, KC], F32)
            B.dma("sp", cw, convwT, w=["lc"])
            B.dma("sp", cb, convbT, w=["lc"])
            B.dma("sp", ba, baT, w=["lc"])
            B.dma("sp", bx, bxT, w=["lc"])
            B.dma("sp", lam, lamT, w=["lam"])
            B.dma("sp", sel, selr, w=["lc"])
            for sq in range(2):
                B.dma("sp", h0s[:, sq, :], st_lruT[sq], w=["lc"])
                B.dma("sp", c0s[:, sq, :, :], st_convT[sq], w=["lc"])
            B.ts(tmpc, lam, -1.0, None, ALU.mult, r=["lam"], w=["tmpc"])
            B.tt(tmpc, tmpc, lam, ALU.max, r=["lam", "tmpc"], w=["tmpc"])
            B.act(tmpc, tmpc, AF.Exp, r=["tmpc"], w=["tmpc"], scale=-1.0)
            B.act(tmpc, tmpc, AF.Ln, r=["tmpc", "one"], w=["tmpc"], bias=one_t[:, 0:1])
            B.ts(cneg, lam, -1.0, 0.0, ALU.mult, ALU.max, r=["lam"], w=["cneg"])
            B.tt(cneg, cneg, tmpc, ALU.add, r=["cneg", "tmpc"], w=["cneg"])
            B.ts(cneg2, cneg, -16.0, None, ALU.mult, r=["cneg"], w=["cneg2"])
            B.ts(cneg, cneg, -8.0, None, ALU.mult, r=["cneg"], w=["cneg"])

            XL = [[A.alloc([1027], F32) for _ in range(2)] for _ in range(2)]
            U2 = [[A.alloc([1024], F32) for _ in range(2)] for _ in range(2)]
            UB2 = [[A.alloc([1024], BF16) for _ in range(2)] for _ in range(2)]
            Rg2 = [[A.alloc([1024], F32) for _ in range(2)] for _ in range(2)]
            Ig2 = [[A.alloc([1024], F32) for _ in range(2)] for _ in range(2)]
            Aa2 = [[A.alloc([1024], F32) for _ in range(2)] for _ in range(2)]
            Hh2 = [[A.alloc([1024], F32) for _ in range(2)] for _ in range(2)]
            HO2 = [[A.alloc([256], F32) for _ in range(2)] for _ in range(2)]
            hprev = A.alloc([2], F32)
            wab = [A.alloc([2, 256], BF16) for _ in range(2)]
            wxb = [A.alloc([2, 256], BF16) for _ in range(2)]
            pieces = [(tt * 1024, 1024, "p", tt) for tt in range(4)] + [(SEQ, 32, "s", 0), (SEQ + 32, 32, "s", 1)]
            for nblk in range(16):
                wsl = nblk % 2
                B.dma("pool", wab[wsl], lru_wa[nblk].rearrange("(ch p) d -> p ch d", p=128), w=[("wa", wsl)])
                B.dma("pool", wxb[wsl], lru_wx[nblk].rearrange("(ch p) d -> p ch d", p=128), w=[("wx", wsl)])
                for pi, (col0, n, kind, idx) in enumerate(pieces):
                    xb = XL[pi % 2]
                    xprev = XL[(pi + 1) % 2]
                    pp = pi % 2
                    U, UB, Rg, Ig, Aa, Hh, HO = U2[pp], UB2[pp], Rg2[pp], Ig2[pp], Aa2[pp], Hh2[pp], HO2[pp]
                    for ch in range(2):
                        chunk = 2 * nblk + ch
                        xk = ("xl", pi % 2, ch)
                        B.dma("sp", xb[ch][:, 3:3 + n], xlT_scr[chunk, :, col0:col0 + n], w=[xk])
                        if kind == "p" and idx == 0:
                            S.add("dve", lambda e, t=xb[ch]: e.memset(t[:, 0:3], 0.0), w=[xk])
                        elif kind == "p":
                            B.copy("dve", xb[ch][:, 0:3], xprev[ch][:, 1024:1027],
                                   r=[("xl", (pi + 1) % 2, ch)], w=[xk])
                        else:
                            B.copy("dve", xb[ch][:, 0:3], c0s[:, idx, chunk, :], r=["lc"], w=[xk])
                        u = U[ch]
                        uk = ("u", pp, ch)
                        B.ts(u[:, 0:n], xb[ch][:, 3:3 + n], cw[:, 3, chunk:chunk + 1], cb[:, chunk:chunk + 1],
                             ALU.mult, ALU.add, r=[xk, "lc"], w=[uk])
                        for j in (2, 1, 0):
                            B.stt(u[:, 0:n], xb[ch][:, j:j + n], cw[:, j, chunk:chunk + 1], u[:, 0:n],
                                  ALU.mult, ALU.add, r=[xk, "lc", uk], w=[uk])
                        B.copy("act", UB[ch][:, 0:n], u[:, 0:n], r=[uk], w=[("ub", pp, ch)])
                    halves = [(0, min(n, 512))] + ([(512, 512)] if n > 512 else [])
                    for dh in range(2):
                        chunk = 2 * nblk + dh
                        for (gbuf, wbuf_, bias, gk, wk) in ((Rg, wab, ba, "rg", "wa"), (Ig, wxb, bx, "ig", "wx")):
                            for (h0, hn) in halves:
                                pb = next_ps()
                                ps = psum[pb][:, 0:hn]
                                B.mm_group(ps, [(wbuf_[wsl][:, ch, dh * 128:(dh + 1) * 128], UB[ch][:, h0:h0 + hn])
                                                for ch in range(2)],
                                           r=[("ub", pp, 0), ("ub", pp, 1), (wk, wsl)], w=[("ps", pb)])
                                B.act(gbuf[dh][:, h0:h0 + hn], ps, AF.Sigmoid, r=[("ps", pb), "lc"],
                                      w=[(gk, pp, dh)], bias=bias[:, chunk:chunk + 1])
                    for dh in range(2):
                        chunk = 2 * nblk + dh
                        B.act(Aa[dh][:, 0:n], Rg[dh][:, 0:n], AF.Exp, r=[("rg", pp, dh), "cneg"], w=[("aa", pp, dh)],
                              scale=cneg[:, chunk:chunk + 1])
                        B.act(Rg[dh][:, 0:n], Rg[dh][:, 0:n], AF.Exp, r=[("rg", pp, dh), "cneg2"], w=[("rg", pp, dh)],
                              scale=cneg2[:, chunk:chunk + 1])
                    for dh in range(2):
                        B.ts(Rg[dh][:, 0:n], Rg[dh][:, 0:n], 1.0, -1.0, ALU.min, ALU.mult, r=[("rg", pp, dh)],
                             w=[("rg", pp, dh)])
                        B.act(Rg[dh][:, 0:n], Rg[dh][:, 0:n], AF.Sqrt, r=[("rg", pp, dh), "one"], w=[("rg", pp, dh)],
                              scale=1.0, bias=one_t[:, 0:1])
                    for dh in range(2):
                        chunk = 2 * nblk + dh
                        B.tt(Ig[dh][:, 0:n], Ig[dh][:, 0:n], U[dh][:, 0:n], ALU.mult, r=[("ig", pp, dh), ("u", pp, dh)],
                             w=[("ig", pp, dh)])
                        B.tt(Ig[dh][:, 0:n], Ig[dh][:, 0:n], Rg[dh][:, 0:n], ALU.mult, r=[("ig", pp, dh), ("rg", pp, dh)],
                             w=[("ig", pp, dh)])
                        if kind == "p" and idx == 0:
                            init = 0.0
                            ir = []
                        elif kind == "p":
                            init = hprev[:, dh:dh + 1]
                            ir = [("hp", dh)]
                        else:
                            init = h0s[:, idx, chunk:chunk + 1]
                            ir = ["lc"]
                        S.add("dve", lambda e, dh=dh, n=n, init=init, Hh=Hh, Aa=Aa, Ig=Ig: e.tensor_tensor_scan(
                            out=Hh[dh][:, 0:n], data0=Aa[dh][:, 0:n], data1=Ig[dh][:, 0:n], initial=init,
                            op0=ALU.mult, op1=ALU.add), r=[("aa", pp, dh), ("ig", pp, dh)] + ir, w=[("hh", pp, dh)])
                        if kind == "p" and idx < 3:
                            B.copy("dve", hprev[:, dh:dh + 1], Hh[dh][:, n - 1:n], r=[("hh", pp, dh)], w=[("hp", dh)])
                        if kind == "s" or idx == 3:
                            row = 0 if kind == "p" else 1 + idx
                            B.copy("dve", hfin[:, row, chunk:chunk + 1], Hh[dh][:, n - 1:n], r=[("hh", pp, dh)],
                                   w=["hfin"])
                            for j in range(3):
                                B.copy("dve", cfin[:, row, j, chunk:chunk + 1],
                                       XL[pi % 2][dh][:, n + j:n + j + 1], r=[("xl", pi % 2, dh)], w=["cfin"])
                        ho = HO[dh]
                        hk = ("ho", pp, dh)
                        if kind == "p":
                            for il in range(2):
                                B.ts(ho[:, il * 128:(il + 1) * 128], Hh[dh][:, (4 * il) * 128:(4 * il + 1) * 128],
                                     sel[:, 0:1], None, ALU.mult, r=[("hh", pp, dh), "lc"], w=[hk])
                                for j in range(1, 4):
                                    B.stt(ho[:, il * 128:(il + 1) * 128],
                                          Hh[dh][:, (4 * il + j) * 128:(4 * il + j + 1) * 128], sel[:, j:j + 1],
                                          ho[:, il * 128:(il + 1) * 128], ALU.mult, ALU.add,
                                          r=[("hh", pp, dh), "lc", hk], w=[hk])
                            B.dma("act", hown_scr[chunk, :, idx * 256:(idx + 1) * 256], ho, r=[hk], w=[B.fresh("ho")])
                        else:
                            B.dma("act", hown_scr[chunk, :, 1024 + idx * 32:1024 + (idx + 1) * 32], Hh[dh][:, 0:32],
                                  r=[("hh", pp, dh)], w=[B.fresh("ho")])
            fst = A.alloc([12, 128], F32)
            pbv = psum[4][:, :].rearrange("p (a b) -> p a b", a=4)
            pbv2 = psum[5][:, :].rearrange("p (a b) -> p a b", a=4)
            pbv3 = psum[6][:, :].rearrange("p (a b) -> p a b", a=4)
            for row in range(3):
                pv = (pbv, pbv2, pbv3)[row]
                pk = ("ps", 4 + row)
                B.transpose(pv[0:32, 0, :], hfin[:, row, :], ident_f, r=["hfin", "ident_f"], w=[pk])
                for j in range(3):
                    B.transpose(pv[0:32, 1 + j, :], cfin[:, row, j, :], ident_f, r=["cfin", "ident_f"], w=[pk])
                B.copy("dve", fst[0:32, row * 4:(row + 1) * 4, :], pv[0:32, :, :], r=[pk], w=["fst"])
                B.dma("sp", o_lru[row].rearrange("(kc p) -> kc p", p=128), fst[0:32, row * 4, :], r=["fst"],
                      w=[B.fresh("o")])
                for j in range(3):
                    B.dma("sp", o_conv[row, j].rearrange("(kc p) -> kc p", p=128), fst[0:32, row * 4 + 1 + j, :],
                          r=["fst"], w=[B.fresh("o")])
            S.barrier()
            A.release(m0)

        if "own" in ph:
            xown = inp("xown", [TO, D])
            w_in = inp("w_in", [D, IN_COLS])
            cs_own = inp("cs_own", [TO, 32])
            m0 = A.mark()
            gmixT = A.alloc([KC], F32)
            B.dma("sp", gmixT, inp("norm_mixT", [128, KC]), w=["gT"])
            xnT = A.alloc([KC, TO], BF16)
            wbufs = [A.alloc([KC, 512], BF16) for _ in range(2)]
            xt = A.alloc([D], F32)
            junk = A.alloc([D], BF16)
            ssb = [A.alloc([1], F32) for _ in range(2)]
            st = make_headproc(3)
            cst = [A.alloc([32], F32) for _ in range(9)]
            blocks = [(xown[j * 128:(j + 1) * 128, :], 128) for j in range(8)] + [(xown[1024:1088, :], 64)]
            brow = [b[1] for b in blocks]

            def extra(bi, rows):
                B.dma("sp", cst[bi][0:rows], cs_own[bi * 128:bi * 128 + rows, :], w=[("cst", bi)])
            norm_transpose(blocks, xnT, gmixT, xt, junk, ssb, extra)

            def evac(tag, bi, rows, t0, ps, pstok, ncols):
                kind, g = tag
                if kind == "q":
                    def dstT(T_, nh, rows_, tk):
                        B.dma("act", QT_scr[g * 4:g * 4 + 4, :, t0:t0 + rows_].rearrange("h d t -> d h t"),
                              T_[:, 0:4, 0:rows_], r=[tk], w=[B.fresh("qt")])
                    headproc(st, ps, pstok, rows, ncols, "q", cst[bi], ("cst", bi), None, dstT)
                elif kind == "iq":
                    def dstT(T_, nh, rows_, tk):
                        B.dma("act", iqT_scr[g * 4:g * 4 + 4, :, t0:t0 + rows_].rearrange("h d t -> d h t"),
                              T_[:, 0:4, 0:rows_], r=[tk], w=[B.fresh("iqt")])
                    headproc(st, ps, pstok, rows, ncols, None, cst[bi], ("cst", bi), None, dstT)
                else:
                    i = stc[0] % st["n"]
                    stc[0] += 1
                    tk = ("st", i)
                    B.copy("act", st["f"][i][0:rows, 0:32], ps, r=[pstok], w=[tk])
                    B.dma("sp", iw_scr[t0:t0 + rows, :], st["f"][i][0:rows, 0:32], r=[tk], w=[B.fresh("iw")])
            groups = [(C_Q + g * 512, 512, ("q", g)) for g in range(8)] + \
                     [(C_IQ + g * 512, 512, ("iq", g)) for g in range(8)] + [(C_IW, 32, ("iw", 0))]
            gemm_tok(xnT, brow, KC, wbufs, lambda c0, n: w_in[:, c0:c0 + n], groups, evac)

            ttiles = [(0, 512), (512, 512), (1024, 64)]
            NE = 2
            ey = [A.alloc([512], F32) for _ in range(NE)]
            et = [A.alloc([512], F32) for _ in range(NE)]
            eh = [A.alloc([512], F32) for _ in range(NE)]
            eb = [A.alloc([512], BF16) for _ in range(NE)]
            ec = [0]

            def evac_feat(which):
                def ev(chunk, tt0, n, pss):
                    ps, pstok = pss[0]
                    i = ec[0] % NE
                    ec[0] += 1
                    tk = ("e", i)
                    if which == "yl":
                        y, t, hh, ob = ey[i], et[i], eh[i], eb[i]
                        B.dma("sp", hh[:, 0:n], hown_scr[chunk, :, tt0:tt0 + n], w=[("eh", i)])
                        B.copy("act", y[:, 0:n], ps, r=[pstok], w=[tk])
                        B.tt(t[:, 0:n], y[:, 0:n], y[:, 0:n], ALU.mult, r=[tk], w=[tk])
                        B.ts(t[:, 0:n], t[:, 0:n], 0.044715, 1.0, ALU.mult, ALU.add, r=[tk], w=[tk])
                        B.tt(t[:, 0:n], t[:, 0:n], y[:, 0:n], ALU.mult, r=[tk], w=[tk])
                        B.act(t[:, 0:n], t[:, 0:n], AF.Sigmoid, r=[tk], w=[tk], scale=1.5957691216057308)
                        B.tt(t[:, 0:n], t[:, 0:n], y[:, 0:n], ALU.mult, r=[tk], w=[tk])
                        B.tt(ob[:, 0:n], t[:, 0:n], hh[:, 0:n], ALU.mult, r=[tk, ("eh", i)], w=[tk])
                        B.dma("act", olruT_scr[chunk, :, tt0:tt0 + n], ob[:, 0:n], r=[tk], w=[B.fresh("ol")])
                    else:
                        y = ey[i]
                        B.act(y[:, 0:n], ps, AF.Sigmoid, r=[pstok], w=[tk])
                        dst = saT_scr if which == "ga" else slT_scr
                        B.dma("act", dst[chunk, :, tt0:tt0 + n], y[:, 0:n], r=[tk], w=[B.fresh("sg")])
                return ev
            for which, cbase in (("yl", C_YL), ("ga", C_GA), ("gl", C_GL)):
                gemm_feat([(xnT, wbufs, lambda c0, n, cbase=cbase: w_in[:, cbase + c0:cbase + c0 + n])], 9, KC,
                          4096, 512, ttiles, evac_feat(which))
            S.barrier()
            A.release(m0)

        if "attn" in ph:
            dsel_in = inp("dsel", [128, 32 * 128])
            mbias_in = inp("maskbias", [128, 512])
            m0 = A.mark()
            dselt = A.alloc([32, 128], BF16)
            dv = dsel_in.rearrange("p (g t) -> p g t", g=32)
            for g0 in range(0, 32, 8):
                B.dma("pool", dselt[:, g0:g0 + 8, :], dv[:, g0:g0 + 8, :], w=[("dsel", g0)])
            S.add("dve", lambda e: e.memset(eps_t, EPS), r=[("dsel", g0) for g0 in range(0, 32, 8)], w=["dsel"])
            mbias = A.alloc([512], F32)
            B.dma("sp", mbias, mbias_in, w=["mbias"])
            ones_b = A.alloc([128], BF16)
            S.add("dve", lambda e: e.memset(ones_b, 1.0), w=["ones"])
            gq = A.alloc([128], F32)
            gk = A.alloc([128], F32)
            B.dma("sp", gq, inp("norm_q", [HD]).partition_broadcast(128), w=["gq"])
            B.dma("sp", gk, inp("norm_k", [HD]).partition_broadcast(128), w=["gk"])
            cq = A.alloc([1], F32)
            ck = A.alloc([1], F32)
            S.add("dve", lambda e: e.tensor_reduce(out=cq, in_=gq, axis=AX.X, op=ALU.max, apply_absolute_value=True),
                  r=["gq"], w=["cq"])
            S.add("dve", lambda e: e.tensor_reduce(out=ck, in_=gk, axis=AX.X, op=ALU.max, apply_absolute_value=True),
                  r=["gk"], w=["ck"])
            B.tt(cq, cq, ck, ALU.mult, r=["cq", "ck"], w=["cq"])
            B.ts(cq, cq, -math.sqrt(128.0), None, ALU.mult, r=["cq"], w=["cq"])

            iqTb = A.alloc([32, 128], BF16)
            iqTg = A.alloc([32, 128], BF16)
            ikT = A.alloc([SEQ], BF16)
            wsel = A.alloc([32, 128], BF16)
            iwb = A.alloc([32], F32)
            iwrep = A.alloc([32, 4], BF16)
            R1 = [A.alloc([512], BF16) for _ in range(3)]
            ImB = [A.alloc([SEQ], F32) for _ in range(2)]
            Wk = A.alloc([SEQ], F32)
            m8 = A.alloc([8], F32)
            thrB = [A.alloc([1], F32) for _ in range(2)]
            maskb = A.alloc([SEQ], BF16)
            maskTB = [A.alloc([32, 128], BF16) for _ in range(2)]
            KTg = [A.alloc([SEQ], BF16) for _ in range(2)]
            Vg = [A.alloc([32, 128], BF16) for _ in range(2)]
            QTg = [A.alloc([512], BF16) for _ in range(2)]
            Pt = [A.alloc([512], BF16) for _ in range(3)]
            Pm = [A.alloc([512], BF16) for _ in range(3)]
            NEV = 4
            zs = [A.alloc([512], F32) for _ in range(NEV)]
            osb = [A.alloc([512], F32) for _ in range(NEV)]
            evc = [0]
            ot = [A.alloc([512], BF16) for _ in range(2)]
            sc_att = 1.0 / math.sqrt(128.0)

            qblocks = [("p", i, i * 128, 128, 512 * (i + 1)) for i in range(7, -1, -1)] + \
                      [("s", sq, 1024 + sq * 32, 32, SS) for sq in range(2)]
            r1c = [0]
            pc = [0]
            gct = [0]

            def srcs(kind, idx):
                if kind == "p":
                    return ikT_scr, KT_scr, V_scr
                return ikT_s[idx], KT_s[idx], V_s[idx]

            def stageA(blk, par):
                kind, idx, t0, R, Sk = blk
                G = R // 4
                Im = ImB[par]
                imk = ("Im", par)
                ikT_src = srcs(kind, idx)[0]
                iqv = iqTb.rearrange("p a b -> p (a b)")[:, 0:32 * R].rearrange("p (h t) -> p h t", h=32)
                B.dma("sp", iqv, iqT_scr[:, :, t0:t0 + R].rearrange("h d t -> d h t"), w=["iqTb"])
                B.copy("pool", iqTg[:, 0:G, :].rearrange("p g (h t) -> p g h t", t=4),
                       iqv.rearrange("p h (g t) -> p g h t", t=4), r=["iqTb"], w=["iqTg"])
                B.dma("sp", ikT[:, 0:Sk], ikT_src[:, 0:Sk], w=["ikT"])
                B.dma("sp", iwb[0:R], iw_scr[t0:t0 + R, :], w=["iwb"])
                B.copy("pool", iwrep[0:R], iwb[0:R].unsqueeze(2).to_broadcast([R, 32, 4]), r=["iwb"], w=["iwrep"])
                pT = psum[3][:, :].bitcast(BF16)
                B.transpose(pT[:, 0:R], iwrep[0:R].rearrange("p h t -> p (h t)"), ident_b[0:R, 0:R],
                            r=["iwrep", "ident_b"], w=[("ps", 3)])
                B.tt(wsel[:, 0:G, 0:R], dselt[:, 0:G, 0:R], pT[:, 0:R].unsqueeze(1).to_broadcast([128, G, R]),
                     ALU.mult, r=[("ps", 3), "dsel"], w=["wsel"])
                chunks = [(c0, min(512, Sk - c0)) for c0 in range(0, Sk, 512)]
                items = [(c0, cn, g) for (c0, cn) in chunks for g in range(G)]
                pend = None
                for it in range(len(items) + 1):
                    cur_item = None
                    if it < len(items):
                        c0, cn, g = items[it]
                        pb = r1c[0] % 2
                        rr = R1[r1c[0] % 3]
                        rk = ("r1", r1c[0] % 3)
                        r1c[0] += 1
                        ps1 = psum[pb][:, 0:cn]
                        B.mm(ps1, iqTg[:, g, :], ikT[:, c0:c0 + cn], True, True, r=["iqTg", "ikT"], w=[("ps", pb)])
                        B.act(rr[:, 0:cn], ps1, AF.Relu, r=[("ps", pb)], w=[rk])
                        cur_item = (c0, cn, g, rr, rk)
                    if pend is not None:
                        c0p, cnp, gp, rrp, rkp = pend
                        psI = psum[2][0:R, 0:cnp]
                        B.mm(psI, wsel[:, gp, 0:R], rrp[:, 0:cnp], gp == 0, gp == G - 1, r=["wsel", rkp], w=[("ps", 2)])
                        if gp == G - 1:
                            B.copy("act", Im[0:R, c0p:c0p + cnp], psI, r=[("ps", 2)], w=[imk])
                    pend = cur_item

            def stageB1(blk, par):
                kind, idx, t0, R, Sk = blk
                Im = ImB[par]
                imk = ("Im", par)
                if kind == "p":
                    B.tt(Im[0:R, Sk - 512:Sk], Im[0:R, Sk - 512:Sk], mbias[0:R, 0:512], ALU.add,
                         r=[imk, "mbias"], w=[imk])
                cur = Im
                for rnd in range(32):
                    S.add("dve", lambda e, cur=cur, R=R, Sk=Sk: e.max(out=m8[0:R], in_=cur[0:R, 0:Sk]),
                          r=[imk, "Wk"], w=["m8"])
                    if rnd < 31:
                        S.add("dve", lambda e, cur=cur, R=R, Sk=Sk: e.match_replace(
                            out=Wk[0:R, 0:Sk], in_to_replace=m8[0:R], in_values=cur[0:R, 0:Sk], imm_value=-3.0e38),
                            r=[imk, "Wk", "m8"], w=["Wk"])
                        cur = Wk
                B.ts(thrB[par][0:R], m8[0:R, 7:8], -5.0e29, None, ALU.max, r=["m8"], w=[("thr", par)])

            def stageB2(blk, par):
                kind, idx, t0, R, Sk = blk
                Im = ImB[par]
                maskT = maskTB[par]
                mk = ("maskT", par)
                B.ts(maskb[0:R, 0:Sk], Im[0:R, 0:Sk], thrB[par][0:R], None, ALU.is_ge,
                     r=[("Im", par), ("thr", par)], w=["maskb"])
                nkb = (Sk + 127) // 128
                for k0 in range(0, nkb, 8):
                    pT8 = psum[3][:, :].bitcast(BF16).rearrange("p (a b) -> p a b", a=8)
                    k1 = min(nkb, k0 + 8)
                    for kb_ in range(k0, k1):
                        kn = min(128, Sk - kb_ * 128)
                        B.transpose(pT8[0:kn, kb_ - k0, 0:R], maskb[0:R, kb_ * 128:kb_ * 128 + kn], ident_b[0:R, 0:R],
                                    r=["maskb", "ident_b"], w=[("ps", 3)])
                    kfull = [kb_ for kb_ in range(k0, k1) if Sk - kb_ * 128 >= 128]
                    if kfull:
                        B.act(maskT[:, kfull[0]:kfull[-1] + 1, 0:R], pT8[:, 0:len(kfull), 0:R], AF.Copy,
                              r=[("ps", 3)], w=[mk], scale=30000.0, bias=-30000.0)
                    if len(kfull) < k1 - k0:
                        kb_ = k1 - 1
                        kn = Sk - kb_ * 128
                        B.act(maskT[0:kn, kb_, 0:R], pT8[0:kn, kb_ - k0, 0:R], AF.Copy, r=[("ps", 3)], w=[mk],
                              scale=30000.0, bias=-30000.0)

            def stageC(blk, par):
                kind, idx, t0, R, Sk = blk
                maskT = maskTB[par]
                mk = ("maskT", par)
                _, KT_src, V_src = srcs(kind, idx)
                nkb = (Sk + 127) // 128
                for g in range(NKV):
                    sl = gct[0] % 2
                    gct[0] += 1
                    B.dma("sp", KTg[sl][:, 0:Sk], KT_src[g, :, 0:Sk], w=[("KTg", sl)])
                    nfull = Sk // 128
                    B.dma("act", Vg[sl][:, 0:nfull, :],
                          V_src[0:nfull * 128, g * 128:(g + 1) * 128].rearrange("(b p) d -> p b d", p=128),
                          w=[("Vg", sl)])
                    if Sk % 128:
                        kn = Sk % 128
                        B.dma("act", Vg[sl][0:kn, nfull, :], V_src[nfull * 128:Sk, g * 128:(g + 1) * 128],
                              w=[("Vg", sl)])
                    B.dma("sp", QTg[sl][:, 0:4 * R].rearrange("p (h t) -> p h t", h=4),
                          QT_scr[4 * g:4 * g + 4, :, t0:t0 + R].rearrange("h d t -> d h t"), w=[("QTg", sl)])
                    N4 = 4 * R
                    psO = psum[6][:, 0:N4]
                    psZ = psum[7][:, 0:N4]
                    pendc = None
                    for it in range(nkb + 1):
                        curc = None
                        if it < nkb:
                            kb_ = it
                            kn = min(128, Sk - kb_ * 128)
                            pb = 4 + (pc[0] % 2)
                            pi_ = pc[0] % 3
                            pc[0] += 1
                            psL = psum[pb][0:kn, 0:N4]
                            B.mm(psL, KTg[sl][:, kb_ * 128:kb_ * 128 + kn], QTg[sl][:, 0:4 * R], True, False,
                                 r=[("KTg", sl), ("QTg", sl)], w=[("ps", pb)])
                            for h in range(4):
                                B.mm(psum[pb][0:kn, h * R:(h + 1) * R], ident_b[0:kn, 0:kn], maskT[0:kn, kb_, 0:R],
                                     False, h == 3, r=["ident_b", mk], w=[("ps", pb)])
                            B.act(Pt[pi_][0:kn, 0:N4], psL, AF.Exp, r=[("ps", pb), "cq"], w=[("pt", pi_)],
                                  scale=sc_att, bias=cq[0:kn, 0:1])
                            curc = (kb_, kn, pi_)
                        if pendc is not None:
                            kbp, knp, pip = pendc
                            B.mm(psO, Vg[sl][0:knp, kbp, :], Pt[pip][0:knp, 0:N4], kbp == 0, kbp == nkb - 1,
                                 r=[("Vg", sl), ("pt", pip)], w=[("ps", 6)])
                            B.mm(psZ, ones_b[0:knp, :], Pt[pip][0:knp, 0:N4], kbp == 0, kbp == nkb - 1,
                                 r=["ones", ("pt", pip)], w=[("ps", 7)])
                        pendc = curc
                    ei = evc[0] % NEV
                    evc[0] += 1
                    B.copy("act", zs[ei][:, 0:N4], psZ, r=[("ps", 7)], w=[("zs", ei)])
                    B.copy("act", osb[ei][:, 0:N4], psO, r=[("ps", 6)], w=[("os", ei)])
                    S.add("dve", lambda e, ei=ei, N4=N4: e.reciprocal(out=zs[ei][:, 0:N4], in_=zs[ei][:, 0:N4]),
                          r=[("zs", ei)], w=[("zs", ei)])
                    B.tt(ot[sl][:, 0:N4], osb[ei][:, 0:N4], zs[ei][:, 0:N4], ALU.mult, r=[("os", ei), ("zs", ei)],
                         w=[("ot", sl)])
                    B.dma("act", oattT_scr[4 * g:4 * g + 4, :, t0:t0 + R].rearrange("h d t -> d h t"),
                          ot[sl][:, 0:N4].rearrange("p (h t) -> p h t", h=4), r=[("ot", sl)], w=[B.fresh("oa")])

            NBK = len(qblocks)
            for step in range(NBK + 2):
                if step < NBK:
                    stageA(qblocks[step], step % 2)
                if 0 <= step - 1 < NBK:
                    stageB1(qblocks[step - 1], (step - 1) % 2)
                if 0 <= step - 2 < NBK:
                    stageC(qblocks[step - 2], (step - 2) % 2)
                if 0 <= step - 1 < NBK:
                    stageB2(qblocks[step - 1], (step - 1) % 2)
            S.barrier()
            A.release(m0)

        if "mix" in ph:
            nps[0] = 8
            w_branch = inp("w_branch", [2 * D, D])
            w_out = inp("w_out", [D, D])
            xown = inp("xown", [TO, D])
            m0 = A.mark()
            oaT = A.alloc([KC, TO], BF16)
            olT = A.alloc([KC, TO], BF16)
            wA = [A.alloc([KC, 128], BF16) for _ in range(2)]
            wL = [A.alloc([KC, 128], BF16) for _ in range(2)]
            for h0 in range(0, 32, 8):
                B.dma("sp", oaT[:, h0:h0 + 8, :], oattT_scr[h0:h0 + 8].rearrange("h d t -> d h t"),
                      w=[("ATx", h0)])
                B.dma("sp", olT[:, h0:h0 + 8, :], olruT_scr[h0:h0 + 8].rearrange("h d t -> d h t"),
                      w=[("ATy", h0)])
            ATR = [("ATx", h0) for h0 in (0, 8, 16, 24)] + [("ATy", h0) for h0 in (0, 8, 16, 24)]
            ttiles = [(0, 512), (512, 512), (1024, 64)]
            NE = 3
            sa_t = [A.alloc([512], F32) for _ in range(NE)]
            sl_t = [A.alloc([512], F32) for _ in range(NE)]
            ta_t = [A.alloc([512], F32) for _ in range(NE)]
            mb_t = [A.alloc([512], BF16) for _ in range(NE)]
            ec = [0]

            def evac_mix(chunk, tt0, n, pss):
                (psA, tokA), (psL, tokL) = pss
                i = ec[0] % NE
                ec[0] += 1
                tk = ("e", i)
                B.dma("sp", sa_t[i][:, 0:n], saT_scr[chunk, :, tt0:tt0 + n], w=[("esa", i)])
                B.dma("sp", sl_t[i][:, 0:n], slT_scr[chunk, :, tt0:tt0 + n], w=[("esl", i)])
                B.tt(ta_t[i][:, 0:n], psA, sa_t[i][:, 0:n], ALU.mult, r=[tokA, ("esa", i)], w=[tk])
                B.tt(sl_t[i][:, 0:n], psL, sl_t[i][:, 0:n], ALU.mult, r=[tokL, ("esl", i)], w=[("esl", i)])
                B.tt(mb_t[i][:, 0:n], ta_t[i][:, 0:n], sl_t[i][:, 0:n], ALU.add, r=[tk, ("esl", i)], w=[tk])
                B.dma("act", mixT_scr[chunk, :, tt0:tt0 + n], mb_t[i][:, 0:n], r=[tk], w=[B.fresh("mx")])
            gemm_feat([(oaT, wA, lambda c0, n: w_branch[0:D, c0:c0 + n]),
                       (olT, wL, lambda c0, n: w_branch[D:2 * D, c0:c0 + n])], 9, KC, 4096, 128, ttiles, evac_mix,
                      attoks=ATR)
            S.barrier()
            A.release(m0)

            m0 = A.mark()
            mxT = A.alloc([KC, TO], BF16)
            wbufs = [A.alloc([KC, 512], BF16) for _ in range(2)]
            for h0 in range(0, 32, 8):
                B.dma("sp", mxT[:, h0:h0 + 8, :], mixT_scr[h0:h0 + 8].rearrange("h d t -> d h t"), w=[("ATz", h0)])
            NE = 3
            xr = [A.alloc([512], F32) for _ in range(NE)]

            def evac_out(tag, bi, rows, t0, ps, pstok, ncols):
                i = ec[0] % NE
                ec[0] += 1
                c0 = tag
                B.dma("sp", xr[i][0:rows], xown[t0:t0 + rows, c0:c0 + 512], w=[("xr", i)])
                B.tt(xr[i][0:rows], ps, xr[i][0:rows], ALU.add, r=[pstok, ("xr", i)], w=[("xr", i)])
                B.dma("act", x1_scr[t0:t0 + rows, c0:c0 + 512], xr[i][0:rows], r=[("xr", i)], w=[B.fresh("x1")])
            gemm_tok(mxT, [128] * 8 + [64], KC, wbufs, lambda c0, n: w_out[:, c0:c0 + n],
                     [(g * 512, 512, g * 512) for g in range(8)], evac_out,
                     attoks=[("ATz", h0) for h0 in (0, 8, 16, 24)])
            S.barrier()
            A.release(m0)

        if "ffn" in ph:
            nps[0] = 8
            w_gate = inp("w_gate", [D, DFF])
            w_up = inp("w_up", [D, DFF])
            w_down = inp("w_down", [DFF, D])
            y_own = B.dout("y_own", [TO, D])
            m0 = A.mark()
            gffT = A.alloc([KC], F32)
            B.dma("sp", gffT, inp("norm_ffnT", [128, KC]), w=["gT"])
            xn2T = A.alloc([KC, TO], BF16)
            xt = A.alloc([D], F32)
            junk = A.alloc([D], BF16)
            ssb = [A.alloc([1], F32) for _ in range(2)]
            blocks = [(x1_scr[j * 128:(j + 1) * 128, :], 128) for j in range(8)] + [(x1_scr[1024:1088, :], 64)]
            norm_transpose(blocks, xn2T, gffT, xt, junk, ssb)
            wG = [A.alloc([KC, 256], BF16) for _ in range(2)]
            wU = [A.alloc([KC, 256], BF16) for _ in range(2)]
            ttiles = [(0, 512), (512, 512), (1024, 64)]
            NE = 3
            sg_t = [A.alloc([512], F32) for _ in range(NE)]
            hb_t = [A.alloc([512], BF16) for _ in range(NE)]
            ec = [0]

            def evac_ffn(chunk, tt0, n, pss):
                (psG, tokG), (psU, tokU) = pss
                i = ec[0] % NE
                ec[0] += 1
                tk = ("e", i)
                B.act(sg_t[i][:, 0:n], psG, AF.Silu, r=[tokG], w=[tk])
                B.tt(hb_t[i][:, 0:n], sg_t[i][:, 0:n], psU, ALU.mult, r=[tk, tokU], w=[tk])
                B.dma("act", hT_scr[chunk, :, tt0:tt0 + n], hb_t[i][:, 0:n], r=[tk], w=[B.fresh("ht")])
            gemm_feat([(xn2T, wG, lambda c0, n: w_gate[:, c0:c0 + n]),
                       (xn2T, wU, lambda c0, n: w_up[:, c0:c0 + n])], 9, KC, DFF, 256, ttiles, evac_ffn)
            S.barrier()
            A.release(m0)

            m0 = A.mark()
            hT = A.alloc([KC, TO], BF16)
            wbufs = [A.alloc([KC, 512], BF16) for _ in range(2)]
            NE = 3
            xr = [A.alloc([512], F32) for _ in range(NE)]
            pieces = [(0, 32), (32, 32), (64, 22)]
            for pi, (k0, kcn) in enumerate(pieces):
                step = 8
                toks = []
                for h0 in range(0, kcn, step):
                    h1 = min(kcn, h0 + step)
                    tk = ("ATz", h0)
                    toks.append(tk)
                    B.dma("sp", hT[:, h0:h1, :], hT_scr[k0 + h0:k0 + h1].rearrange("h d t -> d h t"), w=[tk])
                src = x1_scr
                dst = y_own

                def evac_dn(tag, bi, rows, t0, ps, pstok, ncols, pi=pi):
                    i = ec[0] % NE
                    ec[0] += 1
                    c0 = tag
                    prev = x1_scr if pi == 0 else y_own
                    B.dma("sp", xr[i][0:rows], prev[t0:t0 + rows, c0:c0 + 512], r=[("y", bi, c0)], w=[("xr", i)])
                    B.tt(xr[i][0:rows], ps, xr[i][0:rows], ALU.add, r=[pstok, ("xr", i)], w=[("xr", i)])
                    B.dma("act", y_own[t0:t0 + rows, c0:c0 + 512], xr[i][0:rows], r=[("xr", i)], w=[("y", bi, c0)])
                gemm_tok(hT, [128] * 8 + [64], kcn, wbufs,
                         lambda c0, n, k0=k0, kcn=kcn: w_down[k0 * 128:(k0 + kcn) * 128, c0:c0 + n],
                         [(g * 512, 512, g * 512) for g in range(8)], evac_dn, attoks=toks)
            S.barrier()
            A.release(m0)

        with nc.Block() as block:
            S.emit(nc, block, esem, dsem)
    return nc


def rope_tables(pos):
    half = 16
    inv = (np.float32(500000.0) ** (-np.arange(half, dtype=np.float32) * np.float32(2.0) / np.float32(32))
           ).astype(np.float32)
    ang = pos.astype(np.float32)[:, None] * inv[None, :]
    return np.concatenate([np.cos(ang), np.sin(ang)], axis=1).astype(np.float32)


def make_in_maps(inp, names=None):
    f = np.float32
    in_maps = []
    pos_seq = np.concatenate([np.arange(SEQ), PAST + np.arange(32), PAST + np.arange(32)])
    cs_seq = rope_tables(pos_seq)
    ident = np.eye(128, dtype=f)
    dsel = np.zeros((32, 4, 32, 128), f)
    for g in range(32):
        for tl in range(4):
            dsel[:, tl, g, 4 * g + tl] = 1.0 / 64.0
    dsel = dsel.reshape(128, 32 * 128)

    def T32(v):
        return np.ascontiguousarray(np.asarray(v).reshape(KC, 128).T)

    for c in range(8):
        p, r = c // 4, c % 4
        xp = inp["x_prompt"][p]
        own_blocks = [xp[(4 * i + r) * 128:(4 * i + r + 1) * 128] for i in range(8)]
        xown = np.concatenate(own_blocks + [inp["x_sample"][2 * c], inp["x_sample"][2 * c + 1]], axis=0)
        pos_own = np.concatenate([np.arange((4 * i + r) * 128, (4 * i + r + 1) * 128) for i in range(8)] +
                                 [PAST + np.arange(32), PAST + np.arange(32)])
        sel = np.zeros((128, 4), f)
        sel[:, r] = 1.0
        tl = np.arange(128)[:, None]
        slx = np.arange(512)[None, :]
        mb = np.where(slx < 128 * r + 64 + 64 * (tl >= 64), 0.0, NEGM).astype(f)
        sc = inp["state_conv"][0, 2 * c:2 * c + 2]
        m = {
            "xseq": np.ascontiguousarray(xp),
            "xown": np.ascontiguousarray(xown),
            "w_in": inp["w_in"][0],
            "norm_mixT": T32(inp["norm_mix"][0]),
            "norm_ffnT": T32(inp["norm_ffn"][0]),
            "norm_q": inp["norm_q"][0],
            "norm_k": inp["norm_k"][0],
            "norm_idx_k": inp["norm_idx_k"][0],
            "cs_seq": cs_seq,
            "cs_own": rope_tables(pos_own),
            "ident": ident,
            "dsel": dsel,
            "maskbias": mb,
            "cache_k": np.ascontiguousarray(inp["cache_k"][0, 2 * c:2 * c + 2].reshape(2, PAST, NKV * HD)),
            "cache_v": np.ascontiguousarray(inp["cache_v"][0, 2 * c:2 * c + 2].reshape(2, PAST, NKV * HD)),
            "cache_ik": np.ascontiguousarray(inp["cache_idx_k"][0, 2 * c:2 * c + 2]),
            "state_lruT": np.stack([T32(inp["state_lru"][0, 2 * c + q]) for q in range(2)]),
            "state_convT": np.ascontiguousarray(sc.reshape(2, 3, KC, 128).transpose(0, 3, 2, 1)),
            "convwT": np.ascontiguousarray(inp["conv_w"][0].reshape(4, KC, 128).transpose(2, 0, 1)),
            "convbT": T32(inp["conv_b"][0]),
            "lru_baT": T32(inp["lru_ba"][0]),
            "lru_bxT": T32(inp["lru_bx"][0]),
            "lru_lamT": T32(inp["lru_lambda"][0]),
            "lru_wa": inp["lru_wa"][0],
            "lru_wx": inp["lru_wx"][0],
            "selr": sel,
            "w_branch": inp["w_branch"][0],
            "w_out": inp["w_out"][0],
            "w_gate": inp["w_gate"][0],
            "w_up": inp["w_up"][0],
            "w_down": inp["w_down"][0],
        }
        if names is not None:
            m = {k: v for k, v in m.items() if k in names}
        in_maps.append(m)
    return in_maps


def input_names(nc):
    names = set()
    for alloc in nc.allocations:
        if isinstance(alloc, mybir.MemoryLocationSet) and alloc.kind == "ExternalInput":
            names.add(alloc.memorylocations[0].name)
    return names


_NC_CACHE = {}


def kernel(**inputs):
    inp = {k: np.asarray(v) for k, v in inputs.items()}
    if "nc" not in _NC_CACHE:
        _NC_CACHE["nc"] = build_program()
    nc = _NC_CACHE["nc"]
    in_maps = make_in_maps(inp, input_names(nc))
    res = run_bass_kernel_spmd(nc, in_maps, core_ids=list(range(8)))
    R = res.results
    f = np.float32
    y_prompt = np.zeros((2, SEQ, D), f)
    y_sample = np.zeros((16, DEC_SEQ, D), f)
    k_prompt = np.zeros((1, 2, SEQ, NKV, HD), f)
    v_prompt = np.zeros((1, 2, SEQ, NKV, HD), f)
    ik_prompt = np.zeros((1, 2, SEQ, HD), f)
    lru_prompt = np.zeros((1, 2, D), f)
    conv_prompt = np.zeros((1, 2, 3, D), f)
    k_sample = np.zeros((1, 16, DEC_SEQ, NKV, HD), f)
    v_sample = np.zeros((1, 16, DEC_SEQ, NKV, HD), f)
    ik_sample = np.zeros((1, 16, DEC_SEQ, HD), f)
    lru_sample = np.zeros((1, 16, D), f)
    conv_sample = np.zeros((1, 16, 3, D), f)
    for c in range(8):
        p, r = c // 4, c % 4
        o = R[c]
        y = o["y_own"]
        for i in range(8):
            b = 4 * i + r
            y_prompt[p, b * 128:(b + 1) * 128] = y[i * 128:(i + 1) * 128]
        for q in range(2):
            sq = 2 * c + q
            y_sample[sq] = y[1024 + q * 32:1024 + (q + 1) * 32]
            k_sample[0, sq] = o["o_k"][SEQ + q * 32:SEQ + (q + 1) * 32].reshape(32, NKV, HD)
            v_sample[0, sq] = o["o_v"][SEQ + q * 32:SEQ + (q + 1) * 32].reshape(32, NKV, HD)
            ik_sample[0, sq] = o["o_ik"][SEQ + q * 32:SEQ + (q + 1) * 32]
            lru_sample[0, sq] = o["o_lru"][1 + q]
            conv_sample[0, sq] = o["o_conv"][1 + q]
        if r == 0:
            k_prompt[0, p] = o["o_k"][:SEQ].reshape(SEQ, NKV, HD)
            v_prompt[0, p] = o["o_v"][:SEQ].reshape(SEQ, NKV, HD)
            ik_prompt[0, p] = o["o_ik"][:SEQ]
            lru_prompt[0, p] = o["o_lru"][0]
            conv_prompt[0, p] = o["o_conv"][0]
    return (y_prompt, y_sample, k_prompt, v_prompt, ik_prompt, lru_prompt, conv_prompt,
            k_sample, v_sample, ik_sample, lru_sample, conv_sample)
```

```python
import math
from contextlib import ExitStack

import numpy as np
import concourse.bass as bass
import concourse.mybir as mybir
from concourse.bass_utils import run_bass_kernel_spmd

F32 = mybir.dt.float32
BF16 = mybir.dt.bfloat16
AF = mybir.ActivationFunctionType
ALU = mybir.AluOpType
AX = mybir.AxisListType

D = 4096
SEQ = 4096
NB = 32
DEC_SEQ = 32
PAST = 2048
SS = PAST + DEC_SEQ
NH = 32
NKV = 8
HD = 128
DFF = 11008
KC = 32
TO = 1088
EPS = 1e-6
C_Q, C_K, C_V, C_IQ, C_IW, C_IK, C_XL, C_YL, C_GA, C_GL = (
    0, 4096, 5120, 6144, 10240, 10272, 10400, 14496, 18592, 22688)
IN_COLS = 26784
NEGM = -1.0e30


class Op:
    __slots__ = ("eng", "fn", "deps", "dma", "signal", "pos", "sem", "semval", "K", "waits",
                 "barrier")

    def __init__(self, eng, fn, dma=False, barrier=False):
        self.eng = eng
        self.fn = fn
        self.deps = ()
        self.dma = dma
        self.signal = False
        self.pos = 0
        self.sem = None
        self.semval = 0
        self.K = None
        self.waits = ()
        self.barrier = barrier


class Sched:
    CE = ("pe", "act", "dve", "pool", "sp")
    NDSEM = 10

    def __init__(self):
        self.ops = []
        self.last_w = {}
        self.readers = {}
        self.last_op = {e: None for e in self.CE}

    def add(self, eng, fn, r=(), w=(), dma=False):
        idx = len(self.ops)
        op = Op(eng, fn, dma=dma)
        raw = set()
        oth = set()
        for t in r:
            lw = self.last_w.get(t)
            if lw is not None:
                raw.add(lw)
        for t in w:
            lw = self.last_w.get(t)
            if lw is not None:
                oth.add(lw)
            rs = self.readers.get(t)
            if rs:
                oth.update(rs)
        deps = set()
        for d in raw | oth:
            dop = self.ops[d]
            if (not dop.dma) and (not dma) and dop.eng == eng:
                if eng == "pe":
                    continue
                if d not in raw:
                    continue
            deps.add(d)
            if not dop.dma:
                dop.signal = True
        op.deps = tuple(sorted(deps))
        for t in r:
            self.readers.setdefault(t, []).append(idx)
        for t in w:
            self.last_w[t] = idx
            self.readers[t] = []
        self.ops.append(op)
        if not dma:
            self.last_op[eng] = idx
        return idx

    def barrier(self):
        lasts = dict(self.last_op)
        for e in self.CE:
            op = Op(e, None, barrier=True)
            deps = set()
            for e2, li in lasts.items():
                if li is not None and e2 != e:
                    deps.add(li)
                    self.ops[li].signal = True
            op.deps = tuple(sorted(deps))
            self.ops.append(op)
        self.last_w = {}
        self.readers = {}

    def analyze(self):
        CE = self.CE
        K = {e: {} for e in CE}
        pos = {e: 0 for e in CE}
        sig = {e: 0 for e in CE}
        sigcount = {e: {} for e in CE}
        dq = {e: {"next": 0, "cum": [0] * self.NDSEM, "last": [None] * self.NDSEM} for e in CE}
        for op in self.ops:
            E = op.eng
            KE = K[E]
            waits = {}

            def need(key, val, kafter):
                if KE.get(key, 0) >= val:
                    return
                if waits.get(key, 0) < val:
                    waits[key] = val
                if kafter:
                    for k2, v2 in kafter.items():
                        if KE.get(k2, 0) < v2:
                            KE[k2] = v2
                if KE.get(key, 0) < val:
                    KE[key] = val

            for d in op.deps:
                dop = self.ops[d]
                if dop.dma:
                    need(("S", dop.eng, dop.sem), dop.semval, dop.K)
                else:
                    need(dop.eng, dop.pos, dop.K)
            if op.barrier:
                for q in CE:
                    for s in range(self.NDSEM):
                        if dq[q]["cum"][s] > 0:
                            lo = dq[q]["last"][s]
                            need(("S", q, s), dq[q]["cum"][s], lo.K if lo else None)
            if op.dma:
                q = dq[E]
                s = q["next"]
                q["next"] = (s + 1) % self.NDSEM
                if q["last"][s] is not None:
                    need(("S", E, s), q["cum"][s], q["last"][s].K)
                q["cum"][s] += 16
                op.sem = s
                op.semval = q["cum"][s]
                q["last"][s] = op
                op.K = dict(KE)
            elif not op.barrier:
                pos[E] += 1
                op.pos = pos[E]
                if op.signal:
                    sig[E] += 1
                    sigcount[E][op.pos] = sig[E]
                    kk = dict(KE)
                    kk[E] = op.pos
                    op.K = kk
            op.waits = tuple(waits.items())
        self.sigcount = sigcount
        self.final_dma = {e: list(dq[e]["cum"]) for e in CE}

    def emit(self, nc, block, esem, dsem):
        self.analyze()
        per = {e: [] for e in self.CE}
        for op in self.ops:
            per[op.eng].append(op)
        sigcount = self.sigcount

        def run(e, eng):
            for op in per[e]:
                for key, val in op.waits:
                    if isinstance(key, tuple):
                        eng.wait_ge(dsem[key[1]][key[2]], val)
                    else:
                        eng.wait_ge(esem[key], sigcount[key][val])
                if op.fn is None:
                    continue
                ins = op.fn(eng)
                if op.dma:
                    ins.then_inc(dsem[e][op.sem], 16)
                elif op.signal:
                    ins.then_inc(esem[e], 1)
            if e == "sp":
                for q in self.CE:
                    for s, v in enumerate(self.final_dma[q]):
                        if v > 0:
                            eng.wait_ge(dsem[q][s], v)

        @block.tensor
        def _(eng):
            run("pe", eng)

        @block.scalar
        def _(eng):
            run("act", eng)

        @block.vector
        def _(eng):
            run("dve", eng)

        @block.gpsimd
        def _(eng):
            run("pool", eng)

        @block.sync
        def _(eng):
            run("sp", eng)


class Arena:
    def __init__(self, t, words):
        self.t = t
        self.words = words
        self.off = 0

    def alloc(self, free_shape, dtype, parts=128):
        n = 1
        for s in free_shape:
            n *= s
        nbytes = n * (2 if dtype == BF16 else 4)
        w = (nbytes + 31) // 32 * 8
        off = self.off
        assert off + w <= self.words, f"arena overflow {off + w} > {self.words}"
        self.off += w
        ap = self.t[0:parts, off:off + (nbytes + 3) // 4]
        if dtype == BF16:
            ap = ap.bitcast(BF16)
            if ap.shape[1] != n:
                ap = ap[:, 0:n]
        if len(free_shape) == 2:
            ap = ap.rearrange("p (a b) -> p a b", a=free_shape[0])
        elif len(free_shape) == 3:
            ap = ap.rearrange("p (a b c) -> p a b c", a=free_shape[0], b=free_shape[1])
        return ap

    def mark(self):
        return self.off

    def release(self, m):
        self.off = m


class Builder:
    def __init__(self, debug=False, phases=None, feed=None):
        self.debug = debug
        self.feed = feed or set()
        self.phases = phases
        self.nc = bass.Bass("TRN2", target_bir_lowering=False)
        self.S = Sched()
        self.uid = 0

    def fresh(self, p):
        self.uid += 1
        return (p, self.uid)

    def din(self, name, shape, dt=F32):
        return self.nc.dram_tensor(name, list(shape), dt, kind="ExternalInput").ap()

    def dout(self, name, shape, dt=F32):
        return self.nc.dram_tensor(name, list(shape), dt, kind="ExternalOutput").ap()

    def dscr(self, name, shape, dt=F32):
        kind = "ExternalOutput" if (self.debug and name in self.debug) else "Internal"
        if name in self.feed:
            kind = "ExternalInput"
        return self.nc.dram_tensor(name, list(shape), dt, kind=kind).ap()

    def dma(self, q, out, in_, r=(), w=()):
        self.S.add(q, lambda e: e.dma_start(out=out, in_=in_), r=r, w=w, dma=True)

    def act(self, out, in_, func, r=(), w=(), **kw):
        self.S.add("act", lambda e: e.activation(out=out, in_=in_, func=func, **kw), r=r, w=w)

    def tt(self, out, in0, in1, op, r=(), w=(), eng="dve"):
        self.S.add(eng, lambda e: e.tensor_tensor(out=out, in0=in0, in1=in1, op=op), r=r, w=w)

    def ts(self, out, in0, s1, s2, op0, op1=None, r=(), w=(), eng="dve", **kw):
        if op1 is None:
            self.S.add(eng, lambda e: e.tensor_scalar(out=out, in0=in0, scalar1=s1, scalar2=None,
                                                      op0=op0, **kw), r=r, w=w)
        else:
            self.S.add(eng, lambda e: e.tensor_scalar(out=out, in0=in0, scalar1=s1, scalar2=s2,
                                                      op0=op0, op1=op1, **kw), r=r, w=w)

    def stt(self, out, in0, scalar, in1, op0, op1, r=(), w=()):
        self.S.add("dve", lambda e: e.scalar_tensor_tensor(out=out, in0=in0, scalar=scalar, in1=in1,
                                                           op0=op0, op1=op1), r=r, w=w)

    def copy(self, eng, out, in_, r=(), w=()):
        if eng == "act":
            self.S.add("act", lambda e: e.activation(out=out, in_=in_, func=AF.Copy), r=r, w=w)
        else:
            self.S.add(eng, lambda e: e.tensor_copy(out=out, in_=in_), r=r, w=w)

    def mm_group(self, out, pairs, r=(), w=()):
        n = len(pairs)

        def fn(e):
            ins = None
            for i, (l, rr) in enumerate(pairs):
                ins = e.matmul(out, l, rr, start=(i == 0), stop=(i == n - 1))
            return ins
        self.S.add("pe", fn, r=r, w=w)

    def mm(self, out, lhsT, rhs, start, stop, r=(), w=()):
        self.S.add("pe", lambda e: e.matmul(out, lhsT, rhs, start=start, stop=stop), r=r, w=w)

    def transpose(self, out, in_, ident, r=(), w=()):
        self.S.add("pe", lambda e: e.transpose(out, in_, ident), r=r, w=w)


def build_program(debug=None, phases=None, feed=None):
    B = Builder(debug=debug, phases=phases, feed=feed)
    nc = B.nc
    S = B.S
    ALL = {"seq", "cache", "lru", "own", "attn", "mix", "ffn"}
    ph = set(phases) if phases is not None else ALL
    full = ph == ALL

    _in = {}

    def inp(name, shape):
        if name not in _in:
            _in[name] = B.din(name, shape)
        return _in[name]

    KT_scr = B.dscr("KT_scr", [NKV, 128, SEQ], BF16)
    V_scr = B.dscr("V_scr", [SEQ, NKV * HD], BF16)
    ikT_scr = B.dscr("ikT_scr", [128, SEQ], BF16)
    KT_s = B.dscr("KT_s", [2, NKV, 128, SS], BF16)
    V_s = B.dscr("V_s", [2, SS, NKV * HD], BF16)
    ikT_s = B.dscr("ikT_s", [2, 128, SS], BF16)
    xlT_scr = B.dscr("xlT_scr", [KC, 128, SEQ + 64], F32)
    hown_scr = B.dscr("hown_scr", [KC, 128, TO], F32)
    QT_scr = B.dscr("QT_scr", [NH, 128, TO], BF16)
    iqT_scr = B.dscr("iqT_scr", [NH, 128, TO], BF16)
    iw_scr = B.dscr("iw_scr", [TO, 32], F32)
    olruT_scr = B.dscr("olruT_scr", [KC, 128, TO], BF16)
    saT_scr = B.dscr("saT_scr", [KC, 128, TO], F32)
    slT_scr = B.dscr("slT_scr", [KC, 128, TO], F32)
    oattT_scr = B.dscr("oattT_scr", [NH, 128, TO], BF16)
    mixT_scr = B.dscr("mixT_scr", [KC, 128, TO], BF16)
    x1_scr = B.dscr("x1_scr", [TO, D], F32)
    hT_scr = B.dscr("hT_scr", [DFF // 128, 128, TO], BF16)

    with ExitStack() as es:
        ARENA_KB = 206
        arena_t = es.enter_context(nc.sbuf_tensor("arena", [128, ARENA_KB * 256], F32))
        A = Arena(arena_t, ARENA_KB * 256)
        psum = [es.enter_context(nc.psum_tensor(f"ps{i}", [128, 512], F32)) for i in range(8)]
        esem = {e: es.enter_context(nc.semaphore(f"e_{e}")) for e in Sched.CE}
        dsem = {e: [es.enter_context(nc.semaphore(f"d_{e}{i}")) for i in range(Sched.NDSEM)]
                for e in ("act", "pool", "sp")}
        dsem["pe"] = dsem["sp"]
        dsem["dve"] = dsem["sp"]

        ident_in = inp("ident", [128, 128])
        ident_b = A.alloc([128], BF16)
        ident_f = A.alloc([128], F32)
        B.dma("pool", ident_b, ident_in, w=["ident_b"])
        B.dma("sp", ident_f, ident_in, w=["ident_f"])
        eps_t = A.alloc([1], F32)
        S.add("dve", lambda e: e.memset(eps_t, EPS), w=["eps"])
        one_t = A.alloc([1], F32)
        S.add("dve", lambda e: e.memset(one_t, 1.0), w=["one"])
        g4 = {}

        def load_g4(nm):
            src = inp("norm_" + nm, [HD])
            t = A.alloc([4, 128], F32)
            for j in range(4):
                B.dma("sp", t[:, j, :], src.partition_broadcast(128), w=[("g4", nm, j)])
            g4[nm] = t

        if "seq" in ph:
            load_g4("k")
            load_g4("idx_k")
        if "own" in ph:
            load_g4("q")
        S.barrier()

        psn = [0]

        def next_ps():
            pb = psn[0] % 4
            psn[0] += 1
            return pb

        def norm_transpose(blocks, xnT, gT, xt, junk, ssb, extra=None):
            t0 = 0
            for bi, (src, rows) in enumerate(blocks):
                B.dma("sp", xt[0:rows], src, w=["xt"])
                if extra is not None:
                    extra(bi, rows)
                ss = ssb[bi % 2]
                sk = ("ssb", bi % 2)
                S.add("act", lambda e, rows=rows, ss=ss: e.activation(
                    out=junk[0:rows], in_=xt[0:rows], func=AF.Square, accum_out=ss[0:rows]),
                    r=["xt"], w=["junk", sk])
                B.act(ss[0:rows], ss[0:rows], AF.Sqrt, r=[sk, "eps"], w=[sk], scale=1.0 / D,
                      bias=eps_t[0:rows])
                S.add("dve", lambda e, ss=ss, rows=rows: e.reciprocal(out=ss[0:rows], in_=ss[0:rows]),
                      r=[sk], w=[sk])
                B.ts(xt[0:rows], xt[0:rows], ss[0:rows], None, ALU.mult, r=["xt", sk], w=["xt"])
                for j in range(8):
                    pb = 4 + (j % 2)
                    pv = psum[pb][:, :].rearrange("p (a b) -> p a b", a=4)
                    for q in range(4):
                        kc = j * 4 + q
                        B.transpose(pv[:, q, 0:rows], xt[0:rows, kc * 128:(kc + 1) * 128],
                                    ident_f[0:rows, 0:rows], r=["xt", "ident_f"], w=[("ps", pb)])
                    B.tt(xnT[:, j * 4:(j + 1) * 4, t0:t0 + rows], pv[:, :, 0:rows],
                         gT[:, j * 4:(j + 1) * 4].unsqueeze(2).to_broadcast([128, 4, rows]),
                         ALU.mult, r=[("ps", pb), "gT"], w=[("AT", bi)])
                t0 += rows

        wl = {}

        def load_w(wbufs, src, kcn, ncols):
            wid = id(wbufs)
            slot = wl.get(wid, 0) % len(wbufs)
            wl[wid] = wl.get(wid, 0) + 1
            slot = (wid, slot)
            v = src.rearrange("(kc p) n -> p kc n", p=128)
            step = 8 if ncols > 256 else 16
            for k0 in range(0, kcn, step):
                k1 = min(kcn, k0 + step)
                B.dma("pool", wbufs[slot[1]][:, k0:k1, 0:ncols], v[:, k0:k1, :], w=[("w", slot, k0 // 8)] +
                      ([("w", slot, k0 // 8 + 1)] if step == 16 else []))
            return slot

        def wtoks(slot, kcn):
            return [("w", slot, k) for k in range((kcn + 7) // 8)]

        def gemm_tok(AT, blocks, kcn, wbufs, wsrc, col_groups, evac, attoks=None):
            pend_ev = []
            nxt = load_w(wbufs, wsrc(col_groups[0][0], col_groups[0][1]), kcn, col_groups[0][1])
            for gi, (c0, ncols, tag) in enumerate(col_groups):
                slot = nxt
                if gi + 1 < len(col_groups):
                    nxt = load_w(wbufs, wsrc(col_groups[gi + 1][0], col_groups[gi + 1][1]), kcn, col_groups[gi + 1][1])
                t0 = 0
                for bi, rows in enumerate(blocks):
                    pb = next_ps()
                    ps = psum[pb][0:rows, 0:ncols]
                    B.mm_group(ps, [(AT[:, kc, t0:t0 + rows], wbufs[slot[1]][:, kc, 0:ncols])
                                    for kc in range(kcn)],
                               r=(attoks if attoks is not None else [("AT", bi)]) + wtoks(slot, kcn),
                               w=[("ps", pb)])
                    if pend_ev:
                        evac(*pend_ev.pop(0))
                    pend_ev.append((tag, bi, rows, t0, ps, ("ps", pb), ncols))
                    t0 += rows
            while pend_ev:
                evac(*pend_ev.pop(0))

        def gemm_feat(srcs, nblk, kcn, ncols_total, cgw, ttiles, evac, attoks=None):
            nxts = [load_w(wb, wsrc(0, cgw), kcn, cgw) for (AT, wb, wsrc) in srcs]
            for c0 in range(0, ncols_total, cgw):
                slots = nxts
                if c0 + cgw < ncols_total:
                    nxts = [load_w(wb, wsrc(c0 + cgw, cgw), kcn, cgw) for (AT, wb, wsrc) in srcs]
                for ch in range(cgw // 128):
                    chunk = (c0 // 128) + ch
                    for (tt0, n) in ttiles:
                        pss = []
                        for (AT, wb, wsrc), slot in zip(srcs, slots):
                            pb = next_ps()
                            ps = psum[pb][:, 0:n]
                            B.mm_group(ps, [(wb[slot[1]][:, kc, ch * 128:(ch + 1) * 128], AT[:, kc, tt0:tt0 + n])
                                            for kc in range(kcn)],
                                       r=(attoks if attoks is not None else [("AT", bi) for bi in range(nblk)])
                                       + wtoks(slot, kcn), w=[("ps", pb)])
                            pss.append((ps, ("ps", pb)))
                        evac(chunk, tt0, n, pss)

        stc = [0]

        def make_headproc(NST):
            st = dict(
                f=[A.alloc([512], F32) for _ in range(NST)],
                t=[A.alloc([512], F32) for _ in range(NST)],
                b=[A.alloc([512], BF16) for _ in range(NST)],
                T=[A.alloc([4, 128], BF16) for _ in range(NST)],
                s=[A.alloc([4], F32) for _ in range(NST)],
                r=[A.alloc([4, 16], F32) for _ in range(4 * NST)],
                n=NST)
            return st

        def headproc(st, ps, pstok, rows, ncols, gname, cs, cstok, dst_out, dstT_fn, rope=True):
            nh = ncols // 128
            i = stc[0] % st["n"]
            stc[0] += 1
            f, t, b_, T_, s_ = st["f"][i], st["t"][i], st["b"][i], st["T"][i], st["s"][i]
            r4 = st["r"][4 * i:4 * i + 4]
            tk = ("st", i)
            B.copy("act", f[0:rows, 0:ncols], ps, r=[pstok], w=[tk])
            fv = f[0:rows, 0:ncols].rearrange("p (h d) -> p h d", h=nh)
            tv = t[0:rows, 0:ncols].rearrange("p (h d) -> p h d", h=nh)
            if gname is not None:
                for h in range(nh):
                    S.add("act", lambda e, h=h: e.activation(out=t[0:rows, h * 128:(h + 1) * 128],
                                                            in_=f[0:rows, h * 128:(h + 1) * 128], func=AF.Square,
                                                            accum_out=s_[0:rows, h:h + 1]), r=[tk], w=[tk])
                B.act(s_[0:rows, 0:nh], s_[0:rows, 0:nh], AF.Sqrt, r=[tk, "eps"], w=[tk], scale=1.0 / 128,
                      bias=eps_t[0:rows])
                S.add("dve", lambda e: e.reciprocal(out=s_[0:rows, 0:nh], in_=s_[0:rows, 0:nh]), r=[tk], w=[tk])
                B.tt(fv, fv, s_[0:rows, 0:nh].unsqueeze(2).to_broadcast([rows, nh, 128]), ALU.mult,
                     r=[tk], w=[tk])
                B.tt(fv, fv, g4[gname][0:rows, 0:nh, :], ALU.mult,
                     r=[tk] + [("g4", gname, j) for j in range(4)], w=[tk])
            if rope:
                cb = cs[0:rows, 0:16].unsqueeze(1).to_broadcast([rows, nh, 16])
                sb = cs[0:rows, 16:32].unsqueeze(1).to_broadcast([rows, nh, 16])
                x1 = fv[:, :, 0:16]
                x2 = fv[:, :, 16:32]
                ra, rb, rc, rd = [q[0:rows, 0:nh, :] for q in r4]
                rt = [tk, cstok]
                tkp = ("stp", i)
                tkd = ("std", i)
                B.tt(ra, x1, cb, ALU.mult, r=rt, w=[tkp], eng="pool")
                B.tt(rb, x2, sb, ALU.mult, r=rt, w=[tkp], eng="pool")
                B.tt(rc, x2, cb, ALU.mult, r=rt, w=[tkd])
                B.tt(rd, x1, sb, ALU.mult, r=rt, w=[tkd])
                B.tt(x1, ra, rb, ALU.subtract, r=[tkp, tkd], w=[tk], eng="pool")
                B.tt(x2, rc, rd, ALU.add, r=[tkd, tkp], w=[tk])
            if dst_out is not None:
                B.dma("sp", dst_out, f[0:rows, 0:ncols], r=[tk], w=[B.fresh("o")])
            if dstT_fn is not None:
                B.copy("act", b_[0:rows, 0:ncols], f[0:rows, 0:ncols], r=[tk], w=[tk])
                pb = 6 + (stc[0] % 2)
                pT = psum[pb][:, :].bitcast(BF16)[:, 0:512].rearrange("p (a b) -> p a b", a=4)
                for h in range(nh):
                    B.transpose(pT[:, h, 0:rows], b_[0:rows, h * 128:(h + 1) * 128],
                                ident_b[0:rows, 0:rows], r=[tk, "ident_b"], w=[("ps", pb)])
                B.copy("dve", T_[:, 0:nh, 0:rows], pT[:, 0:nh, 0:rows], r=[("ps", pb)], w=[tk])
                dstT_fn(T_, nh, rows, tk)

        if "seq" in ph:
            xseq = inp("xseq", [SEQ, D])
            xown = inp("xown", [TO, D])
            w_in = inp("w_in", [D, IN_COLS])
            cs_seq = inp("cs_seq", [SEQ + 64, 32])
            o_k = B.dout("o_k", [SEQ + 64, NKV * HD])
            o_v = B.dout("o_v", [SEQ + 64, NKV * HD])
            o_ik = B.dout("o_ik", [SEQ + 64, HD])
            m0 = A.mark()
            gmixT = A.alloc([KC], F32)
            B.dma("sp", gmixT, inp("norm_mixT", [128, KC]), w=["gT"])
            xnT = A.alloc([KC, TO], BF16)
            wbufs = [A.alloc([KC, 512], BF16) for _ in range(2)]
            xt = A.alloc([D], F32)
            junk = A.alloc([D], BF16)
            ssb = [A.alloc([1], F32) for _ in range(2)]
            st = make_headproc(3)
            cst = [A.alloc([32], F32) for _ in range(9)]
            xl_st = [A.alloc([TO], F32) for _ in range(2)]

            for tt in range(4):
                blocks = [(xseq[tt * 1024 + j * 128: tt * 1024 + (j + 1) * 128, :], 128) for j in range(8)]
                orows = [tt * 1024 + j * 128 for j in range(8)]
                if tt == 3:
                    blocks.append((xown[1024:1088, :], 64))
                    orows.append(SEQ)
                ntok = sum(b[1] for b in blocks)

                def extra(bi, rows, orows=orows):
                    B.dma("sp", cst[bi][0:rows], cs_seq[orows[bi]:orows[bi] + rows, :], w=[("cst", bi)])
                norm_transpose(blocks, xnT, gmixT, xt, junk, ssb, extra)

                def evac(tag, bi, rows, t0, ps, pstok, ncols, orows=orows):
                    kind, half = tag
                    orow = orows[bi]
                    is_s = orow >= SEQ
                    if kind == "k":
                        def dstT(T_, nh, rows_, tk):
                            if not is_s:
                                B.dma("act", KT_scr[half * 4:half * 4 + 4, :, orow:orow + rows_]
                                      .rearrange("h d t -> d h t"), T_[:, 0:4, 0:rows_], r=[tk], w=[B.fresh("kt")])
                            else:
                                for sq in range(2):
                                    B.dma("act", KT_s[sq, half * 4:half * 4 + 4, :, PAST:SS]
                                          .rearrange("h d t -> d h t"), T_[:, 0:4, sq * 32:(sq + 1) * 32],
                                          r=[tk], w=[B.fresh("kt")])
                        headproc(st, ps, pstok, rows, ncols, "k", cst[bi], ("cst", bi),
                                 o_k[orow:orow + rows, half * 512:(half + 1) * 512], dstT)
                    elif kind == "ik":
                        def dstT(T_, nh, rows_, tk):
                            if not is_s:
                                B.dma("act", ikT_scr[:, orow:orow + rows_], T_[:, 0, 0:rows_], r=[tk],
                                      w=[B.fresh("ikt")])
                            else:
                                for sq in range(2):
                                    B.dma("act", ikT_s[sq, :, PAST:SS], T_[:, 0, sq * 32:(sq + 1) * 32],
                                          r=[tk], w=[B.fresh("ikt")])
                        headproc(st, ps, pstok, rows, ncols, "idx_k", cst[bi], ("cst", bi),
                                 o_ik[orow:orow + rows, :], dstT)
                    else:
                        i = stc[0] % st["n"]
                        stc[0] += 1
                        tk = ("st", i)
                        B.copy("act", st["f"][i][0:rows, 0:512], ps, r=[pstok], w=[tk])
                        B.copy("dve", st["b"][i][0:rows, 0:512], ps, r=[pstok], w=[tk])
                        B.dma("sp", o_v[orow:orow + rows, half * 512:(half + 1) * 512],
                              st["f"][i][0:rows, 0:512], r=[tk], w=[B.fresh("o")])
                        if not is_s:
                            B.dma("act", V_scr[orow:orow + rows, half * 512:(half + 1) * 512],
                                  st["b"][i][0:rows, 0:512], r=[tk], w=[B.fresh("v")])
                        else:
                            for sq in range(2):
                                B.dma("act", V_s[sq, PAST:SS, half * 512:(half + 1) * 512],
                                      st["b"][i][sq * 32:(sq + 1) * 32, 0:512], r=[tk], w=[B.fresh("v")])

                gemm_tok(xnT, [b[1] for b in blocks], KC, wbufs, lambda c0, n: w_in[:, c0:c0 + n],
                         [(C_K, 512, ("k", 0)), (C_K + 512, 512, ("k", 1)), (C_V, 512, ("v", 0)),
                          (C_V + 512, 512, ("v", 1)), (C_IK, 128, ("ik", 0))], evac)

                ttiles = [(0, 512), (512, 512)] + ([(1024, 64)] if tt == 3 else [])
                col0 = tt * 1024

                def evac_xl(chunk, tt0, n, pss, ntok=ntok, col0=col0, last=ttiles[-1][0]):
                    xs = xl_st[chunk % 2]
                    ps, pstok = pss[0]
                    B.copy("act" if (tt0 // 512) % 2 else "dve", xs[:, tt0:tt0 + n], ps, r=[pstok],
                           w=[("xls", chunk % 2)])
                    if tt0 == last:
                        B.dma("sp", xlT_scr[chunk, :, col0:col0 + ntok], xs[:, 0:ntok],
                              r=[("xls", chunk % 2)], w=[B.fresh("xl")])
                gemm_feat([(xnT, wbufs, lambda c0, n: w_in[:, C_XL + c0:C_XL + c0 + n])], len(blocks), KC,
                          4096, 512, ttiles, evac_xl)
            S.barrier()
            A.release(m0)

        if "cache" in ph:
            cache_k = inp("cache_k", [2, PAST, NKV * HD])
            cache_v = inp("cache_v", [2, PAST, NKV * HD])
            cache_ik = inp("cache_ik", [2, PAST, HD])
            m0 = A.mark()
            kb = A.alloc([16, 1024], BF16)
            vb = A.alloc([16, 1024], BF16)
            ib = A.alloc([16, 128], BF16)
            stg = [A.alloc([8, 128], BF16) for _ in range(2)]
            istg = A.alloc([16, 128], BF16)
            for sq in range(2):
                for q in range(4):
                    B.dma("pool", kb[:, q * 4:(q + 1) * 4, :],
                          cache_k[sq, q * 512:(q + 1) * 512, :].rearrange("(b p) c -> p b c", p=128), w=[("kb", q)])
                    B.dma("pool", vb[:, q * 4:(q + 1) * 4, :],
                          cache_v[sq, q * 512:(q + 1) * 512, :].rearrange("(b p) c -> p b c", p=128), w=[("vb", q)])
                B.dma("pool", ib, cache_ik[sq].rearrange("(b p) c -> p b c", p=128), w=["ib"])
                for q in range(4):
                    B.dma("act", V_s[sq, q * 512:(q + 1) * 512, :].rearrange("(b p) c -> p b c", p=128),
                          vb[:, q * 4:(q + 1) * 4, :], r=[("vb", q)], w=[B.fresh("vs")])
                for blk in range(16):
                    pb = 6 + (blk % 2)
                    pT = psum[pb][:, :].bitcast(BF16).rearrange("p (a b) -> p a b", a=8)
                    for h in range(8):
                        B.transpose(pT[:, h, :], kb[:, blk, h * 128:(h + 1) * 128], ident_b,
                                    r=[("kb", blk // 4), "ident_b"], w=[("ps", pb)])
                    sg = stg[blk % 2]
                    B.copy("dve" if blk % 2 else "act", sg, pT, r=[("ps", pb)], w=[("stg", blk % 2)])
                    B.dma("sp", KT_s[sq, :, :, blk * 128:(blk + 1) * 128].rearrange("h d t -> d h t"), sg,
                          r=[("stg", blk % 2)], w=[B.fresh("kts")])
                for half in range(2):
                    pb = 4 + half
                    pT = psum[pb][:, :].bitcast(BF16).rearrange("p (a b) -> p a b", a=8)
                    for j in range(8):
                        B.transpose(pT[:, j, :], ib[:, half * 8 + j, :], ident_b, r=["ib", "ident_b"],
                                    w=[("ps", pb)])
                    B.copy("dve", istg[:, half * 8:(half + 1) * 8, :], pT, r=[("ps", pb)], w=["istg"])
                B.dma("sp", ikT_s[sq, :, 0:PAST], istg, r=["istg"], w=[B.fresh("ikts")])
            S.barrier()
            A.release(m0)

        if "lru" in ph:
            convwT = inp("convwT", [128, 4, KC])
            convbT = inp("convbT", [128, KC])
            baT = inp("lru_baT", [128, KC])
            bxT = inp("lru_bxT", [128, KC])
            lamT = inp("lru_lamT", [128, KC])
            lru_wa = inp("lru_wa", [16, 256, 256])
            lru_wx = inp("lru_wx", [16, 256, 256])
            st_lruT = inp("state_lruT", [2, 128, KC])
            st_convT = inp("state_convT", [2, 128, KC, 3])
            selr = inp("selr", [128, 4])
            o_lru = B.dout("o_lru", [3, D])
            o_conv = B.dout("o_conv", [3, 3, D])
            m0 = A.mark()
            cw = A.alloc([4, KC], F32)
            cb = A.alloc([KC], F32)
            ba = A.alloc([KC], F32)
            bx = A.alloc([KC], F32)
            lam = A.alloc([KC], F32)
            cneg = A.alloc([KC], F32)
            cneg2 = A.alloc([KC], F32)
            tmpc = A.alloc([KC], F32)
            sel = A.alloc([4], F32)
            h0s = A.alloc([2, KC], F32)
            c0s = A.alloc([2, KC, 3], F32)
            hfin = A.alloc([3, KC], F32)
            cfin = A.alloc([3, 3, KC], F32)
            B.dma("sp", cw, convwT, w=["lc"])
            B.dma("sp", cb, convbT, w=["lc"])
            B.dma("sp", ba, baT, w=["lc"])
            B.dma("sp", bx, bxT, w=["lc"])
            B.dma("sp", lam, lamT, w=["lam"])
            B.dma("sp", sel, selr, w=["lc"])
            for sq in range(2):
                B.dma("sp", h0s[:, sq, :], st_lruT[sq], w=["lc"])
                B.dma("sp", c0s[:, sq, :, :], st_convT[sq], w=["lc"])
            B.ts(tmpc, lam, -1.0, None, ALU.mult, r=["lam"], w=["tmpc"])
            B.tt(tmpc, tmpc, lam, ALU.max, r=["lam", "tmpc"], w=["tmpc"])
            B.act(tmpc, tmpc, AF.Exp, r=["tmpc"], w=["tmpc"], scale=-1.0)
            B.act(tmpc, tmpc, AF.Ln, r=["tmpc", "one"], w=["tmpc"], bias=one_t[:, 0:1])
            B.ts(cneg, lam, -1.0, 0.0, ALU.mult, ALU.max, r=["lam"], w=["cneg"])
            B.tt(cneg, cneg, tmpc, ALU.add, r=["cneg", "tmpc"], w=["cneg"])
            B.ts(cneg2, cneg, -16.0, None, ALU.mult, r=["cneg"], w=["cneg2"])
            B.ts(cneg, cneg, -8.0, None, ALU.mult, r=["cneg"], w=["cneg"])

            XL = [[A.alloc([1027], F32) for _ in range(2)] for _ in range(2)]
            U2 = [[A.alloc([1024], F32) for _ in range(2)] for _ in range(2)]
            UB2 = [[A.alloc([1024], BF16) for _ in range(2)] for _ in range(2)]
            Rg2 = [[A.alloc([1024], F32) for _ in range(2)] for _ in range(2)]
            Ig2 = [[A.alloc([1024], F32) for _ in range(2)] for _ in range(2)]
            Aa2 = [[A.alloc([1024], F32) for _ in range(2)] for _ in range(2)]
            Hh2 = [[A.alloc([1024], F32) for _ in range(2)] for _ in range(2)]
            HO2 = [[A.alloc([256], F32) for _ in range(2)] for _ in range(2)]
            hprev = A.alloc([2], F32)
            wab = [A.alloc([2, 256], BF16) for _ in range(2)]
            wxb = [A.alloc([2, 256], BF16) for _ in range(2)]
            pieces = [(tt * 1024, 1024, "p", tt) for tt in range(4)] + [(SEQ, 32, "s", 0), (SEQ + 32, 32, "s", 1)]
            for nblk in range(16):
                wsl = nblk % 2
                B.dma("pool", wab[wsl], lru_wa[nblk].rearrange("(ch p) d -> p ch d", p=128), w=[("wa", wsl)])
                B.dma("pool", wxb[wsl], lru_wx[nblk].rearrange("(ch p) d -> p ch d", p=128), w=[("wx", wsl)])
                for pi, (col0, n, kind, idx) in enumerate(pieces):
                    xb = XL[pi % 2]
                    xprev = XL[(pi + 1) % 2]
                    pp = pi % 2
                    U, UB, Rg, Ig, Aa, Hh, HO = U2[pp], UB2[pp], Rg2[pp], Ig2[pp], Aa2[pp], Hh2[pp], HO2[pp]
                    for ch in range(2):
                        chunk = 2 * nblk + ch
                        xk = ("xl", pi % 2, ch)
                        B.dma("sp", xb[ch][:, 3:3 + n], xlT_scr[chunk, :, col0:col0 + n], w=[xk])
                        if kind == "p" and idx == 0:
                            S.add("dve", lambda e, t=xb[ch]: e.memset(t[:, 0:3], 0.0), w=[xk])
                        elif kind == "p":
                            B.copy("dve", xb[ch][:, 0:3], xprev[ch][:, 1024:1027],
                                   r=[("xl", (pi + 1) % 2, ch)], w=[xk])
                        else:
                            B.copy("dve", xb[ch][:, 0:3], c0s[:, idx, chunk, :], r=["lc"], w=[xk])
                        u = U[ch]
                        uk = ("u", pp, ch)
                        B.ts(u[:, 0:n], xb[ch][:, 3:3 + n], cw[:, 3, chunk:chunk + 1], cb[:, chunk:chunk + 1],
                             ALU.mult, ALU.add, r=[xk, "lc"], w=[uk])
                        for j in (2, 1, 0):
                            B.stt(u[:, 0:n], xb[ch][:, j:j + n], cw[:, j, chunk:chunk + 1], u[:, 0:n],
                                  ALU.mult, ALU.add, r=[xk, "lc", uk], w=[uk])
                        B.copy("act", UB[ch][:, 0:n], u[:, 0:n], r=[uk], w=[("ub", pp, ch)])
                    halves = [(0, min(n, 512))] + ([(512, 512)] if n > 512 else [])
                    for dh in range(2):
                        chunk = 2 * nblk + dh
                        for (gbuf, wbuf_, bias, gk, wk) in ((Rg, wab, ba, "rg", "wa"), (Ig, wxb, bx, "ig", "wx")):
                            for (h0, hn) in halves:
                                pb = next_ps()
                                ps = psum[pb][:, 0:hn]
                                B.mm_group(ps, [(wbuf_[wsl][:, ch, dh * 128:(dh + 1) * 128], UB[ch][:, h0:h0 + hn])
                                                for ch in range(2)],
                                           r=[("ub", pp, 0), ("ub", pp, 1), (wk, wsl)], w=[("ps", pb)])
                                B.act(gbuf[dh][:, h0:h0 + hn], ps, AF.Sigmoid, r=[("ps", pb), "lc"],
                                      w=[(gk, pp, dh)], bias=bias[:, chunk:chunk + 1])
                    for dh in range(2):
                        chunk = 2 * nblk + dh
                        B.act(Aa[dh][:, 0:n], Rg[dh][:, 0:n], AF.Exp, r=[("rg", pp, dh), "cneg"], w=[("aa", pp, dh)],
                              scale=cneg[:, chunk:chunk + 1])
                        B.act(Rg[dh][:, 0:n], Rg[dh][:, 0:n], AF.Exp, r=[("rg", pp, dh), "cneg2"], w=[("rg", pp, dh)],
                              scale=cneg2[:, chunk:chunk + 1])
                    for dh in range(2):
                        B.ts(Rg[dh][:, 0:n], Rg[dh][:, 0:n], 1.0, -1.0, ALU.min, ALU.mult, r=[("rg", pp, dh)],
                             w=[("rg", pp, dh)])
                        B.act(Rg[dh][:, 0:n], Rg[dh][:, 0:n], AF.Sqrt, r=[("rg", pp, dh), "one"], w=[("rg", pp, dh)],
                              scale=1.0, bias=one_t[:, 0:1])
                    for dh in range(2):
                        chunk = 2 * nblk + dh
                        B.tt(Ig[dh][:, 0:n], Ig[dh][:, 0:n], U[dh][:, 0:n], ALU.mult, r=[("ig", pp, dh), ("u", pp, dh)],
                             w=[("ig", pp, dh)])
                        B.tt(Ig[dh][:, 0:n], Ig[dh][:, 0:n], Rg[dh][:, 0:n], ALU.mult, r=[("ig", pp, dh), ("rg", pp, dh)],
                             w=[("ig", pp, dh)])
                        if kind == "p" and idx == 0:
                            init = 0.0
                            ir = []
                        elif kind == "p":
                            init = hprev[:, dh:dh + 1]
                            ir = [("hp", dh)]
                        else:
                            init = h0s[:, idx, chunk:chunk + 1]
                            ir = ["lc"]
                        S.add("dve", lambda e, dh=dh, n=n, init=init, Hh=Hh, Aa=Aa, Ig=Ig: e.tensor_tensor_scan(
                            out=Hh[dh][:, 0:n], data0=Aa[dh][:, 0:n], data1=Ig[dh][:, 0:n], initial=init,
                            op0=ALU.mult, op1=ALU.add), r=[("aa", pp, dh), ("ig", pp, dh)] + ir, w=[("hh", pp, dh)])
                        if kind == "p" and idx < 3:
                            B.copy("dve", hprev[:, dh:dh + 1], Hh[dh][:, n - 1:n], r=[("hh", pp, dh)], w=[("hp", dh)])
                        if kind == "s" or idx == 3:
                            row = 0 if kind == "p" else 1 + idx
                            B.copy("dve", hfin[:, row, chunk:chunk + 1], Hh[dh][:, n - 1:n], r=[("hh", pp, dh)],
                                   w=["hfin"])
                            for j in range(3):
                                B.copy("dve", cfin[:, row, j, chunk:chunk + 1],
                                       XL[pi % 2][dh][:, n + j:n + j + 1], r=[("xl", pi % 2, dh)], w=["cfin"])
                        ho = HO[dh]
                        hk = ("ho", pp, dh)
                        if kind == "p":
                            for il in range(2):
                                B.ts(ho[:, il * 128:(il + 1) * 128], Hh[dh][:, (4 * il) * 128:(4 * il + 1) * 128],
                                     sel[:, 0:1], None, ALU.mult, r=[("hh", pp, dh), "lc"], w=[hk])
                                for j in range(1, 4):
                                    B.stt(ho[:, il * 128:(il + 1) * 128],
                                          Hh[dh][:, (4 * il + j) * 128:(4 * il + j + 1) * 128], sel[:, j:j + 1],
                                          ho[:, il * 128:(il + 1) * 128], ALU.mult, ALU.add,
                                          r=[("hh", pp, dh), "lc", hk], w=[hk])
                            B.dma("act", hown_scr[chunk, :, idx * 256:(idx + 1) * 256], ho, r=[hk], w=[B.fresh("ho")])
                        else:
                            B.dma("act", hown_scr[chunk, :, 1024 + idx * 32:1024 + (idx + 1) * 32], Hh[dh][:, 0:32],
                                  r=[("hh", pp, dh)], w=[B.fresh("ho")])
            fst = A.alloc([12, 128], F32)
            pbv = psum[4][:, :].rearrange("p (a b) -> p a b", a=4)
            pbv2 = psum[5][:, :].rearrange("p (a b) -> p a b", a=4)
            pbv3 = psum[6][:, :].rearrange("p (a b) -> p a b", a=4)
            for row in range(3):
                pv = (pbv, pbv2, pbv3)[row]
                pk = ("ps", 4 + row)
                B.transpose(pv[0:32, 0, :], hfin[:, row, :], ident_f, r=["hfin", "ident_f"], w=[pk])
                for j in range(3):
                    B.transpose(pv[0:32, 1 + j, :], cfin[:, row, j, :], ident_f, r=["cfin", "ident_f"], w=[pk])
                B.copy("dve", fst[0:32, row * 4:(row + 1) * 4, :], pv[0:32, :, :], r=[pk], w=["fst"])
                B.dma("sp", o_lru[row].rearrange("(kc p) -> kc p", p=128), fst[0:32, row * 4, :], r=["fst"],
                      w=[B.fresh("o")])
                for j in range(3):
                    B.dma("sp", o_conv[row, j].rearrange("(kc p) -> kc p", p=128), fst[0:32, row * 4 + 1 + j, :],
                          r=["fst"], w=[B.fresh("o")])
            S.barrier()
            A.release(m0)

        if "own" in ph:
            xown = inp("xown", [TO, D])
            w_in = inp("w_in", [D, IN_COLS])
            cs_own = inp("cs_own", [TO, 32])
            m0 = A.mark()
            gmixT = A.alloc([KC], F32)
            B.dma("sp", gmixT, inp("norm_mixT", [128, KC]), w=["gT"])
            xnT = A.alloc([KC, TO], BF16)
            wbufs = [A.alloc([KC, 512], BF16) for _ in range(2)]
            xt = A.alloc([D], F32)
            junk = A.alloc([D], BF16)
            ssb = [A.alloc([1], F32) for _ in range(2)]
            st = make_headproc(3)
            cst = [A.alloc([32], F32) for _ in range(9)]
            blocks = [(xown[j * 128:(j + 1) * 128, :], 128) for j in range(8)] + [(xown[1024:1088, :], 64)]
            brow = [b[1] for b in blocks]

            def extra(bi, rows):
                B.dma("sp", cst[bi][0:rows], cs_own[bi * 128:bi * 128 + rows, :], w=[("cst", bi)])
            norm_transpose(blocks, xnT, gmixT, xt, junk, ssb, extra)

            def evac(tag, bi, rows, t0, ps, pstok, ncols):
                kind, g = tag
                if kind == "q":
                    def dstT(T_, nh, rows_, tk):
                        B.dma("act", QT_scr[g * 4:g * 4 + 4, :, t0:t0 + rows_].rearrange("h d t -> d h t"),
                              T_[:, 0:4, 0:rows_], r=[tk], w=[B.fresh("qt")])
                    headproc(st, ps, pstok, rows, ncols, "q", cst[bi], ("cst", bi), None, dstT)
                elif kind == "iq":
                    def dstT(T_, nh, rows_, tk):
                        B.dma("act", iqT_scr[g * 4:g * 4 + 4, :, t0:t0 + rows_].rearrange("h d t -> d h t"),
                              T_[:, 0:4, 0:rows_], r=[tk], w=[B.fresh("iqt")])
                    headproc(st, ps, pstok, rows, ncols, None, cst[bi], ("cst", bi), None, dstT)
                else:
                    i = stc[0] % st["n"]
                    stc[0] += 1
                    tk = ("st", i)
                    B.copy("act", st["f"][i][0:rows, 0:32], ps, r=[pstok], w=[tk])
                    B.dma("sp", iw_scr[t0:t0 + rows, :], st["f"][i][0:rows, 0:32], r=[tk], w=[B.fresh("iw")])
            groups = [(C_Q + g * 512, 512, ("q", g)) for g in range(8)] + \
                     [(C_IQ + g * 512, 512, ("iq", g)) for g in range(8)] + [(C_IW, 32, ("iw", 0))]
            gemm_tok(xnT, brow, KC, wbufs, lambda c0, n: w_in[:, c0:c0 + n], groups, evac)

            ttiles = [(0, 512), (512, 512), (1024, 64)]
            NE = 2
            ey = [A.alloc([512], F32) for _ in range(NE)]
            et = [A.alloc([512], F32) for _ in range(NE)]
            eh = [A.alloc([512], F32) for _ in range(NE)]
            eb = [A.alloc([512], BF16) for _ in range(NE)]
            ec = [0]

            def evac_feat(which):
                def ev(chunk, tt0, n, pss):
                    ps, pstok = pss[0]
                    i = ec[0] % NE
                    ec[0] += 1
                    tk = ("e", i)
                    if which == "yl":
                        y, t, hh, ob = ey[i], et[i], eh[i], eb[i]
                        B.dma("sp", hh[:, 0:n], hown_scr[chunk, :, tt0:tt0 + n], w=[("eh", i)])
                        B.copy("act", y[:, 0:n], ps, r=[pstok], w=[tk])
                        B.tt(t[:, 0:n], y[:, 0:n], y[:, 0:n], ALU.mult, r=[tk], w=[tk])
                        B.ts(t[:, 0:n], t[:, 0:n], 0.044715, 1.0, ALU.mult, ALU.add, r=[tk], w=[tk])
                        B.tt(t[:, 0:n], t[:, 0:n], y[:, 0:n], ALU.mult, r=[tk], w=[tk])
                        B.act(t[:, 0:n], t[:, 0:n], AF.Sigmoid, r=[tk], w=[tk], scale=1.5957691216057308)
                        B.tt(t[:, 0:n], t[:, 0:n], y[:, 0:n], ALU.mult, r=[tk], w=[tk])
                        B.tt(ob[:, 0:n], t[:, 0:n], hh[:, 0:n], ALU.mult, r=[tk, ("eh", i)], w=[tk])
                        B.dma("act", olruT_scr[chunk, :, tt0:tt0 + n], ob[:, 0:n], r=[tk], w=[B.fresh("ol")])
                    else:
                        y = ey[i]
                        B.act(y[:, 0:n], ps, AF.Sigmoid, r=[pstok], w=[tk])
                        dst = saT_scr if which == "ga" else slT_scr
                        B.dma("act", dst[chunk, :, tt0:tt0 + n], y[:, 0:n], r=[tk], w=[B.fresh("sg")])
                return ev
            for which, cbase in (("yl", C_YL), ("ga", C_GA), ("gl", C_GL)):
                gemm_feat([(xnT, wbufs, lambda c0, n, cbase=cbase: w_in[:, cbase + c0:cbase + c0 + n])], 9, KC,
                          4096, 512, ttiles, evac_feat(which))
            S.barrier()
            A.release(m0)

        if "attn" in ph:
            dsel_in = inp("dsel", [128, 32 * 128])
            mbias_in = inp("maskbias", [128, 512])
            m0 = A.mark()
            dselt = A.alloc([32, 128], BF16)
            dv = dsel_in.rearrange("p (g t) -> p g t", g=32)
            for g0 in range(0, 32, 8):
                B.dma("pool", dselt[:, g0:g0 + 8, :], dv[:, g0:g0 + 8, :], w=[("dsel", g0)])
            S.add("dve", lambda e: e.memset(eps_t, EPS), r=[("dsel", g0) for g0 in range(0, 32, 8)], w=["dsel"])
            mbias = A.alloc([512], F32)
            B.dma("sp", mbias, mbias_in, w=["mbias"])
            ones_b = A.alloc([128], BF16)
            S.add("dve", lambda e: e.memset(ones_b, 1.0), w=["ones"])
            gq = A.alloc([128], F32)
            gk = A.alloc([128], F32)
            B.dma("sp", gq, inp("norm_q", [HD]).partition_broadcast(128), w=["gq"])
            B.dma("sp", gk, inp("norm_k", [HD]).partition_broadcast(128), w=["gk"])
            cq = A.alloc([1], F32)
            ck = A.alloc([1], F32)
            S.add("dve", lambda e: e.tensor_reduce(out=cq, in_=gq, axis=AX.X, op=ALU.max, apply_absolute_value=True),
                  r=["gq"], w=["cq"])
            S.add("dve", lambda e: e.tensor_reduce(out=ck, in_=gk, axis=AX.X, op=ALU.max, apply_absolute_value=True),
                  r=["gk"], w=["ck"])
            B.tt(cq, cq, ck, ALU.mult, r=["cq", "ck"], w=["cq"])
            B.ts(cq, cq, -math.sqrt(128.0), None, ALU.mult, r=["cq"], w=["cq"])

            iqTb = A.alloc([32, 128], BF16)
            iqTg = A.alloc([32, 128], BF16)
            ikT = A.alloc([SEQ], BF16)
            wsel = A.alloc([32, 128], BF16)
            iwb = A.alloc([32], F32)
            iwrep = A.alloc([32, 4], BF16)
            R1 = [A.alloc([512], BF16) for _ in range(3)]
            ImB = [A.alloc([SEQ], F32) for _ in range(2)]
            Wk = A.alloc([SEQ], F32)
            m8 = A.alloc([8], F32)
            thrB = [A.alloc([1], F32) for _ in range(2)]
            maskb = A.alloc([SEQ], BF16)
            maskTB = [A.alloc([32, 128], BF16) for _ in range(2)]
            KTg = [A.alloc([SEQ], BF16) for _ in range(2)]
            Vg = [A.alloc([32, 128], BF16) for _ in range(2)]
            QTg = [A.alloc([512], BF16) for _ in range(2)]
            Pt = [A.alloc([512], BF16) for _ in range(3)]
            Pm = [A.alloc([512], BF16) for _ in range(3)]
            NEV = 8
            zs = [A.alloc([512], F32) for _ in range(NEV)]
            osb = [A.alloc([512], F32) for _ in range(NEV)]
            evc = [0]
            ot = [A.alloc([512], BF16) for _ in range(2)]
            sc_att = 1.0 / math.sqrt(128.0)

            qblocks = [("p", i, i * 128, 128, 512 * (i + 1)) for i in range(7, -1, -1)] + \
                      [("s", sq, 1024 + sq * 32, 32, SS) for sq in range(2)]
            r1c = [0]
            pc = [0]
            gct = [0]

            def srcs(kind, idx):
                if kind == "p":
                    return ikT_scr, KT_scr, V_scr
                return ikT_s[idx], KT_s[idx], V_s[idx]

            def stageA(blk, par):
                kind, idx, t0, R, Sk = blk
                G = R // 4
                Im = ImB[par]
                imk = ("Im", par)
                ikT_src = srcs(kind, idx)[0]
                iqv = iqTb.rearrange("p a b -> p (a b)")[:, 0:32 * R].rearrange("p (h t) -> p h t", h=32)
                B.dma("sp", iqv, iqT_scr[:, :, t0:t0 + R].rearrange("h d t -> d h t"), w=["iqTb"])
                B.copy("pool", iqTg[:, 0:G, :].rearrange("p g (h t) -> p g h t", t=4),
                       iqv.rearrange("p h (g t) -> p g h t", t=4), r=["iqTb"], w=["iqTg"])
                B.dma("sp", ikT[:, 0:Sk], ikT_src[:, 0:Sk], w=["ikT"])
                B.dma("sp", iwb[0:R], iw_scr[t0:t0 + R, :], w=["iwb"])
                B.copy("pool", iwrep[0:R], iwb[0:R].unsqueeze(2).to_broadcast([R, 32, 4]), r=["iwb"], w=["iwrep"])
                pT = psum[3][:, :].bitcast(BF16)
                B.transpose(pT[:, 0:R], iwrep[0:R].rearrange("p h t -> p (h t)"), ident_b[0:R, 0:R],
                            r=["iwrep", "ident_b"], w=[("ps", 3)])
                B.tt(wsel[:, 0:G, 0:R], dselt[:, 0:G, 0:R], pT[:, 0:R].unsqueeze(1).to_broadcast([128, G, R]),
                     ALU.mult, r=[("ps", 3), "dsel"], w=["wsel"])
                chunks = [(c0, min(512, Sk - c0)) for c0 in range(0, Sk, 512)]
                items = [(c0, cn, g) for (c0, cn) in chunks for g in range(G)]
                pend = None
                for it in range(len(items) + 1):
                    cur_item = None
                    if it < len(items):
                        c0, cn, g = items[it]
                        pb = r1c[0] % 2
                        rr = R1[r1c[0] % 3]
                        rk = ("r1", r1c[0] % 3)
                        r1c[0] += 1
                        ps1 = psum[pb][:, 0:cn]
                        B.mm(ps1, iqTg[:, g, :], ikT[:, c0:c0 + cn], True, True, r=["iqTg", "ikT"], w=[("ps", pb)])
                        B.act(rr[:, 0:cn], ps1, AF.Relu, r=[("ps", pb)], w=[rk])
                        cur_item = (c0, cn, g, rr, rk)
                    if pend is not None:
                        c0p, cnp, gp, rrp, rkp = pend
                        psI = psum[2][0:R, 0:cnp]
                        B.mm(psI, wsel[:, gp, 0:R], rrp[:, 0:cnp], gp == 0, gp == G - 1, r=["wsel", rkp], w=[("ps", 2)])
                        if gp == G - 1:
                            B.copy("act", Im[0:R, c0p:c0p + cnp], psI, r=[("ps", 2)], w=[imk])
                    pend = cur_item

            def stageB1(blk, par):
                kind, idx, t0, R, Sk = blk
                Im = ImB[par]
                imk = ("Im", par)
                if kind == "p":
                    B.tt(Im[0:R, Sk - 512:Sk], Im[0:R, Sk - 512:Sk], mbias[0:R, 0:512], ALU.add,
                         r=[imk, "mbias"], w=[imk])
                cur = Im
                for rnd in range(32):
                    S.add("dve", lambda e, cur=cur, R=R, Sk=Sk: e.max(out=m8[0:R], in_=cur[0:R, 0:Sk]),
                          r=[imk, "Wk"], w=["m8"])
                    if rnd < 31:
                        S.add("dve", lambda e, cur=cur, R=R, Sk=Sk: e.match_replace(
                            out=Wk[0:R, 0:Sk], in_to_replace=m8[0:R], in_values=cur[0:R, 0:Sk], imm_value=-3.0e38),
                            r=[imk, "Wk", "m8"], w=["Wk"])
                        cur = Wk
                B.ts(thrB[par][0:R], m8[0:R, 7:8], -5.0e29, None, ALU.max, r=["m8"], w=[("thr", par)])

            def stageB2(blk, par):
                kind, idx, t0, R, Sk = blk
                Im = ImB[par]
                maskT = maskTB[par]
                mk = ("maskT", par)
                B.ts(maskb[0:R, 0:Sk], Im[0:R, 0:Sk], thrB[par][0:R], None, ALU.is_ge,
                     r=[("Im", par), ("thr", par)], w=["maskb"])
                nkb = (Sk + 127) // 128
                for k0 in range(0, nkb, 8):
                    pT8 = psum[3][:, :].bitcast(BF16).rearrange("p (a b) -> p a b", a=8)
                    k1 = min(nkb, k0 + 8)
                    for kb_ in range(k0, k1):
                        kn = min(128, Sk - kb_ * 128)
                        B.transpose(pT8[0:kn, kb_ - k0, 0:R], maskb[0:R, kb_ * 128:kb_ * 128 + kn], ident_b[0:R, 0:R],
                                    r=["maskb", "ident_b"], w=[("ps", 3)])
                    kfull = [kb_ for kb_ in range(k0, k1) if Sk - kb_ * 128 >= 128]
                    if kfull:
                        B.act(maskT[:, kfull[0]:kfull[-1] + 1, 0:R], pT8[:, 0:len(kfull), 0:R], AF.Copy,
                              r=[("ps", 3)], w=[mk], scale=30000.0, bias=-30000.0)
                    if len(kfull) < k1 - k0:
                        kb_ = k1 - 1
                        kn = Sk - kb_ * 128
                        B.act(maskT[0:kn, kb_, 0:R], pT8[0:kn, kb_ - k0, 0:R], AF.Copy, r=[("ps", 3)], w=[mk],
                              scale=30000.0, bias=-30000.0)

            def stageC(blk, par):
                kind, idx, t0, R, Sk = blk
                maskT = maskTB[par]
                mk = ("maskT", par)
                _, KT_src, V_src = srcs(kind, idx)
                nkb = (Sk + 127) // 128
                for g in range(NKV):
                    sl = gct[0] % 2
                    gct[0] += 1
                    B.dma("sp", KTg[sl][:, 0:Sk], KT_src[g, :, 0:Sk], w=[("KTg", sl)])
                    nfull = Sk // 128
                    B.dma("act", Vg[sl][:, 0:nfull, :],
                          V_src[0:nfull * 128, g * 128:(g + 1) * 128].rearrange("(b p) d -> p b d", p=128),
                          w=[("Vg", sl)])
                    if Sk % 128:
                        kn = Sk % 128
                        B.dma("act", Vg[sl][0:kn, nfull, :], V_src[nfull * 128:Sk, g * 128:(g + 1) * 128],
                              w=[("Vg", sl)])
                    B.dma("sp", QTg[sl][:, 0:4 * R].rearrange("p (h t) -> p h t", h=4),
                          QT_scr[4 * g:4 * g + 4, :, t0:t0 + R].rearrange("h d t -> d h t"), w=[("QTg", sl)])
                    N4 = 4 * R
                    psO = psum[6][:, 0:N4]
                    psZ = psum[7][:, 0:N4]
                    pendc = None
                    for it in range(nkb + 1):
                        curc = None
                        if it < nkb:
                            kb_ = it
                            kn = min(128, Sk - kb_ * 128)
                            pb = 4 + (pc[0] % 2)
                            pi_ = pc[0] % 3
                            pc[0] += 1
                            psL = psum[pb][0:kn, 0:N4]
                            B.mm(psL, KTg[sl][:, kb_ * 128:kb_ * 128 + kn], QTg[sl][:, 0:4 * R], True, False,
                                 r=[("KTg", sl), ("QTg", sl)], w=[("ps", pb)])
                            for h in range(4):
                                B.mm(psum[pb][0:kn, h * R:(h + 1) * R], ident_b[0:kn, 0:kn], maskT[0:kn, kb_, 0:R],
                                     False, h == 3, r=["ident_b", mk], w=[("ps", pb)])
                            B.act(Pt[pi_][0:kn, 0:N4], psL, AF.Exp, r=[("ps", pb), "cq"], w=[("pt", pi_)],
                                  scale=sc_att, bias=cq[0:kn, 0:1])
                            curc = (kb_, kn, pi_)
                        if pendc is not None:
                            kbp, knp, pip = pendc
                            B.mm(psO, Vg[sl][0:knp, kbp, :], Pt[pip][0:knp, 0:N4], kbp == 0, kbp == nkb - 1,
                                 r=[("Vg", sl), ("pt", pip)], w=[("ps", 6)])
                            B.mm(psZ, ones_b[0:knp, :], Pt[pip][0:knp, 0:N4], kbp == 0, kbp == nkb - 1,
                                 r=["ones", ("pt", pip)], w=[("ps", 7)])
                        pendc = curc
                    ei = evc[0] % NEV
                    evc[0] += 1
                    B.copy("act", zs[ei][:, 0:N4], psZ, r=[("ps", 7)], w=[("zs", ei)])
                    B.copy("act", osb[ei][:, 0:N4], psO, r=[("ps", 6)], w=[("os", ei)])
                    S.add("dve", lambda e, ei=ei, N4=N4: e.reciprocal(out=zs[ei][:, 0:N4], in_=zs[ei][:, 0:N4]),
                          r=[("zs", ei)], w=[("zs", ei)])
                    B.tt(ot[sl][:, 0:N4], osb[ei][:, 0:N4], zs[ei][:, 0:N4], ALU.mult, r=[("os", ei), ("zs", ei)],
                         w=[("ot", sl)])
                    B.dma("act", oattT_scr[4 * g:4 * g + 4, :, t0:t0 + R].rearrange("h d t -> d h t"),
                          ot[sl][:, 0:N4].rearrange("p (h t) -> p h t", h=4), r=[("ot", sl)], w=[B.fresh("oa")])

            NBK = len(qblocks)
            for step in range(NBK + 2):
                if step < NBK:
                    stageA(qblocks[step], step % 2)
                if 0 <= step - 1 < NBK:
                    stageB1(qblocks[step - 1], (step - 1) % 2)
                if 0 <= step - 2 < NBK:
                    stageC(qblocks[step - 2], (step - 2) % 2)
                if 0 <= step - 1 < NBK:
                    stageB2(qblocks[step - 1], (step - 1) % 2)
            S.barrier()
            A.release(m0)

        if "mix" in ph:
            w_branch = inp("w_branch", [2 * D, D])
            w_out = inp("w_out", [D, D])
            xown = inp("xown", [TO, D])
            m0 = A.mark()
            oaT = A.alloc([KC, TO], BF16)
            olT = A.alloc([KC, TO], BF16)
            wA = [A.alloc([KC, 128], BF16) for _ in range(2)]
            wL = [A.alloc([KC, 128], BF16) for _ in range(2)]
            for h0 in range(0, 32, 8):
                B.dma("sp", oaT[:, h0:h0 + 8, :], oattT_scr[h0:h0 + 8].rearrange("h d t -> d h t"),
                      w=[("ATx", h0)])
                B.dma("sp", olT[:, h0:h0 + 8, :], olruT_scr[h0:h0 + 8].rearrange("h d t -> d h t"),
                      w=[("ATy", h0)])
            ATR = [("ATx", h0) for h0 in (0, 8, 16, 24)] + [("ATy", h0) for h0 in (0, 8, 16, 24)]
            ttiles = [(0, 512), (512, 512), (1024, 64)]
            NE = 3
            sa_t = [A.alloc([512], F32) for _ in range(NE)]
            sl_t = [A.alloc([512], F32) for _ in range(NE)]
            ta_t = [A.alloc([512], F32) for _ in range(NE)]
            mb_t = [A.alloc([512], BF16) for _ in range(NE)]
            ec = [0]

            def evac_mix(chunk, tt0, n, pss):
                (psA, tokA), (psL, tokL) = pss
                i = ec[0] % NE
                ec[0] += 1
                tk = ("e", i)
                B.dma("sp", sa_t[i][:, 0:n], saT_scr[chunk, :, tt0:tt0 + n], w=[("esa", i)])
                B.dma("sp", sl_t[i][:, 0:n], slT_scr[chunk, :, tt0:tt0 + n], w=[("esl", i)])
                B.tt(ta_t[i][:, 0:n], psA, sa_t[i][:, 0:n], ALU.mult, r=[tokA, ("esa", i)], w=[tk])
                B.tt(sl_t[i][:, 0:n], psL, sl_t[i][:, 0:n], ALU.mult, r=[tokL, ("esl", i)], w=[("esl", i)])
                B.tt(mb_t[i][:, 0:n], ta_t[i][:, 0:n], sl_t[i][:, 0:n], ALU.add, r=[tk, ("esl", i)], w=[tk])
                B.dma("act", mixT_scr[chunk, :, tt0:tt0 + n], mb_t[i][:, 0:n], r=[tk], w=[B.fresh("mx")])
            gemm_feat([(oaT, wA, lambda c0, n: w_branch[0:D, c0:c0 + n]),
                       (olT, wL, lambda c0, n: w_branch[D:2 * D, c0:c0 + n])], 9, KC, 4096, 128, ttiles, evac_mix,
                      attoks=ATR)
            S.barrier()
            A.release(m0)

            m0 = A.mark()
            mxT = A.alloc([KC, TO], BF16)
            wbufs = [A.alloc([KC, 512], BF16) for _ in range(2)]
            for h0 in range(0, 32, 8):
                B.dma("sp", mxT[:, h0:h0 + 8, :], mixT_scr[h0:h0 + 8].rearrange("h d t -> d h t"), w=[("ATz", h0)])
            NE = 3
            xr = [A.alloc([512], F32) for _ in range(NE)]

            def evac_out(tag, bi, rows, t0, ps, pstok, ncols):
                i = ec[0] % NE
                ec[0] += 1
                c0 = tag
                B.dma("sp", xr[i][0:rows], xown[t0:t0 + rows, c0:c0 + 512], w=[("xr", i)])
                B.tt(xr[i][0:rows], ps, xr[i][0:rows], ALU.add, r=[pstok, ("xr", i)], w=[("xr", i)])
                B.dma("act", x1_scr[t0:t0 + rows, c0:c0 + 512], xr[i][0:rows], r=[("xr", i)], w=[B.fresh("x1")])
            gemm_tok(mxT, [128] * 8 + [64], KC, wbufs, lambda c0, n: w_out[:, c0:c0 + n],
                     [(g * 512, 512, g * 512) for g in range(8)], evac_out,
                     attoks=[("ATz", h0) for h0 in (0, 8, 16, 24)])
            S.barrier()
            A.release(m0)

        if "ffn" in ph:
            w_gate = inp("w_gate", [D, DFF])
            w_up = inp("w_up", [D, DFF])
            w_down = inp("w_down", [DFF, D])
            y_own = B.dout("y_own", [TO, D])
            m0 = A.mark()
            gffT = A.alloc([KC], F32)
            B.dma("sp", gffT, inp("norm_ffnT", [128, KC]), w=["gT"])
            xn2T = A.alloc([KC, TO], BF16)
            xt = A.alloc([D], F32)
            junk = A.alloc([D], BF16)
            ssb = [A.alloc([1], F32) for _ in range(2)]
            blocks = [(x1_scr[j * 128:(j + 1) * 128, :], 128) for j in range(8)] + [(x1_scr[1024:1088, :], 64)]
            norm_transpose(blocks, xn2T, gffT, xt, junk, ssb)
            wG = [A.alloc([KC, 256], BF16) for _ in range(2)]
            wU = [A.alloc([KC, 256], BF16) for _ in range(2)]
            ttiles = [(0, 512), (512, 512), (1024, 64)]
            NE = 3
            sg_t = [A.alloc([512], F32) for _ in range(NE)]
            hb_t = [A.alloc([512], BF16) for _ in range(NE)]
            ec = [0]

            def evac_ffn(chunk, tt0, n, pss):
                (psG, tokG), (psU, tokU) = pss
                i = ec[0] % NE
                ec[0] += 1
                tk = ("e", i)
                B.act(sg_t[i][:, 0:n], psG, AF.Silu, r=[tokG], w=[tk])
                B.tt(hb_t[i][:, 0:n], sg_t[i][:, 0:n], psU, ALU.mult, r=[tk, tokU], w=[tk])
                B.dma("act", hT_scr[chunk, :, tt0:tt0 + n], hb_t[i][:, 0:n], r=[tk], w=[B.fresh("ht")])
            gemm_feat([(xn2T, wG, lambda c0, n: w_gate[:, c0:c0 + n]),
                       (xn2T, wU, lambda c0, n: w_up[:, c0:c0 + n])], 9, KC, DFF, 256, ttiles, evac_ffn)
            S.barrier()
            A.release(m0)

            m0 = A.mark()
            hT = A.alloc([KC, TO], BF16)
            wbufs = [A.alloc([KC, 512], BF16) for _ in range(2)]
            NE = 3
            xr = [A.alloc([512], F32) for _ in range(NE)]
            pieces = [(0, 32), (32, 32), (64, 22)]
            for pi, (k0, kcn) in enumerate(pieces):
                step = 8
                toks = []
                for h0 in range(0, kcn, step):
                    h1 = min(kcn, h0 + step)
                    tk = ("ATz", h0)
                    toks.append(tk)
                    B.dma("sp", hT[:, h0:h1, :], hT_scr[k0 + h0:k0 + h1].rearrange("h d t -> d h t"), w=[tk])
                src = x1_scr
                dst = y_own

                def evac_dn(tag, bi, rows, t0, ps, pstok, ncols, pi=pi):
                    i = ec[0] % NE
                    ec[0] += 1
                    c0 = tag
                    prev = x1_scr if pi == 0 else y_own
                    B.dma("sp", xr[i][0:rows], prev[t0:t0 + rows, c0:c0 + 512], r=[("y", bi, c0)], w=[("xr", i)])
                    B.tt(xr[i][0:rows], ps, xr[i][0:rows], ALU.add, r=[pstok, ("xr", i)], w=[("xr", i)])
                    B.dma("act", y_own[t0:t0 + rows, c0:c0 + 512], xr[i][0:rows], r=[("xr", i)], w=[("y", bi, c0)])
                gemm_tok(hT, [128] * 8 + [64], kcn, wbufs,
                         lambda c0, n, k0=k0, kcn=kcn: w_down[k0 * 128:(k0 + kcn) * 128, c0:c0 + n],
                         [(g * 512, 512, g * 512) for g in range(8)], evac_dn, attoks=toks)
            S.barrier()
            A.release(m0)

        with nc.Block() as block:
            S.emit(nc, block, esem, dsem)
    return nc


def rope_tables(pos):
    half = 16
    inv = (np.float32(500000.0) ** (-np.arange(half, dtype=np.float32) * np.float32(2.0) / np.float32(32))
           ).astype(np.float32)
    ang = pos.astype(np.float32)[:, None] * inv[None, :]
    return np.concatenate([np.cos(ang), np.sin(ang)], axis=1).astype(np.float32)


def make_in_maps(inp, names=None):
    f = np.float32
    in_maps = []
    pos_seq = np.concatenate([np.arange(SEQ), PAST + np.arange(32), PAST + np.arange(32)])
    cs_seq = rope_tables(pos_seq)
    ident = np.eye(128, dtype=f)
    dsel = np.zeros((32, 4, 32, 128), f)
    for g in range(32):
        for tl in range(4):
            dsel[:, tl, g, 4 * g + tl] = 1.0 / 64.0
    dsel = dsel.reshape(128, 32 * 128)

    def T32(v):
        return np.ascontiguousarray(np.asarray(v).reshape(KC, 128).T)

    for c in range(8):
        p, r = c // 4, c % 4
        xp = inp["x_prompt"][p]
        own_blocks = [xp[(4 * i + r) * 128:(4 * i + r + 1) * 128] for i in range(8)]
        xown = np.concatenate(own_blocks + [inp["x_sample"][2 * c], inp["x_sample"][2 * c + 1]], axis=0)
        pos_own = np.concatenate([np.arange((4 * i + r) * 128, (4 * i + r + 1) * 128) for i in range(8)] +
                                 [PAST + np.arange(32), PAST + np.arange(32)])
        sel = np.zeros((128, 4), f)
        sel[:, r] = 1.0
        tl = np.arange(128)[:, None]
        slx = np.arange(512)[None, :]
        mb = np.where(slx < 128 * r + 64 + 64 * (tl >= 64), 0.0, NEGM).astype(f)
        sc = inp["state_conv"][0, 2 * c:2 * c + 2]
        m = {
            "xseq": np.ascontiguousarray(xp),
            "xown": np.ascontiguousarray(xown),
            "w_in": inp["w_in"][0],
            "norm_mixT": T32(inp["norm_mix"][0]),
            "norm_ffnT": T32(inp["norm_ffn"][0]),
            "norm_q": inp["norm_q"][0],
            "norm_k": inp["norm_k"][0],
            "norm_idx_k": inp["norm_idx_k"][0],
            "cs_seq": cs_seq,
            "cs_own": rope_tables(pos_own),
            "ident": ident,
            "dsel": dsel,
            "maskbias": mb,
            "cache_k": np.ascontiguousarray(inp["cache_k"][0, 2 * c:2 * c + 2].reshape(2, PAST, NKV * HD)),
            "cache_v": np.ascontiguousarray(inp["cache_v"][0, 2 * c:2 * c + 2].reshape(2, PAST, NKV * HD)),
            "cache_ik": np.ascontiguousarray(inp["cache_idx_k"][0, 2 * c:2 * c + 2]),
            "state_lruT": np.stack([T32(inp["state_lru"][0, 2 * c + q]) for q in range(2)]),
            "state_convT": np.ascontiguousarray(sc.reshape(2, 3, KC, 128).transpose(0, 3, 2, 1)),
            "convwT": np.ascontiguousarray(inp["conv_w"][0].reshape(4, KC, 128).transpose(2, 0, 1)),
            "convbT": T32(inp["conv_b"][0]),
            "lru_baT": T32(inp["lru_ba"][0]),
            "lru_bxT": T32(inp["lru_bx"][0]),
            "lru_lamT": T32(inp["lru_lambda"][0]),
            "lru_wa": inp["lru_wa"][0],
            "lru_wx": inp["lru_wx"][0],
            "selr": sel,
            "w_branch": inp["w_branch"][0],
            "w_out": inp["w_out"][0],
            "w_gate": inp["w_gate"][0],
            "w_up": inp["w_up"][0],
            "w_down": inp["w_down"][0],
        }
        if names is not None:
            m = {k: v for k, v in m.items() if k in names}
        in_maps.append(m)
    return in_maps


def input_names(nc):
    names = set()
    for alloc in nc.allocations:
        if isinstance(alloc, mybir.MemoryLocationSet) and alloc.kind == "ExternalInput":
            names.add(alloc.memorylocations[0].name)
    return names


_NC_CACHE = {}


def kernel(**inputs):
    inp = {k: np.asarray(v) for k, v in inputs.items()}
    if "nc" not in _NC_CACHE:
        _NC_CACHE["nc"] = build_program()
    nc = _NC_CACHE["nc"]
    in_maps = make_in_maps(inp, input_names(nc))
    res = run_bass_kernel_spmd(nc, in_maps, core_ids=list(range(8)))
    R = res.results
    f = np.float32
    y_prompt = np.zeros((2, SEQ, D), f)
    y_sample = np.zeros((16, DEC_SEQ, D), f)
    k_prompt = np.zeros((1, 2, SEQ, NKV, HD), f)
    v_prompt = np.zeros((1, 2, SEQ, NKV, HD), f)
    ik_prompt = np.zeros((1, 2, SEQ, HD), f)
    lru_prompt = np.zeros((1, 2, D), f)
    conv_prompt = np.zeros((1, 2, 3, D), f)
    k_sample = np.zeros((1, 16, DEC_SEQ, NKV, HD), f)
    v_sample = np.zeros((1, 16, DEC_SEQ, NKV, HD), f)
    ik_sample = np.zeros((1, 16, DEC_SEQ, HD), f)
    lru_sample = np.zeros((1, 16, D), f)
    conv_sample = np.zeros((1, 16, 3, D), f)
    for c in range(8):
        p, r = c // 4, c % 4
        o = R[c]
        y = o["y_own"]
        for i in range(8):
            b = 4 * i + r
            y_prompt[p, b * 128:(b + 1) * 128] = y[i * 128:(i + 1) * 128]
        for q in range(2):
            sq = 2 * c + q
            y_sample[sq] = y[1024 + q * 32:1024 + (q + 1) * 32]
            k_sample[0, sq] = o["o_k"][SEQ + q * 32:SEQ + (q + 1) * 32].reshape(32, NKV, HD)
            v_sample[0, sq] = o["o_v"][SEQ + q * 32:SEQ + (q + 1) * 32].reshape(32, NKV, HD)
            ik_sample[0, sq] = o["o_ik"][SEQ + q * 32:SEQ + (q + 1) * 32]
            lru_sample[0, sq] = o["o_lru"][1 + q]
            conv_sample[0, sq] = o["o_conv"][1 + q]
        if r == 0:
            k_prompt[0, p] = o["o_k"][:SEQ].reshape(SEQ, NKV, HD)
            v_prompt[0, p] = o["o_v"][:SEQ].reshape(SEQ, NKV, HD)
            ik_prompt[0, p] = o["o_ik"][:SEQ]
            lru_prompt[0, p] = o["o_lru"][0]
            conv_prompt[0, p] = o["o_conv"][0]
    return (y_prompt, y_sample, k_prompt, v_prompt, ik_prompt, lru_prompt, conv_prompt,
            k_sample, v_sample, ik_sample, lru_sample, conv_sample)
```
